# Optimizing a Trainium2 kernel written in Bass

```python
import math
import jax, jax.numpy as jnp
from jax import lax
import numpy as np

D_MODEL = 2048
BATCH = 4
SEQ = 2048
DEPTH = 4
DEC_BATCH = 32
DEC_SEQ = 1
PAST_LEN = 16384
PAGE_SIZE = 128

MIX_WIDTH = D_MODEL
HEAD_DIM = 64
ATTN_WIDTH = MIX_WIDTH // 2
N_HEADS = ATTN_WIDTH // HEAD_DIM
N_KV_HEADS = max(1, N_HEADS // 8)
GRP = N_HEADS // N_KV_HEADS
WINDOW = 128
SSM_WIDTH = MIX_WIDTH - ATTN_WIDTH
SSM_CH_GROUP = 16
SSM_GROUPS = SSM_WIDTH // SSM_CH_GROUP
SSM_STATE = 64
D_FF = -(-8 * D_MODEL // (3 * 256)) * 256
PLE_DIM = 256
Q_WIDTH = N_HEADS * HEAD_DIM
KV_WIDTH = N_KV_HEADS * HEAD_DIM
IN_WIDTH = Q_WIDTH + 2 * KV_WIDTH + SSM_WIDTH
NORM_EPS = 1e-6
NEG_INF = -1e30

kernel_name = 'hymba_swa_sink_s5_decoder_step'


def rms_norm(x, g):
    xf = x.astype(jnp.float32)
    var = jnp.mean(xf * xf, axis=-1, keepdims=True)
    return (xf * lax.rsqrt(var + NORM_EPS) * g.astype(jnp.float32)).astype(x.dtype)


def alibi_slopes():
    return 2.0 ** (-8.0 * jnp.arange(1, N_HEADS + 1, dtype=jnp.float32) / N_HEADS)


def sink_attend(q, keys, vals, dist, valid, sinks):
    s = jnp.einsum('...qkgd,...skd->...kgqs', q.astype(jnp.float32), keys.astype(jnp.float32)) * (HEAD_DIM ** -0.5)
    slopes = alibi_slopes().reshape(N_KV_HEADS, GRP, 1, 1)
    s = jnp.where(valid, s - slopes * dist, NEG_INF)
    sink = sinks.astype(jnp.float32).reshape(N_KV_HEADS, GRP, 1, 1)
    m = jnp.maximum(jnp.max(s, axis=-1, keepdims=True), sink)
    e = jnp.exp(s - m)
    p = e / (jnp.sum(e, axis=-1, keepdims=True) + jnp.exp(sink - m))
    out = jnp.einsum('...kgqs,...skd->...qkgd', p, vals.astype(jnp.float32))
    return out.astype(q.dtype)


def attn_prompt(q, k, v, sinks):
    n, t = q.shape[:2]
    nb = t // WINDOW
    qb = q.reshape(n, nb, WINDOW, N_KV_HEADS, GRP, HEAD_DIM)
    kb = k.reshape(n, nb, WINDOW, N_KV_HEADS, HEAD_DIM)
    vb = v.reshape(n, nb, WINDOW, N_KV_HEADS, HEAD_DIM)
    pad = ((0, 0), (1, 0), (0, 0), (0, 0), (0, 0))
    keys = jnp.concatenate([jnp.pad(kb, pad)[:, :-1], kb], axis=2)
    vals = jnp.concatenate([jnp.pad(vb, pad)[:, :-1], vb], axis=2)
    i = jnp.arange(WINDOW)[:, None]
    j = jnp.arange(2 * WINDOW)[None, :]
    dist = i - j + WINDOW
    blk = jnp.arange(nb)[:, None, None]
    valid = (dist >= 0) & (dist <= WINDOW) & ((blk > 0) | (j >= WINDOW))
    out = sink_attend(qb, keys, vals, dist.astype(jnp.float32), valid[:, None, None], sinks)
    return out.reshape(n, t, Q_WIDTH), k[:, -WINDOW:], v[:, -WINDOW:]


def attn_sample(q, k, v, sinks, k_buf, v_buf):
    n, t = q.shape[:2]
    keys = jnp.concatenate([k_buf.astype(k.dtype), k], axis=1)
    vals = jnp.concatenate([v_buf.astype(v.dtype), v], axis=1)
    i = jnp.arange(t)[:, None]
    j = jnp.arange(WINDOW + t)[None, :]
    dist = i - j + WINDOW
    valid = (dist >= 0) & (dist <= WINDOW)
    out = sink_attend(q, keys, vals, dist.astype(jnp.float32), valid, sinks)
    return out.reshape(n, t, Q_WIDTH), keys[:, -WINDOW:], vals[:, -WINDOW:]


def _cplx_combine(x, y):
    ar1, ai1, br1, bi1 = x
    ar2, ai2, br2, bi2 = y
    return (ar2 * ar1 - ai2 * ai1, ar2 * ai1 + ai2 * ar1,
            ar2 * br1 - ai2 * bi1 + br2, ar2 * bi1 + ai2 * br1 + bi2)


def ssm_mix(u, prm, s0):
    n, t = u.shape[:2]
    f32 = jnp.float32
    uf = u.astype(f32).reshape(n, t, SSM_GROUPS, SSM_CH_GROUP)
    a_re = prm['ssm_a_re'].astype(f32)
    a_im = prm['ssm_a_im'].astype(f32)
    dt = jnp.exp(prm['ssm_log_dt'].astype(f32))[:, None]
    dta_re, dta_im = dt * a_re, dt * a_im
    mag = jnp.exp(dta_re)
    ab_re, ab_im = mag * jnp.cos(dta_im), mag * jnp.sin(dta_im)
    den = a_re * a_re + a_im * a_im
    f_re = ((ab_re - 1.0) * a_re + ab_im * a_im) / den
    f_im = (ab_im * a_re - (ab_re - 1.0) * a_im) / den
    b_re = prm['ssm_b_re'].astype(f32)
    b_im = prm['ssm_b_im'].astype(f32)
    bb_re = f_re[..., None] * b_re - f_im[..., None] * b_im
    bb_im = f_re[..., None] * b_im + f_im[..., None] * b_re
    bu_re = jnp.einsum('ntgc,gpc->ntgp', uf, bb_re)
    bu_im = jnp.einsum('ntgc,gpc->ntgp', uf, bb_im)
    a_seq_re = jnp.broadcast_to(ab_re, (1, t) + ab_re.shape)
    a_seq_im = jnp.broadcast_to(ab_im, (1, t) + ab_im.shape)
    _, _, s_re, s_im = lax.associative_scan(_cplx_combine, (a_seq_re, a_seq_im, bu_re, bu_im), axis=1)
    if s0 is not None:
        kk = jnp.arange(1, t + 1, dtype=f32)[:, None, None]
        pmag = jnp.exp(kk * dta_re)
        pr, pi = pmag * jnp.cos(kk * dta_im), pmag * jnp.sin(kk * dta_im)
        s0r = s0[0].astype(f32)[:, None]
        s0i = s0[1].astype(f32)[:, None]
        s_re, s_im = s_re + pr * s0r - pi * s0i, s_im + pr * s0i + pi * s0r
    y = (jnp.einsum('ntgp,gcp->ntgc', s_re, prm['ssm_c_re'].astype(f32))
         - jnp.einsum('ntgp,gcp->ntgc', s_im, prm['ssm_c_im'].astype(f32))
         + prm['ssm_d'].astype(f32) * uf)
    z = jnp.einsum('ntgc,gce->ntge', jax.nn.gelu(y), prm['ssm_w_glu'].astype(f32))
    out = z[..., :SSM_CH_GROUP] * jax.nn.sigmoid(z[..., SSM_CH_GROUP:])
    return out.reshape(n, t, SSM_WIDTH).astype(u.dtype), s_re[:, -1], s_im[:, -1]


def trunk_layer(x, pe, prm, cache):
    n, t, _ = x.shape
    h = rms_norm(x, prm['g_pre_mix'])
    z = h @ prm['w_in']
    q = z[..., :Q_WIDTH].reshape(n, t, N_KV_HEADS, GRP, HEAD_DIM)
    k = z[..., Q_WIDTH:Q_WIDTH + KV_WIDTH].reshape(n, t, N_KV_HEADS, HEAD_DIM)
    v = z[..., Q_WIDTH + KV_WIDTH:Q_WIDTH + 2 * KV_WIDTH].reshape(n, t, N_KV_HEADS, HEAD_DIM)
    u = z[..., Q_WIDTH + 2 * KV_WIDTH:]
    if cache is None:
        attn, k_new, v_new = attn_prompt(q, k, v, prm['attn_sinks'])
        s0 = None
    else:
        k_buf, v_buf, s_re0, s_im0 = cache
        attn, k_new, v_new = attn_sample(q, k, v, prm['attn_sinks'], k_buf, v_buf)
        s0 = (s_re0, s_im0)
    ssm, s_re, s_im = ssm_mix(u, prm, s0)
    merged = jnp.concatenate([rms_norm(attn, prm['g_attn_out']), rms_norm(ssm, prm['g_ssm_out'])], axis=-1)
    x = x + rms_norm(merged @ prm['w_out'], prm['g_post_mix'])
    h = rms_norm(x, prm['g_pre_ffn'])
    gu = h @ prm['w_gate_up']
    f = (jax.nn.silu(gu[..., :D_FF]) * gu[..., D_FF:]) @ prm['w_down']
    x = x + rms_norm(f, prm['g_post_ffn'])
    x = x + jax.nn.sigmoid(x @ prm['w_ple_gate']) * (pe @ prm['w_ple_proj'])
    return x, k_new, v_new, s_re, s_im


def setup_inputs(seed: int = 0) -> dict:
    key = jax.random.key(seed)
    ks = jax.random.split(key, 40)
    f32 = jnp.float32

    def nrm(k, shape, scale=1.0):
        return scale * jax.random.normal(k, shape, f32)

    def gain(k, shape):
        return 1.0 + 0.05 * jax.random.normal(k, shape, f32)

    n_idx = jnp.arange(SSM_STATE, dtype=f32)
    return {
        'x_prompt': nrm(ks[0], (BATCH, SEQ, D_MODEL)),
        'x_sample': nrm(ks[1], (DEC_BATCH, DEC_SEQ, D_MODEL)),
        'cache_k': nrm(ks[2], (DEPTH, DEC_BATCH, WINDOW, N_KV_HEADS, HEAD_DIM)),
        'cache_v': nrm(ks[3], (DEPTH, DEC_BATCH, WINDOW, N_KV_HEADS, HEAD_DIM)),
        'state_ssm_re': nrm(ks[4], (DEPTH, DEC_BATCH, SSM_GROUPS, SSM_STATE), 0.5),
        'state_ssm_im': nrm(ks[5], (DEPTH, DEC_BATCH, SSM_GROUPS, SSM_STATE), 0.5),
        'p_prompt': nrm(ks[6], (DEPTH, BATCH, SEQ, PLE_DIM)),
        'p_sample': nrm(ks[7], (DEPTH, DEC_BATCH, DEC_SEQ, PLE_DIM)),
        'g_pre_mix': gain(ks[8], (DEPTH, D_MODEL)),
        'w_in': nrm(ks[9], (DEPTH, D_MODEL, IN_WIDTH), D_MODEL ** -0.5),
        'attn_sinks': nrm(ks[10], (DEPTH, N_HEADS), 0.5),
        'ssm_a_re': -0.5 + nrm(ks[11], (DEPTH, SSM_GROUPS, SSM_STATE), 0.01),
        'ssm_a_im': math.pi * n_idx + nrm(ks[12], (DEPTH, SSM_GROUPS, SSM_STATE), 0.01),
        'ssm_log_dt': jax.random.uniform(ks[13], (DEPTH, SSM_GROUPS), f32, minval=math.log(0.001), maxval=math.log(0.1)),
        'ssm_b_re': nrm(ks[14], (DEPTH, SSM_GROUPS, SSM_STATE, SSM_CH_GROUP), (2 * SSM_CH_GROUP) ** -0.5),
        'ssm_b_im': nrm(ks[15], (DEPTH, SSM_GROUPS, SSM_STATE, SSM_CH_GROUP), (2 * SSM_CH_GROUP) ** -0.5),
        'ssm_c_re': nrm(ks[16], (DEPTH, SSM_GROUPS, SSM_CH_GROUP, SSM_STATE), (2 * SSM_STATE) ** -0.5),
        'ssm_c_im': nrm(ks[17], (DEPTH, SSM_GROUPS, SSM_CH_GROUP, SSM_STATE), (2 * SSM_STATE) ** -0.5),
        'ssm_d': nrm(ks[18], (DEPTH, SSM_GROUPS, SSM_CH_GROUP)),
        'ssm_w_glu': nrm(ks[19], (DEPTH, SSM_GROUPS, SSM_CH_GROUP, 2 * SSM_CH_GROUP), SSM_CH_GROUP ** -0.5),
        'g_attn_out': gain(ks[20], (DEPTH, ATTN_WIDTH)),
        'g_ssm_out': gain(ks[21], (DEPTH, SSM_WIDTH)),
        'w_out': nrm(ks[22], (DEPTH, MIX_WIDTH, D_MODEL), MIX_WIDTH ** -0.5),
        'g_post_mix': gain(ks[23], (DEPTH, D_MODEL)),
        'g_pre_ffn': gain(ks[24], (DEPTH, D_MODEL)),
        'w_gate_up': nrm(ks[25], (DEPTH, D_MODEL, 2 * D_FF), D_MODEL ** -0.5),
        'w_down': nrm(ks[26], (DEPTH, D_FF, D_MODEL), D_FF ** -0.5),
        'g_post_ffn': gain(ks[27], (DEPTH, D_MODEL)),
        'w_ple_gate': nrm(ks[28], (DEPTH, D_MODEL, D_MODEL), D_MODEL ** -0.5),
        'w_ple_proj': nrm(ks[29], (DEPTH, PLE_DIM, D_MODEL), PLE_DIM ** -0.5),
    }


def reference(x_prompt, x_sample, cache_k, cache_v, state_ssm_re, state_ssm_im, p_prompt, p_sample,
              g_pre_mix, w_in, attn_sinks, ssm_a_re, ssm_a_im, ssm_log_dt, ssm_b_re, ssm_b_im,
              ssm_c_re, ssm_c_im, ssm_d, ssm_w_glu, g_attn_out, g_ssm_out, w_out, g_post_mix,
              g_pre_ffn, w_gate_up, w_down, g_post_ffn, w_ple_gate, w_ple_proj):
    xp, xs = x_prompt, x_sample
    kp_l, vp_l, srp_l, sip_l = [], [], [], []
    ks_l, vs_l, srs_l, sis_l = [], [], [], []
    for l in range(DEPTH):
        prm = {
            'g_pre_mix': g_pre_mix[l], 'w_in': w_in[l], 'attn_sinks': attn_sinks[l],
            'ssm_a_re': ssm_a_re[l], 'ssm_a_im': ssm_a_im[l], 'ssm_log_dt': ssm_log_dt[l],
            'ssm_b_re': ssm_b_re[l], 'ssm_b_im': ssm_b_im[l], 'ssm_c_re': ssm_c_re[l],
            'ssm_c_im': ssm_c_im[l], 'ssm_d': ssm_d[l], 'ssm_w_glu': ssm_w_glu[l],
            'g_attn_out': g_attn_out[l], 'g_ssm_out': g_ssm_out[l], 'w_out': w_out[l],
            'g_post_mix': g_post_mix[l], 'g_pre_ffn': g_pre_ffn[l], 'w_gate_up': w_gate_up[l],
            'w_down': w_down[l], 'g_post_ffn': g_post_ffn[l], 'w_ple_gate': w_ple_gate[l],
            'w_ple_proj': w_ple_proj[l],
        }
        xp, kp, vp, srp, sip = trunk_layer(xp, p_prompt[l], prm, None)
        xs, kk, vv, srs, sis = trunk_layer(
            xs, p_sample[l], prm, (cache_k[l], cache_v[l], state_ssm_re[l], state_ssm_im[l]))
        kp_l.append(kp); vp_l.append(vp); srp_l.append(srp); sip_l.append(sip)
        ks_l.append(kk); vs_l.append(vv); srs_l.append(srs); sis_l.append(sis)
    return (xp, xs,
            jnp.stack(kp_l), jnp.stack(vp_l), jnp.stack(srp_l), jnp.stack(sip_l),
            jnp.stack(ks_l), jnp.stack(vs_l), jnp.stack(srs_l), jnp.stack(sis_l))
```

```python
import math
import numpy as np
import concourse.bass as bass
import concourse.mybir as mybir
from concourse.bass_utils import run_bass_kernel_spmd

F32 = mybir.dt.float32
BF16 = mybir.dt.bfloat16
I32 = mybir.dt.int32
AF = mybir.ActivationFunctionType
ALU = mybir.AluOpType
AX = mybir.AxisListType

ENGS = ('pe', 'act', 'dve', 'pool', 'sp')
SEM_ROT = 20000
NDMASEM = 56
EPS = 1e-6
BIG = 1.0e9
TWO_PI = 6.283184
GELU_C = 1.5957691216057308
MAGIC = 12582912.0


class Cfg:
    def __init__(self, D=2048, SEQ=2048, DEPTH=4, BATCH=4, DEC=32):
        self.D, self.SEQ, self.DEPTH, self.BATCH, self.DEC = D, SEQ, DEPTH, BATCH, DEC
        self.NS = DEC // BATCH
        self.T = SEQ + self.NS
        self.AW = D // 2
        self.NH = self.AW // 64
        self.NKV = max(1, self.NH // 8)
        self.GRP = self.NH // self.NKV
        self.KW = self.NKV * 64
        self.SW = D - self.AW
        self.NG = self.SW // 16
        self.NKS = self.SW // 128
        self.NPAIR = self.NG // 2
        self.INW = self.AW + 2 * self.KW + self.SW
        self.DFF = -(-8 * D // (3 * 256)) * 256
        self.KTD = D // 128
        self.KTA = self.AW // 128
        self.KTF = self.DFF // 128
        self.NQB = SEQ // 128
        half = self.T // 2
        assert self.T % 2 == 0
        a = -(-half // 3)
        ch = []
        for h0 in (0, half):
            o = h0
            for i in range(3):
                n = min(a, h0 + half - o)
                ch.append((o, n))
                o += n
        self.chunks = ch
        self.NMAX = a
        assert a <= 512
        self.TQ = 256 if SEQ >= 256 else SEQ
        self.slopes = [2.0 ** (-8.0 * (h + 1) / self.NH) for h in range(self.NH)]


class Buf:
    def __init__(self, ap, excl=False):
        self.ap = ap
        self.w = {}
        self.r = {}
        self.excl = excl


def _upd(d, tok):
    k = tok[0].num
    if k not in d or d[k][1] < tok[1]:
        d[k] = tok


class _Rec:
    def __init__(self):
        self.call = None

    def __getattr__(self, name):
        def f(*a, **k):
            assert self.call is None
            self.call = (name, a, k)
            return self
        return f


class KB:
    def __init__(self, nc):
        self.nc = nc
        self.ops = {e: [] for e in ENGS}
        self.cur = {}
        self.nsem = 0
        self.waited = {}
        self.ndma = 0
        self.dpool = []
        for e in ENGS:
            self._new_sem(e)
        for i in range(NDMASEM):
            self.dpool.append([nc.alloc_semaphore("dq%d" % i), 0])

    def _new_sem(self, e):
        self.nsem += 1
        self.cur[e] = [self.nc.alloc_semaphore("p_%s_%d" % (e, self.nsem)), 0]
        if not hasattr(self, 'owner'):
            self.owner = {}
        self.owner[self.cur[e][0].num] = e

    def _waits(self, eng, deps):
        waits = []
        for d in deps:
            sem, val = d
            if eng == 'pe' and self.owner.get(sem.num) == 'pe':
                continue
            key = (eng, sem.num)
            if self.waited.get(key, 0) >= val:
                continue
            self.waited[key] = val
            waits.append((sem, val))
        return waits

    def _deps(self, R, W):
        deps = []
        for b in R:
            deps.extend(b.w.values())
            if b.excl:
                deps.extend(b.r.values())
        for b in W:
            deps.extend(b.w.values())
            deps.extend(b.r.values())
        return deps

    def _record(self, tok, R, W):
        for b in R:
            _upd(b.r, tok)
        for b in W:
            _upd(b.w, tok)

    def do(self, eng, fn, R=(), W=()):
        rec = _Rec()
        fn(rec)
        if getattr(self, 'buf', None) is not None:
            self.buf.append(('do', eng, rec.call, list(R), list(W)))
            return None
        return self._do(eng, rec.call, R, W)

    def begin_buffer(self):
        self.buf = []

    def end_buffer(self):
        b, self.buf = self.buf, None
        return b

    def flush_interleaved(self, lists):
        m = max(len(x) for x in lists)
        for k in range(m):
            for x in lists:
                if k < len(x):
                    kind, eng, call, R, W = x[k]
                    self._do(eng, call, R, W)

    def _do(self, eng, call, R=(), W=()):
        waits = self._waits(eng, self._deps(R, W))
        if self.cur[eng][1] >= SEM_ROT:
            self._new_sem(eng)
        c = self.cur[eng]
        c[1] += 1
        name, a, k = call
        self.ops[eng].append((waits, (lambda e, name=name, a=a, k=k: getattr(e, name)(*a, **k)), c[0], 1))
        tok = (c[0], c[1])
        self._record(tok, R, W)
        return tok

    def dma(self, eng, out, in_, R=(), W=(), **kw):
        waits = self._waits(eng, self._deps(R, W))
        ds = self.dpool[self.ndma % NDMASEM]
        self.ndma += 1
        ds[1] += 16
        self.ops[eng].append((waits, lambda e: e.dma_start(out=out, in_=in_, **kw), ds[0], 16))
        tok = (ds[0], ds[1])
        self._record(tok, R, W)
        return tok

    def wait_only(self, eng, deps):
        waits = self._waits(eng, deps)
        if waits:
            self.ops[eng].append((waits, None, None, 0))

    def barrier(self):
        toks = [(c[0], c[1]) for c in self.cur.values() if c[1] > 0]
        toks += [(d[0], d[1]) for d in self.dpool if d[1] > 0]
        for e in ENGS:
            self.wait_only(e, toks)

    def emit(self):
        nc = self.nc
        with nc.Block() as block:
            def run(name):
                def f(e):
                    for waits, fn, sem, inc in self.ops[name]:
                        for (s, v) in waits:
                            e.wait_ge(s, v)
                        if fn is not None:
                            fn(e).then_inc(sem, inc)
                return f
            block.tensor(run('pe'))
            block.scalar(run('act'))
            block.vector(run('dve'))
            block.gpsimd(run('pool'))
            block.sync(run('sp'))


class Arena:
    LO = 16512
    HI = 229376

    def __init__(self, nc):
        self.nc = nc
        self.cur = self.LO
        self.n = 0

    def alloc(self, shape, dt=F32):
        esz = 2 if dt == BF16 else 4
        nb = esz
        for s in shape[1:]:
            nb *= s
        nb = (nb + 63) // 64 * 64
        assert self.cur + nb <= self.HI, ("SBUF overflow", self.cur, nb)
        self.n += 1
        t = self.nc.alloc_sbuf_tensor_at("sb%d" % self.n, list(shape), dt, offset=self.cur)
        self.cur += nb
        return Buf(t.ap())


def build(cfg):
    C = cfg
    nc = bass.Bass("TRN2", target_bir_lowering=False)
    kb = KB(nc)
    ar = Arena(nc)
    D, T, SEQ, NS, DEPTH = C.D, C.T, C.SEQ, C.NS, C.DEPTH
    AW, NH, NKV, KW, SW, NKS, NPAIR, INW, DFF = C.AW, C.NH, C.NKV, C.KW, C.SW, C.NKS, C.NPAIR, C.INW, C.DFF
    KTD, KTA, KTF, NQB = C.KTD, C.KTA, C.KTF, C.NQB
    chunks, NMAX, TQ = C.chunks, C.NMAX, C.TQ
    KTMAX = max(KTF, KTD)

    def din(name, shape):
        return nc.dram_tensor(name, list(shape), F32, kind="ExternalInput").ap()

    def dout(name, shape):
        return nc.dram_tensor(name, list(shape), F32, kind="ExternalOutput").ap()

    def dscr(name, shape, dt=F32):
        return nc.dram_tensor(name, list(shape), dt).ap()

    xT_in = din("xT", [D, T])
    peT_in = din("peT", [DEPTH, 256, T])
    w_in = din("w_in", [DEPTH, D, INW])
    w_out = din("w_out", [DEPTH, D, D])
    w_gu = din("w_gate_up", [DEPTH, D, 2 * DFF])
    w_dn = din("w_down", [DEPTH, DFF, D])
    w_pg = din("w_ple_gate", [DEPTH, D, D])
    w_pp = din("w_ple_proj", [DEPTH, 256, D])
    gains = din("gains", [DEPTH, 4, 128, KTD])
    gssm_in = din("gssm", [DEPTH, 128, NKS])
    gattn_in = din("gattn", [DEPTH, 128, AW])
    gattnT_in = din("gattnT", [DEPTH, 64, NH])
    sinks_in = din("sinks", [DEPTH, 128, NH])
    ckT_in = din("ckT", [DEPTH, NS, NKV, 64, 128])
    ck_in = din("ck", [DEPTH, NS, 128, KW])
    cv_in = din("cv", [DEPTH, NS, 128, KW])
    ssm_ps_in = din("ssm_ps", [DEPTH, 128, 3, NPAIR])
    bpad_in = din("bpad", [DEPTH, 128, 2, NPAIR, 128])
    cpad_in = din("cpad", [DEPTH, 128, NKS, 2, 4, 128])
    dcol_in = din("dcol", [DEPTH, 128, NKS])
    wglu_in = din("wglu", [DEPTH, 128, NKS, 2, 128])
    st0_in = din("st0", [DEPTH, 128, 2, NPAIR, NS])
    consts_in = din("consts", [128, 128 + 256 + 256 + 512])

    yT_out = dout("yT", [D, T])
    kvp_out = dout("kvp", [DEPTH, 2 * KW, 128])
    ks_out = dout("ks", [DEPTH, NS, 128, KW])
    vs_out = dout("vs", [DEPTH, NS, 128, KW])
    st_out = dout("st", [DEPTH, 128, 2, NPAIR, 1 + NS])

    Xd = dscr("X", [D, T])
    ZBd = dscr("ZB", [INW, T], BF16)
    ZFd = dscr("ZF", [2 * KW, T])
    Od = dscr("O", [D, T])
    SSd = dscr("SS", [SW, T])
    ACTd = dscr("ACTs", [DFF, T], BF16)
    MGSd = dscr("MGS", [AW, NS], BF16)

    ident = ar.alloc([128, 128], BF16)
    identf = ar.alloc([128, 128], F32)
    ones = ar.alloc([128, 128], BF16)
    DIST = ar.alloc([128, 256])
    DIST0 = ar.alloc([128, 256])
    IOT = ar.alloc([128, 512])
    srcbuf = ar.alloc([128, max(KTD * T, KTF * (T // 2))], BF16)
    phase_mark = ar.cur

    PS = [Buf(nc.alloc_psum_tensor("ps%d" % i, [128, 512], F32).ap(), excl=True) for i in range(8)]

    def src3(kt_n, tn):
        return srcbuf.ap[:, 0:kt_n * tn].rearrange("p (k t) -> p k t", t=tn)

    SRC_D = src3(KTD, T)

    outtoks = []

    class _Stop(Exception):
        pass
    import os as _os
    _stop_after = int(_os.environ.get('STOP_AFTER', '100000'))
    _pc = [0]

    _marks = []

    def new_phase(name=None):
        import inspect
        if name is None:
            name = inspect.stack()[1].function + ':' + str(inspect.stack()[1].lineno)
        _marks.append((name, sum(1 for o in kb.ops['pe'] if o[1] is not None)))
        _pc[0] += 1
        if _pc[0] > _stop_after:
            raise _Stop()
        kb.barrier()
        ar.cur = phase_mark

    kb.dma('sp', identf.ap, consts_in[:, 0:128], W=[identf])
    kb.dma('pool', ident.ap, consts_in[:, 0:128], W=[ident])
    kb.dma('sp', DIST.ap, consts_in[:, 128:384], W=[DIST])
    kb.dma('sp', DIST0.ap, consts_in[:, 384:640], W=[DIST0])
    kb.dma('sp', IOT.ap, consts_in[:, 640:1152], W=[IOT])
    kb.do('pool', lambda e: e.memset(ones.ap, 1.0), W=[ones])
    xw = Buf(Xd)
    for r in range(KTD):
        kb.dma('sp', Xd[r * 128:(r + 1) * 128, :], xT_in[r * 128:(r + 1) * 128, :], W=[xw])

    rot = {}

    def nxt(key, n):
        rot[key] = (rot.get(key, -1) + 1) % n
        return rot[key]

    def norm_phase(src_d, KT, g1_ap, resid_d=None, out=None, g2_ap=None, dst_kt0=0, Fdim=None):
        new_phase()
        Fdim = KT * 128
        g1 = ar.alloc([128, KT])
        kb.dma('sp', g1.ap, g1_ap, W=[g1])
        g2 = None
        if g2_ap is not None:
            g2 = ar.alloc([128, KT])
            kb.dma('sp', g2.ap, g2_ap, W=[g2])
        xin = [ar.alloc([128, KT, NMAX]) for _ in range(2)]
        xr = [ar.alloc([128, KT, NMAX]) for _ in range(2)] if resid_d is not None else None
        sq = [ar.alloc([128, NMAX], BF16) for _ in range(3)]
        rs = [ar.alloc([128, NMAX]) for _ in range(2)]
        tmp = [ar.alloc([128, NMAX]) for _ in range(3)]
        sv = src_d.rearrange("(k p) t -> p k t", p=128)
        xv = resid_d.rearrange("(k p) t -> p k t", p=128) if resid_d is not None else None
        srcB = Buf(src_d)
        psA, psB = PS[6], PS[7]

        def rstd_of(buf_in, n, psb, rsb):
            for kt in range(KT):
                s = sq[nxt('sq', 3)]
                kb.do('act', lambda e, kt=kt, s=s: e.activation(out=s.ap[:, 0:n], in_=buf_in.ap[:, kt, 0:n], func=AF.Square),
                      R=[buf_in], W=[s])
                kb.do('pe', lambda e, kt=kt, s=s: e.matmul(psb.ap[:, 0:n], lhsT=ones.ap, rhs=s.ap[:, 0:n],
                                                           start=(kt == 0), stop=(kt == KT - 1)), R=[s, ones], W=[psb])
            kb.do('dve', lambda e: e.tensor_scalar(out=rsb.ap[:, 0:n], in0=psb.ap[:, 0:n], scalar1=1.0 / Fdim, scalar2=EPS,
                                                   op0=ALU.mult, op1=ALU.add), R=[psb], W=[rsb])
            kb.do('act', lambda e: e.activation(out=rsb.ap[:, 0:n], in_=rsb.ap[:, 0:n], func=AF.Ln), R=[rsb], W=[rsb])
            kb.do('act', lambda e: e.activation(out=rsb.ap[:, 0:n], in_=rsb.ap[:, 0:n], func=AF.Exp, scale=-0.5), R=[rsb], W=[rsb])

        for ci, (t0, n) in enumerate(chunks):
            xi = xin[ci % 2]
            kb.dma('sp', xi.ap[:, :, 0:n], sv[:, :, t0:t0 + n], R=[srcB], W=[xi])
            r1 = rs[0]
            rstd_of(xi, n, psA, r1)
            if resid_d is None:
                for kt in range(KT):
                    kb.do('dve', lambda e, kt=kt: e.scalar_tensor_tensor(
                        out=SRC_D[:, dst_kt0 + kt, t0:t0 + n], in0=xi.ap[:, kt, 0:n], scalar=g1.ap[:, kt:kt + 1],
                        in1=r1.ap[:, 0:n], op0=ALU.mult, op1=ALU.mult), R=[xi, g1, r1], W=[srcbuf])
                continue
            xx = xr[ci % 2]
            xB = Buf(resid_d)
            kb.dma('sp', xx.ap[:, :, 0:n], xv[:, :, t0:t0 + n], R=[xB], W=[xx])
            for kt in range(KT):
                tb = tmp[nxt('tmp', 3)]
                kb.do('dve', lambda e, kt=kt, tb=tb: e.scalar_tensor_tensor(
                    out=tb.ap[:, 0:n], in0=xi.ap[:, kt, 0:n], scalar=g1.ap[:, kt:kt + 1], in1=r1.ap[:, 0:n],
                    op0=ALU.mult, op1=ALU.mult), R=[xi, g1, r1], W=[tb])
                kb.do('dve', lambda e, kt=kt, tb=tb: e.tensor_tensor(out=xx.ap[:, kt, 0:n], in0=xx.ap[:, kt, 0:n],
                                                                      in1=tb.ap[:, 0:n], op=ALU.add), R=[tb], W=[xx])
            outtoks.append(kb.dma('sp', xv[:, :, t0:t0 + n], xx.ap[:, :, 0:n], R=[xx], W=[xB]))
            if out == 'norm':
                r2 = rs[1]
                rstd_of(xx, n, psB, r2)
                for kt in range(KT):
                    kb.do('dve', lambda e, kt=kt: e.scalar_tensor_tensor(
                        out=SRC_D[:, dst_kt0 + kt, t0:t0 + n], in0=xx.ap[:, kt, 0:n], scalar=g2.ap[:, kt:kt + 1],
                        in1=r2.ap[:, 0:n], op0=ALU.mult, op1=ALU.mult), R=[xx, g2, r2], W=[srcbuf])
            elif out == 'cast':
                kb.do('act', lambda e: e.activation(out=SRC_D[:, dst_kt0:dst_kt0 + KT, t0:t0 + n], in_=xx.ap[:, :, 0:n],
                                                    func=AF.Copy), R=[xx], W=[srcbuf])

    def linear_phase(srcv, KT, W_ap, cols, epi, W2cols=None, tchunks=None, fresh=True):
        if fresh:
            new_phase()
        wb = [ar.alloc([128, KT, 128], BF16) for _ in range(3)]
        wb2 = [ar.alloc([128, KT, 128], BF16) for _ in range(2)] if W2cols is not None else None
        wv = W_ap.rearrange("(k p) n -> p k n", p=128)
        tch = tchunks if tchunks is not None else chunks
        st = {}

        def go(mi):
            c0 = cols[mi]
            w = wb[mi % 3]
            kb.dma('pool', w.ap, wv[:, :, c0:c0 + 128], W=[w])
            w2 = None
            if W2cols is not None:
                w2 = wb2[mi % 2]
                kb.dma('pool', w2.ap, wv[:, :, W2cols[mi]:W2cols[mi] + 128], W=[w2])
            for ci, (t0, n, s0) in enumerate(tch):
                if W2cols is None:
                    pb = PS[nxt('lin', 6)]
                    pb2 = None
                else:
                    j = nxt('lin2', 3)
                    pb, pb2 = PS[2 * j], PS[2 * j + 1]
                for kt in range(KT):
                    kb.do('pe', lambda e, kt=kt, pb=pb, w=w: e.matmul(pb.ap[:, 0:n], lhsT=w.ap[:, kt, :], rhs=srcv[:, kt, s0:s0 + n],
                                                                   start=(kt == 0), stop=(kt == KT - 1)), R=[w, srcbuf], W=[pb])
                if pb2 is not None:
                    for kt in range(KT):
                        kb.do('pe', lambda e, kt=kt, pb2=pb2, w2=w2: e.matmul(pb2.ap[:, 0:n], lhsT=w2.ap[:, kt, :], rhs=srcv[:, kt, s0:s0 + n],
                                                                         start=(kt == 0), stop=(kt == KT - 1)), R=[w2, srcbuf], W=[pb2])
                epi(mi, ci, t0, n, pb, pb2)
        return go, st

    full_chunks = [(t0, n, t0) for (t0, n) in chunks]

    def attention_phase(l):
        new_phase()
        kTd = ar.alloc([128, NKV, 128 + T], BF16)
        Vtm = ar.alloc([128, NQB + 1, KW], BF16)
        vTi = [ar.alloc([128, 128], BF16) for _ in range(2)]
        qTb = [ar.alloc([128, KTA, 128], BF16) for _ in range(2)]
        atm = [ar.alloc([128, AW]) for _ in range(2)]
        hn = [ar.alloc([128, AW], BF16) for _ in range(2)]
        junk = ar.alloc([128, AW], BF16)
        sm2 = [ar.alloc([128, 4]) for _ in range(2)]
        gat = ar.alloc([128, AW])
        snk = ar.alloc([128, NH])
        zb = Buf(ZBd)
        kb.dma('sp', gat.ap, gattn_in[l], W=[gat])
        kb.dma('sp', snk.ap, sinks_in[l], W=[snk])
        kb.do('pool', lambda e: e.memset(kTd.ap[:, :, 0:128], 0.0), W=[kTd])
        kb.do('pool', lambda e: e.memset(Vtm.ap[:, 0, :], 0.0), W=[Vtm])
        for g in range(NKV):
            for cp in range(2):
                kb.dma('sp', kTd.ap[cp * 64:(cp + 1) * 64, g, 128:128 + T], ZBd[AW + g * 64:AW + (g + 1) * 64, :], R=[zb], W=[kTd])
        PSb = [Buf(PS[i].ap.bitcast(BF16), excl=True) for i in range(8)]
        for i in range(8):
            PSb[i].w, PSb[i].r = PS[i].w, PS[i].r
        for b in range(NQB):
            vi = vTi[b % 2]
            kb.dma('sp', vi.ap[0:KW, :], ZBd[AW + KW:AW + 2 * KW, b * 128:(b + 1) * 128], R=[zb], W=[vi])
            pt = PSb[7]
            kb.do('pe', lambda e, vi=vi, pt=pt: e.transpose(out=pt.ap[:, 0:KW], in_=vi.ap[0:KW, :], identity=ident.ap[0:KW, 0:KW]),
                  R=[vi, ident], W=[pt])
            kb.do('act', lambda e, b=b, pt=pt: e.activation(out=Vtm.ap[:, b + 1, :], in_=pt.ap[:, 0:KW], func=AF.Copy), R=[pt], W=[Vtm])

        NRR = 8
        Sb = [ar.alloc([128, 256]) for _ in range(NRR)]
        Pb = [ar.alloc([128, 256], BF16) for _ in range(NRR)]
        PTs = [ar.alloc([128, 2, 128], BF16) for _ in range(NRR)]
        sm = [ar.alloc([128, 8]) for _ in range(NRR)]

        def pipeline(units, stages, after=None):
            ns = len(stages)
            for t in range(len(units) + ns - 1):
                for si, st in enumerate(stages):
                    ui = t - si
                    if 0 <= ui < len(units):
                        st(units[ui])
                        if si == ns - 1 and after is not None:
                            after(ui)

        def sA(u):
            S_, m_, npart, nk, h, sps = Sb[u['i']], sm[u['i']], u['np'], u['nk'], u['h'], u['sps']
            kb.do('dve', lambda e: e.scalar_tensor_tensor(out=S_.ap[0:npart, 0:nk], in0=u['dist'], scalar=-C.slopes[h],
                                                          in1=sps.ap[0:npart, 0:nk], op0=ALU.mult, op1=ALU.add),
                  R=[sps, DIST, DIST0], W=[S_])
            kb.do('dve', lambda e: e.reduce_max(out=m_.ap[0:npart, 0:1], in_=S_.ap[0:npart, 0:nk], axis=AX.X), R=[S_], W=[m_])
            kb.do('dve', lambda e: e.tensor_tensor(out=m_.ap[0:npart, 0:1], in0=m_.ap[0:npart, 0:1], in1=snk.ap[0:npart, h:h + 1],
                                                   op=ALU.max), R=[snk], W=[m_])
            kb.do('dve', lambda e: e.tensor_scalar(out=m_.ap[0:npart, 1:2], in0=m_.ap[0:npart, 0:1], scalar1=-1.0, scalar2=None,
                                                   op0=ALU.mult), R=[], W=[m_])

        def sB(u):
            S_, P_, m_, npart, nk, h = Sb[u['i']], Pb[u['i']], sm[u['i']], u['np'], u['nk'], u['h']
            kb.do('act', lambda e: e.activation(out=P_.ap[0:npart, 0:nk], in_=S_.ap[0:npart, 0:nk], func=AF.Exp,
                                                bias=m_.ap[0:npart, 1:2], scale=1.0, accum_out=m_.ap[0:npart, 2:3]),
                  R=[S_, m_], W=[P_, m_])
            kb.do('act', lambda e: e.activation(out=m_.ap[0:npart, 3:4], in_=snk.ap[0:npart, h:h + 1], func=AF.Exp,
                                                bias=m_.ap[0:npart, 1:2], scale=1.0), R=[snk, m_], W=[m_])

        def sC(u):
            m_, npart = sm[u['i']], u['np']
            kb.do('dve', lambda e: e.tensor_tensor(out=m_.ap[0:npart, 4:5], in0=m_.ap[0:npart, 2:3], in1=m_.ap[0:npart, 3:4],
                                                   op=ALU.add), R=[m_], W=[m_])
            kb.do('dve', lambda e: e.reciprocal(out=m_.ap[0:npart, 5:6], in_=m_.ap[0:npart, 4:5]), R=[m_], W=[m_])

        units = []
        for b in range(NQB):
            for h in range(NH):
                units.append(dict(b=b, h=h, g=h // C.GRP, hp=(h % 2) * 64, np=128, nk=256,
                                  dist=(DIST0 if b == 0 else DIST).ap))

        def pA(u):
            b, h, g, hp = u['b'], u['h'], u['g'], u['hp']
            if h == 0:
                qb = qTb[b % 2]
                kb.dma('sp', qb.ap, ZBd[0:AW, b * 128:(b + 1) * 128].rearrange("(k p) t -> p k t", p=128), R=[zb], W=[qb])
            qb = qTb[b % 2]
            u['i'] = nxt('att', NRR)
            u['sps'] = sps = PS[nxt('sps', 3)]
            kb.do('pe', lambda e: e.matmul(sps.ap[:, 0:256], lhsT=qb.ap[hp:hp + 64, h // 2, :],
                                           rhs=kTd.ap[hp:hp + 64, g, b * 128:b * 128 + 256], start=True, stop=True), R=[qb, kTd], W=[sps])
            sA(u)

        def pC(u):
            sC(u)
            i = u['i']
            u['ptp'] = ptp = PSb[3 + nxt('ptp', 2)]
            for j in range(2):
                kb.do('pe', lambda e, j=j: e.transpose(out=ptp.ap[:, j * 128:(j + 1) * 128], in_=Pb[i].ap[:, j * 128:(j + 1) * 128],
                                                       identity=ident.ap), R=[Pb[i], ident], W=[ptp])

        def pD(u):
            i, ptp = u['i'], u['ptp']
            kb.do('act', lambda e: e.activation(out=PTs[i].ap, in_=ptp.ap[:, 0:256].rearrange("p (j q) -> p j q", j=2), func=AF.Copy),
                  R=[ptp], W=[PTs[i]])

        def pE(u):
            i, b, g = u['i'], u['b'], u['g']
            u['ops'] = ops_ = PS[5 + nxt('ops', 2)]
            for j in range(2):
                kb.do('pe', lambda e, j=j: e.matmul(ops_.ap[:, 0:64], lhsT=PTs[i].ap[:, j, :], rhs=Vtm.ap[:, b + j, g * 64:(g + 1) * 64],
                                                    start=(j == 0), stop=(j == 1)), R=[PTs[i], Vtm], W=[ops_])

        def pF(u):
            i, h, ops_ = u['i'], u['h'], u['ops']
            am = atm[u['b'] % 2]
            kb.do('dve', lambda e: e.tensor_scalar(out=am.ap[:, h * 64:(h + 1) * 64], in0=ops_.ap[:, 0:64], scalar1=sm[i].ap[:, 5:6],
                                                   scalar2=None, op0=ALU.mult), R=[ops_, sm[i]], W=[am])

        def block_done(ui):
            u = units[ui]
            if u['h'] != NH - 1:
                return
            b = u['b']
            am, s2, hb = atm[b % 2], sm2[b % 2], hn[b % 2]
            kb.do('act', lambda e: e.activation(out=junk.ap, in_=am.ap, func=AF.Square, accum_out=s2.ap[:, 0:1]), R=[am], W=[junk, s2])
            kb.do('dve', lambda e: e.tensor_scalar(out=s2.ap[:, 1:2], in0=s2.ap[:, 0:1], scalar1=1.0 / AW, scalar2=EPS,
                                                   op0=ALU.mult, op1=ALU.add), R=[s2], W=[s2])
            kb.do('act', lambda e: e.activation(out=s2.ap[:, 1:2], in_=s2.ap[:, 1:2], func=AF.Ln), R=[s2], W=[s2])
            kb.do('act', lambda e: e.activation(out=s2.ap[:, 1:2], in_=s2.ap[:, 1:2], func=AF.Exp, scale=-0.5), R=[s2], W=[s2])
            kb.do('dve', lambda e: e.scalar_tensor_tensor(out=hb.ap, in0=am.ap, scalar=s2.ap[:, 1:2], in1=gat.ap, op0=ALU.mult, op1=ALU.mult),
                  R=[am, s2, gat], W=[hb])
            for kt in range(KTA):
                pt = PSb[7]
                kb.do('pe', lambda e, kt=kt: e.transpose(out=pt.ap[:, 0:128], in_=hb.ap[:, kt * 128:(kt + 1) * 128], identity=ident.ap),
                      R=[hb, ident], W=[pt])
                kb.do('act', lambda e, kt=kt: e.activation(out=SRC_D[:, kt, b * 128:(b + 1) * 128], in_=pt.ap[:, 0:128], func=AF.Copy),
                      R=[pt], W=[srcbuf])

        pipeline(units, [pA, sB, pC, pD, pE, pF], after=block_done)

        kS = [ar.alloc([128, NKV, 132], BF16) for _ in range(2)]
        vS = [ar.alloc([128, KW], BF16) for _ in range(2)]
        vN = [ar.alloc([1, KW], BF16) for _ in range(2)]
        qS = ar.alloc([128, KTA, NS], BF16)
        PnS = [ar.alloc([1, 132], BF16) for _ in range(NRR)]
        PTS = [ar.alloc([128, 2], BF16) for _ in range(NRR)]
        aS = ar.alloc([64, NH, NS])
        zf = Buf(ZFd)
        kb.dma('sp', qS.ap, ZBd[0:AW, SEQ:SEQ + NS].rearrange("(k p) t -> p k t", p=128), R=[zb], W=[qS])
        sunits = []
        for n in range(NS):
            for h in range(NH):
                sunits.append(dict(n=n, h=h, g=h // C.GRP, hp=(h % 2) * 64, np=1, nk=129, dist=DIST.ap[0:1, 0:129]))

        def qA(u):
            n, h, g, hp = u['n'], u['h'], u['g'], u['hp']
            ks_, vs_, vn_ = kS[n % 2], vS[n % 2], vN[n % 2]
            if h == 0:
                for g_ in range(NKV):
                    for cp in range(2):
                        kb.dma('pool', ks_.ap[cp * 64:(cp + 1) * 64, g_, 0:128], ckT_in[l, n, g_], W=[ks_])
                kb.do('pool', lambda e: e.tensor_copy(out=ks_.ap[:, :, 128:129], in_=kTd.ap[:, :, 128 + SEQ + n:128 + SEQ + n + 1]),
                      R=[kTd], W=[ks_])
                kb.dma('pool', vs_.ap, cv_in[l, n], W=[vs_])
                kb.dma('pool', vn_.ap, ZFd[KW:2 * KW, SEQ + n:SEQ + n + 1].rearrange("k o -> o k"), R=[zf], W=[vn_], allow_slow_non_contiguous=True)
                outtoks.append(kb.dma('sp', ks_out[l, n, 0:127, :], ck_in[l, n, 1:128, :]))
                outtoks.append(kb.dma('sp', vs_out[l, n, 0:127, :], cv_in[l, n, 1:128, :]))
                outtoks.append(kb.dma('sp', ks_out[l, n, 127:128, :], ZFd[0:KW, SEQ + n:SEQ + n + 1].rearrange("k o -> o k"), R=[zf], allow_slow_non_contiguous=True))
                outtoks.append(kb.dma('sp', vs_out[l, n, 127:128, :], ZFd[KW:2 * KW, SEQ + n:SEQ + n + 1].rearrange("k o -> o k"), R=[zf], allow_slow_non_contiguous=True))
            u['i'] = nxt('att', NRR)
            u['sps'] = sps = PS[nxt('sps', 3)]
            kb.do('pe', lambda e: e.matmul(sps.ap[0:1, 0:129], lhsT=qS.ap[hp:hp + 64, h // 2, n:n + 1], rhs=ks_.ap[hp:hp + 64, g, 0:129],
                                           start=True, stop=True), R=[qS, ks_], W=[sps])
            sA(u)

        def qC(u):
            sC(u)
            i = u['i']
            pn = PnS[i]
            kb.do('dve', lambda e: e.tensor_scalar(out=pn.ap[0:1, 0:129], in0=Pb[i].ap[0:1, 0:129], scalar1=sm[i].ap[0:1, 5:6], scalar2=None,
                                                   op0=ALU.mult), R=[Pb[i], sm[i]], W=[pn])
            u['ptp'] = ptp = PSb[3 + nxt('ptp', 2)]
            kb.do('pe', lambda e: e.transpose(out=ptp.ap[:, 0:1], in_=pn.ap[0:1, 0:128], identity=ident.ap[0:1, 0:1]), R=[pn, ident], W=[ptp])

        def qD(u):
            i, ptp = u['i'], u['ptp']
            kb.do('act', lambda e: e.activation(out=PTS[i].ap[:, 0:1], in_=ptp.ap[:, 0:1], func=AF.Copy), R=[ptp], W=[PTS[i]])

        def qE(u):
            i, g, n = u['i'], u['g'], u['n']
            vs_, vn_, pn = vS[n % 2], vN[n % 2], PnS[i]
            u['ops'] = ops_ = PS[5 + nxt('ops', 2)]
            kb.do('pe', lambda e: e.matmul(ops_.ap[0:64, 0:1], lhsT=vs_.ap[:, g * 64:(g + 1) * 64], rhs=PTS[i].ap[:, 0:1], start=True, stop=False),
                  R=[PTS[i], vs_], W=[ops_])
            kb.do('pe', lambda e: e.matmul(ops_.ap[0:64, 0:1], lhsT=vn_.ap[0:1, g * 64:(g + 1) * 64], rhs=pn.ap[0:1, 128:129], start=False, stop=True),
                  R=[pn, vn_], W=[ops_])

        def qF(u):
            ops_, h, n = u['ops'], u['h'], u['n']
            kb.do('act', lambda e: e.activation(out=aS.ap[:, h, n:n + 1], in_=ops_.ap[0:64, 0:1], func=AF.Copy), R=[ops_], W=[aS])

        if not _os.environ.get('PIPE_SAMPLE'):
            for u_ in sunits:
                for st_ in (qA, sB, qC, qD, qE, qF):
                    st_(u_)
        else:
            pipeline(sunits, [qA, sB, qC, qD, qE, qF])
        sqS = ar.alloc([64, NH * NS], BF16)
        ssS = ar.alloc([64, NS])
        gT = ar.alloc([64, NH])
        hS = ar.alloc([64, NH, NS])
        hSb = ar.alloc([64, NH, NS], BF16)
        kb.dma('sp', gT.ap, gattnT_in[l], W=[gT])
        kb.do('act', lambda e: e.activation(out=sqS.ap, in_=aS.ap.rearrange("p h n -> p (h n)"), func=AF.Square), R=[aS], W=[sqS])
        kb.do('pe', lambda e: e.matmul(PS[7].ap[0:64, 0:NH * NS], lhsT=ones.ap[0:64, 0:64], rhs=sqS.ap, start=True, stop=True),
              R=[sqS, ones], W=[PS[7]])
        kb.do('dve', lambda e: e.tensor_reduce(out=ssS.ap, in_=PS[7].ap[0:64, 0:NH * NS].rearrange("p (h n) -> p n h", n=NS),
                                               axis=AX.X, op=ALU.add), R=[PS[7]], W=[ssS])
        kb.do('dve', lambda e: e.tensor_scalar(out=ssS.ap, in0=ssS.ap, scalar1=1.0 / AW, scalar2=EPS, op0=ALU.mult, op1=ALU.add),
              R=[ssS], W=[ssS])
        kb.do('act', lambda e: e.activation(out=ssS.ap, in_=ssS.ap, func=AF.Ln), R=[ssS], W=[ssS])
        kb.do('act', lambda e: e.activation(out=ssS.ap, in_=ssS.ap, func=AF.Exp, scale=-0.5), R=[ssS], W=[ssS])
        kb.do('dve', lambda e: e.tensor_tensor(out=hS.ap, in0=aS.ap, in1=gT.ap.unsqueeze(2).broadcast_to([64, NH, NS]), op=ALU.mult),
              R=[aS, gT], W=[hS])
        kb.do('dve', lambda e: e.tensor_tensor(out=hSb.ap, in0=hS.ap, in1=ssS.ap.unsqueeze(1).broadcast_to([64, NH, NS]), op=ALU.mult),
              R=[hS, ssS], W=[hSb])
        mg = Buf(MGSd)
        kb.dma('sp', MGSd.rearrange("(h d) n -> d h n", d=64), hSb.ap, R=[hSb], W=[mg])
        kb.dma('sp', SRC_D[:, 0:KTA, SEQ:SEQ + NS], MGSd.rearrange("(k p) n -> p k n", p=128), R=[mg], W=[srcbuf])

    def ssm_phase(l):
        new_phase()
        ps_ = ar.alloc([128, 3, NPAIR])
        kb.dma('sp', ps_.ap, ssm_ps_in[l], W=[ps_])
        NV = 17
        v = ar.alloc([128, NV, NPAIR])
        vi = ar.alloc([128, NPAIR], I32)
        are, aim, ldt = ps_.ap[:, 0, :], ps_.ap[:, 1, :], ps_.ap[:, 2, :]
        V_DT, V_DRE, V_TH, V_R, V_A, V_SIN, V_COS, V_ABR, V_ABI, V_FR, V_FI, V_IFR, V_IFI, V_T1, V_T2, V_T3, V_NFI = range(17)

        def vv(i):
            return v.ap[:, i, :]

        def tiny(eng, fn):
            kb.do(eng, fn, R=[v, ps_], W=[v])

        def wrap_turns(eng_ap_in, out_i):
            kb.do('dve', lambda e: e.tensor_copy(out=vi.ap, in_=eng_ap_in), R=[v], W=[vi])
            kb.do('dve', lambda e: e.tensor_copy(out=vv(V_T1), in_=vi.ap), R=[vi, v], W=[v])
            tiny('dve', lambda e: e.tensor_tensor(out=vv(out_i), in0=eng_ap_in, in1=vv(V_T1), op=ALU.subtract))
            tiny('dve', lambda e: e.tensor_scalar(out=vv(V_T1), in0=vv(out_i), scalar1=0.5, scalar2=None, op0=ALU.is_gt))
            tiny('dve', lambda e: e.tensor_tensor(out=vv(out_i), in0=vv(out_i), in1=vv(V_T1), op=ALU.subtract))
            tiny('dve', lambda e: e.tensor_scalar(out=vv(V_T1), in0=vv(out_i), scalar1=-0.5, scalar2=None, op0=ALU.is_lt))
            tiny('dve', lambda e: e.tensor_tensor(out=vv(out_i), in0=vv(out_i), in1=vv(V_T1), op=ALU.add))

        tiny('act', lambda e: e.activation(out=vv(V_DT), in_=ldt, func=AF.Exp))
        tiny('dve', lambda e: e.tensor_tensor(out=vv(V_DRE), in0=vv(V_DT), in1=are, op=ALU.mult))
        tiny('dve', lambda e: e.tensor_tensor(out=vv(V_TH), in0=vv(V_DT), in1=aim, op=ALU.mult))
        tiny('act', lambda e: e.activation(out=vv(V_R), in_=vv(V_DRE), func=AF.Exp))
        tiny('dve', lambda e: e.tensor_scalar(out=vv(V_T2), in0=vv(V_TH), scalar1=1.0 / (2 * math.pi), scalar2=None, op0=ALU.mult))
        wrap_turns(vv(V_T2), V_A)
        tiny('act', lambda e: e.activation(out=vv(V_SIN), in_=vv(V_A), func=AF.Sin, scale=TWO_PI))
        tiny('dve', lambda e: e.tensor_scalar(out=vv(V_T2), in0=vv(V_A), scalar1=0.25, scalar2=None, op0=ALU.add))
        wrap_turns(vv(V_T2), V_T3)
        tiny('act', lambda e: e.activation(out=vv(V_COS), in_=vv(V_T3), func=AF.Sin, scale=TWO_PI))
        tiny('dve', lambda e: e.tensor_tensor(out=vv(V_ABR), in0=vv(V_R), in1=vv(V_COS), op=ALU.mult))
        tiny('dve', lambda e: e.tensor_tensor(out=vv(V_ABI), in0=vv(V_R), in1=vv(V_SIN), op=ALU.mult))
        tiny('dve', lambda e: e.tensor_tensor(out=vv(V_T1), in0=are, in1=are, op=ALU.mult))
        tiny('dve', lambda e: e.tensor_tensor(out=vv(V_T2), in0=aim, in1=aim, op=ALU.mult))
        tiny('dve', lambda e: e.tensor_tensor(out=vv(V_T1), in0=vv(V_T1), in1=vv(V_T2), op=ALU.add))
        tiny('dve', lambda e: e.reciprocal(out=vv(V_T1), in_=vv(V_T1)))
        tiny('dve', lambda e: e.tensor_scalar(out=vv(V_T2), in0=vv(V_ABR), scalar1=-1.0, scalar2=None, op0=ALU.add))
        tiny('dve', lambda e: e.tensor_tensor(out=vv(V_FR), in0=vv(V_T2), in1=are, op=ALU.mult))
        tiny('dve', lambda e: e.tensor_tensor(out=vv(V_T3), in0=vv(V_ABI), in1=aim, op=ALU.mult))
        tiny('dve', lambda e: e.tensor_tensor(out=vv(V_FR), in0=vv(V_FR), in1=vv(V_T3), op=ALU.add))
        tiny('dve', lambda e: e.tensor_tensor(out=vv(V_FR), in0=vv(V_FR), in1=vv(V_T1), op=ALU.mult))
        tiny('dve', lambda e: e.tensor_tensor(out=vv(V_FI), in0=vv(V_ABI), in1=are, op=ALU.mult))
        tiny('dve', lambda e: e.tensor_tensor(out=vv(V_T3), in0=vv(V_T2), in1=aim, op=ALU.mult))
        tiny('dve', lambda e: e.tensor_tensor(out=vv(V_FI), in0=vv(V_FI), in1=vv(V_T3), op=ALU.subtract))
        tiny('dve', lambda e: e.tensor_tensor(out=vv(V_FI), in0=vv(V_FI), in1=vv(V_T1), op=ALU.mult))
        tiny('dve', lambda e: e.tensor_tensor(out=vv(V_T1), in0=vv(V_FR), in1=vv(V_FR), op=ALU.mult))
        tiny('dve', lambda e: e.tensor_tensor(out=vv(V_T2), in0=vv(V_FI), in1=vv(V_FI), op=ALU.mult))
        tiny('dve', lambda e: e.tensor_tensor(out=vv(V_T1), in0=vv(V_T1), in1=vv(V_T2), op=ALU.add))
        tiny('dve', lambda e: e.reciprocal(out=vv(V_T1), in_=vv(V_T1)))
        tiny('dve', lambda e: e.tensor_tensor(out=vv(V_IFR), in0=vv(V_FR), in1=vv(V_T1), op=ALU.mult))
        tiny('dve', lambda e: e.tensor_tensor(out=vv(V_IFI), in0=vv(V_FI), in1=vv(V_T1), op=ALU.mult))
        tiny('dve', lambda e: e.tensor_scalar(out=vv(V_IFI), in0=vv(V_IFI), scalar1=-1.0, scalar2=None, op0=ALU.mult))
        tiny('dve', lambda e: e.tensor_scalar(out=vv(V_NFI), in0=vv(V_FI), scalar1=-1.0, scalar2=None, op0=ALU.mult))

        bpk = [ar.alloc([128, 2, 4, 128], BF16) for _ in range(2)]
        wg = ar.alloc([128, NKS, 2, 128], BF16)
        kb.dma('pool', wg.ap, wglu_in[l], W=[wg], max_dma_last_dim=4096)
        dc = ar.alloc([128, NKS])
        kb.dma('sp', dc.ap, dcol_in[l], W=[dc])
        st0 = ar.alloc([128, 2, NPAIR, NS])
        kb.dma('sp', st0.ap, st0_in[l], W=[st0])
        sto = ar.alloc([128, 2, NPAIR, 1 + NS])
        uT = [ar.alloc([128, T], BF16) for _ in range(1)]
        cpf = [ar.alloc([128, 2, 4, 128]) for _ in range(1)]
        cf = [ar.alloc([128, 2, 4, 128], BF16) for _ in range(2)]
        cft = ar.alloc([128, 128])
        RT = [ar.alloc([128, TQ]) for _ in range(4)]
        NW = 4
        AT = [ar.alloc([128, TQ]) for _ in range(NW)]
        A2 = [ar.alloc([128, TQ]) for _ in range(NW)]
        NF = [ar.alloc([128, TQ]) for _ in range(NW)]
        NA = [ar.alloc([128, TQ]) for _ in range(NW)]
        CS = [ar.alloc([128, TQ]) for _ in range(NW)]
        SN = [ar.alloc([128, TQ]) for _ in range(NW)]
        PW1 = [ar.alloc([128, TQ]) for _ in range(2)]
        PW2 = [ar.alloc([128, TQ]) for _ in range(2)]
        THO = ar.alloc([128, SEQ // TQ, NPAIR])
        HPI = ar.alloc([128, 1])
        kb.do('dve', lambda e: e.memset(HPI.ap, math.pi / 2), W=[HPI])
        for tq_ in range(SEQ // TQ):
            kb.do('dve', lambda e, tq_=tq_: e.tensor_scalar(out=THO.ap[:, tq_, :], in0=vv(V_A), scalar1=float(tq_ * TQ), scalar2=None, op0=ALU.mult),
                  R=[v], W=[THO])
        W1 = [ar.alloc([128, TQ]) for _ in range(NW)]
        W2 = [ar.alloc([128, TQ]) for _ in range(NW)]
        XR = [ar.alloc([128, TQ]) for _ in range(NW)]
        XI = [ar.alloc([128, TQ]) for _ in range(NW)]
        SR = [ar.alloc([128, TQ]) for _ in range(NW)]
        SI = [ar.alloc([128, TQ]) for _ in range(NW)]
        SB2 = [ar.alloc([128, 2, TQ], BF16) for _ in range(4)]
        carry = ar.alloc([128, 2, NPAIR])
        aoff = ar.alloc([128, NPAIR])
        FIN = [ar.alloc([128, 8]) for _ in range(2)]
        ssbb = ar.alloc([128, 2, 4, NS], BF16)
        sw = ar.alloc([128, 3, 4, NS])
        craw = ar.alloc([128, 2, 4, 128], BF16)
        TS = ar.alloc([128, 2, NPAIR, NS])
        tsw = ar.alloc([128, 2, NPAIR, NS])
        abrb = v.ap[:, V_ABR, :].unsqueeze(2).broadcast_to([128, NPAIR, NS])
        abib = v.ap[:, V_ABI, :].unsqueeze(2).broadcast_to([128, NPAIR, NS])
        kb.do('dve', lambda e: e.tensor_tensor(out=TS.ap[:, 0], in0=st0.ap[:, 0], in1=abrb, op=ALU.mult), R=[st0, v], W=[TS])
        kb.do('dve', lambda e: e.tensor_tensor(out=tsw.ap[:, 0], in0=st0.ap[:, 1], in1=abib, op=ALU.mult), R=[st0, v], W=[tsw])
        kb.do('dve', lambda e: e.tensor_tensor(out=TS.ap[:, 0], in0=TS.ap[:, 0], in1=tsw.ap[:, 0], op=ALU.subtract), R=[tsw], W=[TS])
        kb.do('dve', lambda e: e.tensor_tensor(out=TS.ap[:, 1], in0=st0.ap[:, 0], in1=abib, op=ALU.mult), R=[st0, v], W=[TS])
        kb.do('dve', lambda e: e.tensor_tensor(out=tsw.ap[:, 1], in0=st0.ap[:, 1], in1=abrb, op=ALU.mult), R=[st0, v], W=[tsw])
        kb.do('dve', lambda e: e.tensor_tensor(out=TS.ap[:, 1], in0=TS.ap[:, 1], in1=tsw.ap[:, 1], op=ALU.add), R=[tsw], W=[TS])
        yst = [ar.alloc([128, T]) for _ in range(1)]
        E1 = [ar.alloc([128, TQ]) for _ in range(1)]
        E2 = [ar.alloc([128, TQ]) for _ in range(1)]
        GB = [ar.alloc([128, TQ], BF16) for _ in range(2)]
        zb = Buf(ZBd)
        ssB = Buf(SSd)
        kb.do('pool', lambda e: e.memset(carry.ap, 0.0), W=[carry])
        NTQ = SEQ // TQ

        for kt in range(NKS):
            u = uT[kt % len(uT)]
            kb.dma('sp', u.ap, ZBd[AW + 2 * KW + kt * 128:AW + 2 * KW + (kt + 1) * 128, :], R=[zb], W=[u])
            cp_, cf_ = cpf[0], cf[kt % 2]
            bp = bpk[kt % 2]
            for ri_ in range(2):
                kb.dma('pool', bp.ap[:, ri_], bpad_in[l, :, ri_, kt * 4:(kt + 1) * 4, :], W=[bp])
            kb.dma('sp', cp_.ap, cpad_in[l, :, kt], W=[cp_])
            kb.do('act', lambda e: e.activation(out=craw.ap[:, 0], in_=cp_.ap[:, 0], func=AF.Copy), R=[cp_], W=[craw])
            kb.do('act', lambda e: e.activation(out=craw.ap[:, 1], in_=cp_.ap[:, 1], func=AF.Copy, scale=-1.0), R=[cp_], W=[craw])
            for pr in range(4):
                q = kt * 4 + pr
                fr, fi = v.ap[:, V_FR, q:q + 1], v.ap[:, V_FI, q:q + 1]
                nfi = v.ap[:, V_NFI, q:q + 1]
                kb.do('dve', lambda e, pr=pr, fi=fi: e.tensor_scalar(out=cft.ap, in0=cp_.ap[:, 1, pr, :], scalar1=fi, scalar2=None, op0=ALU.mult),
                      R=[cp_, v], W=[cft])
                kb.do('dve', lambda e, pr=pr, fr=fr: e.scalar_tensor_tensor(out=cf_.ap[:, 0, pr, :], in0=cp_.ap[:, 0, pr, :], scalar=fr, in1=cft.ap,
                                                                         op0=ALU.mult, op1=ALU.subtract), R=[cp_, v, cft], W=[cf_])
                kb.do('dve', lambda e, pr=pr, fr=fr: e.tensor_scalar(out=cft.ap, in0=cp_.ap[:, 1, pr, :], scalar1=fr, scalar2=-1.0, op0=ALU.mult,
                                                                  op1=ALU.mult), R=[cp_, v, cf_], W=[cft])
                kb.do('dve', lambda e, pr=pr, nfi=nfi: e.scalar_tensor_tensor(out=cf_.ap[:, 1, pr, :], in0=cp_.ap[:, 0, pr, :], scalar=nfi, in1=cft.ap,
                                                                           op0=ALU.mult, op1=ALU.add), R=[cp_, v, cft], W=[cf_])
            ys = yst[kt % len(yst)]
            for tq in range(NTQ + 1):
                samp = (tq == NTQ)
                t0 = tq * TQ
                n = NS if samp else TQ
                ypb = PS[4 + nxt('ypb', 2)]
                pend = []
                if samp:
                    xs_ = PS[0]
                    for pr in range(4):
                        for ri in range(2):
                            c0_ = (ri * 4 + pr) * NS
                            kb.do('pe', lambda e, ri=ri, pr=pr, c0_=c0_: e.matmul(xs_.ap[:, c0_:c0_ + NS], lhsT=bp.ap[:, ri, pr, :], rhs=u.ap[:, t0:t0 + NS],
                                                                               start=True, stop=True), R=[bp, u], W=[xs_])
                    xv_ = xs_.ap[:, 0:8 * NS].rearrange("p (r q n) -> p r q n", r=2, q=4)
                    frb = v.ap[:, V_FR, kt * 4:kt * 4 + 4].unsqueeze(2).broadcast_to([128, 4, NS])
                    fib = v.ap[:, V_FI, kt * 4:kt * 4 + 4].unsqueeze(2).broadcast_to([128, 4, NS])
                    wr, wi, wt_ = sw.ap[:, 0], sw.ap[:, 1], sw.ap[:, 2]
                    kb.do('dve', lambda e: e.tensor_tensor(out=wr, in0=xv_[:, 0], in1=frb, op=ALU.mult), R=[xs_, v], W=[sw])
                    kb.do('dve', lambda e: e.tensor_tensor(out=wt_, in0=xv_[:, 1], in1=fib, op=ALU.mult), R=[xs_, v], W=[sw])
                    kb.do('dve', lambda e: e.tensor_tensor(out=wr, in0=wr, in1=wt_, op=ALU.subtract), R=[], W=[sw])
                    kb.do('dve', lambda e: e.tensor_tensor(out=wi, in0=xv_[:, 0], in1=fib, op=ALU.mult), R=[xs_, v], W=[sw])
                    kb.do('dve', lambda e: e.tensor_tensor(out=wt_, in0=xv_[:, 1], in1=frb, op=ALU.mult), R=[xs_, v], W=[sw])
                    kb.do('dve', lambda e: e.tensor_tensor(out=wi, in0=wi, in1=wt_, op=ALU.add), R=[], W=[sw])
                    kb.do('dve', lambda e: e.tensor_tensor(out=sto.ap[:, 0, kt * 4:kt * 4 + 4, 1:1 + NS], in0=wr, in1=TS.ap[:, 0, kt * 4:kt * 4 + 4, :], op=ALU.add),
                          R=[sw, TS], W=[sto])
                    kb.do('dve', lambda e: e.tensor_tensor(out=sto.ap[:, 1, kt * 4:kt * 4 + 4, 1:1 + NS], in0=wi, in1=TS.ap[:, 1, kt * 4:kt * 4 + 4, :], op=ALU.add),
                          R=[sw, TS], W=[sto])
                    kb.do('act', lambda e: e.activation(out=ssbb.ap, in_=sto.ap[:, :, kt * 4:kt * 4 + 4, 1:1 + NS], func=AF.Copy), R=[sto], W=[ssbb])
                for pr in range(4):
                    q = kt * 4 + pr
                    if not samp:
                        kb.begin_buffer()
                        xps_r, xps_i = PS[2 * nxt('xps', 2)], None
                        xps_i = PS[PS.index(xps_r) + 1]
                        for ri, xp in ((0, xps_r), (1, xps_i)):
                            kb.do('pe', lambda e, ri=ri, xp=xp, q=q: e.matmul(xp.ap[:, 0:n], lhsT=bp.ap[:, ri, pr, :], rhs=u.ap[:, t0:t0 + n],
                                                                           start=True, stop=True), R=[bp, u], W=[xp])
                    s2 = SB2[nxt('sb2', 4)]
                    if samp:
                        rhs_r, rhs_i = ssbb.ap[:, 0, pr, :], ssbb.ap[:, 1, pr, :]
                        rd = [ssbb]
                    else:
                        fin = FIN[pr % 2]
                        w = nxt('ssw', NW)
                        w1, w2, xr_, xi_, sr_, si_ = W1[w], W2[w], XR[w], XI[w], SR[w], SI[w]
                        tw = w
                        pw1, pw2 = PW1[pr % 2], PW2[pr % 2]
                        at, a2, nf, na, cs, sn = AT[tw], A2[tw], NF[tw], NA[tw], CS[tw], SN[tw]
                        rt = RT[pr]
                        if tq == 0:
                            kb.do('act', lambda e, rt=rt, q=q: e.activation(out=rt.ap, in_=IOT.ap[:, 0:TQ], func=AF.Identity, scale=0.0,
                                                                           bias=v.ap[:, V_R, q:q + 1]), R=[IOT, v], W=[rt])
                        kb.do('act', lambda e, at=at, q=q: e.activation(out=at.ap, in_=IOT.ap[:, 0:TQ], func=AF.Identity, scale=v.ap[:, V_A, q:q + 1],
                                                                       bias=THO.ap[:, tq, q:q + 1]), R=[IOT, v, THO], W=[at])
                        kb.do('dve', lambda e, at=at, a2=a2: e.tensor_scalar(out=a2.ap, in0=at.ap, scalar1=MAGIC, scalar2=None, op0=ALU.add), R=[at], W=[a2])
                        kb.do('dve', lambda e, at=at, a2=a2, nf=nf: e.scalar_tensor_tensor(out=nf.ap, in0=a2.ap, scalar=MAGIC, in1=at.ap, op0=ALU.subtract,
                                                                                       op1=ALU.subtract), R=[at, a2], W=[nf])
                        kb.do('act', lambda e, nf=nf, sn=sn: e.activation(out=sn.ap, in_=nf.ap, func=AF.Sin, scale=-TWO_PI), R=[nf], W=[sn])
                        kb.do('act', lambda e, nf=nf, na=na: e.activation(out=na.ap, in_=nf.ap, func=AF.Sin, scale=-TWO_PI / 2), R=[nf], W=[na])
                        kb.do('act', lambda e, na=na: e.activation(out=na.ap, in_=na.ap, func=AF.Square), R=[], W=[na])
                        kb.do('act', lambda e, na=na, cs=cs: e.activation(out=cs.ap, in_=na.ap, func=AF.Identity, scale=-2.0, bias=1.0), R=[na], W=[cs])
                        kb.do('dve', lambda e, cs=cs, w1=w1: e.tensor_tensor(out=w1.ap, in0=cs.ap, in1=xps_r.ap[:, 0:n], op=ALU.mult), R=[cs, xps_r], W=[w1])
                        kb.do('dve', lambda e, sn=sn, w2=w2: e.tensor_tensor(out=w2.ap, in0=sn.ap, in1=xps_i.ap[:, 0:n], op=ALU.mult), R=[sn, xps_i], W=[w2])
                        kb.do('dve', lambda e, w1=w1, w2=w2, xr_=xr_: e.tensor_tensor(out=xr_.ap, in0=w1.ap, in1=w2.ap, op=ALU.add), R=[w1, w2], W=[xr_])
                        kb.do('dve', lambda e, cs=cs, w1=w1: e.tensor_tensor(out=w1.ap, in0=cs.ap, in1=xps_i.ap[:, 0:n], op=ALU.mult), R=[cs, xps_i, xr_], W=[w1])
                        kb.do('dve', lambda e, sn=sn, w2=w2: e.tensor_tensor(out=w2.ap, in0=sn.ap, in1=xps_r.ap[:, 0:n], op=ALU.mult), R=[sn, xps_r, xr_], W=[w2])
                        kb.do('dve', lambda e, w1=w1, w2=w2, xi_=xi_: e.tensor_tensor(out=xi_.ap, in0=w1.ap, in1=w2.ap, op=ALU.subtract), R=[w1, w2], W=[xi_])
                        kb.do('dve', lambda e, rt=rt, xr_=xr_, sr_=sr_, q=q: e.tensor_tensor_scan(out=sr_.ap, data0=rt.ap, data1=xr_.ap,
                                                                                             initial=carry.ap[:, 0, q:q + 1], op0=ALU.mult, op1=ALU.add),
                              R=[rt, xr_, carry], W=[sr_])
                        kb.do('dve', lambda e, rt=rt, xi_=xi_, si_=si_, q=q: e.tensor_tensor_scan(out=si_.ap, data0=rt.ap, data1=xi_.ap,
                                                                                             initial=carry.ap[:, 1, q:q + 1], op0=ALU.mult, op1=ALU.add),
                              R=[rt, xi_, carry], W=[si_])
                        kb.do('act', lambda e, sr_=sr_, q=q: e.activation(out=carry.ap[:, 0, q:q + 1], in_=sr_.ap[:, TQ - 1:TQ], func=AF.Copy), R=[sr_], W=[carry])
                        kb.do('act', lambda e, si_=si_, q=q: e.activation(out=carry.ap[:, 1, q:q + 1], in_=si_.ap[:, TQ - 1:TQ], func=AF.Copy), R=[si_], W=[carry])
                        kb.do('pool', lambda e, cs=cs, sr_=sr_, w1=pw1: e.tensor_tensor(out=w1.ap, in0=cs.ap, in1=sr_.ap, op=ALU.mult), R=[cs, sr_], W=[pw1])
                        kb.do('pool', lambda e, sn=sn, si_=si_, w2=pw2: e.tensor_tensor(out=w2.ap, in0=sn.ap, in1=si_.ap, op=ALU.mult), R=[sn, si_], W=[pw2])
                        kb.do('pool', lambda e, w1=pw1, w2=pw2, s2=s2: e.tensor_tensor(out=s2.ap[:, 0, :], in0=w1.ap, in1=w2.ap, op=ALU.subtract), R=[pw1, pw2], W=[s2])
                        if tq == NTQ - 1:
                            kb.do('dve', lambda e, w1=pw1, w2=pw2: e.tensor_tensor(out=fin.ap[:, 0:1], in0=w1.ap[:, TQ - 1:TQ], in1=w2.ap[:, TQ - 1:TQ],
                                                                               op=ALU.subtract), R=[pw1, pw2], W=[fin])
                        kb.do('pool', lambda e, cs=cs, si_=si_, w1=pw1: e.tensor_tensor(out=w1.ap, in0=cs.ap, in1=si_.ap, op=ALU.mult), R=[cs, si_, s2, fin], W=[pw1])
                        kb.do('pool', lambda e, sn=sn, sr_=sr_, w2=pw2: e.tensor_tensor(out=w2.ap, in0=sn.ap, in1=sr_.ap, op=ALU.mult), R=[sn, sr_, s2, fin], W=[pw2])
                        kb.do('pool', lambda e, w1=pw1, w2=pw2, s2=s2: e.tensor_tensor(out=s2.ap[:, 1, :], in0=w1.ap, in1=w2.ap, op=ALU.add), R=[pw1, pw2], W=[s2])
                        if tq == NTQ - 1:
                            fr, fi = v.ap[:, V_FR, q:q + 1], v.ap[:, V_FI, q:q + 1]
                            kb.do('dve', lambda e, w1=pw1, w2=pw2: e.tensor_tensor(out=fin.ap[:, 1:2], in0=w1.ap[:, TQ - 1:TQ], in1=w2.ap[:, TQ - 1:TQ],
                                                                               op=ALU.add), R=[pw1, pw2], W=[fin])
                            kb.do('dve', lambda e, fi=fi: e.tensor_scalar(out=fin.ap[:, 2:3], in0=fin.ap[:, 1:2], scalar1=fi, scalar2=None, op0=ALU.mult), R=[v], W=[fin])
                            kb.do('dve', lambda e, fr=fr, q=q: e.scalar_tensor_tensor(out=sto.ap[:, 0, q, 0:1], in0=fin.ap[:, 0:1], scalar=fr, in1=fin.ap[:, 2:3],
                                                                                   op0=ALU.mult, op1=ALU.subtract), R=[v, fin], W=[sto])
                            kb.do('dve', lambda e, fr=fr: e.tensor_scalar(out=fin.ap[:, 2:3], in0=fin.ap[:, 1:2], scalar1=fr, scalar2=None, op0=ALU.mult), R=[v], W=[fin])
                            kb.do('dve', lambda e, fi=fi, q=q: e.scalar_tensor_tensor(out=sto.ap[:, 1, q, 0:1], in0=fin.ap[:, 0:1], scalar=fi, in1=fin.ap[:, 2:3],
                                                                                   op0=ALU.mult, op1=ALU.add), R=[v, fin], W=[sto])
                        rhs_r, rhs_i = s2.ap[:, 0, :], s2.ap[:, 1, :]
                        rd = [s2]
                    cw_ = craw if samp else cf_
                    kb.do('pe', lambda e, pr=pr, rhs_r=rhs_r, ypb=ypb: e.matmul(ypb.ap[:, 0:n], lhsT=cw_.ap[:, 0, pr, :], rhs=rhs_r,
                                                                              start=(pr == 0), stop=False), R=[cw_] + rd, W=[ypb])
                    kb.do('pe', lambda e, pr=pr, rhs_i=rhs_i, ypb=ypb: e.matmul(ypb.ap[:, 0:n], lhsT=cw_.ap[:, 1, pr, :], rhs=rhs_i,
                                                                              start=False, stop=(pr == 3)), R=[cw_] + rd, W=[ypb])
                    if not samp:
                        pend.append(kb.end_buffer())
                        if len(pend) == 2:
                            kb.flush_interleaved(pend)
                            pend = []
                e1, e2, gb = E1[0], E2[0], GB[tq % 2]
                kb.do('dve', lambda e, e1=e1, ypb=ypb: e.scalar_tensor_tensor(out=e1.ap[:, 0:n], in0=u.ap[:, t0:t0 + n], scalar=dc.ap[:, kt:kt + 1],
                                                                           in1=ypb.ap[:, 0:n], op0=ALU.mult, op1=ALU.add), R=[u, dc, ypb], W=[e1])
                kb.do('act', lambda e, e1=e1, e2=e2: e.activation(out=e2.ap[:, 0:n], in_=e1.ap[:, 0:n], func=AF.Square), R=[e1], W=[e2])
                kb.do('dve', lambda e, e2=e2: e.tensor_scalar(out=e2.ap[:, 0:n], in0=e2.ap[:, 0:n], scalar1=0.044715, scalar2=1.0, op0=ALU.mult, op1=ALU.add),
                      R=[], W=[e2])
                kb.do('dve', lambda e, e1=e1, e2=e2: e.tensor_tensor(out=e2.ap[:, 0:n], in0=e2.ap[:, 0:n], in1=e1.ap[:, 0:n], op=ALU.mult), R=[e1], W=[e2])
                kb.do('act', lambda e, e2=e2: e.activation(out=e2.ap[:, 0:n], in_=e2.ap[:, 0:n], func=AF.Sigmoid, scale=GELU_C), R=[], W=[e2])
                kb.do('dve', lambda e, e1=e1, e2=e2, gb=gb: e.tensor_tensor(out=gb.ap[:, 0:n], in0=e2.ap[:, 0:n], in1=e1.ap[:, 0:n], op=ALU.mult),
                      R=[e1, e2], W=[gb])
                z1, z2 = PS[6], PS[7]
                kb.do('pe', lambda e, gb=gb: e.matmul(z1.ap[:, 0:n], lhsT=wg.ap[:, kt, 0, :], rhs=gb.ap[:, 0:n], start=True, stop=True), R=[wg, gb], W=[z1])
                kb.do('pe', lambda e, gb=gb: e.matmul(z2.ap[:, 0:n], lhsT=wg.ap[:, kt, 1, :], rhs=gb.ap[:, 0:n], start=True, stop=True), R=[wg, gb], W=[z2])
                kb.do('act', lambda e, e2=e2: e.activation(out=e2.ap[:, 0:n], in_=z2.ap[:, 0:n], func=AF.Sigmoid), R=[z2], W=[e2])
                kb.do('dve', lambda e, e2=e2: e.tensor_tensor(out=ys.ap[:, t0:t0 + n], in0=e2.ap[:, 0:n], in1=z1.ap[:, 0:n], op=ALU.mult), R=[e2, z1], W=[ys])
            kb.dma('sp', SSd[kt * 128:(kt + 1) * 128, :], ys.ap, R=[ys], W=[ssB])
        outtoks.append(kb.dma('sp', st_out[l], sto.ap, R=[sto]))

    def evac_store(dst_d, row0_of, dt, scale_of=None, also_f32=None):
        stg = [ar.alloc([128, T], dt) for _ in range(2)]
        stf = [ar.alloc([128, T]) for _ in range(2)] if also_f32 is not None else None
        dB = Buf(dst_d)
        nch = len(chunks)

        def ep(mi, ci, t0, n, pb, pb2):
            s = stg[mi % 2]
            sc = 1.0 if scale_of is None else scale_of(mi)
            f32r = also_f32(mi) if also_f32 is not None else None
            if f32r is None:
                kb.do('act', lambda e: e.activation(out=s.ap[:, t0:t0 + n], in_=pb.ap[:, 0:n], func=AF.Copy, scale=sc), R=[pb], W=[s])
            else:
                sf = stf[mi % 2]
                kb.do('act', lambda e: e.activation(out=sf.ap[:, t0:t0 + n], in_=pb.ap[:, 0:n], func=AF.Copy), R=[pb], W=[sf])
                kb.do('dve', lambda e: e.tensor_scalar(out=s.ap[:, t0:t0 + n], in0=sf.ap[:, t0:t0 + n], scalar1=sc, scalar2=None, op0=ALU.mult),
                      R=[sf], W=[s])
            if ci == nch - 1:
                r0 = row0_of(mi)
                kb.dma('sp', dst_d[r0:r0 + 128, :], s.ap, R=[s], W=[dB])
                if f32r is not None:
                    dd, rr, nr = f32r
                    kb.dma('sp', dd[rr:rr + nr, :], stf[mi % 2].ap[0:nr, :], R=[stf[mi % 2]], W=[Buf(dd)])
        return ep

    def _layer(l):
            norm_phase(Xd, KTD, gains[l, 0], out='norm')
            new_phase()
            nmt = INW // 128

            if KW == 64:
                f32sel = lambda mi: (ZFd, 0, 128) if mi * 128 == AW else None
            else:
                f32sel = lambda mi: (ZFd, mi * 128 - AW, 128) if AW <= mi * 128 < AW + 2 * KW else None
            ep = evac_store(ZBd, lambda mi: mi * 128, BF16, scale_of=lambda mi: 0.125 if mi * 128 < AW else 1.0, also_f32=f32sel)
            go, _ = linear_phase(SRC_D, KTD, w_in[l], [m * 128 for m in range(nmt)], ep, tchunks=full_chunks, fresh=False)
            for mi in range(min(nmt, int(_os.environ.get('LIMIT_MT', '999')))):
                go(mi)
            kb.barrier()
            outtoks.append(kb.dma('sp', kvp_out[l], ZFd[:, SEQ - 128:SEQ]))
            import os
            if not os.environ.get('SKIP_ATT'):
                attention_phase(l)
            if not os.environ.get('SKIP_SSM'):
                ssm_phase(l)
            norm_phase(SSd, NKS, gssm_in[l], out='norm', dst_kt0=KTA)
            new_phase()
            ep = evac_store(Od, lambda mi: mi * 128, F32)
            go, _ = linear_phase(SRC_D, KTD, w_out[l], [m * 128 for m in range(KTD)], ep, tchunks=full_chunks, fresh=False)
            for mi in range(KTD):
                go(mi)
            norm_phase(Od, KTD, gains[l, 1], resid_d=Xd, out='norm', g2_ap=gains[l, 2])
            new_phase()
            stg = [ar.alloc([128, T], BF16) for _ in range(2)]
            sil = [ar.alloc([128, NMAX]) for _ in range(3)]
            aB = Buf(ACTd)

            def ep_gu(mi, ci, t0, n, pb, pb2):
                s = stg[mi % 2]
                sl = sil[nxt('sil', 3)]
                kb.do('act', lambda e: e.activation(out=sl.ap[:, 0:n], in_=pb.ap[:, 0:n], func=AF.Silu), R=[pb], W=[sl])
                kb.do('dve', lambda e: e.tensor_tensor(out=s.ap[:, t0:t0 + n], in0=sl.ap[:, 0:n], in1=pb2.ap[:, 0:n], op=ALU.mult), R=[sl, pb2], W=[s])
                if ci == len(chunks) - 1:
                    kb.dma('sp', ACTd[mi * 128:(mi + 1) * 128, :], s.ap, R=[s], W=[aB])
            go, _ = linear_phase(SRC_D, KTD, w_gu[l], [m * 128 for m in range(KTF)], ep_gu, W2cols=[DFF + m * 128 for m in range(KTF)],
                                 tchunks=full_chunks, fresh=False)
            for mi in range(KTF):
                go(mi)
            half = T // 2
            for hf in range(2):
                new_phase()
                SRC_F = src3(KTF, half)
                kb.dma('sp', SRC_F, ACTd[:, hf * half:(hf + 1) * half].rearrange("(k p) t -> p k t", p=128), W=[srcbuf])
                stg2 = [ar.alloc([128, half]) for _ in range(2)]
                oB = Buf(Od)
                hch = [(t0, n, t0 - hf * half) for (t0, n) in chunks[3 * hf:3 * hf + 3]]

                def ep_dn(mi, ci, t0, n, pb, pb2, stg2=stg2, oB=oB, hf=hf):
                    s = stg2[mi % 2]
                    kb.do('act', lambda e: e.activation(out=s.ap[:, t0 - hf * half:t0 - hf * half + n], in_=pb.ap[:, 0:n], func=AF.Copy), R=[pb], W=[s])
                    if ci == 2:
                        kb.dma('sp', Od[mi * 128:(mi + 1) * 128, hf * half:(hf + 1) * half], s.ap, R=[s], W=[oB])
                go, _ = linear_phase(SRC_F, KTF, w_dn[l], [m * 128 for m in range(KTD)], ep_dn, tchunks=hch, fresh=False)
                for mi in range(KTD):
                    go(mi)
            norm_phase(Od, KTD, gains[l, 3], resid_d=Xd, out='cast')
            new_phase()
            peb = ar.alloc([128, 2, T], BF16)
            for k2 in range(2):
                kb.dma('pool', peb.ap[:, k2, :], peT_in[l, k2 * 128:(k2 + 1) * 128, :], W=[peb], max_dma_last_dim=4096)
            wpp = [ar.alloc([128, 2, 128], BF16) for _ in range(2)]
            xrow = [ar.alloc([128, T]) for _ in range(2)]
            sg = [ar.alloc([128, NMAX]) for _ in range(3)]
            wppv = w_pp[l].rearrange("(k p) n -> p k n", p=128)
            xB = Buf(Xd)

            def ep_ple(mi, ci, t0, n, pb, pb2):
                xr_ = xrow[mi % 2]
                wp_ = wpp[mi % 2]
                if ci == 0:
                    kb.dma('pool', wp_.ap, wppv[:, :, mi * 128:(mi + 1) * 128], W=[wp_])
                    kb.dma('sp', xr_.ap, Xd[mi * 128:(mi + 1) * 128, :], R=[xB], W=[xr_])
                pj = PS[6 + nxt('pj', 2)]
                for kt in range(2):
                    kb.do('pe', lambda e, kt=kt: e.matmul(pj.ap[:, 0:n], lhsT=wp_.ap[:, kt, :], rhs=peb.ap[:, kt, t0:t0 + n], start=(kt == 0), stop=(kt == 1)),
                          R=[wp_, peb], W=[pj])
                s_ = sg[nxt('sg', 3)]
                kb.do('act', lambda e: e.activation(out=s_.ap[:, 0:n], in_=pb.ap[:, 0:n], func=AF.Sigmoid), R=[pb], W=[s_])
                kb.do('dve', lambda e: e.tensor_tensor(out=s_.ap[:, 0:n], in0=s_.ap[:, 0:n], in1=pj.ap[:, 0:n], op=ALU.mult), R=[pj], W=[s_])
                kb.do('dve', lambda e: e.tensor_tensor(out=xr_.ap[:, t0:t0 + n], in0=xr_.ap[:, t0:t0 + n], in1=s_.ap[:, 0:n], op=ALU.add), R=[s_], W=[xr_])
                if ci == len(chunks) - 1:
                    outtoks.append(kb.dma('sp', Xd[mi * 128:(mi + 1) * 128, :], xr_.ap, R=[xr_], W=[xB]))
            go, _ = linear_phase(SRC_D, KTD, w_pg[l], [m * 128 for m in range(KTD)], ep_ple, tchunks=full_chunks, fresh=False)
            for mi in range(KTD):
                go(mi)


    try:
        for l in range(DEPTH):
            _layer(l)
    except _Stop:
        print('stopped after phase', _stop_after)

    kb.barrier()
    for r in range(KTD):
        outtoks.append(kb.dma('sp', yT_out[r * 128:(r + 1) * 128, :], Xd[r * 128:(r + 1) * 128, :]))
    kb.barrier()
    if _os.environ.get('PHASE_LOG'):
        import json as _json
        _json.dump(_marks, open(_os.environ['PHASE_LOG'], 'w'))
    kb.emit()
    return nc


def _consts():
    c = np.zeros((128, 1152), np.float32)
    c[:, 0:128] = np.eye(128, dtype=np.float32)
    i = np.arange(128)[:, None]
    j = np.arange(256)[None, :]
    dist = (i - j + 128).astype(np.float32)
    valid = (dist >= 0) & (dist <= 128)
    dm = np.where(valid, dist, np.float32(BIG)).astype(np.float32)
    c[:, 128:384] = dm
    d0 = dm.copy()
    d0[:, 0:128] = BIG
    c[:, 384:640] = d0
    c[:, 640:1152] = np.arange(1, 513, dtype=np.float32)[None, :]
    return c


def prepare_inputs(C, inp):
    f = np.float32
    A = lambda k: np.asarray(inp[k], f)
    L = C.DEPTH
    NP_, NKS = C.NPAIR, C.NKS
    gains = np.stack([A(k) for k in ('g_pre_mix', 'g_post_mix', 'g_pre_ffn', 'g_post_ffn')], axis=1)
    gains = np.ascontiguousarray(gains.reshape(L, 4, C.KTD, 128).transpose(0, 1, 3, 2))
    gssm = np.ascontiguousarray(A('g_ssm_out').reshape(L, NKS, 128).transpose(0, 2, 1))
    gattn = np.ascontiguousarray(np.broadcast_to(A('g_attn_out')[:, None, :], (L, 128, C.AW)))
    gattnT = np.ascontiguousarray(A('g_attn_out').reshape(L, C.NH, 64).transpose(0, 2, 1))
    sinks = np.ascontiguousarray(np.broadcast_to(A('attn_sinks')[:, None, :], (L, 128, C.NH)))
    are = A('ssm_a_re').reshape(L, NP_, 2, 64).transpose(0, 2, 3, 1).reshape(L, 128, NP_)
    aim = A('ssm_a_im').reshape(L, NP_, 2, 64).transpose(0, 2, 3, 1).reshape(L, 128, NP_)
    ldt = np.broadcast_to(A('ssm_log_dt').reshape(L, NP_, 2, 1), (L, NP_, 2, 64)).transpose(0, 2, 3, 1).reshape(L, 128, NP_)
    ssm_ps = np.ascontiguousarray(np.stack([are, aim, ldt], axis=2))
    bpad = np.zeros((L, 128, 2, NP_, 128), f)
    for ri, key in enumerate(('ssm_b_re', 'ssm_b_im')):
        b = A(key)
        for g in range(C.NG):
            q, gh, g8 = g // 2, g % 2, g % 8
            bpad[:, g8 * 16:(g8 + 1) * 16, ri, q, gh * 64:(gh + 1) * 64] = b[:, g].transpose(0, 2, 1)
    cpad = np.zeros((L, 128, NKS, 2, 4, 128), f)
    for ri, key in enumerate(('ssm_c_re', 'ssm_c_im')):
        c = A(key)
        for g in range(C.NG):
            kt, pr, gh, g8 = g // 8, (g % 8) // 2, g % 2, g % 8
            cpad[:, gh * 64:(gh + 1) * 64, kt, ri, pr, g8 * 16:(g8 + 1) * 16] = c[:, g].transpose(0, 2, 1)
    dcol = np.ascontiguousarray(A('ssm_d').reshape(L, NKS, 128).transpose(0, 2, 1))
    wglu = np.zeros((L, 128, NKS, 2, 128), f)
    wg = A('ssm_w_glu')
    for g in range(C.NG):
        kt, g8 = g // 8, g % 8
        for hf in range(2):
            wglu[:, g8 * 16:(g8 + 1) * 16, kt, hf, g8 * 16:(g8 + 1) * 16] = wg[:, g, :, hf * 16:(hf + 1) * 16]
    shared = dict(
        w_in=A('w_in'), w_out=A('w_out'), w_gate_up=A('w_gate_up'), w_down=A('w_down'), w_ple_gate=A('w_ple_gate'),
        w_ple_proj=A('w_ple_proj'), gains=gains, gssm=gssm, gattn=gattn, gattnT=gattnT, sinks=sinks, ssm_ps=ssm_ps,
        bpad=bpad, cpad=cpad, dcol=dcol, wglu=wglu, consts=_consts())
    xp, xs = A('x_prompt'), A('x_sample')
    pp, psm = A('p_prompt'), A('p_sample')
    ck, cv = A('cache_k'), A('cache_v')
    sre, sim = A('state_ssm_re'), A('state_ssm_im')
    in_maps = []
    NS = C.NS
    for c in range(8):
        b = c % C.BATCH
        sl = slice(NS * b, NS * (b + 1))
        m = dict(shared)
        m['xT'] = np.ascontiguousarray(np.concatenate([xp[b].T, xs[sl, 0].T], axis=1))
        m['peT'] = np.ascontiguousarray(np.concatenate([pp[:, b].transpose(0, 2, 1), psm[:, sl, 0].transpose(0, 2, 1)], axis=2))
        ckc = ck[:, sl].reshape(L, NS, 128, C.KW)
        m['ck'] = np.ascontiguousarray(ckc)
        m['cv'] = np.ascontiguousarray(cv[:, sl].reshape(L, NS, 128, C.KW))
        m['ckT'] = np.ascontiguousarray(ckc.reshape(L, NS, 128, C.NKV, 64).transpose(0, 1, 3, 4, 2))
        st = np.stack([sre[:, sl], sim[:, sl]], axis=1)
        st = st.reshape(L, 2, NS, NP_, 2, 64).transpose(0, 4, 5, 1, 3, 2).reshape(L, 128, 2, NP_, NS)
        m['st0'] = np.ascontiguousarray(st)
        in_maps.append(m)
    return in_maps


def assemble(C, results):
    f = np.float32
    L, NS, B = C.DEPTH, C.NS, C.BATCH
    yp = np.zeros((B, C.SEQ, C.D), f)
    ys = np.zeros((C.DEC, 1, C.D), f)
    kp = np.zeros((L, B, 128, C.NKV, 64), f)
    vp = np.zeros_like(kp)
    srp = np.zeros((L, B, C.NG, 64), f)
    sip = np.zeros_like(srp)
    ksm = np.zeros((L, C.DEC, 128, C.NKV, 64), f)
    vsm = np.zeros_like(ksm)
    srs = np.zeros((L, C.DEC, C.NG, 64), f)
    sis = np.zeros_like(srs)
    for b in range(B):
        r = results[b]
        sl = slice(NS * b, NS * (b + 1))
        yT = r['yT']
        yp[b] = yT[:, :C.SEQ].T
        ys[sl, 0] = yT[:, C.SEQ:].T
        kv = r['kvp']
        kp[:, b] = kv[:, :C.KW].transpose(0, 2, 1).reshape(L, 128, C.NKV, 64)
        vp[:, b] = kv[:, C.KW:].transpose(0, 2, 1).reshape(L, 128, C.NKV, 64)
        ksm[:, sl] = r['ks'].reshape(L, NS, 128, C.NKV, 64)
        vsm[:, sl] = r['vs'].reshape(L, NS, 128, C.NKV, 64)
        st = r['st'].reshape(L, 2, 64, 2, C.NPAIR, 1 + NS)
        st = st.transpose(0, 3, 5, 4, 1, 2).reshape(L, 2, 1 + NS, C.NG, 64)
        srp[:, b], sip[:, b] = st[:, 0, 0], st[:, 1, 0]
        srs[:, sl], sis[:, sl] = st[:, 0, 1:], st[:, 1, 1:]
    return (yp, ys, kp, vp, srp, sip, ksm, vsm, srs, sis)


_CACHE = {}


def run(C, inp):
    key = (C.D, C.SEQ, C.DEPTH, C.BATCH, C.DEC)
    if key not in _CACHE:
        _CACHE[key] = build(C)
    in_maps = prepare_inputs(C, inp)
    res = run_bass_kernel_spmd(_CACHE[key], in_maps, core_ids=list(range(8)))
    return assemble(C, res.results)


def kernel(**inputs):
    return run(Cfg(), inputs)
```

```python
import math
import numpy as np
import concourse.bass as bass
import concourse.mybir as mybir
from concourse.bass_utils import run_bass_kernel_spmd

F32 = mybir.dt.float32
BF16 = mybir.dt.bfloat16
I32 = mybir.dt.int32
AF = mybir.ActivationFunctionType
ALU = mybir.AluOpType
AX = mybir.AxisListType

ENGS = ('pe', 'act', 'dve', 'pool', 'sp')
SEM_ROT = 20000
NDMASEM = 56
EPS = 1e-6
BIG = 1.0e9
TWO_PI = 6.283184
GELU_C = 1.5957691216057308
MAGIC = 12582912.0


class Cfg:
    def __init__(self, D=2048, SEQ=2048, DEPTH=4, BATCH=4, DEC=32):
        self.D, self.SEQ, self.DEPTH, self.BATCH, self.DEC = D, SEQ, DEPTH, BATCH, DEC
        self.NS = DEC // BATCH
        self.T = SEQ + self.NS
        self.AW = D // 2
        self.NH = self.AW // 64
        self.NKV = max(1, self.NH // 8)
        self.GRP = self.NH // self.NKV
        self.KW = self.NKV * 64
        self.SW = D - self.AW
        self.NG = self.SW // 16
        self.NKS = self.SW // 128
        self.NPAIR = self.NG // 2
        self.INW = self.AW + 2 * self.KW + self.SW
        self.DFF = -(-8 * D // (3 * 256)) * 256
        self.KTD = D // 128
        self.KTA = self.AW // 128
        self.KTF = self.DFF // 128
        self.NQB = SEQ // 128
        half = self.T // 2
        assert self.T % 2 == 0
        a = -(-half // 3)
        ch = []
        for h0 in (0, half):
            o = h0
            for i in range(3):
                n = min(a, h0 + half - o)
                ch.append((o, n))
                o += n
        self.chunks = ch
        self.NMAX = a
        assert a <= 512
        self.TQ = 256 if SEQ >= 256 else SEQ
        self.slopes = [2.0 ** (-8.0 * (h + 1) / self.NH) for h in range(self.NH)]


class Buf:
    def __init__(self, ap, excl=False):
        self.ap = ap
        self.w = {}
        self.r = {}
        self.excl = excl


def _upd(d, tok):
    k = tok[0].num
    if k not in d or d[k][1] < tok[1]:
        d[k] = tok


class _Rec:
    def __init__(self):
        self.call = None

    def __getattr__(self, name):
        def f(*a, **k):
            assert self.call is None
            self.call = (name, a, k)
            return self
        return f


class KB:
    def __init__(self, nc):
        self.nc = nc
        self.ops = {e: [] for e in ENGS}
        self.cur = {}
        self.nsem = 0
        self.waited = {}
        self.ndma = 0
        self.dpool = []
        for e in ENGS:
            self._new_sem(e)
        for i in range(NDMASEM):
            self.dpool.append([nc.alloc_semaphore("dq%d" % i), 0])

    def _new_sem(self, e):
        self.nsem += 1
        self.cur[e] = [self.nc.alloc_semaphore("p_%s_%d" % (e, self.nsem)), 0]
        if not hasattr(self, 'owner'):
            self.owner = {}
        self.owner[self.cur[e][0].num] = e

    def _waits(self, eng, deps):
        waits = []
        for d in deps:
            sem, val = d
            if eng == 'pe' and self.owner.get(sem.num) == 'pe':
                continue
            key = (eng, sem.num)
            if self.waited.get(key, 0) >= val:
                continue
            self.waited[key] = val
            waits.append((sem, val))
        return waits

    def _deps(self, R, W):
        deps = []
        for b in R:
            deps.extend(b.w.values())
            if b.excl:
                deps.extend(b.r.values())
        for b in W:
            deps.extend(b.w.values())
            deps.extend(b.r.values())
        return deps

    def _record(self, tok, R, W):
        for b in R:
            _upd(b.r, tok)
        for b in W:
            _upd(b.w, tok)

    def do(self, eng, fn, R=(), W=()):
        rec = _Rec()
        fn(rec)
        if getattr(self, 'buf', None) is not None:
            self.buf.append(('do', eng, rec.call, list(R), list(W)))
            return None
        return self._do(eng, rec.call, R, W)

    def begin_buffer(self):
        self.buf = []

    def end_buffer(self):
        b, self.buf = self.buf, None
        return b

    def flush_interleaved(self, lists):
        m = max(len(x) for x in lists)
        for k in range(m):
            for x in lists:
                if k < len(x):
                    kind, eng, call, R, W = x[k]
                    self._do(eng, call, R, W)

    def _do(self, eng, call, R=(), W=()):
        waits = self._waits(eng, self._deps(R, W))
        if self.cur[eng][1] >= SEM_ROT:
            self._new_sem(eng)
        c = self.cur[eng]
        c[1] += 1
        name, a, k = call
        self.ops[eng].append((waits, (lambda e, name=name, a=a, k=k: getattr(e, name)(*a, **k)), c[0], 1))
        tok = (c[0], c[1])
        self._record(tok, R, W)
        return tok

    def dma(self, eng, out, in_, R=(), W=(), **kw):
        waits = self._waits(eng, self._deps(R, W))
        ds = self.dpool[self.ndma % NDMASEM]
        self.ndma += 1
        ds[1] += 16
        self.ops[eng].append((waits, lambda e: e.dma_start(out=out, in_=in_, **kw), ds[0], 16))
        tok = (ds[0], ds[1])
        self._record(tok, R, W)
        return tok

    def wait_only(self, eng, deps):
        waits = self._waits(eng, deps)
        if waits:
            self.ops[eng].append((waits, None, None, 0))

    def barrier(self):
        toks = [(c[0], c[1]) for c in self.cur.values() if c[1] > 0]
        toks += [(d[0], d[1]) for d in self.dpool if d[1] > 0]
        for e in ENGS:
            self.wait_only(e, toks)

    def emit(self):
        nc = self.nc
        with nc.Block() as block:
            def run(name):
                def f(e):
                    for waits, fn, sem, inc in self.ops[name]:
                        for (s, v) in waits:
                            e.wait_ge(s, v)
                        if fn is not None:
                            fn(e).then_inc(sem, inc)
                return f
            block.tensor(run('pe'))
            block.scalar(run('act'))
            block.vector(run('dve'))
            block.gpsimd(run('pool'))
            block.sync(run('sp'))


class Arena:
    LO = 16512
    HI = 229376

    def __init__(self, nc):
        self.nc = nc
        self.cur = self.LO
        self.n = 0

    def alloc(self, shape, dt=F32):
        esz = 2 if dt == BF16 else 4
        nb = esz
        for s in shape[1:]:
            nb *= s
        nb = (nb + 63) // 64 * 64
        assert self.cur + nb <= self.HI, ("SBUF overflow", self.cur, nb)
        self.n += 1
        t = self.nc.alloc_sbuf_tensor_at("sb%d" % self.n, list(shape), dt, offset=self.cur)
        self.cur += nb
        return Buf(t.ap())


def build(cfg):
    C = cfg
    nc = bass.Bass("TRN2", target_bir_lowering=False)
    kb = KB(nc)
    ar = Arena(nc)
    D, T, SEQ, NS, DEPTH = C.D, C.T, C.SEQ, C.NS, C.DEPTH
    AW, NH, NKV, KW, SW, NKS, NPAIR, INW, DFF = C.AW, C.NH, C.NKV, C.KW, C.SW, C.NKS, C.NPAIR, C.INW, C.DFF
    KTD, KTA, KTF, NQB = C.KTD, C.KTA, C.KTF, C.NQB
    chunks, NMAX, TQ = C.chunks, C.NMAX, C.TQ
    KTMAX = max(KTF, KTD)

    def din(name, shape):
        return nc.dram_tensor(name, list(shape), F32, kind="ExternalInput").ap()

    def dout(name, shape):
        return nc.dram_tensor(name, list(shape), F32, kind="ExternalOutput").ap()

    def dscr(name, shape, dt=F32):
        return nc.dram_tensor(name, list(shape), dt).ap()

    xT_in = din("xT", [D, T])
    peT_in = din("peT", [DEPTH, 256, T])
    w_in = din("w_in", [DEPTH, D, INW])
    w_out = din("w_out", [DEPTH, D, D])
    w_gu = din("w_gate_up", [DEPTH, D, 2 * DFF])
    w_dn = din("w_down", [DEPTH, DFF, D])
    w_pg = din("w_ple_gate", [DEPTH, D, D])
    w_pp = din("w_ple_proj", [DEPTH, 256, D])
    gains = din("gains", [DEPTH, 4, 128, KTD])
    gssm_in = din("gssm", [DEPTH, 128, NKS])
    gattn_in = din("gattn", [DEPTH, 128, AW])
    gattnT_in = din("gattnT", [DEPTH, 64, NH])
    sinks_in = din("sinks", [DEPTH, 128, NH])
    ckT_in = din("ckT", [DEPTH, NS, NKV, 64, 128])
    ck_in = din("ck", [DEPTH, NS, 128, KW])
    cv_in = din("cv", [DEPTH, NS, 128, KW])
    ssm_ps_in = din("ssm_ps", [DEPTH, 128, 3, NPAIR])
    bpad_in = din("bpad", [DEPTH, 128, 2, NPAIR, 128])
    cpad_in = din("cpad", [DEPTH, 128, NKS, 2, 4, 128])
    dcol_in = din("dcol", [DEPTH, 128, NKS])
    wglu_in = din("wglu", [DEPTH, 128, NKS, 2, 128])
    st0_in = din("st0", [DEPTH, 128, 2, NPAIR, NS])
    consts_in = din("consts", [128, 128 + 256 + 256 + 512])

    yT_out = dout("yT", [D, T])
    kvp_out = dout("kvp", [DEPTH, 2 * KW, 128])
    ks_out = dout("ks", [DEPTH, NS, 128, KW])
    vs_out = dout("vs", [DEPTH, NS, 128, KW])
    st_out = dout("st", [DEPTH, 128, 2, NPAIR, 1 + NS])

    Xd = dscr("X", [D, T])
    ZBd = dscr("ZB", [INW, T], BF16)
    ZFd = dscr("ZF", [2 * KW, T])
    Od = dscr("O", [D, T])
    SSd = dscr("SS", [SW, T])
    ACTd = dscr("ACTs", [DFF, T], BF16)
    MGSd = dscr("MGS", [AW, NS], BF16)

    ident = ar.alloc([128, 128], BF16)
    identf = ar.alloc([128, 128], F32)
    ones = ar.alloc([128, 128], BF16)
    DIST = ar.alloc([128, 256])
    DIST0 = ar.alloc([128, 256])
    IOT = ar.alloc([128, 512])
    srcbuf = ar.alloc([128, max(KTD * T, KTF * (T // 2))], BF16)
    phase_mark = ar.cur

    PS = [Buf(nc.alloc_psum_tensor("ps%d" % i, [128, 512], F32).ap(), excl=True) for i in range(8)]

    def src3(kt_n, tn):
        return srcbuf.ap[:, 0:kt_n * tn].rearrange("p (k t) -> p k t", t=tn)

    SRC_D = src3(KTD, T)

    outtoks = []

    class _Stop(Exception):
        pass
    import os as _os
    _stop_after = int(_os.environ.get('STOP_AFTER', '100000'))
    _pc = [0]

    _marks = []

    def new_phase(name=None):
        import inspect
        if name is None:
            name = inspect.stack()[1].function + ':' + str(inspect.stack()[1].lineno)
        _marks.append((name, sum(1 for o in kb.ops['pe'] if o[1] is not None)))
        _pc[0] += 1
        if _pc[0] > _stop_after:
            raise _Stop()
        kb.barrier()
        ar.cur = phase_mark

    kb.dma('sp', identf.ap, consts_in[:, 0:128], W=[identf])
    kb.dma('pool', ident.ap, consts_in[:, 0:128], W=[ident])
    kb.dma('sp', DIST.ap, consts_in[:, 128:384], W=[DIST])
    kb.dma('sp', DIST0.ap, consts_in[:, 384:640], W=[DIST0])
    kb.dma('sp', IOT.ap, consts_in[:, 640:1152], W=[IOT])
    kb.do('pool', lambda e: e.memset(ones.ap, 1.0), W=[ones])
    xw = Buf(Xd)
    for r in range(KTD):
        kb.dma('sp', Xd[r * 128:(r + 1) * 128, :], xT_in[r * 128:(r + 1) * 128, :], W=[xw])

    rot = {}

    def nxt(key, n):
        rot[key] = (rot.get(key, -1) + 1) % n
        return rot[key]

    def norm_phase(src_d, KT, g1_ap, resid_d=None, out=None, g2_ap=None, dst_kt0=0, Fdim=None):
        new_phase()
        Fdim = KT * 128
        g1 = ar.alloc([128, KT])
        kb.dma('sp', g1.ap, g1_ap, W=[g1])
        g2 = None
        if g2_ap is not None:
            g2 = ar.alloc([128, KT])
            kb.dma('sp', g2.ap, g2_ap, W=[g2])
        xin = [ar.alloc([128, KT, NMAX]) for _ in range(2)]
        xr = [ar.alloc([128, KT, NMAX]) for _ in range(2)] if resid_d is not None else None
        sq = [ar.alloc([128, NMAX], BF16) for _ in range(3)]
        rs = [ar.alloc([128, NMAX]) for _ in range(2)]
        tmp = [ar.alloc([128, NMAX]) for _ in range(3)]
        sv = src_d.rearrange("(k p) t -> p k t", p=128)
        xv = resid_d.rearrange("(k p) t -> p k t", p=128) if resid_d is not None else None
        srcB = Buf(src_d)
        psA, psB = PS[6], PS[7]

        def rstd_of(buf_in, n, psb, rsb):
            for kt in range(KT):
                s = sq[nxt('sq', 3)]
                kb.do('act', lambda e, kt=kt, s=s: e.activation(out=s.ap[:, 0:n], in_=buf_in.ap[:, kt, 0:n], func=AF.Square),
                      R=[buf_in], W=[s])
                kb.do('pe', lambda e, kt=kt, s=s: e.matmul(psb.ap[:, 0:n], lhsT=ones.ap, rhs=s.ap[:, 0:n],
                                                           start=(kt == 0), stop=(kt == KT - 1)), R=[s, ones], W=[psb])
            kb.do('dve', lambda e: e.tensor_scalar(out=rsb.ap[:, 0:n], in0=psb.ap[:, 0:n], scalar1=1.0 / Fdim, scalar2=EPS,
                                                   op0=ALU.mult, op1=ALU.add), R=[psb], W=[rsb])
            kb.do('act', lambda e: e.activation(out=rsb.ap[:, 0:n], in_=rsb.ap[:, 0:n], func=AF.Ln), R=[rsb], W=[rsb])
            kb.do('act', lambda e: e.activation(out=rsb.ap[:, 0:n], in_=rsb.ap[:, 0:n], func=AF.Exp, scale=-0.5), R=[rsb], W=[rsb])

        for ci, (t0, n) in enumerate(chunks):
            xi = xin[ci % 2]
            kb.dma('sp', xi.ap[:, :, 0:n], sv[:, :, t0:t0 + n], R=[srcB], W=[xi])
            r1 = rs[0]
            rstd_of(xi, n, psA, r1)
            if resid_d is None:
                for kt in range(KT):
                    kb.do('dve', lambda e, kt=kt: e.scalar_tensor_tensor(
                        out=SRC_D[:, dst_kt0 + kt, t0:t0 + n], in0=xi.ap[:, kt, 0:n], scalar=g1.ap[:, kt:kt + 1],
                        in1=r1.ap[:, 0:n], op0=ALU.mult, op1=ALU.mult), R=[xi, g1, r1], W=[srcbuf])
                continue
            xx = xr[ci % 2]
            xB = Buf(resid_d)
            kb.dma('sp', xx.ap[:, :, 0:n], xv[:, :, t0:t0 + n], R=[xB], W=[xx])
            for kt in range(KT):
                tb = tmp[nxt('tmp', 3)]
                kb.do('dve', lambda e, kt=kt, tb=tb: e.scalar_tensor_tensor(
                    out=tb.ap[:, 0:n], in0=xi.ap[:, kt, 0:n], scalar=g1.ap[:, kt:kt + 1], in1=r1.ap[:, 0:n],
                    op0=ALU.mult, op1=ALU.mult), R=[xi, g1, r1], W=[tb])
                kb.do('dve', lambda e, kt=kt, tb=tb: e.tensor_tensor(out=xx.ap[:, kt, 0:n], in0=xx.ap[:, kt, 0:n],
                                                                      in1=tb.ap[:, 0:n], op=ALU.add), R=[tb], W=[xx])
            outtoks.append(kb.dma('sp', xv[:, :, t0:t0 + n], xx.ap[:, :, 0:n], R=[xx], W=[xB]))
            if out == 'norm':
                r2 = rs[1]
                rstd_of(xx, n, psB, r2)
                for kt in range(KT):
                    kb.do('dve', lambda e, kt=kt: e.scalar_tensor_tensor(
                        out=SRC_D[:, dst_kt0 + kt, t0:t0 + n], in0=xx.ap[:, kt, 0:n], scalar=g2.ap[:, kt:kt + 1],
                        in1=r2.ap[:, 0:n], op0=ALU.mult, op1=ALU.mult), R=[xx, g2, r2], W=[srcbuf])
            elif out == 'cast':
                kb.do('act', lambda e: e.activation(out=SRC_D[:, dst_kt0:dst_kt0 + KT, t0:t0 + n], in_=xx.ap[:, :, 0:n],
                                                    func=AF.Copy), R=[xx], W=[srcbuf])

    def linear_phase(srcv, KT, W_ap, cols, epi, W2cols=None, tchunks=None, fresh=True):
        if fresh:
            new_phase()
        wb = [ar.alloc([128, KT, 128], BF16) for _ in range(3)]
        wb2 = [ar.alloc([128, KT, 128], BF16) for _ in range(2)] if W2cols is not None else None
        wv = W_ap.rearrange("(k p) n -> p k n", p=128)
        tch = tchunks if tchunks is not None else chunks
        st = {}

        def go(mi):
            c0 = cols[mi]
            w = wb[mi % 3]
            kb.dma('pool', w.ap, wv[:, :, c0:c0 + 128], W=[w])
            w2 = None
            if W2cols is not None:
                w2 = wb2[mi % 2]
                kb.dma('pool', w2.ap, wv[:, :, W2cols[mi]:W2cols[mi] + 128], W=[w2])
            for ci, (t0, n, s0) in enumerate(tch):
                if W2cols is None:
                    pb = PS[nxt('lin', 6)]
                    pb2 = None
                else:
                    j = nxt('lin2', 3)
                    pb, pb2 = PS[2 * j], PS[2 * j + 1]
                for kt in range(KT):
                    kb.do('pe', lambda e, kt=kt, pb=pb, w=w: e.matmul(pb.ap[:, 0:n], lhsT=w.ap[:, kt, :], rhs=srcv[:, kt, s0:s0 + n],
                                                                   start=(kt == 0), stop=(kt == KT - 1)), R=[w, srcbuf], W=[pb])
                if pb2 is not None:
                    for kt in range(KT):
                        kb.do('pe', lambda e, kt=kt, pb2=pb2, w2=w2: e.matmul(pb2.ap[:, 0:n], lhsT=w2.ap[:, kt, :], rhs=srcv[:, kt, s0:s0 + n],
                                                                         start=(kt == 0), stop=(kt == KT - 1)), R=[w2, srcbuf], W=[pb2])
                epi(mi, ci, t0, n, pb, pb2)
        return go, st

    full_chunks = [(t0, n, t0) for (t0, n) in chunks]

    def attention_phase(l):
        new_phase()
        kTd = ar.alloc([128, NKV, 128 + T], BF16)
        Vtm = ar.alloc([128, NQB + 1, KW], BF16)
        vTi = [ar.alloc([128, 128], BF16) for _ in range(2)]
        qTb = [ar.alloc([128, KTA, 128], BF16) for _ in range(2)]
        atm = [ar.alloc([128, AW]) for _ in range(2)]
        hn = [ar.alloc([128, AW], BF16) for _ in range(2)]
        junk = ar.alloc([128, AW], BF16)
        sm2 = [ar.alloc([128, 4]) for _ in range(2)]
        gat = ar.alloc([128, AW])
        snk = ar.alloc([128, NH])
        zb = Buf(ZBd)
        kb.dma('sp', gat.ap, gattn_in[l], W=[gat])
        kb.dma('sp', snk.ap, sinks_in[l], W=[snk])
        kb.do('pool', lambda e: e.memset(kTd.ap[:, :, 0:128], 0.0), W=[kTd])
        kb.do('pool', lambda e: e.memset(Vtm.ap[:, 0, :], 0.0), W=[Vtm])
        for g in range(NKV):
            for cp in range(2):
                kb.dma('sp', kTd.ap[cp * 64:(cp + 1) * 64, g, 128:128 + T], ZBd[AW + g * 64:AW + (g + 1) * 64, :], R=[zb], W=[kTd])
        PSb = [Buf(PS[i].ap.bitcast(BF16), excl=True) for i in range(8)]
        for i in range(8):
            PSb[i].w, PSb[i].r = PS[i].w, PS[i].r
        for b in range(NQB):
            vi = vTi[b % 2]
            kb.dma('sp', vi.ap[0:KW, :], ZBd[AW + KW:AW + 2 * KW, b * 128:(b + 1) * 128], R=[zb], W=[vi])
            pt = PSb[7]
            kb.do('pe', lambda e, vi=vi, pt=pt: e.transpose(out=pt.ap[:, 0:KW], in_=vi.ap[0:KW, :], identity=ident.ap[0:KW, 0:KW]),
                  R=[vi, ident], W=[pt])
            kb.do('act', lambda e, b=b, pt=pt: e.activation(out=Vtm.ap[:, b + 1, :], in_=pt.ap[:, 0:KW], func=AF.Copy), R=[pt], W=[Vtm])

        NRR = 8
        Sb = [ar.alloc([128, 256]) for _ in range(NRR)]
        Pb = [ar.alloc([128, 256], BF16) for _ in range(NRR)]
        PTs = [ar.alloc([128, 2, 128], BF16) for _ in range(NRR)]
        sm = [ar.alloc([128, 8]) for _ in range(NRR)]

        def pipeline(units, stages, after=None):
            ns = len(stages)
            for t in range(len(units) + ns - 1):
                for si, st in enumerate(stages):
                    ui = t - si
                    if 0 <= ui < len(units):
                        st(units[ui])
                        if si == ns - 1 and after is not None:
                            after(ui)

        def sA(u):
            S_, m_, npart, nk, h, sps = Sb[u['i']], sm[u['i']], u['np'], u['nk'], u['h'], u['sps']
            kb.do('dve', lambda e: e.scalar_tensor_tensor(out=S_.ap[0:npart, 0:nk], in0=u['dist'], scalar=-C.slopes[h],
                                                          in1=sps.ap[0:npart, 0:nk], op0=ALU.mult, op1=ALU.add),
                  R=[sps, DIST, DIST0], W=[S_])
            kb.do('dve', lambda e: e.reduce_max(out=m_.ap[0:npart, 0:1], in_=S_.ap[0:npart, 0:nk], axis=AX.X), R=[S_], W=[m_])
            kb.do('dve', lambda e: e.tensor_tensor(out=m_.ap[0:npart, 0:1], in0=m_.ap[0:npart, 0:1], in1=snk.ap[0:npart, h:h + 1],
                                                   op=ALU.max), R=[snk], W=[m_])
            kb.do('dve', lambda e: e.tensor_scalar(out=m_.ap[0:npart, 1:2], in0=m_.ap[0:npart, 0:1], scalar1=-1.0, scalar2=None,
                                                   op0=ALU.mult), R=[], W=[m_])

        def sB(u):
            S_, P_, m_, npart, nk, h = Sb[u['i']], Pb[u['i']], sm[u['i']], u['np'], u['nk'], u['h']
            kb.do('act', lambda e: e.activation(out=P_.ap[0:npart, 0:nk], in_=S_.ap[0:npart, 0:nk], func=AF.Exp,
                                                bias=m_.ap[0:npart, 1:2], scale=1.0, accum_out=m_.ap[0:npart, 2:3]),
                  R=[S_, m_], W=[P_, m_])
            kb.do('act', lambda e: e.activation(out=m_.ap[0:npart, 3:4], in_=snk.ap[0:npart, h:h + 1], func=AF.Exp,
                                                bias=m_.ap[0:npart, 1:2], scale=1.0), R=[snk, m_], W=[m_])

        def sC(u):
            m_, npart = sm[u['i']], u['np']
            kb.do('dve', lambda e: e.tensor_tensor(out=m_.ap[0:npart, 4:5], in0=m_.ap[0:npart, 2:3], in1=m_.ap[0:npart, 3:4],
                                                   op=ALU.add), R=[m_], W=[m_])
            kb.do('dve', lambda e: e.reciprocal(out=m_.ap[0:npart, 5:6], in_=m_.ap[0:npart, 4:5]), R=[m_], W=[m_])

        units = []
        for b in range(NQB):
            for h in range(NH):
                units.append(dict(b=b, h=h, g=h // C.GRP, hp=(h % 2) * 64, np=128, nk=256,
                                  dist=(DIST0 if b == 0 else DIST).ap))

        def pA(u):
            b, h, g, hp = u['b'], u['h'], u['g'], u['hp']
            if h == 0:
                qb = qTb[b % 2]
                kb.dma('sp', qb.ap, ZBd[0:AW, b * 128:(b + 1) * 128].rearrange("(k p) t -> p k t", p=128), R=[zb], W=[qb])
            qb = qTb[b % 2]
            u['i'] = nxt('att', NRR)
            u['sps'] = sps = PS[nxt('sps', 3)]
            kb.do('pe', lambda e: e.matmul(sps.ap[:, 0:256], lhsT=qb.ap[hp:hp + 64, h // 2, :],
                                           rhs=kTd.ap[hp:hp + 64, g, b * 128:b * 128 + 256], start=True, stop=True), R=[qb, kTd], W=[sps])
            sA(u)

        def pC(u):
            sC(u)
            i = u['i']
            u['ptp'] = ptp = PSb[3 + nxt('ptp', 2)]
            for j in range(2):
                kb.do('pe', lambda e, j=j: e.transpose(out=ptp.ap[:, j * 128:(j + 1) * 128], in_=Pb[i].ap[:, j * 128:(j + 1) * 128],
                                                       identity=ident.ap), R=[Pb[i], ident], W=[ptp])

        def pD(u):
            i, ptp = u['i'], u['ptp']
            kb.do('act', lambda e: e.activation(out=PTs[i].ap, in_=ptp.ap[:, 0:256].rearrange("p (j q) -> p j q", j=2), func=AF.Copy),
                  R=[ptp], W=[PTs[i]])

        def pE(u):
            i, b, g = u['i'], u['b'], u['g']
            u['ops'] = ops_ = PS[5 + nxt('ops', 2)]
            for j in range(2):
                kb.do('pe', lambda e, j=j: e.matmul(ops_.ap[:, 0:64], lhsT=PTs[i].ap[:, j, :], rhs=Vtm.ap[:, b + j, g * 64:(g + 1) * 64],
                                                    start=(j == 0), stop=(j == 1)), R=[PTs[i], Vtm], W=[ops_])

        def pF(u):
            i, h, ops_ = u['i'], u['h'], u['ops']
            am = atm[u['b'] % 2]
            kb.do('dve', lambda e: e.tensor_scalar(out=am.ap[:, h * 64:(h + 1) * 64], in0=ops_.ap[:, 0:64], scalar1=sm[i].ap[:, 5:6],
                                                   scalar2=None, op0=ALU.mult), R=[ops_, sm[i]], W=[am])

        def block_done(ui):
            u = units[ui]
            if u['h'] != NH - 1:
                return
            b = u['b']
            am, s2, hb = atm[b % 2], sm2[b % 2], hn[b % 2]
            kb.do('act', lambda e: e.activation(out=junk.ap, in_=am.ap, func=AF.Square, accum_out=s2.ap[:, 0:1]), R=[am], W=[junk, s2])
            kb.do('dve', lambda e: e.tensor_scalar(out=s2.ap[:, 1:2], in0=s2.ap[:, 0:1], scalar1=1.0 / AW, scalar2=EPS,
                                                   op0=ALU.mult, op1=ALU.add), R=[s2], W=[s2])
            kb.do('act', lambda e: e.activation(out=s2.ap[:, 1:2], in_=s2.ap[:, 1:2], func=AF.Ln), R=[s2], W=[s2])
            kb.do('act', lambda e: e.activation(out=s2.ap[:, 1:2], in_=s2.ap[:, 1:2], func=AF.Exp, scale=-0.5), R=[s2], W=[s2])
            kb.do('dve', lambda e: e.scalar_tensor_tensor(out=hb.ap, in0=am.ap, scalar=s2.ap[:, 1:2], in1=gat.ap, op0=ALU.mult, op1=ALU.mult),
                  R=[am, s2, gat], W=[hb])
            for kt in range(KTA):
                pt = PSb[7]
                kb.do('pe', lambda e, kt=kt: e.transpose(out=pt.ap[:, 0:128], in_=hb.ap[:, kt * 128:(kt + 1) * 128], identity=ident.ap),
                      R=[hb, ident], W=[pt])
                kb.do('act', lambda e, kt=kt: e.activation(out=SRC_D[:, kt, b * 128:(b + 1) * 128], in_=pt.ap[:, 0:128], func=AF.Copy),
                      R=[pt], W=[srcbuf])

        pipeline(units, [pA, sB, pC, pD, pE, pF], after=block_done)

        kS = [ar.alloc([128, NKV, 132], BF16) for _ in range(2)]
        vS = [ar.alloc([128, KW], BF16) for _ in range(2)]
        vN = [ar.alloc([1, KW], BF16) for _ in range(2)]
        qS = ar.alloc([128, KTA, NS], BF16)
        PnS = [ar.alloc([1, 132], BF16) for _ in range(NRR)]
        PTS = [ar.alloc([128, 2], BF16) for _ in range(NRR)]
        aS = ar.alloc([64, NH, NS])
        zf = Buf(ZFd)
        kb.dma('sp', qS.ap, ZBd[0:AW, SEQ:SEQ + NS].rearrange("(k p) t -> p k t", p=128), R=[zb], W=[qS])
        sunits = []
        for n in range(NS):
            for h in range(NH):
                sunits.append(dict(n=n, h=h, g=h // C.GRP, hp=(h % 2) * 64, np=1, nk=129, dist=DIST.ap[0:1, 0:129]))

        def qA(u):
            n, h, g, hp = u['n'], u['h'], u['g'], u['hp']
            ks_, vs_, vn_ = kS[n % 2], vS[n % 2], vN[n % 2]
            if h == 0:
                for g_ in range(NKV):
                    for cp in range(2):
                        kb.dma('pool', ks_.ap[cp * 64:(cp + 1) * 64, g_, 0:128], ckT_in[l, n, g_], W=[ks_])
                kb.do('pool', lambda e: e.tensor_copy(out=ks_.ap[:, :, 128:129], in_=kTd.ap[:, :, 128 + SEQ + n:128 + SEQ + n + 1]),
                      R=[kTd], W=[ks_])
                kb.dma('pool', vs_.ap, cv_in[l, n], W=[vs_])
                kb.dma('pool', vn_.ap, ZFd[KW:2 * KW, SEQ + n:SEQ + n + 1].rearrange("k o -> o k"), R=[zf], W=[vn_], allow_slow_non_contiguous=True)
                outtoks.append(kb.dma('sp', ks_out[l, n, 0:127, :], ck_in[l, n, 1:128, :]))
                outtoks.append(kb.dma('sp', vs_out[l, n, 0:127, :], cv_in[l, n, 1:128, :]))
                outtoks.append(kb.dma('sp', ks_out[l, n, 127:128, :], ZFd[0:KW, SEQ + n:SEQ + n + 1].rearrange("k o -> o k"), R=[zf], allow_slow_non_contiguous=True))
                outtoks.append(kb.dma('sp', vs_out[l, n, 127:128, :], ZFd[KW:2 * KW, SEQ + n:SEQ + n + 1].rearrange("k o -> o k"), R=[zf], allow_slow_non_contiguous=True))
            u['i'] = nxt('att', NRR)
            u['sps'] = sps = PS[nxt('sps', 3)]
            kb.do('pe', lambda e: e.matmul(sps.ap[0:1, 0:129], lhsT=qS.ap[hp:hp + 64, h // 2, n:n + 1], rhs=ks_.ap[hp:hp + 64, g, 0:129],
                                           start=True, stop=True), R=[qS, ks_], W=[sps])
            sA(u)

        def qC(u):
            sC(u)
            i = u['i']
            pn = PnS[i]
            kb.do('dve', lambda e: e.tensor_scalar(out=pn.ap[0:1, 0:129], in0=Pb[i].ap[0:1, 0:129], scalar1=sm[i].ap[0:1, 5:6], scalar2=None,
                                                   op0=ALU.mult), R=[Pb[i], sm[i]], W=[pn])
            u['ptp'] = ptp = PSb[3 + nxt('ptp', 2)]
            kb.do('pe', lambda e: e.transpose(out=ptp.ap[:, 0:1], in_=pn.ap[0:1, 0:128], identity=ident.ap[0:1, 0:1]), R=[pn, ident], W=[ptp])

        def qD(u):
            i, ptp = u['i'], u['ptp']
            kb.do('act', lambda e: e.activation(out=PTS[i].ap[:, 0:1], in_=ptp.ap[:, 0:1], func=AF.Copy), R=[ptp], W=[PTS[i]])

        def qE(u):
            i, g, n = u['i'], u['g'], u['n']
            vs_, vn_, pn = vS[n % 2], vN[n % 2], PnS[i]
            u['ops'] = ops_ = PS[5 + nxt('ops', 2)]
            kb.do('pe', lambda e: e.matmul(ops_.ap[0:64, 0:1], lhsT=vs_.ap[:, g * 64:(g + 1) * 64], rhs=PTS[i].ap[:, 0:1], start=True, stop=False),
                  R=[PTS[i], vs_], W=[ops_])
            kb.do('pe', lambda e: e.matmul(ops_.ap[0:64, 0:1], lhsT=vn_.ap[0:1, g * 64:(g + 1) * 64], rhs=pn.ap[0:1, 128:129], start=False, stop=True),
                  R=[pn, vn_], W=[ops_])

        def qF(u):
            ops_, h, n = u['ops'], u['h'], u['n']
            kb.do('act', lambda e: e.activation(out=aS.ap[:, h, n:n + 1], in_=ops_.ap[0:64, 0:1], func=AF.Copy), R=[ops_], W=[aS])

        if not _os.environ.get('PIPE_SAMPLE'):
            for u_ in sunits:
                for st_ in (qA, sB, qC, qD, qE, qF):
                    st_(u_)
        else:
            pipeline(sunits, [qA, sB, qC, qD, qE, qF])
        sqS = ar.alloc([64, NH * NS], BF16)
        ssS = ar.alloc([64, NS])
        gT = ar.alloc([64, NH])
        hS = ar.alloc([64, NH, NS])
        hSb = ar.alloc([64, NH, NS], BF16)
        kb.dma('sp', gT.ap, gattnT_in[l], W=[gT])
        kb.do('act', lambda e: e.activation(out=sqS.ap, in_=aS.ap.rearrange("p h n -> p (h n)"), func=AF.Square), R=[aS], W=[sqS])
        kb.do('pe', lambda e: e.matmul(PS[7].ap[0:64, 0:NH * NS], lhsT=ones.ap[0:64, 0:64], rhs=sqS.ap, start=True, stop=True),
              R=[sqS, ones], W=[PS[7]])
        kb.do('dve', lambda e: e.tensor_reduce(out=ssS.ap, in_=PS[7].ap[0:64, 0:NH * NS].rearrange("p (h n) -> p n h", n=NS),
                                               axis=AX.X, op=ALU.add), R=[PS[7]], W=[ssS])
        kb.do('dve', lambda e: e.tensor_scalar(out=ssS.ap, in0=ssS.ap, scalar1=1.0 / AW, scalar2=EPS, op0=ALU.mult, op1=ALU.add),
              R=[ssS], W=[ssS])
        kb.do('act', lambda e: e.activation(out=ssS.ap, in_=ssS.ap, func=AF.Ln), R=[ssS], W=[ssS])
        kb.do('act', lambda e: e.activation(out=ssS.ap, in_=ssS.ap, func=AF.Exp, scale=-0.5), R=[ssS], W=[ssS])
        kb.do('dve', lambda e: e.tensor_tensor(out=hS.ap, in0=aS.ap, in1=gT.ap.unsqueeze(2).broadcast_to([64, NH, NS]), op=ALU.mult),
              R=[aS, gT], W=[hS])
        kb.do('dve', lambda e: e.tensor_tensor(out=hSb.ap, in0=hS.ap, in1=ssS.ap.unsqueeze(1).broadcast_to([64, NH, NS]), op=ALU.mult),
              R=[hS, ssS], W=[hSb])
        mg = Buf(MGSd)
        kb.dma('sp', MGSd.rearrange("(h d) n -> d h n", d=64), hSb.ap, R=[hSb], W=[mg])
        kb.dma('sp', SRC_D[:, 0:KTA, SEQ:SEQ + NS], MGSd.rearrange("(k p) n -> p k n", p=128), R=[mg], W=[srcbuf])

    def ssm_phase(l):
        new_phase()
        ps_ = ar.alloc([128, 3, NPAIR])
        kb.dma('sp', ps_.ap, ssm_ps_in[l], W=[ps_])
        NV = 17
        v = ar.alloc([128, NV, NPAIR])
        vi = ar.alloc([128, NPAIR], I32)
        are, aim, ldt = ps_.ap[:, 0, :], ps_.ap[:, 1, :], ps_.ap[:, 2, :]
        V_DT, V_DRE, V_TH, V_R, V_A, V_SIN, V_COS, V_ABR, V_ABI, V_FR, V_FI, V_IFR, V_IFI, V_T1, V_T2, V_T3, V_NFI = range(17)

        def vv(i):
            return v.ap[:, i, :]

        def tiny(eng, fn):
            kb.do(eng, fn, R=[v, ps_], W=[v])

        def wrap_turns(eng_ap_in, out_i):
            kb.do('dve', lambda e: e.tensor_copy(out=vi.ap, in_=eng_ap_in), R=[v], W=[vi])
            kb.do('dve', lambda e: e.tensor_copy(out=vv(V_T1), in_=vi.ap), R=[vi, v], W=[v])
            tiny('dve', lambda e: e.tensor_tensor(out=vv(out_i), in0=eng_ap_in, in1=vv(V_T1), op=ALU.subtract))
            tiny('dve', lambda e: e.tensor_scalar(out=vv(V_T1), in0=vv(out_i), scalar1=0.5, scalar2=None, op0=ALU.is_gt))
            tiny('dve', lambda e: e.tensor_tensor(out=vv(out_i), in0=vv(out_i), in1=vv(V_T1), op=ALU.subtract))
            tiny('dve', lambda e: e.tensor_scalar(out=vv(V_T1), in0=vv(out_i), scalar1=-0.5, scalar2=None, op0=ALU.is_lt))
            tiny('dve', lambda e: e.tensor_tensor(out=vv(out_i), in0=vv(out_i), in1=vv(V_T1), op=ALU.add))

        tiny('act', lambda e: e.activation(out=vv(V_DT), in_=ldt, func=AF.Exp))
        tiny('dve', lambda e: e.tensor_tensor(out=vv(V_DRE), in0=vv(V_DT), in1=are, op=ALU.mult))
        tiny('dve', lambda e: e.tensor_tensor(out=vv(V_TH), in0=vv(V_DT), in1=aim, op=ALU.mult))
        tiny('act', lambda e: e.activation(out=vv(V_R), in_=vv(V_DRE), func=AF.Exp))
        tiny('dve', lambda e: e.tensor_scalar(out=vv(V_T2), in0=vv(V_TH), scalar1=1.0 / (2 * math.pi), scalar2=None, op0=ALU.mult))
        wrap_turns(vv(V_T2), V_A)
        tiny('act', lambda e: e.activation(out=vv(V_SIN), in_=vv(V_A), func=AF.Sin, scale=TWO_PI))
        tiny('dve', lambda e: e.tensor_scalar(out=vv(V_T2), in0=vv(V_A), scalar1=0.25, scalar2=None, op0=ALU.add))
        wrap_turns(vv(V_T2), V_T3)
        tiny('act', lambda e: e.activation(out=vv(V_COS), in_=vv(V_T3), func=AF.Sin, scale=TWO_PI))
        tiny('dve', lambda e: e.tensor_tensor(out=vv(V_ABR), in0=vv(V_R), in1=vv(V_COS), op=ALU.mult))
        tiny('dve', lambda e: e.tensor_tensor(out=vv(V_ABI), in0=vv(V_R), in1=vv(V_SIN), op=ALU.mult))
        tiny('dve', lambda e: e.tensor_tensor(out=vv(V_T1), in0=are, in1=are, op=ALU.mult))
        tiny('dve', lambda e: e.tensor_tensor(out=vv(V_T2), in0=aim, in1=aim, op=ALU.mult))
        tiny('dve', lambda e: e.tensor_tensor(out=vv(V_T1), in0=vv(V_T1), in1=vv(V_T2), op=ALU.add))
        tiny('dve', lambda e: e.reciprocal(out=vv(V_T1), in_=vv(V_T1)))
        tiny('dve', lambda e: e.tensor_scalar(out=vv(V_T2), in0=vv(V_ABR), scalar1=-1.0, scalar2=None, op0=ALU.add))
        tiny('dve', lambda e: e.tensor_tensor(out=vv(V_FR), in0=vv(V_T2), in1=are, op=ALU.mult))
        tiny('dve', lambda e: e.tensor_tensor(out=vv(V_T3), in0=vv(V_ABI), in1=aim, op=ALU.mult))
        tiny('dve', lambda e: e.tensor_tensor(out=vv(V_FR), in0=vv(V_FR), in1=vv(V_T3), op=ALU.add))
        tiny('dve', lambda e: e.tensor_tensor(out=vv(V_FR), in0=vv(V_FR), in1=vv(V_T1), op=ALU.mult))
        tiny('dve', lambda e: e.tensor_tensor(out=vv(V_FI), in0=vv(V_ABI), in1=are, op=ALU.mult))
        tiny('dve', lambda e: e.tensor_tensor(out=vv(V_T3), in0=vv(V_T2), in1=aim, op=ALU.mult))
        tiny('dve', lambda e: e.tensor_tensor(out=vv(V_FI), in0=vv(V_FI), in1=vv(V_T3), op=ALU.subtract))
        tiny('dve', lambda e: e.tensor_tensor(out=vv(V_FI), in0=vv(V_FI), in1=vv(V_T1), op=ALU.mult))
        tiny('dve', lambda e: e.tensor_tensor(out=vv(V_T1), in0=vv(V_FR), in1=vv(V_FR), op=ALU.mult))
        tiny('dve', lambda e: e.tensor_tensor(out=vv(V_T2), in0=vv(V_FI), in1=vv(V_FI), op=ALU.mult))
        tiny('dve', lambda e: e.tensor_tensor(out=vv(V_T1), in0=vv(V_T1), in1=vv(V_T2), op=ALU.add))
        tiny('dve', lambda e: e.reciprocal(out=vv(V_T1), in_=vv(V_T1)))
        tiny('dve', lambda e: e.tensor_tensor(out=vv(V_IFR), in0=vv(V_FR), in1=vv(V_T1), op=ALU.mult))
        tiny('dve', lambda e: e.tensor_tensor(out=vv(V_IFI), in0=vv(V_FI), in1=vv(V_T1), op=ALU.mult))
        tiny('dve', lambda e: e.tensor_scalar(out=vv(V_IFI), in0=vv(V_IFI), scalar1=-1.0, scalar2=None, op0=ALU.mult))
        tiny('dve', lambda e: e.tensor_scalar(out=vv(V_NFI), in0=vv(V_FI), scalar1=-1.0, scalar2=None, op0=ALU.mult))

        bpk = [ar.alloc([128, 2, 4, 128], BF16) for _ in range(2)]
        wg = ar.alloc([128, NKS, 2, 128], BF16)
        kb.dma('pool', wg.ap, wglu_in[l], W=[wg], max_dma_last_dim=4096)
        dc = ar.alloc([128, NKS])
        kb.dma('sp', dc.ap, dcol_in[l], W=[dc])
        st0 = ar.alloc([128, 2, NPAIR, NS])
        kb.dma('sp', st0.ap, st0_in[l], W=[st0])
        sto = ar.alloc([128, 2, NPAIR, 1 + NS])
        uT = [ar.alloc([128, T], BF16) for _ in range(1)]
        cpf = [ar.alloc([128, 2, 4, 128]) for _ in range(1)]
        cf = [ar.alloc([128, 2, 4, 128], BF16) for _ in range(2)]
        cft = ar.alloc([128, 128])
        RT = [ar.alloc([128, TQ]) for _ in range(4)]
        NW = 4
        AT = [ar.alloc([128, TQ]) for _ in range(NW)]
        A2 = [ar.alloc([128, TQ]) for _ in range(NW)]
        NF = [ar.alloc([128, TQ]) for _ in range(NW)]
        NA = [ar.alloc([128, TQ]) for _ in range(NW)]
        CS = [ar.alloc([128, TQ]) for _ in range(NW)]
        SN = [ar.alloc([128, TQ]) for _ in range(NW)]
        PW1 = [ar.alloc([128, TQ]) for _ in range(2)]
        PW2 = [ar.alloc([128, TQ]) for _ in range(2)]
        THO = ar.alloc([128, SEQ // TQ, NPAIR])
        HPI = ar.alloc([128, 1])
        kb.do('dve', lambda e: e.memset(HPI.ap, math.pi / 2), W=[HPI])
        for tq_ in range(SEQ // TQ):
            kb.do('dve', lambda e, tq_=tq_: e.tensor_scalar(out=THO.ap[:, tq_, :], in0=vv(V_A), scalar1=float(tq_ * TQ), scalar2=None, op0=ALU.mult),
                  R=[v], W=[THO])
        W1 = [ar.alloc([128, TQ]) for _ in range(NW)]
        W2 = [ar.alloc([128, TQ]) for _ in range(NW)]
        XR = [ar.alloc([128, TQ]) for _ in range(NW)]
        XI = [ar.alloc([128, TQ]) for _ in range(NW)]
        SR = [ar.alloc([128, TQ]) for _ in range(NW)]
        SI = [ar.alloc([128, TQ]) for _ in range(NW)]
        SB2R = [ar.alloc([128, TQ], BF16) for _ in range(4)]
        SB2I = [ar.alloc([128, TQ], BF16) for _ in range(4)]
        carry = ar.alloc([128, 2, NPAIR])
        aoff = ar.alloc([128, NPAIR])
        FIN = [ar.alloc([128, 8]) for _ in range(2)]
        ssbb = ar.alloc([128, 2, 4, NS], BF16)
        sw = ar.alloc([128, 3, 4, NS])
        craw = ar.alloc([128, 2, 4, 128], BF16)
        TS = ar.alloc([128, 2, NPAIR, NS])
        tsw = ar.alloc([128, 2, NPAIR, NS])
        abrb = v.ap[:, V_ABR, :].unsqueeze(2).broadcast_to([128, NPAIR, NS])
        abib = v.ap[:, V_ABI, :].unsqueeze(2).broadcast_to([128, NPAIR, NS])
        kb.do('dve', lambda e: e.tensor_tensor(out=TS.ap[:, 0], in0=st0.ap[:, 0], in1=abrb, op=ALU.mult), R=[st0, v], W=[TS])
        kb.do('dve', lambda e: e.tensor_tensor(out=tsw.ap[:, 0], in0=st0.ap[:, 1], in1=abib, op=ALU.mult), R=[st0, v], W=[tsw])
        kb.do('dve', lambda e: e.tensor_tensor(out=TS.ap[:, 0], in0=TS.ap[:, 0], in1=tsw.ap[:, 0], op=ALU.subtract), R=[tsw], W=[TS])
        kb.do('dve', lambda e: e.tensor_tensor(out=TS.ap[:, 1], in0=st0.ap[:, 0], in1=abib, op=ALU.mult), R=[st0, v], W=[TS])
        kb.do('dve', lambda e: e.tensor_tensor(out=tsw.ap[:, 1], in0=st0.ap[:, 1], in1=abrb, op=ALU.mult), R=[st0, v], W=[tsw])
        kb.do('dve', lambda e: e.tensor_tensor(out=TS.ap[:, 1], in0=TS.ap[:, 1], in1=tsw.ap[:, 1], op=ALU.add), R=[tsw], W=[TS])
        yst = [ar.alloc([128, T]) for _ in range(1)]
        E1 = [ar.alloc([128, TQ]) for _ in range(1)]
        E2 = [ar.alloc([128, TQ]) for _ in range(1)]
        GB = [ar.alloc([128, TQ], BF16) for _ in range(2)]
        zb = Buf(ZBd)
        ssB = Buf(SSd)
        kb.do('pool', lambda e: e.memset(carry.ap, 0.0), W=[carry])
        NTQ = SEQ // TQ

        deferred = []

        def run_deferred():
            for lst in deferred:
                kb.flush_interleaved(lst)
            del deferred[:]

        for kt in range(NKS):
            u = uT[kt % len(uT)]
            kb.dma('sp', u.ap, ZBd[AW + 2 * KW + kt * 128:AW + 2 * KW + (kt + 1) * 128, :], R=[zb], W=[u])
            cp_, cf_ = cpf[0], cf[kt % 2]
            bp = bpk[kt % 2]
            for ri_ in range(2):
                kb.dma('pool', bp.ap[:, ri_], bpad_in[l, :, ri_, kt * 4:(kt + 1) * 4, :], W=[bp])
            kb.dma('sp', cp_.ap, cpad_in[l, :, kt], W=[cp_])
            kb.do('act', lambda e: e.activation(out=craw.ap[:, 0], in_=cp_.ap[:, 0], func=AF.Copy), R=[cp_], W=[craw])
            kb.do('act', lambda e: e.activation(out=craw.ap[:, 1], in_=cp_.ap[:, 1], func=AF.Copy, scale=-1.0), R=[cp_], W=[craw])
            for pr in range(4):
                q = kt * 4 + pr
                fr, fi = v.ap[:, V_FR, q:q + 1], v.ap[:, V_FI, q:q + 1]
                nfi = v.ap[:, V_NFI, q:q + 1]
                kb.do('dve', lambda e, pr=pr, fi=fi: e.tensor_scalar(out=cft.ap, in0=cp_.ap[:, 1, pr, :], scalar1=fi, scalar2=None, op0=ALU.mult),
                      R=[cp_, v], W=[cft])
                kb.do('dve', lambda e, pr=pr, fr=fr: e.scalar_tensor_tensor(out=cf_.ap[:, 0, pr, :], in0=cp_.ap[:, 0, pr, :], scalar=fr, in1=cft.ap,
                                                                         op0=ALU.mult, op1=ALU.subtract), R=[cp_, v, cft], W=[cf_])
                kb.do('dve', lambda e, pr=pr, fr=fr: e.tensor_scalar(out=cft.ap, in0=cp_.ap[:, 1, pr, :], scalar1=fr, scalar2=-1.0, op0=ALU.mult,
                                                                  op1=ALU.mult), R=[cp_, v, cf_], W=[cft])
                kb.do('dve', lambda e, pr=pr, nfi=nfi: e.scalar_tensor_tensor(out=cf_.ap[:, 1, pr, :], in0=cp_.ap[:, 0, pr, :], scalar=nfi, in1=cft.ap,
                                                                           op0=ALU.mult, op1=ALU.add), R=[cp_, v, cft], W=[cf_])
            ys = yst[kt % len(yst)]
            for tq in range(NTQ + 1):
                samp = (tq == NTQ)
                t0 = tq * TQ
                n = NS if samp else TQ
                ypb = PS[4 + nxt('ypb', 2)]
                pend = []
                if samp:
                    run_deferred()
                    xs_ = PS[0]
                    for pr in range(4):
                        for ri in range(2):
                            c0_ = (ri * 4 + pr) * NS
                            kb.do('pe', lambda e, ri=ri, pr=pr, c0_=c0_: e.matmul(xs_.ap[:, c0_:c0_ + NS], lhsT=bp.ap[:, ri, pr, :], rhs=u.ap[:, t0:t0 + NS],
                                                                               start=True, stop=True), R=[bp, u], W=[xs_])
                    xv_ = xs_.ap[:, 0:8 * NS].rearrange("p (r q n) -> p r q n", r=2, q=4)
                    frb = v.ap[:, V_FR, kt * 4:kt * 4 + 4].unsqueeze(2).broadcast_to([128, 4, NS])
                    fib = v.ap[:, V_FI, kt * 4:kt * 4 + 4].unsqueeze(2).broadcast_to([128, 4, NS])
                    wr, wi, wt_ = sw.ap[:, 0], sw.ap[:, 1], sw.ap[:, 2]
                    kb.do('dve', lambda e: e.tensor_tensor(out=wr, in0=xv_[:, 0], in1=frb, op=ALU.mult), R=[xs_, v], W=[sw])
                    kb.do('dve', lambda e: e.tensor_tensor(out=wt_, in0=xv_[:, 1], in1=fib, op=ALU.mult), R=[xs_, v], W=[sw])
                    kb.do('dve', lambda e: e.tensor_tensor(out=wr, in0=wr, in1=wt_, op=ALU.subtract), R=[], W=[sw])
                    kb.do('dve', lambda e: e.tensor_tensor(out=wi, in0=xv_[:, 0], in1=fib, op=ALU.mult), R=[xs_, v], W=[sw])
                    kb.do('dve', lambda e: e.tensor_tensor(out=wt_, in0=xv_[:, 1], in1=frb, op=ALU.mult), R=[xs_, v], W=[sw])
                    kb.do('dve', lambda e: e.tensor_tensor(out=wi, in0=wi, in1=wt_, op=ALU.add), R=[], W=[sw])
                    kb.do('dve', lambda e: e.tensor_tensor(out=sto.ap[:, 0, kt * 4:kt * 4 + 4, 1:1 + NS], in0=wr, in1=TS.ap[:, 0, kt * 4:kt * 4 + 4, :], op=ALU.add),
                          R=[sw, TS], W=[sto])
                    kb.do('dve', lambda e: e.tensor_tensor(out=sto.ap[:, 1, kt * 4:kt * 4 + 4, 1:1 + NS], in0=wi, in1=TS.ap[:, 1, kt * 4:kt * 4 + 4, :], op=ALU.add),
                          R=[sw, TS], W=[sto])
                    kb.do('act', lambda e: e.activation(out=ssbb.ap, in_=sto.ap[:, :, kt * 4:kt * 4 + 4, 1:1 + NS], func=AF.Copy), R=[sto], W=[ssbb])
                for pr in range(4):
                    q = kt * 4 + pr
                    if not samp:
                        kb.begin_buffer()
                        xps_r, xps_i = PS[2 * nxt('xps', 2)], None
                        xps_i = PS[PS.index(xps_r) + 1]
                        for ri, xp in ((0, xps_r), (1, xps_i)):
                            kb.do('pe', lambda e, ri=ri, xp=xp, q=q: e.matmul(xp.ap[:, 0:n], lhsT=bp.ap[:, ri, pr, :], rhs=u.ap[:, t0:t0 + n],
                                                                           start=True, stop=True), R=[bp, u], W=[xp])
                    _k = nxt('sb2', 4)
                    s2r, s2i = SB2R[_k], SB2I[_k]
                    if samp:
                        rhs_r, rhs_i = ssbb.ap[:, 0, pr, :], ssbb.ap[:, 1, pr, :]
                        rd = [ssbb]
                    else:
                        fin = FIN[pr % 2]
                        w = nxt('ssw', NW)
                        w1, w2, xr_, xi_, sr_, si_ = W1[w], W2[w], XR[w], XI[w], SR[w], SI[w]
                        tw = w
                        pw1, pw2 = PW1[pr % 2], PW2[pr % 2]
                        at, a2, nf, na, cs, sn = AT[tw], A2[tw], NF[tw], NA[tw], CS[tw], SN[tw]
                        rt = RT[pr]
                        if tq == 0:
                            kb.do('act', lambda e, rt=rt, q=q: e.activation(out=rt.ap, in_=IOT.ap[:, 0:TQ], func=AF.Identity, scale=0.0,
                                                                           bias=v.ap[:, V_R, q:q + 1]), R=[IOT, v], W=[rt])
                        kb.do('act', lambda e, at=at, q=q: e.activation(out=at.ap, in_=IOT.ap[:, 0:TQ], func=AF.Identity, scale=v.ap[:, V_A, q:q + 1],
                                                                       bias=THO.ap[:, tq, q:q + 1]), R=[IOT, v, THO], W=[at])
                        kb.do('dve', lambda e, at=at, a2=a2: e.tensor_scalar(out=a2.ap, in0=at.ap, scalar1=MAGIC, scalar2=None, op0=ALU.add), R=[at], W=[a2])
                        kb.do('dve', lambda e, at=at, a2=a2, nf=nf: e.scalar_tensor_tensor(out=nf.ap, in0=a2.ap, scalar=MAGIC, in1=at.ap, op0=ALU.subtract,
                                                                                       op1=ALU.subtract), R=[at, a2], W=[nf])
                        kb.do('act', lambda e, nf=nf, sn=sn: e.activation(out=sn.ap, in_=nf.ap, func=AF.Sin, scale=-TWO_PI), R=[nf], W=[sn])
                        kb.do('act', lambda e, nf=nf, na=na: e.activation(out=na.ap, in_=nf.ap, func=AF.Sin, scale=-TWO_PI / 2), R=[nf], W=[na])
                        kb.do('act', lambda e, na=na: e.activation(out=na.ap, in_=na.ap, func=AF.Square), R=[], W=[na])
                        kb.do('act', lambda e, na=na, cs=cs: e.activation(out=cs.ap, in_=na.ap, func=AF.Identity, scale=-2.0, bias=1.0), R=[na], W=[cs])
                        kb.do('dve', lambda e, cs=cs, w1=w1: e.tensor_tensor(out=w1.ap, in0=cs.ap, in1=xps_r.ap[:, 0:n], op=ALU.mult), R=[cs, xps_r], W=[w1])
                        kb.do('dve', lambda e, sn=sn, w2=w2: e.tensor_tensor(out=w2.ap, in0=sn.ap, in1=xps_i.ap[:, 0:n], op=ALU.mult), R=[sn, xps_i], W=[w2])
                        kb.do('dve', lambda e, w1=w1, w2=w2, xr_=xr_: e.tensor_tensor(out=xr_.ap, in0=w1.ap, in1=w2.ap, op=ALU.add), R=[w1, w2], W=[xr_])
                        kb.do('dve', lambda e, cs=cs, w1=w1: e.tensor_tensor(out=w1.ap, in0=cs.ap, in1=xps_i.ap[:, 0:n], op=ALU.mult), R=[cs, xps_i, xr_], W=[w1])
                        kb.do('dve', lambda e, sn=sn, w2=w2: e.tensor_tensor(out=w2.ap, in0=sn.ap, in1=xps_r.ap[:, 0:n], op=ALU.mult), R=[sn, xps_r, xr_], W=[w2])
                        kb.do('dve', lambda e, w1=w1, w2=w2, xi_=xi_: e.tensor_tensor(out=xi_.ap, in0=w1.ap, in1=w2.ap, op=ALU.subtract), R=[w1, w2], W=[xi_])
                        kb.do('dve', lambda e, rt=rt, xr_=xr_, sr_=sr_, q=q: e.tensor_tensor_scan(out=sr_.ap, data0=rt.ap, data1=xr_.ap,
                                                                                             initial=carry.ap[:, 0, q:q + 1], op0=ALU.mult, op1=ALU.add),
                              R=[rt, xr_, carry], W=[sr_])
                        kb.do('dve', lambda e, rt=rt, xi_=xi_, si_=si_, q=q: e.tensor_tensor_scan(out=si_.ap, data0=rt.ap, data1=xi_.ap,
                                                                                             initial=carry.ap[:, 1, q:q + 1], op0=ALU.mult, op1=ALU.add),
                              R=[rt, xi_, carry], W=[si_])
                        kb.do('act', lambda e, sr_=sr_, q=q: e.activation(out=carry.ap[:, 0, q:q + 1], in_=sr_.ap[:, TQ - 1:TQ], func=AF.Copy), R=[sr_], W=[carry])
                        kb.do('act', lambda e, si_=si_, q=q: e.activation(out=carry.ap[:, 1, q:q + 1], in_=si_.ap[:, TQ - 1:TQ], func=AF.Copy), R=[si_], W=[carry])
                        kb.do('dve', lambda e: e.tensor_tensor(out=w1.ap, in0=cs.ap, in1=sr_.ap, op=ALU.mult), R=[cs, sr_], W=[w1])
                        kb.do('dve', lambda e: e.tensor_tensor(out=w2.ap, in0=sn.ap, in1=si_.ap, op=ALU.mult), R=[sn, si_], W=[w2])
                        kb.do('dve', lambda e: e.tensor_tensor(out=s2r.ap, in0=w1.ap, in1=w2.ap, op=ALU.subtract), R=[w1, w2], W=[s2r])
                        if tq == NTQ - 1:
                            kb.do('dve', lambda e: e.tensor_tensor(out=fin.ap[:, 0:1], in0=w1.ap[:, TQ - 1:TQ], in1=w2.ap[:, TQ - 1:TQ],
                                                                   op=ALU.subtract), R=[w1, w2], W=[fin])
                        kb.do('pool', lambda e: e.tensor_tensor(out=pw1.ap, in0=cs.ap, in1=si_.ap, op=ALU.mult), R=[cs, si_], W=[pw1])
                        kb.do('pool', lambda e: e.tensor_tensor(out=pw2.ap, in0=sn.ap, in1=sr_.ap, op=ALU.mult), R=[sn, sr_], W=[pw2])
                        kb.do('pool', lambda e: e.tensor_tensor(out=s2i.ap, in0=pw1.ap, in1=pw2.ap, op=ALU.add), R=[pw1, pw2], W=[s2i])
                        if tq == NTQ - 1:
                            fr, fi = v.ap[:, V_FR, q:q + 1], v.ap[:, V_FI, q:q + 1]
                            kb.do('dve', lambda e, w1=pw1, w2=pw2: e.tensor_tensor(out=fin.ap[:, 1:2], in0=w1.ap[:, TQ - 1:TQ], in1=w2.ap[:, TQ - 1:TQ],
                                                                               op=ALU.add), R=[pw1, pw2], W=[fin])
                            kb.do('dve', lambda e, fi=fi: e.tensor_scalar(out=fin.ap[:, 2:3], in0=fin.ap[:, 1:2], scalar1=fi, scalar2=None, op0=ALU.mult), R=[v], W=[fin])
                            kb.do('dve', lambda e, fr=fr, q=q: e.scalar_tensor_tensor(out=sto.ap[:, 0, q, 0:1], in0=fin.ap[:, 0:1], scalar=fr, in1=fin.ap[:, 2:3],
                                                                                   op0=ALU.mult, op1=ALU.subtract), R=[v, fin], W=[sto])
                            kb.do('dve', lambda e, fr=fr: e.tensor_scalar(out=fin.ap[:, 2:3], in0=fin.ap[:, 1:2], scalar1=fr, scalar2=None, op0=ALU.mult), R=[v], W=[fin])
                            kb.do('dve', lambda e, fi=fi, q=q: e.scalar_tensor_tensor(out=sto.ap[:, 1, q, 0:1], in0=fin.ap[:, 0:1], scalar=fi, in1=fin.ap[:, 2:3],
                                                                                   op0=ALU.mult, op1=ALU.add), R=[v, fin], W=[sto])
                        rhs_r, rhs_i = s2r.ap, s2i.ap
                        rd = [s2r, s2i]
                    cw_ = craw if samp else cf_
                    kb.do('pe', lambda e, pr=pr, rhs_r=rhs_r, ypb=ypb: e.matmul(ypb.ap[:, 0:n], lhsT=cw_.ap[:, 0, pr, :], rhs=rhs_r,
                                                                              start=(pr == 0), stop=False), R=[cw_] + rd, W=[ypb])
                    kb.do('pe', lambda e, pr=pr, rhs_i=rhs_i, ypb=ypb: e.matmul(ypb.ap[:, 0:n], lhsT=cw_.ap[:, 1, pr, :], rhs=rhs_i,
                                                                              start=False, stop=(pr == 3)), R=[cw_] + rd, W=[ypb])
                    if not samp:
                        pend.append(kb.end_buffer())
                        if len(pend) == 2:
                            kb.flush_interleaved([x[:-2] for x in pend])
                            run_deferred()
                            deferred.append([x[-2:] for x in pend])
                            pend = []
                if not samp:
                    kb.begin_buffer()
                e1, e2, gb = E1[0], E2[0], GB[tq % 2]
                kb.do('dve', lambda e, e1=e1, ypb=ypb: e.scalar_tensor_tensor(out=e1.ap[:, 0:n], in0=u.ap[:, t0:t0 + n], scalar=dc.ap[:, kt:kt + 1],
                                                                           in1=ypb.ap[:, 0:n], op0=ALU.mult, op1=ALU.add), R=[u, dc, ypb], W=[e1])
                kb.do('act', lambda e, e1=e1, e2=e2: e.activation(out=e2.ap[:, 0:n], in_=e1.ap[:, 0:n], func=AF.Square), R=[e1], W=[e2])
                kb.do('dve', lambda e, e2=e2: e.tensor_scalar(out=e2.ap[:, 0:n], in0=e2.ap[:, 0:n], scalar1=0.044715, scalar2=1.0, op0=ALU.mult, op1=ALU.add),
                      R=[], W=[e2])
                kb.do('dve', lambda e, e1=e1, e2=e2: e.tensor_tensor(out=e2.ap[:, 0:n], in0=e2.ap[:, 0:n], in1=e1.ap[:, 0:n], op=ALU.mult), R=[e1], W=[e2])
                kb.do('act', lambda e, e2=e2: e.activation(out=e2.ap[:, 0:n], in_=e2.ap[:, 0:n], func=AF.Sigmoid, scale=GELU_C), R=[], W=[e2])
                kb.do('dve', lambda e, e1=e1, e2=e2, gb=gb: e.tensor_tensor(out=gb.ap[:, 0:n], in0=e2.ap[:, 0:n], in1=e1.ap[:, 0:n], op=ALU.mult),
                      R=[e1, e2], W=[gb])
                z1, z2 = PS[6], PS[7]
                kb.do('pe', lambda e, gb=gb: e.matmul(z1.ap[:, 0:n], lhsT=wg.ap[:, kt, 0, :], rhs=gb.ap[:, 0:n], start=True, stop=True), R=[wg, gb], W=[z1])
                kb.do('pe', lambda e, gb=gb: e.matmul(z2.ap[:, 0:n], lhsT=wg.ap[:, kt, 1, :], rhs=gb.ap[:, 0:n], start=True, stop=True), R=[wg, gb], W=[z2])
                kb.do('act', lambda e, e2=e2: e.activation(out=e2.ap[:, 0:n], in_=z2.ap[:, 0:n], func=AF.Sigmoid), R=[z2], W=[e2])
                kb.do('dve', lambda e, e2=e2: e.tensor_tensor(out=ys.ap[:, t0:t0 + n], in0=e2.ap[:, 0:n], in1=z1.ap[:, 0:n], op=ALU.mult), R=[e2, z1], W=[ys])
                if not samp:
                    deferred.append([kb.end_buffer()])
            run_deferred()
            kb.dma('sp', SSd[kt * 128:(kt + 1) * 128, :], ys.ap, R=[ys], W=[ssB])
        outtoks.append(kb.dma('sp', st_out[l], sto.ap, R=[sto]))

    def evac_store(dst_d, row0_of, dt, scale_of=None, also_f32=None):
        stg = [ar.alloc([128, T], dt) for _ in range(2)]
        stf = [ar.alloc([128, T]) for _ in range(2)] if also_f32 is not None else None
        dB = Buf(dst_d)
        nch = len(chunks)

        def ep(mi, ci, t0, n, pb, pb2):
            s = stg[mi % 2]
            sc = 1.0 if scale_of is None else scale_of(mi)
            f32r = also_f32(mi) if also_f32 is not None else None
            if f32r is None:
                kb.do('act', lambda e: e.activation(out=s.ap[:, t0:t0 + n], in_=pb.ap[:, 0:n], func=AF.Copy, scale=sc), R=[pb], W=[s])
            else:
                sf = stf[mi % 2]
                kb.do('act', lambda e: e.activation(out=sf.ap[:, t0:t0 + n], in_=pb.ap[:, 0:n], func=AF.Copy), R=[pb], W=[sf])
                kb.do('dve', lambda e: e.tensor_scalar(out=s.ap[:, t0:t0 + n], in0=sf.ap[:, t0:t0 + n], scalar1=sc, scalar2=None, op0=ALU.mult),
                      R=[sf], W=[s])
            if ci == nch - 1:
                r0 = row0_of(mi)
                kb.dma('sp', dst_d[r0:r0 + 128, :], s.ap, R=[s], W=[dB])
                if f32r is not None:
                    dd, rr, nr = f32r
                    kb.dma('sp', dd[rr:rr + nr, :], stf[mi % 2].ap[0:nr, :], R=[stf[mi % 2]], W=[Buf(dd)])
        return ep

    def _layer(l):
            norm_phase(Xd, KTD, gains[l, 0], out='norm')
            new_phase()
            nmt = INW // 128

            if KW == 64:
                f32sel = lambda mi: (ZFd, 0, 128) if mi * 128 == AW else None
            else:
                f32sel = lambda mi: (ZFd, mi * 128 - AW, 128) if AW <= mi * 128 < AW + 2 * KW else None
            ep = evac_store(ZBd, lambda mi: mi * 128, BF16, scale_of=lambda mi: 0.125 if mi * 128 < AW else 1.0, also_f32=f32sel)
            go, _ = linear_phase(SRC_D, KTD, w_in[l], [m * 128 for m in range(nmt)], ep, tchunks=full_chunks, fresh=False)
            for mi in range(min(nmt, int(_os.environ.get('LIMIT_MT', '999')))):
                go(mi)
            kb.barrier()
            outtoks.append(kb.dma('sp', kvp_out[l], ZFd[:, SEQ - 128:SEQ]))
            import os
            if not os.environ.get('SKIP_ATT'):
                attention_phase(l)
            if not os.environ.get('SKIP_SSM'):
                ssm_phase(l)
            norm_phase(SSd, NKS, gssm_in[l], out='norm', dst_kt0=KTA)
            new_phase()
            ep = evac_store(Od, lambda mi: mi * 128, F32)
            go, _ = linear_phase(SRC_D, KTD, w_out[l], [m * 128 for m in range(KTD)], ep, tchunks=full_chunks, fresh=False)
            for mi in range(KTD):
                go(mi)
            norm_phase(Od, KTD, gains[l, 1], resid_d=Xd, out='norm', g2_ap=gains[l, 2])
            new_phase()
            stg = [ar.alloc([128, T], BF16) for _ in range(2)]
            sil = [ar.alloc([128, NMAX]) for _ in range(3)]
            aB = Buf(ACTd)

            def ep_gu(mi, ci, t0, n, pb, pb2):
                s = stg[mi % 2]
                sl = sil[nxt('sil', 3)]
                kb.do('act', lambda e: e.activation(out=sl.ap[:, 0:n], in_=pb.ap[:, 0:n], func=AF.Silu), R=[pb], W=[sl])
                kb.do('dve', lambda e: e.tensor_tensor(out=s.ap[:, t0:t0 + n], in0=sl.ap[:, 0:n], in1=pb2.ap[:, 0:n], op=ALU.mult), R=[sl, pb2], W=[s])
                if ci == len(chunks) - 1:
                    kb.dma('sp', ACTd[mi * 128:(mi + 1) * 128, :], s.ap, R=[s], W=[aB])
            go, _ = linear_phase(SRC_D, KTD, w_gu[l], [m * 128 for m in range(KTF)], ep_gu, W2cols=[DFF + m * 128 for m in range(KTF)],
                                 tchunks=full_chunks, fresh=False)
            for mi in range(KTF):
                go(mi)
            half = T // 2
            for hf in range(2):
                new_phase()
                SRC_F = src3(KTF, half)
                kb.dma('sp', SRC_F, ACTd[:, hf * half:(hf + 1) * half].rearrange("(k p) t -> p k t", p=128), W=[srcbuf])
                stg2 = [ar.alloc([128, half]) for _ in range(2)]
                oB = Buf(Od)
                hch = [(t0, n, t0 - hf * half) for (t0, n) in chunks[3 * hf:3 * hf + 3]]

                def ep_dn(mi, ci, t0, n, pb, pb2, stg2=stg2, oB=oB, hf=hf):
                    s = stg2[mi % 2]
                    kb.do('act', lambda e: e.activation(out=s.ap[:, t0 - hf * half:t0 - hf * half + n], in_=pb.ap[:, 0:n], func=AF.Copy), R=[pb], W=[s])
                    if ci == 2:
                        kb.dma('sp', Od[mi * 128:(mi + 1) * 128, hf * half:(hf + 1) * half], s.ap, R=[s], W=[oB])
                go, _ = linear_phase(SRC_F, KTF, w_dn[l], [m * 128 for m in range(KTD)], ep_dn, tchunks=hch, fresh=False)
                for mi in range(KTD):
                    go(mi)
            norm_phase(Od, KTD, gains[l, 3], resid_d=Xd, out='cast')
            new_phase()
            peb = ar.alloc([128, 2, T], BF16)
            for k2 in range(2):
                kb.dma('pool', peb.ap[:, k2, :], peT_in[l, k2 * 128:(k2 + 1) * 128, :], W=[peb], max_dma_last_dim=4096)
            wpp = [ar.alloc([128, 2, 128], BF16) for _ in range(2)]
            xrow = [ar.alloc([128, T]) for _ in range(2)]
            sg = [ar.alloc([128, NMAX]) for _ in range(3)]
            wppv = w_pp[l].rearrange("(k p) n -> p k n", p=128)
            xB = Buf(Xd)

            def ep_ple(mi, ci, t0, n, pb, pb2):
                xr_ = xrow[mi % 2]
                wp_ = wpp[mi % 2]
                if ci == 0:
                    kb.dma('pool', wp_.ap, wppv[:, :, mi * 128:(mi + 1) * 128], W=[wp_])
                    kb.dma('sp', xr_.ap, Xd[mi * 128:(mi + 1) * 128, :], R=[xB], W=[xr_])
                pj = PS[6 + nxt('pj', 2)]
                for kt in range(2):
                    kb.do('pe', lambda e, kt=kt: e.matmul(pj.ap[:, 0:n], lhsT=wp_.ap[:, kt, :], rhs=peb.ap[:, kt, t0:t0 + n], start=(kt == 0), stop=(kt == 1)),
                          R=[wp_, peb], W=[pj])
                s_ = sg[nxt('sg', 3)]
                kb.do('act', lambda e: e.activation(out=s_.ap[:, 0:n], in_=pb.ap[:, 0:n], func=AF.Sigmoid), R=[pb], W=[s_])
                kb.do('dve', lambda e: e.tensor_tensor(out=s_.ap[:, 0:n], in0=s_.ap[:, 0:n], in1=pj.ap[:, 0:n], op=ALU.mult), R=[pj], W=[s_])
                kb.do('dve', lambda e: e.tensor_tensor(out=xr_.ap[:, t0:t0 + n], in0=xr_.ap[:, t0:t0 + n], in1=s_.ap[:, 0:n], op=ALU.add), R=[s_], W=[xr_])
                if ci == len(chunks) - 1:
                    outtoks.append(kb.dma('sp', Xd[mi * 128:(mi + 1) * 128, :], xr_.ap, R=[xr_], W=[xB]))
            go, _ = linear_phase(SRC_D, KTD, w_pg[l], [m * 128 for m in range(KTD)], ep_ple, tchunks=full_chunks, fresh=False)
            for mi in range(KTD):
                go(mi)


    try:
        for l in range(DEPTH):
            _layer(l)
    except _Stop:
        print('stopped after phase', _stop_after)

    kb.barrier()
    for r in range(KTD):
        outtoks.append(kb.dma('sp', yT_out[r * 128:(r + 1) * 128, :], Xd[r * 128:(r + 1) * 128, :]))
    kb.barrier()
    if _os.environ.get('PHASE_LOG'):
        import json as _json
        _json.dump(_marks, open(_os.environ['PHASE_LOG'], 'w'))
    kb.emit()
    return nc


def _consts():
    c = np.zeros((128, 1152), np.float32)
    c[:, 0:128] = np.eye(128, dtype=np.float32)
    i = np.arange(128)[:, None]
    j = np.arange(256)[None, :]
    dist = (i - j + 128).astype(np.float32)
    valid = (dist >= 0) & (dist <= 128)
    dm = np.where(valid, dist, np.float32(BIG)).astype(np.float32)
    c[:, 128:384] = dm
    d0 = dm.copy()
    d0[:, 0:128] = BIG
    c[:, 384:640] = d0
    c[:, 640:1152] = np.arange(1, 513, dtype=np.float32)[None, :]
    return c


def prepare_inputs(C, inp):
    f = np.float32
    A = lambda k: np.asarray(inp[k], f)
    L = C.DEPTH
    NP_, NKS = C.NPAIR, C.NKS
    gains = np.stack([A(k) for k in ('g_pre_mix', 'g_post_mix', 'g_pre_ffn', 'g_post_ffn')], axis=1)
    gains = np.ascontiguousarray(gains.reshape(L, 4, C.KTD, 128).transpose(0, 1, 3, 2))
    gssm = np.ascontiguousarray(A('g_ssm_out').reshape(L, NKS, 128).transpose(0, 2, 1))
    gattn = np.ascontiguousarray(np.broadcast_to(A('g_attn_out')[:, None, :], (L, 128, C.AW)))
    gattnT = np.ascontiguousarray(A('g_attn_out').reshape(L, C.NH, 64).transpose(0, 2, 1))
    sinks = np.ascontiguousarray(np.broadcast_to(A('attn_sinks')[:, None, :], (L, 128, C.NH)))
    are = A('ssm_a_re').reshape(L, NP_, 2, 64).transpose(0, 2, 3, 1).reshape(L, 128, NP_)
    aim = A('ssm_a_im').reshape(L, NP_, 2, 64).transpose(0, 2, 3, 1).reshape(L, 128, NP_)
    ldt = np.broadcast_to(A('ssm_log_dt').reshape(L, NP_, 2, 1), (L, NP_, 2, 64)).transpose(0, 2, 3, 1).reshape(L, 128, NP_)
    ssm_ps = np.ascontiguousarray(np.stack([are, aim, ldt], axis=2))
    bpad = np.zeros((L, 128, 2, NP_, 128), f)
    for ri, key in enumerate(('ssm_b_re', 'ssm_b_im')):
        b = A(key)
        for g in range(C.NG):
            q, gh, g8 = g // 2, g % 2, g % 8
            bpad[:, g8 * 16:(g8 + 1) * 16, ri, q, gh * 64:(gh + 1) * 64] = b[:, g].transpose(0, 2, 1)
    cpad = np.zeros((L, 128, NKS, 2, 4, 128), f)
    for ri, key in enumerate(('ssm_c_re', 'ssm_c_im')):
        c = A(key)
        for g in range(C.NG):
            kt, pr, gh, g8 = g // 8, (g % 8) // 2, g % 2, g % 8
            cpad[:, gh * 64:(gh + 1) * 64, kt, ri, pr, g8 * 16:(g8 + 1) * 16] = c[:, g].transpose(0, 2, 1)
    dcol = np.ascontiguousarray(A('ssm_d').reshape(L, NKS, 128).transpose(0, 2, 1))
    wglu = np.zeros((L, 128, NKS, 2, 128), f)
    wg = A('ssm_w_glu')
    for g in range(C.NG):
        kt, g8 = g // 8, g % 8
        for hf in range(2):
            wglu[:, g8 * 16:(g8 + 1) * 16, kt, hf, g8 * 16:(g8 + 1) * 16] = wg[:, g, :, hf * 16:(hf + 1) * 16]
    shared = dict(
        w_in=A('w_in'), w_out=A('w_out'), w_gate_up=A('w_gate_up'), w_down=A('w_down'), w_ple_gate=A('w_ple_gate'),
        w_ple_proj=A('w_ple_proj'), gains=gains, gssm=gssm, gattn=gattn, gattnT=gattnT, sinks=sinks, ssm_ps=ssm_ps,
        bpad=bpad, cpad=cpad, dcol=dcol, wglu=wglu, consts=_consts())
    xp, xs = A('x_prompt'), A('x_sample')
    pp, psm = A('p_prompt'), A('p_sample')
    ck, cv = A('cache_k'), A('cache_v')
    sre, sim = A('state_ssm_re'), A('state_ssm_im')
    in_maps = []
    NS = C.NS
    for c in range(8):
        b = c % C.BATCH
        sl = slice(NS * b, NS * (b + 1))
        m = dict(shared)
        m['xT'] = np.ascontiguousarray(np.concatenate([xp[b].T, xs[sl, 0].T], axis=1))
        m['peT'] = np.ascontiguousarray(np.concatenate([pp[:, b].transpose(0, 2, 1), psm[:, sl, 0].transpose(0, 2, 1)], axis=2))
        ckc = ck[:, sl].reshape(L, NS, 128, C.KW)
        m['ck'] = np.ascontiguousarray(ckc)
        m['cv'] = np.ascontiguousarray(cv[:, sl].reshape(L, NS, 128, C.KW))
        m['ckT'] = np.ascontiguousarray(ckc.reshape(L, NS, 128, C.NKV, 64).transpose(0, 1, 3, 4, 2))
        st = np.stack([sre[:, sl], sim[:, sl]], axis=1)
        st = st.reshape(L, 2, NS, NP_, 2, 64).transpose(0, 4, 5, 1, 3, 2).reshape(L, 128, 2, NP_, NS)
        m['st0'] = np.ascontiguousarray(st)
        in_maps.append(m)
    return in_maps


def assemble(C, results):
    f = np.float32
    L, NS, B = C.DEPTH, C.NS, C.BATCH
    yp = np.zeros((B, C.SEQ, C.D), f)
    ys = np.zeros((C.DEC, 1, C.D), f)
    kp = np.zeros((L, B, 128, C.NKV, 64), f)
    vp = np.zeros_like(kp)
    srp = np.zeros((L, B, C.NG, 64), f)
    sip = np.zeros_like(srp)
    ksm = np.zeros((L, C.DEC, 128, C.NKV, 64), f)
    vsm = np.zeros_like(ksm)
    srs = np.zeros((L, C.DEC, C.NG, 64), f)
    sis = np.zeros_like(srs)
    for b in range(B):
        r = results[b]
        sl = slice(NS * b, NS * (b + 1))
        yT = r['yT']
        yp[b] = yT[:, :C.SEQ].T
        ys[sl, 0] = yT[:, C.SEQ:].T
        kv = r['kvp']
        kp[:, b] = kv[:, :C.KW].transpose(0, 2, 1).reshape(L, 128, C.NKV, 64)
        vp[:, b] = kv[:, C.KW:].transpose(0, 2, 1).reshape(L, 128, C.NKV, 64)
        ksm[:, sl] = r['ks'].reshape(L, NS, 128, C.NKV, 64)
        vsm[:, sl] = r['vs'].reshape(L, NS, 128, C.NKV, 64)
        st = r['st'].reshape(L, 2, 64, 2, C.NPAIR, 1 + NS)
        st = st.transpose(0, 3, 5, 4, 1, 2).reshape(L, 2, 1 + NS, C.NG, 64)
        srp[:, b], sip[:, b] = st[:, 0, 0], st[:, 1, 0]
        srs[:, sl], sis[:, sl] = st[:, 0, 1:], st[:, 1, 1:]
    return (yp, ys, kp, vp, srp, sip, ksm, vsm, srs, sis)


_CACHE = {}


def run(C, inp):
    key = (C.D, C.SEQ, C.DEPTH, C.BATCH, C.DEC)
    if key not in _CACHE:
        _CACHE[key] = build(C)
    in_maps = prepare_inputs(C, inp)
    res = run_bass_kernel_spmd(_CACHE[key], in_maps, core_ids=list(range(8)))
    return assemble(C, res.results)


def kernel(**inputs):
    return run(Cfg(), inputs)
```

```python
import math
import numpy as np
import concourse.bass as bass
import concourse.mybir as mybir
from concourse.bass_utils import run_bass_kernel_spmd

F32 = mybir.dt.float32
BF16 = mybir.dt.bfloat16
I32 = mybir.dt.int32
AF = mybir.ActivationFunctionType
ALU = mybir.AluOpType
AX = mybir.AxisListType

ENGS = ('pe', 'act', 'dve', 'pool', 'sp')
SEM_ROT = 20000
NDMASEM = 56
EPS = 1e-6
BIG = 1.0e9
TWO_PI = 6.283184
GELU_C = 1.5957691216057308
MAGIC = 12582912.0


class Cfg:
    def __init__(self, D=2048, SEQ=2048, DEPTH=4, BATCH=4, DEC=32):
        self.D, self.SEQ, self.DEPTH, self.BATCH, self.DEC = D, SEQ, DEPTH, BATCH, DEC
        self.NS = DEC // BATCH
        self.T = SEQ + self.NS
        self.AW = D // 2
        self.NH = self.AW // 64
        self.NKV = max(1, self.NH // 8)
        self.GRP = self.NH // self.NKV
        self.KW = self.NKV * 64
        self.SW = D - self.AW
        self.NG = self.SW // 16
        self.NKS = self.SW // 128
        self.NPAIR = self.NG // 2
        self.INW = self.AW + 2 * self.KW + self.SW
        self.DFF = -(-8 * D // (3 * 256)) * 256
        self.KTD = D // 128
        self.KTA = self.AW // 128
        self.KTF = self.DFF // 128
        self.NQB = SEQ // 128
        half = self.T // 2
        assert self.T % 2 == 0
        a = -(-half // 3)
        ch = []
        for h0 in (0, half):
            o = h0
            for i in range(3):
                n = min(a, h0 + half - o)
                ch.append((o, n))
                o += n
        self.chunks = ch
        self.NMAX = a
        assert a <= 512
        self.TQ = 256 if SEQ >= 256 else SEQ
        self.slopes = [2.0 ** (-8.0 * (h + 1) / self.NH) for h in range(self.NH)]


class Buf:
    def __init__(self, ap, excl=False):
        self.ap = ap
        self.w = {}
        self.r = {}
        self.excl = excl


def _upd(d, tok):
    k = tok[0].num
    if k not in d or d[k][1] < tok[1]:
        d[k] = tok


class _Rec:
    def __init__(self):
        self.call = None

    def __getattr__(self, name):
        def f(*a, **k):
            assert self.call is None
            self.call = (name, a, k)
            return self
        return f


class KB:
    def __init__(self, nc):
        self.nc = nc
        self.ops = {e: [] for e in ENGS}
        self.cur = {}
        self.nsem = 0
        self.waited = {}
        self.ndma = 0
        self.dpool = []
        for e in ENGS:
            self._new_sem(e)
        for i in range(NDMASEM):
            self.dpool.append([nc.alloc_semaphore("dq%d" % i), 0])

    def _new_sem(self, e):
        self.nsem += 1
        self.cur[e] = [self.nc.alloc_semaphore("p_%s_%d" % (e, self.nsem)), 0]
        if not hasattr(self, 'owner'):
            self.owner = {}
        self.owner[self.cur[e][0].num] = e

    def _waits(self, eng, deps):
        waits = []
        for d in deps:
            sem, val = d
            if eng == 'pe' and self.owner.get(sem.num) == 'pe':
                continue
            key = (eng, sem.num)
            if self.waited.get(key, 0) >= val:
                continue
            self.waited[key] = val
            waits.append((sem, val))
        return waits

    def _deps(self, R, W):
        deps = []
        for b in R:
            deps.extend(b.w.values())
            if b.excl:
                deps.extend(b.r.values())
        for b in W:
            deps.extend(b.w.values())
            deps.extend(b.r.values())
        return deps

    def _record(self, tok, R, W):
        for b in R:
            _upd(b.r, tok)
        for b in W:
            _upd(b.w, tok)

    def do(self, eng, fn, R=(), W=()):
        rec = _Rec()
        fn(rec)
        if getattr(self, 'buf', None) is not None:
            self.buf.append(('do', eng, rec.call, list(R), list(W)))
            return None
        return self._do(eng, rec.call, R, W)

    def begin_buffer(self):
        self.buf = []

    def end_buffer(self):
        b, self.buf = self.buf, None
        return b

    def flush_interleaved(self, lists):
        m = max(len(x) for x in lists)
        for k in range(m):
            for x in lists:
                if k < len(x):
                    kind, eng, call, R, W = x[k]
                    self._do(eng, call, R, W)

    def _do(self, eng, call, R=(), W=()):
        waits = self._waits(eng, self._deps(R, W))
        if self.cur[eng][1] >= SEM_ROT:
            self._new_sem(eng)
        c = self.cur[eng]
        c[1] += 1
        name, a, k = call
        self.ops[eng].append((waits, (lambda e, name=name, a=a, k=k: getattr(e, name)(*a, **k)), c[0], 1))
        tok = (c[0], c[1])
        self._record(tok, R, W)
        return tok

    def dma(self, eng, out, in_, R=(), W=(), **kw):
        waits = self._waits(eng, self._deps(R, W))
        ds = self.dpool[self.ndma % NDMASEM]
        self.ndma += 1
        ds[1] += 16
        self.ops[eng].append((waits, lambda e: e.dma_start(out=out, in_=in_, **kw), ds[0], 16))
        tok = (ds[0], ds[1])
        self._record(tok, R, W)
        return tok

    def wait_only(self, eng, deps):
        waits = self._waits(eng, deps)
        if waits:
            self.ops[eng].append((waits, None, None, 0))

    def barrier(self):
        toks = [(c[0], c[1]) for c in self.cur.values() if c[1] > 0]
        toks += [(d[0], d[1]) for d in self.dpool if d[1] > 0]
        for e in ENGS:
            self.wait_only(e, toks)

    def emit(self):
        nc = self.nc
        with nc.Block() as block:
            def run(name):
                def f(e):
                    for waits, fn, sem, inc in self.ops[name]:
                        for (s, v) in waits:
                            e.wait_ge(s, v)
                        if fn is not None:
                            fn(e).then_inc(sem, inc)
                return f
            block.tensor(run('pe'))
            block.scalar(run('act'))
            block.vector(run('dve'))
            block.gpsimd(run('pool'))
            block.sync(run('sp'))


class Arena:
    LO = 16512
    HI = 229376

    def __init__(self, nc):
        self.nc = nc
        self.cur = self.LO
        self.n = 0

    def alloc(self, shape, dt=F32):
        esz = 2 if dt == BF16 else 4
        nb = esz
        for s in shape[1:]:
            nb *= s
        nb = (nb + 63) // 64 * 64
        assert self.cur + nb <= self.HI, ("SBUF overflow", self.cur, nb)
        self.n += 1
        t = self.nc.alloc_sbuf_tensor_at("sb%d" % self.n, list(shape), dt, offset=self.cur)
        self.cur += nb
        return Buf(t.ap())


def build(cfg):
    C = cfg
    nc = bass.Bass("TRN2", target_bir_lowering=False)
    kb = KB(nc)
    ar = Arena(nc)
    D, T, SEQ, NS, DEPTH = C.D, C.T, C.SEQ, C.NS, C.DEPTH
    AW, NH, NKV, KW, SW, NKS, NPAIR, INW, DFF = C.AW, C.NH, C.NKV, C.KW, C.SW, C.NKS, C.NPAIR, C.INW, C.DFF
    KTD, KTA, KTF, NQB = C.KTD, C.KTA, C.KTF, C.NQB
    chunks, NMAX, TQ = C.chunks, C.NMAX, C.TQ
    KTMAX = max(KTF, KTD)

    def din(name, shape):
        return nc.dram_tensor(name, list(shape), F32, kind="ExternalInput").ap()

    def dout(name, shape):
        return nc.dram_tensor(name, list(shape), F32, kind="ExternalOutput").ap()

    def dscr(name, shape, dt=F32):
        return nc.dram_tensor(name, list(shape), dt).ap()

    xT_in = din("xT", [D, T])
    peT_in = din("peT", [DEPTH, 256, T])
    w_in = din("w_in", [DEPTH, D, INW])
    w_out = din("w_out", [DEPTH, D, D])
    w_gu = din("w_gate_up", [DEPTH, D, 2 * DFF])
    w_dn = din("w_down", [DEPTH, DFF, D])
    w_pg = din("w_ple_gate", [DEPTH, D, D])
    w_pp = din("w_ple_proj", [DEPTH, 256, D])
    gains = din("gains", [DEPTH, 4, 128, KTD])
    gssm_in = din("gssm", [DEPTH, 128, NKS])
    gattn_in = din("gattn", [DEPTH, 128, AW])
    gattnT_in = din("gattnT", [DEPTH, 64, NH])
    sinks_in = din("sinks", [DEPTH, 128, NH])
    ckT_in = din("ckT", [DEPTH, NS, NKV, 64, 128])
    ck_in = din("ck", [DEPTH, NS, 128, KW])
    cv_in = din("cv", [DEPTH, NS, 128, KW])
    ssm_ps_in = din("ssm_ps", [DEPTH, 128, 3, NPAIR])
    bpad_in = din("bpad", [DEPTH, 128, 2, NPAIR, 128])
    cpad_in = din("cpad", [DEPTH, 128, NKS, 2, 4, 128])
    dcol_in = din("dcol", [DEPTH, 128, NKS])
    wglu_in = din("wglu", [DEPTH, 128, NKS, 2, 128])
    st0_in = din("st0", [DEPTH, 128, 2, NPAIR, NS])
    consts_in = din("consts", [128, 128 + 256 + 256 + 512])

    yT_out = dout("yT", [D, T])
    kvp_out = dout("kvp", [DEPTH, 2 * KW, 128])
    ks_out = dout("ks", [DEPTH, NS, 128, KW])
    vs_out = dout("vs", [DEPTH, NS, 128, KW])
    st_out = dout("st", [DEPTH, 128, 2, NPAIR, 1 + NS])

    Xd = dscr("X", [D, T])
    ZBd = dscr("ZB", [INW, T], BF16)
    ZFd = dscr("ZF", [2 * KW, T])
    Od = dscr("O", [D, T])
    SSd = dscr("SS", [SW, T])
    ACTd = dscr("ACTs", [DFF, T], BF16)
    MGSd = dscr("MGS", [AW, NS], BF16)

    ident = ar.alloc([128, 128], BF16)
    ones = ar.alloc([128, 128], BF16)
    DIST = ar.alloc([128, 256])
    DIST0 = ar.alloc([128, 256])
    IOT = ar.alloc([128, 512])
    srcbuf = ar.alloc([128, max(KTD * T, KTF * (T // 2))], BF16)
    phase_mark = ar.cur

    PS = [Buf(nc.alloc_psum_tensor("ps%d" % i, [128, 512], F32).ap(), excl=True) for i in range(8)]

    def src3(kt_n, tn):
        return srcbuf.ap[:, 0:kt_n * tn].rearrange("p (k t) -> p k t", t=tn)

    SRC_D = src3(KTD, T)

    outtoks = []

    class _Stop(Exception):
        pass
    import os as _os
    _stop_after = int(_os.environ.get('STOP_AFTER', '100000'))
    _pc = [0]

    _marks = []

    def new_phase(name=None):
        import inspect
        if name is None:
            name = inspect.stack()[1].function + ':' + str(inspect.stack()[1].lineno)
        _marks.append((name, sum(1 for o in kb.ops['pe'] if o[1] is not None)))
        _pc[0] += 1
        if _pc[0] > _stop_after:
            raise _Stop()
        kb.barrier()
        ar.cur = phase_mark

    kb.dma('pool', ident.ap, consts_in[:, 0:128], W=[ident])
    kb.dma('sp', DIST.ap, consts_in[:, 128:384], W=[DIST])
    kb.dma('sp', DIST0.ap, consts_in[:, 384:640], W=[DIST0])
    kb.dma('sp', IOT.ap, consts_in[:, 640:1152], W=[IOT])
    kb.do('pool', lambda e: e.memset(ones.ap, 1.0), W=[ones])
    xw = Buf(Xd)
    for r in range(KTD):
        kb.dma('sp', Xd[r * 128:(r + 1) * 128, :], xT_in[r * 128:(r + 1) * 128, :], W=[xw])

    rot = {}

    def nxt(key, n):
        rot[key] = (rot.get(key, -1) + 1) % n
        return rot[key]

    def norm_phase(src_d, KT, g1_ap, resid_d=None, out=None, g2_ap=None, dst_kt0=0, Fdim=None):
        new_phase()
        Fdim = KT * 128
        g1 = ar.alloc([128, KT])
        kb.dma('sp', g1.ap, g1_ap, W=[g1])
        g2 = None
        if g2_ap is not None:
            g2 = ar.alloc([128, KT])
            kb.dma('sp', g2.ap, g2_ap, W=[g2])
        xin = [ar.alloc([128, KT, NMAX]) for _ in range(2)]
        xr = [ar.alloc([128, KT, NMAX]) for _ in range(2)] if resid_d is not None else None
        sq = [ar.alloc([128, NMAX], BF16) for _ in range(3)]
        rs = [ar.alloc([128, NMAX]) for _ in range(2)]
        tmp = [ar.alloc([128, NMAX]) for _ in range(3)]
        sv = src_d.rearrange("(k p) t -> p k t", p=128)
        xv = resid_d.rearrange("(k p) t -> p k t", p=128) if resid_d is not None else None
        srcB = Buf(src_d)
        psA, psB = PS[6], PS[7]

        def rstd_of(buf_in, n, psb, rsb):
            for kt in range(KT):
                s = sq[nxt('sq', 3)]
                kb.do('act', lambda e, kt=kt, s=s: e.activation(out=s.ap[:, 0:n], in_=buf_in.ap[:, kt, 0:n], func=AF.Square),
                      R=[buf_in], W=[s])
                kb.do('pe', lambda e, kt=kt, s=s: e.matmul(psb.ap[:, 0:n], lhsT=ones.ap, rhs=s.ap[:, 0:n],
                                                           start=(kt == 0), stop=(kt == KT - 1)), R=[s, ones], W=[psb])
            kb.do('dve', lambda e: e.tensor_scalar(out=rsb.ap[:, 0:n], in0=psb.ap[:, 0:n], scalar1=1.0 / Fdim, scalar2=EPS,
                                                   op0=ALU.mult, op1=ALU.add), R=[psb], W=[rsb])
            kb.do('act', lambda e: e.activation(out=rsb.ap[:, 0:n], in_=rsb.ap[:, 0:n], func=AF.Ln), R=[rsb], W=[rsb])
            kb.do('act', lambda e: e.activation(out=rsb.ap[:, 0:n], in_=rsb.ap[:, 0:n], func=AF.Exp, scale=-0.5), R=[rsb], W=[rsb])

        for ci, (t0, n) in enumerate(chunks):
            xi = xin[ci % 2]
            kb.dma('sp', xi.ap[:, :, 0:n], sv[:, :, t0:t0 + n], R=[srcB], W=[xi])
            r1 = rs[0]
            rstd_of(xi, n, psA, r1)
            if resid_d is None:
                for kt in range(KT):
                    kb.do('dve', lambda e, kt=kt: e.scalar_tensor_tensor(
                        out=SRC_D[:, dst_kt0 + kt, t0:t0 + n], in0=xi.ap[:, kt, 0:n], scalar=g1.ap[:, kt:kt + 1],
                        in1=r1.ap[:, 0:n], op0=ALU.mult, op1=ALU.mult), R=[xi, g1, r1], W=[srcbuf])
                continue
            xx = xr[ci % 2]
            xB = Buf(resid_d)
            kb.dma('sp', xx.ap[:, :, 0:n], xv[:, :, t0:t0 + n], R=[xB], W=[xx])
            for kt in range(KT):
                tb = tmp[nxt('tmp', 3)]
                kb.do('dve', lambda e, kt=kt, tb=tb: e.scalar_tensor_tensor(
                    out=tb.ap[:, 0:n], in0=xi.ap[:, kt, 0:n], scalar=g1.ap[:, kt:kt + 1], in1=r1.ap[:, 0:n],
                    op0=ALU.mult, op1=ALU.mult), R=[xi, g1, r1], W=[tb])
                kb.do('dve', lambda e, kt=kt, tb=tb: e.tensor_tensor(out=xx.ap[:, kt, 0:n], in0=xx.ap[:, kt, 0:n],
                                                                      in1=tb.ap[:, 0:n], op=ALU.add), R=[tb], W=[xx])
            outtoks.append(kb.dma('sp', xv[:, :, t0:t0 + n], xx.ap[:, :, 0:n], R=[xx], W=[xB]))
            if out == 'norm':
                r2 = rs[1]
                rstd_of(xx, n, psB, r2)
                for kt in range(KT):
                    kb.do('dve', lambda e, kt=kt: e.scalar_tensor_tensor(
                        out=SRC_D[:, dst_kt0 + kt, t0:t0 + n], in0=xx.ap[:, kt, 0:n], scalar=g2.ap[:, kt:kt + 1],
                        in1=r2.ap[:, 0:n], op0=ALU.mult, op1=ALU.mult), R=[xx, g2, r2], W=[srcbuf])
            elif out == 'cast':
                kb.do('act', lambda e: e.activation(out=SRC_D[:, dst_kt0:dst_kt0 + KT, t0:t0 + n], in_=xx.ap[:, :, 0:n],
                                                    func=AF.Copy), R=[xx], W=[srcbuf])

    def linear_phase(srcv, KT, W_ap, cols, epi, W2cols=None, tchunks=None, fresh=True):
        if fresh:
            new_phase()
        wb = [ar.alloc([128, KT, 128], BF16) for _ in range(3)]
        wb2 = [ar.alloc([128, KT, 128], BF16) for _ in range(2)] if W2cols is not None else None
        wv = W_ap.rearrange("(k p) n -> p k n", p=128)
        tch = tchunks if tchunks is not None else chunks
        st = {}

        def go(mi):
            c0 = cols[mi]
            w = wb[mi % 3]
            kb.dma('pool', w.ap, wv[:, :, c0:c0 + 128], W=[w])
            w2 = None
            if W2cols is not None:
                w2 = wb2[mi % 2]
                kb.dma('pool', w2.ap, wv[:, :, W2cols[mi]:W2cols[mi] + 128], W=[w2])
            for ci, (t0, n, s0) in enumerate(tch):
                if W2cols is None:
                    pb = PS[nxt('lin', 6)]
                    pb2 = None
                else:
                    j = nxt('lin2', 3)
                    pb, pb2 = PS[2 * j], PS[2 * j + 1]
                for kt in range(KT):
                    kb.do('pe', lambda e, kt=kt, pb=pb, w=w: e.matmul(pb.ap[:, 0:n], lhsT=w.ap[:, kt, :], rhs=srcv[:, kt, s0:s0 + n],
                                                                   start=(kt == 0), stop=(kt == KT - 1)), R=[w, srcbuf], W=[pb])
                if pb2 is not None:
                    for kt in range(KT):
                        kb.do('pe', lambda e, kt=kt, pb2=pb2, w2=w2: e.matmul(pb2.ap[:, 0:n], lhsT=w2.ap[:, kt, :], rhs=srcv[:, kt, s0:s0 + n],
                                                                         start=(kt == 0), stop=(kt == KT - 1)), R=[w2, srcbuf], W=[pb2])
                epi(mi, ci, t0, n, pb, pb2)
        return go, st

    full_chunks = [(t0, n, t0) for (t0, n) in chunks]

    def attention_phase(l):
        new_phase()
        kTd = ar.alloc([128, NKV, 128 + T], BF16)
        Vtm = ar.alloc([128, NQB + 1, KW], BF16)
        vTi = [ar.alloc([128, 128], BF16) for _ in range(2)]
        qTb = [ar.alloc([128, KTA, 128], BF16) for _ in range(2)]
        atm = [ar.alloc([128, AW]) for _ in range(2)]
        hn = [ar.alloc([128, AW], BF16) for _ in range(2)]
        junk = ar.alloc([128, AW], BF16)
        sm2 = [ar.alloc([128, 4]) for _ in range(2)]
        gat = ar.alloc([128, AW])
        snk = ar.alloc([128, NH])
        zb = Buf(ZBd)
        kb.dma('sp', gat.ap, gattn_in[l], W=[gat])
        kb.dma('sp', snk.ap, sinks_in[l], W=[snk])
        kb.do('pool', lambda e: e.memset(kTd.ap[:, :, 0:128], 0.0), W=[kTd])
        kb.do('pool', lambda e: e.memset(Vtm.ap[:, 0, :], 0.0), W=[Vtm])
        for g in range(NKV):
            for cp in range(2):
                kb.dma('sp', kTd.ap[cp * 64:(cp + 1) * 64, g, 128:128 + T], ZBd[AW + g * 64:AW + (g + 1) * 64, :], R=[zb], W=[kTd])
        PSb = [Buf(PS[i].ap.bitcast(BF16), excl=True) for i in range(8)]
        for i in range(8):
            PSb[i].w, PSb[i].r = PS[i].w, PS[i].r
        for b in range(NQB):
            vi = vTi[b % 2]
            kb.dma('sp', vi.ap[0:KW, :], ZBd[AW + KW:AW + 2 * KW, b * 128:(b + 1) * 128], R=[zb], W=[vi])
            pt = PSb[7]
            kb.do('pe', lambda e, vi=vi, pt=pt: e.transpose(out=pt.ap[:, 0:KW], in_=vi.ap[0:KW, :], identity=ident.ap[0:KW, 0:KW]),
                  R=[vi, ident], W=[pt])
            kb.do('act', lambda e, b=b, pt=pt: e.activation(out=Vtm.ap[:, b + 1, :], in_=pt.ap[:, 0:KW], func=AF.Copy), R=[pt], W=[Vtm])

        NRR = 8
        Sb = [ar.alloc([128, 256]) for _ in range(NRR)]
        Pb = [ar.alloc([128, 256], BF16) for _ in range(NRR)]
        PTs = [ar.alloc([128, 2, 128], BF16) for _ in range(NRR)]
        sm = [ar.alloc([128, 8]) for _ in range(NRR)]

        def pipeline(units, stages, after=None):
            ns = len(stages)
            for t in range(len(units) + ns - 1):
                for si, st in enumerate(stages):
                    ui = t - si
                    if 0 <= ui < len(units):
                        st(units[ui])
                        if si == ns - 1 and after is not None:
                            after(ui)

        def sA(u):
            S_, m_, npart, nk, h, sps = Sb[u['i']], sm[u['i']], u['np'], u['nk'], u['h'], u['sps']
            kb.do('dve', lambda e: e.scalar_tensor_tensor(out=S_.ap[0:npart, 0:nk], in0=u['dist'], scalar=-C.slopes[h],
                                                          in1=sps.ap[0:npart, 0:nk], op0=ALU.mult, op1=ALU.add),
                  R=[sps, DIST, DIST0], W=[S_])
            kb.do('dve', lambda e: e.reduce_max(out=m_.ap[0:npart, 0:1], in_=S_.ap[0:npart, 0:nk], axis=AX.X), R=[S_], W=[m_])
            kb.do('dve', lambda e: e.tensor_tensor(out=m_.ap[0:npart, 0:1], in0=m_.ap[0:npart, 0:1], in1=snk.ap[0:npart, h:h + 1],
                                                   op=ALU.max), R=[snk], W=[m_])
            kb.do('dve', lambda e: e.tensor_scalar(out=m_.ap[0:npart, 1:2], in0=m_.ap[0:npart, 0:1], scalar1=-1.0, scalar2=None,
                                                   op0=ALU.mult), R=[], W=[m_])

        def sB(u):
            S_, P_, m_, npart, nk, h = Sb[u['i']], Pb[u['i']], sm[u['i']], u['np'], u['nk'], u['h']
            kb.do('act', lambda e: e.activation(out=P_.ap[0:npart, 0:nk], in_=S_.ap[0:npart, 0:nk], func=AF.Exp,
                                                bias=m_.ap[0:npart, 1:2], scale=1.0, accum_out=m_.ap[0:npart, 2:3]),
                  R=[S_, m_], W=[P_, m_])
            kb.do('act', lambda e: e.activation(out=m_.ap[0:npart, 3:4], in_=snk.ap[0:npart, h:h + 1], func=AF.Exp,
                                                bias=m_.ap[0:npart, 1:2], scale=1.0), R=[snk, m_], W=[m_])

        def sC(u):
            m_, npart = sm[u['i']], u['np']
            kb.do('dve', lambda e: e.tensor_tensor(out=m_.ap[0:npart, 4:5], in0=m_.ap[0:npart, 2:3], in1=m_.ap[0:npart, 3:4],
                                                   op=ALU.add), R=[m_], W=[m_])
            kb.do('dve', lambda e: e.reciprocal(out=m_.ap[0:npart, 5:6], in_=m_.ap[0:npart, 4:5]), R=[m_], W=[m_])

        units = []
        for b in range(NQB):
            for h in range(NH):
                units.append(dict(b=b, h=h, g=h // C.GRP, hp=(h % 2) * 64, np=128, nk=256,
                                  dist=(DIST0 if b == 0 else DIST).ap))

        def pA(u):
            b, h, g, hp = u['b'], u['h'], u['g'], u['hp']
            if h == 0:
                qb = qTb[b % 2]
                kb.dma('sp', qb.ap, ZBd[0:AW, b * 128:(b + 1) * 128].rearrange("(k p) t -> p k t", p=128), R=[zb], W=[qb])
            qb = qTb[b % 2]
            u['i'] = nxt('att', NRR)
            u['sps'] = sps = PS[nxt('sps', 3)]
            kb.do('pe', lambda e: e.matmul(sps.ap[:, 0:256], lhsT=qb.ap[hp:hp + 64, h // 2, :],
                                           rhs=kTd.ap[hp:hp + 64, g, b * 128:b * 128 + 256], start=True, stop=True), R=[qb, kTd], W=[sps])
            sA(u)

        def pC(u):
            sC(u)
            i = u['i']
            u['ptp'] = ptp = PSb[3 + nxt('ptp', 2)]
            for j in range(2):
                kb.do('pe', lambda e, j=j: e.transpose(out=ptp.ap[:, j * 128:(j + 1) * 128], in_=Pb[i].ap[:, j * 128:(j + 1) * 128],
                                                       identity=ident.ap), R=[Pb[i], ident], W=[ptp])

        def pD(u):
            i, ptp = u['i'], u['ptp']
            kb.do('act', lambda e: e.activation(out=PTs[i].ap, in_=ptp.ap[:, 0:256].rearrange("p (j q) -> p j q", j=2), func=AF.Copy),
                  R=[ptp], W=[PTs[i]])

        def pE(u):
            i, b, g = u['i'], u['b'], u['g']
            u['ops'] = ops_ = PS[5 + nxt('ops', 2)]
            for j in range(2):
                kb.do('pe', lambda e, j=j: e.matmul(ops_.ap[:, 0:64], lhsT=PTs[i].ap[:, j, :], rhs=Vtm.ap[:, b + j, g * 64:(g + 1) * 64],
                                                    start=(j == 0), stop=(j == 1)), R=[PTs[i], Vtm], W=[ops_])

        def pF(u):
            i, h, ops_ = u['i'], u['h'], u['ops']
            am = atm[u['b'] % 2]
            kb.do('dve', lambda e: e.tensor_scalar(out=am.ap[:, h * 64:(h + 1) * 64], in0=ops_.ap[:, 0:64], scalar1=sm[i].ap[:, 5:6],
                                                   scalar2=None, op0=ALU.mult), R=[ops_, sm[i]], W=[am])

        def block_done(ui):
            u = units[ui]
            if u['h'] != NH - 1:
                return
            b = u['b']
            am, s2, hb = atm[b % 2], sm2[b % 2], hn[b % 2]
            kb.do('act', lambda e: e.activation(out=junk.ap, in_=am.ap, func=AF.Square, accum_out=s2.ap[:, 0:1]), R=[am], W=[junk, s2])
            kb.do('dve', lambda e: e.tensor_scalar(out=s2.ap[:, 1:2], in0=s2.ap[:, 0:1], scalar1=1.0 / AW, scalar2=EPS,
                                                   op0=ALU.mult, op1=ALU.add), R=[s2], W=[s2])
            kb.do('act', lambda e: e.activation(out=s2.ap[:, 1:2], in_=s2.ap[:, 1:2], func=AF.Ln), R=[s2], W=[s2])
            kb.do('act', lambda e: e.activation(out=s2.ap[:, 1:2], in_=s2.ap[:, 1:2], func=AF.Exp, scale=-0.5), R=[s2], W=[s2])
            kb.do('dve', lambda e: e.scalar_tensor_tensor(out=hb.ap, in0=am.ap, scalar=s2.ap[:, 1:2], in1=gat.ap, op0=ALU.mult, op1=ALU.mult),
                  R=[am, s2, gat], W=[hb])
            for kt in range(KTA):
                pt = PSb[7]
                kb.do('pe', lambda e, kt=kt: e.transpose(out=pt.ap[:, 0:128], in_=hb.ap[:, kt * 128:(kt + 1) * 128], identity=ident.ap),
                      R=[hb, ident], W=[pt])
                kb.do('act', lambda e, kt=kt: e.activation(out=SRC_D[:, kt, b * 128:(b + 1) * 128], in_=pt.ap[:, 0:128], func=AF.Copy),
                      R=[pt], W=[srcbuf])

        pipeline(units, [pA, sB, pC, pD, pE, pF], after=block_done)

        kS = [ar.alloc([128, NKV, 132], BF16) for _ in range(2)]
        vS = [ar.alloc([128, KW], BF16) for _ in range(2)]
        vN = [ar.alloc([1, KW], BF16) for _ in range(2)]
        qS = ar.alloc([128, KTA, NS], BF16)
        PnS = [ar.alloc([1, 132], BF16) for _ in range(NRR)]
        PTS = [ar.alloc([128, 2], BF16) for _ in range(NRR)]
        aS = ar.alloc([64, NH, NS])
        zf = Buf(ZFd)
        kb.dma('sp', qS.ap, ZBd[0:AW, SEQ:SEQ + NS].rearrange("(k p) t -> p k t", p=128), R=[zb], W=[qS])
        sunits = []
        for n in range(NS):
            for h in range(NH):
                sunits.append(dict(n=n, h=h, g=h // C.GRP, hp=(h % 2) * 64, np=1, nk=129, dist=DIST.ap[0:1, 0:129]))

        def qA(u):
            n, h, g, hp = u['n'], u['h'], u['g'], u['hp']
            ks_, vs_, vn_ = kS[n % 2], vS[n % 2], vN[n % 2]
            if h == 0:
                for g_ in range(NKV):
                    for cp in range(2):
                        kb.dma('pool', ks_.ap[cp * 64:(cp + 1) * 64, g_, 0:128], ckT_in[l, n, g_], W=[ks_])
                kb.do('pool', lambda e: e.tensor_copy(out=ks_.ap[:, :, 128:129], in_=kTd.ap[:, :, 128 + SEQ + n:128 + SEQ + n + 1]),
                      R=[kTd], W=[ks_])
                kb.dma('pool', vs_.ap, cv_in[l, n], W=[vs_])
                kb.dma('pool', vn_.ap, ZFd[KW:2 * KW, SEQ + n:SEQ + n + 1].rearrange("k o -> o k"), R=[zf], W=[vn_], allow_slow_non_contiguous=True)
                outtoks.append(kb.dma('sp', ks_out[l, n, 0:127, :], ck_in[l, n, 1:128, :]))
                outtoks.append(kb.dma('sp', vs_out[l, n, 0:127, :], cv_in[l, n, 1:128, :]))
                outtoks.append(kb.dma('sp', ks_out[l, n, 127:128, :], ZFd[0:KW, SEQ + n:SEQ + n + 1].rearrange("k o -> o k"), R=[zf], allow_slow_non_contiguous=True))
                outtoks.append(kb.dma('sp', vs_out[l, n, 127:128, :], ZFd[KW:2 * KW, SEQ + n:SEQ + n + 1].rearrange("k o -> o k"), R=[zf], allow_slow_non_contiguous=True))
            u['i'] = nxt('att', NRR)
            u['sps'] = sps = PS[nxt('sps', 3)]
            kb.do('pe', lambda e: e.matmul(sps.ap[0:1, 0:129], lhsT=qS.ap[hp:hp + 64, h // 2, n:n + 1], rhs=ks_.ap[hp:hp + 64, g, 0:129],
                                           start=True, stop=True), R=[qS, ks_], W=[sps])
            sA(u)

        def qC(u):
            sC(u)
            i = u['i']
            pn = PnS[i]
            kb.do('dve', lambda e: e.tensor_scalar(out=pn.ap[0:1, 0:129], in0=Pb[i].ap[0:1, 0:129], scalar1=sm[i].ap[0:1, 5:6], scalar2=None,
                                                   op0=ALU.mult), R=[Pb[i], sm[i]], W=[pn])
            u['ptp'] = ptp = PSb[3 + nxt('ptp', 2)]
            kb.do('pe', lambda e: e.transpose(out=ptp.ap[:, 0:1], in_=pn.ap[0:1, 0:128], identity=ident.ap[0:1, 0:1]), R=[pn, ident], W=[ptp])

        def qD(u):
            i, ptp = u['i'], u['ptp']
            kb.do('act', lambda e: e.activation(out=PTS[i].ap[:, 0:1], in_=ptp.ap[:, 0:1], func=AF.Copy), R=[ptp], W=[PTS[i]])

        def qE(u):
            i, g, n = u['i'], u['g'], u['n']
            vs_, vn_, pn = vS[n % 2], vN[n % 2], PnS[i]
            u['ops'] = ops_ = PS[5 + nxt('ops', 2)]
            kb.do('pe', lambda e: e.matmul(ops_.ap[0:64, 0:1], lhsT=vs_.ap[:, g * 64:(g + 1) * 64], rhs=PTS[i].ap[:, 0:1], start=True, stop=False),
                  R=[PTS[i], vs_], W=[ops_])
            kb.do('pe', lambda e: e.matmul(ops_.ap[0:64, 0:1], lhsT=vn_.ap[0:1, g * 64:(g + 1) * 64], rhs=pn.ap[0:1, 128:129], start=False, stop=True),
                  R=[pn, vn_], W=[ops_])

        def qF(u):
            ops_, h, n = u['ops'], u['h'], u['n']
            kb.do('act', lambda e: e.activation(out=aS.ap[:, h, n:n + 1], in_=ops_.ap[0:64, 0:1], func=AF.Copy), R=[ops_], W=[aS])

        if not _os.environ.get('PIPE_SAMPLE'):
            for u_ in sunits:
                for st_ in (qA, sB, qC, qD, qE, qF):
                    st_(u_)
        else:
            pipeline(sunits, [qA, sB, qC, qD, qE, qF])
        sqS = ar.alloc([64, NH * NS], BF16)
        ssS = ar.alloc([64, NS])
        gT = ar.alloc([64, NH])
        hS = ar.alloc([64, NH, NS])
        hSb = ar.alloc([64, NH, NS], BF16)
        kb.dma('sp', gT.ap, gattnT_in[l], W=[gT])
        kb.do('act', lambda e: e.activation(out=sqS.ap, in_=aS.ap.rearrange("p h n -> p (h n)"), func=AF.Square), R=[aS], W=[sqS])
        kb.do('pe', lambda e: e.matmul(PS[7].ap[0:64, 0:NH * NS], lhsT=ones.ap[0:64, 0:64], rhs=sqS.ap, start=True, stop=True),
              R=[sqS, ones], W=[PS[7]])
        kb.do('dve', lambda e: e.tensor_reduce(out=ssS.ap, in_=PS[7].ap[0:64, 0:NH * NS].rearrange("p (h n) -> p n h", n=NS),
                                               axis=AX.X, op=ALU.add), R=[PS[7]], W=[ssS])
        kb.do('dve', lambda e: e.tensor_scalar(out=ssS.ap, in0=ssS.ap, scalar1=1.0 / AW, scalar2=EPS, op0=ALU.mult, op1=ALU.add),
              R=[ssS], W=[ssS])
        kb.do('act', lambda e: e.activation(out=ssS.ap, in_=ssS.ap, func=AF.Ln), R=[ssS], W=[ssS])
        kb.do('act', lambda e: e.activation(out=ssS.ap, in_=ssS.ap, func=AF.Exp, scale=-0.5), R=[ssS], W=[ssS])
        kb.do('dve', lambda e: e.tensor_tensor(out=hS.ap, in0=aS.ap, in1=gT.ap.unsqueeze(2).broadcast_to([64, NH, NS]), op=ALU.mult),
              R=[aS, gT], W=[hS])
        kb.do('dve', lambda e: e.tensor_tensor(out=hSb.ap, in0=hS.ap, in1=ssS.ap.unsqueeze(1).broadcast_to([64, NH, NS]), op=ALU.mult),
              R=[hS, ssS], W=[hSb])
        mg = Buf(MGSd)
        kb.dma('sp', MGSd.rearrange("(h d) n -> d h n", d=64), hSb.ap, R=[hSb], W=[mg])
        kb.dma('sp', SRC_D[:, 0:KTA, SEQ:SEQ + NS], MGSd.rearrange("(k p) n -> p k n", p=128), R=[mg], W=[srcbuf])

    def ssm_phase(l):
        new_phase()
        ps_ = ar.alloc([128, 3, NPAIR])
        kb.dma('sp', ps_.ap, ssm_ps_in[l], W=[ps_])
        NV = 17
        v = ar.alloc([128, NV, NPAIR])
        vi = ar.alloc([128, NPAIR], I32)
        are, aim, ldt = ps_.ap[:, 0, :], ps_.ap[:, 1, :], ps_.ap[:, 2, :]
        V_DT, V_DRE, V_TH, V_R, V_A, V_SIN, V_COS, V_ABR, V_ABI, V_FR, V_FI, V_IFR, V_IFI, V_T1, V_T2, V_T3, V_NFI = range(17)

        def vv(i):
            return v.ap[:, i, :]

        def tiny(eng, fn):
            kb.do(eng, fn, R=[v, ps_], W=[v])

        def wrap_turns(eng_ap_in, out_i):
            kb.do('dve', lambda e: e.tensor_copy(out=vi.ap, in_=eng_ap_in), R=[v], W=[vi])
            kb.do('dve', lambda e: e.tensor_copy(out=vv(V_T1), in_=vi.ap), R=[vi, v], W=[v])
            tiny('dve', lambda e: e.tensor_tensor(out=vv(out_i), in0=eng_ap_in, in1=vv(V_T1), op=ALU.subtract))
            tiny('dve', lambda e: e.tensor_scalar(out=vv(V_T1), in0=vv(out_i), scalar1=0.5, scalar2=None, op0=ALU.is_gt))
            tiny('dve', lambda e: e.tensor_tensor(out=vv(out_i), in0=vv(out_i), in1=vv(V_T1), op=ALU.subtract))
            tiny('dve', lambda e: e.tensor_scalar(out=vv(V_T1), in0=vv(out_i), scalar1=-0.5, scalar2=None, op0=ALU.is_lt))
            tiny('dve', lambda e: e.tensor_tensor(out=vv(out_i), in0=vv(out_i), in1=vv(V_T1), op=ALU.add))

        tiny('act', lambda e: e.activation(out=vv(V_DT), in_=ldt, func=AF.Exp))
        tiny('dve', lambda e: e.tensor_tensor(out=vv(V_DRE), in0=vv(V_DT), in1=are, op=ALU.mult))
        tiny('dve', lambda e: e.tensor_tensor(out=vv(V_TH), in0=vv(V_DT), in1=aim, op=ALU.mult))
        tiny('act', lambda e: e.activation(out=vv(V_R), in_=vv(V_DRE), func=AF.Exp))
        tiny('dve', lambda e: e.tensor_scalar(out=vv(V_T2), in0=vv(V_TH), scalar1=1.0 / (2 * math.pi), scalar2=None, op0=ALU.mult))
        wrap_turns(vv(V_T2), V_A)
        tiny('act', lambda e: e.activation(out=vv(V_SIN), in_=vv(V_A), func=AF.Sin, scale=TWO_PI))
        tiny('dve', lambda e: e.tensor_scalar(out=vv(V_T2), in0=vv(V_A), scalar1=0.25, scalar2=None, op0=ALU.add))
        wrap_turns(vv(V_T2), V_T3)
        tiny('act', lambda e: e.activation(out=vv(V_COS), in_=vv(V_T3), func=AF.Sin, scale=TWO_PI))
        tiny('dve', lambda e: e.tensor_tensor(out=vv(V_ABR), in0=vv(V_R), in1=vv(V_COS), op=ALU.mult))
        tiny('dve', lambda e: e.tensor_tensor(out=vv(V_ABI), in0=vv(V_R), in1=vv(V_SIN), op=ALU.mult))
        tiny('dve', lambda e: e.tensor_tensor(out=vv(V_T1), in0=are, in1=are, op=ALU.mult))
        tiny('dve', lambda e: e.tensor_tensor(out=vv(V_T2), in0=aim, in1=aim, op=ALU.mult))
        tiny('dve', lambda e: e.tensor_tensor(out=vv(V_T1), in0=vv(V_T1), in1=vv(V_T2), op=ALU.add))
        tiny('dve', lambda e: e.reciprocal(out=vv(V_T1), in_=vv(V_T1)))
        tiny('dve', lambda e: e.tensor_scalar(out=vv(V_T2), in0=vv(V_ABR), scalar1=-1.0, scalar2=None, op0=ALU.add))
        tiny('dve', lambda e: e.tensor_tensor(out=vv(V_FR), in0=vv(V_T2), in1=are, op=ALU.mult))
        tiny('dve', lambda e: e.tensor_tensor(out=vv(V_T3), in0=vv(V_ABI), in1=aim, op=ALU.mult))
        tiny('dve', lambda e: e.tensor_tensor(out=vv(V_FR), in0=vv(V_FR), in1=vv(V_T3), op=ALU.add))
        tiny('dve', lambda e: e.tensor_tensor(out=vv(V_FR), in0=vv(V_FR), in1=vv(V_T1), op=ALU.mult))
        tiny('dve', lambda e: e.tensor_tensor(out=vv(V_FI), in0=vv(V_ABI), in1=are, op=ALU.mult))
        tiny('dve', lambda e: e.tensor_tensor(out=vv(V_T3), in0=vv(V_T2), in1=aim, op=ALU.mult))
        tiny('dve', lambda e: e.tensor_tensor(out=vv(V_FI), in0=vv(V_FI), in1=vv(V_T3), op=ALU.subtract))
        tiny('dve', lambda e: e.tensor_tensor(out=vv(V_FI), in0=vv(V_FI), in1=vv(V_T1), op=ALU.mult))
        tiny('dve', lambda e: e.tensor_tensor(out=vv(V_T1), in0=vv(V_FR), in1=vv(V_FR), op=ALU.mult))
        tiny('dve', lambda e: e.tensor_tensor(out=vv(V_T2), in0=vv(V_FI), in1=vv(V_FI), op=ALU.mult))
        tiny('dve', lambda e: e.tensor_tensor(out=vv(V_T1), in0=vv(V_T1), in1=vv(V_T2), op=ALU.add))
        tiny('dve', lambda e: e.reciprocal(out=vv(V_T1), in_=vv(V_T1)))
        tiny('dve', lambda e: e.tensor_tensor(out=vv(V_IFR), in0=vv(V_FR), in1=vv(V_T1), op=ALU.mult))
        tiny('dve', lambda e: e.tensor_tensor(out=vv(V_IFI), in0=vv(V_FI), in1=vv(V_T1), op=ALU.mult))
        tiny('dve', lambda e: e.tensor_scalar(out=vv(V_IFI), in0=vv(V_IFI), scalar1=-1.0, scalar2=None, op0=ALU.mult))
        tiny('dve', lambda e: e.tensor_scalar(out=vv(V_NFI), in0=vv(V_FI), scalar1=-1.0, scalar2=None, op0=ALU.mult))

        bpk = [ar.alloc([128, 2, 4, 128], BF16) for _ in range(2)]
        wg = ar.alloc([128, NKS, 2, 128], BF16)
        kb.dma('pool', wg.ap, wglu_in[l], W=[wg], max_dma_last_dim=4096)
        dc = ar.alloc([128, NKS])
        kb.dma('sp', dc.ap, dcol_in[l], W=[dc])
        st0 = ar.alloc([128, 2, NPAIR, NS])
        kb.dma('sp', st0.ap, st0_in[l], W=[st0])
        sto = ar.alloc([128, 2, NPAIR, 1 + NS])
        uT = [ar.alloc([128, T], BF16) for _ in range(1)]
        cpf = [ar.alloc([128, 2, 4, 128]) for _ in range(1)]
        cf = [ar.alloc([128, 2, 4, 128], BF16) for _ in range(2)]
        cft = ar.alloc([128, 128])
        RT = [ar.alloc([128, TQ]) for _ in range(4)]
        NW = 4
        AT = [ar.alloc([128, TQ]) for _ in range(NW)]
        A2 = [ar.alloc([128, TQ]) for _ in range(NW)]
        NF = [ar.alloc([128, TQ]) for _ in range(NW)]
        NA = [ar.alloc([128, TQ]) for _ in range(NW)]
        CS = [ar.alloc([128, TQ]) for _ in range(NW)]
        SN = [ar.alloc([128, TQ]) for _ in range(NW)]
        PW1 = [ar.alloc([128, TQ]) for _ in range(2)]
        PW2 = [ar.alloc([128, TQ]) for _ in range(2)]
        THO = ar.alloc([128, SEQ // TQ, NPAIR])
        HPI = ar.alloc([128, 1])
        kb.do('dve', lambda e: e.memset(HPI.ap, math.pi / 2), W=[HPI])
        for tq_ in range(SEQ // TQ):
            kb.do('dve', lambda e, tq_=tq_: e.tensor_scalar(out=THO.ap[:, tq_, :], in0=vv(V_A), scalar1=float(tq_ * TQ), scalar2=None, op0=ALU.mult),
                  R=[v], W=[THO])
        W1 = [ar.alloc([128, TQ]) for _ in range(NW)]
        W2 = [ar.alloc([128, TQ]) for _ in range(NW)]
        XR = [ar.alloc([128, TQ]) for _ in range(NW)]
        XI = [ar.alloc([128, TQ]) for _ in range(NW)]
        SR = [ar.alloc([128, TQ]) for _ in range(NW)]
        SI = [ar.alloc([128, TQ]) for _ in range(NW)]
        SB2R = [ar.alloc([128, TQ], BF16) for _ in range(4)]
        SB2I = [ar.alloc([128, TQ], BF16) for _ in range(4)]
        carry = ar.alloc([128, 2, NPAIR])
        aoff = ar.alloc([128, NPAIR])
        FIN = [ar.alloc([128, 8]) for _ in range(2)]
        ssbb = ar.alloc([128, 2, 4, NS], BF16)
        sw = ar.alloc([128, 3, 4, NS])
        craw = ar.alloc([128, 2, 4, 128], BF16)
        TS = ar.alloc([128, 2, NPAIR, NS])
        tsw = ar.alloc([128, 2, NPAIR, NS])
        abrb = v.ap[:, V_ABR, :].unsqueeze(2).broadcast_to([128, NPAIR, NS])
        abib = v.ap[:, V_ABI, :].unsqueeze(2).broadcast_to([128, NPAIR, NS])
        kb.do('dve', lambda e: e.tensor_tensor(out=TS.ap[:, 0], in0=st0.ap[:, 0], in1=abrb, op=ALU.mult), R=[st0, v], W=[TS])
        kb.do('dve', lambda e: e.tensor_tensor(out=tsw.ap[:, 0], in0=st0.ap[:, 1], in1=abib, op=ALU.mult), R=[st0, v], W=[tsw])
        kb.do('dve', lambda e: e.tensor_tensor(out=TS.ap[:, 0], in0=TS.ap[:, 0], in1=tsw.ap[:, 0], op=ALU.subtract), R=[tsw], W=[TS])
        kb.do('dve', lambda e: e.tensor_tensor(out=TS.ap[:, 1], in0=st0.ap[:, 0], in1=abib, op=ALU.mult), R=[st0, v], W=[TS])
        kb.do('dve', lambda e: e.tensor_tensor(out=tsw.ap[:, 1], in0=st0.ap[:, 1], in1=abrb, op=ALU.mult), R=[st0, v], W=[tsw])
        kb.do('dve', lambda e: e.tensor_tensor(out=TS.ap[:, 1], in0=TS.ap[:, 1], in1=tsw.ap[:, 1], op=ALU.add), R=[tsw], W=[TS])
        yst = [ar.alloc([128, T]) for _ in range(1)]
        EY = ar.alloc([128, T])
        E2 = [ar.alloc([128, 512]) for _ in range(1)]
        GB = [ar.alloc([128, 512], BF16) for _ in range(1)]
        zb = Buf(ZBd)
        ssB = Buf(SSd)
        kb.do('pool', lambda e: e.memset(carry.ap, 0.0), W=[carry])
        NTQ = SEQ // TQ

        deferred = []

        def run_deferred():
            for lst in deferred:
                kb.flush_interleaved(lst)
            del deferred[:]

        for kt in range(NKS):
            u = uT[kt % len(uT)]
            kb.dma('sp', u.ap, ZBd[AW + 2 * KW + kt * 128:AW + 2 * KW + (kt + 1) * 128, :], R=[zb], W=[u])
            cp_, cf_ = cpf[0], cf[kt % 2]
            bp = bpk[kt % 2]
            for ri_ in range(2):
                kb.dma('pool', bp.ap[:, ri_], bpad_in[l, :, ri_, kt * 4:(kt + 1) * 4, :], W=[bp])
            kb.dma('sp', cp_.ap, cpad_in[l, :, kt], W=[cp_])
            kb.do('act', lambda e: e.activation(out=craw.ap[:, 0], in_=cp_.ap[:, 0], func=AF.Copy), R=[cp_], W=[craw])
            kb.do('act', lambda e: e.activation(out=craw.ap[:, 1], in_=cp_.ap[:, 1], func=AF.Copy, scale=-1.0), R=[cp_], W=[craw])
            for pr in range(4):
                q = kt * 4 + pr
                fr, fi = v.ap[:, V_FR, q:q + 1], v.ap[:, V_FI, q:q + 1]
                nfi = v.ap[:, V_NFI, q:q + 1]
                kb.do('dve', lambda e, pr=pr, fi=fi: e.tensor_scalar(out=cft.ap, in0=cp_.ap[:, 1, pr, :], scalar1=fi, scalar2=None, op0=ALU.mult),
                      R=[cp_, v], W=[cft])
                kb.do('dve', lambda e, pr=pr, fr=fr: e.scalar_tensor_tensor(out=cf_.ap[:, 0, pr, :], in0=cp_.ap[:, 0, pr, :], scalar=fr, in1=cft.ap,
                                                                         op0=ALU.mult, op1=ALU.subtract), R=[cp_, v, cft], W=[cf_])
                kb.do('dve', lambda e, pr=pr, fr=fr: e.tensor_scalar(out=cft.ap, in0=cp_.ap[:, 1, pr, :], scalar1=fr, scalar2=-1.0, op0=ALU.mult,
                                                                  op1=ALU.mult), R=[cp_, v, cf_], W=[cft])
                kb.do('dve', lambda e, pr=pr, nfi=nfi: e.scalar_tensor_tensor(out=cf_.ap[:, 1, pr, :], in0=cp_.ap[:, 0, pr, :], scalar=nfi, in1=cft.ap,
                                                                           op0=ALU.mult, op1=ALU.add), R=[cp_, v, cft], W=[cf_])
            ys = yst[kt % len(yst)]
            for tq in range(NTQ + 1):
                samp = (tq == NTQ)
                t0 = tq * TQ
                n = NS if samp else TQ
                ypb = PS[4 + nxt('ypb', 2)]
                pend = []
                if samp:
                    run_deferred()
                    xs_ = PS[0]
                    for pr in range(4):
                        for ri in range(2):
                            c0_ = (ri * 4 + pr) * NS
                            kb.do('pe', lambda e, ri=ri, pr=pr, c0_=c0_: e.matmul(xs_.ap[:, c0_:c0_ + NS], lhsT=bp.ap[:, ri, pr, :], rhs=u.ap[:, t0:t0 + NS],
                                                                               start=True, stop=True), R=[bp, u], W=[xs_])
                    xv_ = xs_.ap[:, 0:8 * NS].rearrange("p (r q n) -> p r q n", r=2, q=4)
                    frb = v.ap[:, V_FR, kt * 4:kt * 4 + 4].unsqueeze(2).broadcast_to([128, 4, NS])
                    fib = v.ap[:, V_FI, kt * 4:kt * 4 + 4].unsqueeze(2).broadcast_to([128, 4, NS])
                    wr, wi, wt_ = sw.ap[:, 0], sw.ap[:, 1], sw.ap[:, 2]
                    kb.do('dve', lambda e: e.tensor_tensor(out=wr, in0=xv_[:, 0], in1=frb, op=ALU.mult), R=[xs_, v], W=[sw])
                    kb.do('dve', lambda e: e.tensor_tensor(out=wt_, in0=xv_[:, 1], in1=fib, op=ALU.mult), R=[xs_, v], W=[sw])
                    kb.do('dve', lambda e: e.tensor_tensor(out=wr, in0=wr, in1=wt_, op=ALU.subtract), R=[], W=[sw])
                    kb.do('dve', lambda e: e.tensor_tensor(out=wi, in0=xv_[:, 0], in1=fib, op=ALU.mult), R=[xs_, v], W=[sw])
                    kb.do('dve', lambda e: e.tensor_tensor(out=wt_, in0=xv_[:, 1], in1=frb, op=ALU.mult), R=[xs_, v], W=[sw])
                    kb.do('dve', lambda e: e.tensor_tensor(out=wi, in0=wi, in1=wt_, op=ALU.add), R=[], W=[sw])
                    kb.do('dve', lambda e: e.tensor_tensor(out=sto.ap[:, 0, kt * 4:kt * 4 + 4, 1:1 + NS], in0=wr, in1=TS.ap[:, 0, kt * 4:kt * 4 + 4, :], op=ALU.add),
                          R=[sw, TS], W=[sto])
                    kb.do('dve', lambda e: e.tensor_tensor(out=sto.ap[:, 1, kt * 4:kt * 4 + 4, 1:1 + NS], in0=wi, in1=TS.ap[:, 1, kt * 4:kt * 4 + 4, :], op=ALU.add),
                          R=[sw, TS], W=[sto])
                    kb.do('act', lambda e: e.activation(out=ssbb.ap, in_=sto.ap[:, :, kt * 4:kt * 4 + 4, 1:1 + NS], func=AF.Copy), R=[sto], W=[ssbb])
                for pr in range(4):
                    q = kt * 4 + pr
                    if not samp:
                        kb.begin_buffer()
                        xps_r, xps_i = PS[2 * nxt('xps', 2)], None
                        xps_i = PS[PS.index(xps_r) + 1]
                        for ri, xp in ((0, xps_r), (1, xps_i)):
                            kb.do('pe', lambda e, ri=ri, xp=xp, q=q: e.matmul(xp.ap[:, 0:n], lhsT=bp.ap[:, ri, pr, :], rhs=u.ap[:, t0:t0 + n],
                                                                           start=True, stop=True), R=[bp, u], W=[xp])
                    _k = nxt('sb2', 4)
                    s2r, s2i = SB2R[_k], SB2I[_k]
                    if samp:
                        rhs_r, rhs_i = ssbb.ap[:, 0, pr, :], ssbb.ap[:, 1, pr, :]
                        rd = [ssbb]
                    else:
                        fin = FIN[pr % 2]
                        w = nxt('ssw', NW)
                        w1, w2, xr_, xi_, sr_, si_ = W1[w], W2[w], XR[w], XI[w], SR[w], SI[w]
                        tw = w
                        pw1, pw2 = PW1[pr % 2], PW2[pr % 2]
                        at, a2, nf, na, cs, sn = AT[tw], A2[tw], NF[tw], NA[tw], CS[tw], SN[tw]
                        rt = RT[pr]
                        if tq == 0:
                            kb.do('act', lambda e, rt=rt, q=q: e.activation(out=rt.ap, in_=IOT.ap[:, 0:TQ], func=AF.Identity, scale=0.0,
                                                                           bias=v.ap[:, V_R, q:q + 1]), R=[IOT, v], W=[rt])
                        kb.do('act', lambda e, at=at, q=q: e.activation(out=at.ap, in_=IOT.ap[:, 0:TQ], func=AF.Identity, scale=v.ap[:, V_A, q:q + 1],
                                                                       bias=THO.ap[:, tq, q:q + 1]), R=[IOT, v, THO], W=[at])
                        kb.do('dve', lambda e, at=at, a2=a2: e.tensor_scalar(out=a2.ap, in0=at.ap, scalar1=MAGIC, scalar2=None, op0=ALU.add), R=[at], W=[a2])
                        kb.do('dve', lambda e, at=at, a2=a2, nf=nf: e.scalar_tensor_tensor(out=nf.ap, in0=a2.ap, scalar=MAGIC, in1=at.ap, op0=ALU.subtract,
                                                                                       op1=ALU.subtract), R=[at, a2], W=[nf])
                        kb.do('act', lambda e, nf=nf, sn=sn: e.activation(out=sn.ap, in_=nf.ap, func=AF.Sin, scale=-TWO_PI), R=[nf], W=[sn])
                        kb.do('act', lambda e, nf=nf, na=na: e.activation(out=na.ap, in_=nf.ap, func=AF.Sin, scale=-TWO_PI / 2), R=[nf], W=[na])
                        kb.do('act', lambda e, na=na: e.activation(out=na.ap, in_=na.ap, func=AF.Square), R=[], W=[na])
                        kb.do('act', lambda e, na=na, cs=cs: e.activation(out=cs.ap, in_=na.ap, func=AF.Identity, scale=-2.0, bias=1.0), R=[na], W=[cs])
                        kb.do('dve', lambda e, cs=cs, w1=w1: e.tensor_tensor(out=w1.ap, in0=cs.ap, in1=xps_r.ap[:, 0:n], op=ALU.mult), R=[cs, xps_r], W=[w1])
                        kb.do('dve', lambda e, sn=sn, w2=w2: e.tensor_tensor(out=w2.ap, in0=sn.ap, in1=xps_i.ap[:, 0:n], op=ALU.mult), R=[sn, xps_i], W=[w2])
                        kb.do('dve', lambda e, w1=w1, w2=w2, xr_=xr_: e.tensor_tensor(out=xr_.ap, in0=w1.ap, in1=w2.ap, op=ALU.add), R=[w1, w2], W=[xr_])
                        kb.do('dve', lambda e, cs=cs, w1=w1: e.tensor_tensor(out=w1.ap, in0=cs.ap, in1=xps_i.ap[:, 0:n], op=ALU.mult), R=[cs, xps_i, xr_], W=[w1])
                        kb.do('dve', lambda e, sn=sn, w2=w2: e.tensor_tensor(out=w2.ap, in0=sn.ap, in1=xps_r.ap[:, 0:n], op=ALU.mult), R=[sn, xps_r, xr_], W=[w2])
                        kb.do('dve', lambda e, w1=w1, w2=w2, xi_=xi_: e.tensor_tensor(out=xi_.ap, in0=w1.ap, in1=w2.ap, op=ALU.subtract), R=[w1, w2], W=[xi_])
                        kb.do('dve', lambda e, rt=rt, xr_=xr_, sr_=sr_, q=q: e.tensor_tensor_scan(out=sr_.ap, data0=rt.ap, data1=xr_.ap,
                                                                                             initial=carry.ap[:, 0, q:q + 1], op0=ALU.mult, op1=ALU.add),
                              R=[rt, xr_, carry], W=[sr_])
                        kb.do('dve', lambda e, rt=rt, xi_=xi_, si_=si_, q=q: e.tensor_tensor_scan(out=si_.ap, data0=rt.ap, data1=xi_.ap,
                                                                                             initial=carry.ap[:, 1, q:q + 1], op0=ALU.mult, op1=ALU.add),
                              R=[rt, xi_, carry], W=[si_])
                        kb.do('act', lambda e, sr_=sr_, q=q: e.activation(out=carry.ap[:, 0, q:q + 1], in_=sr_.ap[:, TQ - 1:TQ], func=AF.Copy), R=[sr_], W=[carry])
                        kb.do('act', lambda e, si_=si_, q=q: e.activation(out=carry.ap[:, 1, q:q + 1], in_=si_.ap[:, TQ - 1:TQ], func=AF.Copy), R=[si_], W=[carry])
                        kb.do('dve', lambda e: e.tensor_tensor(out=w1.ap, in0=cs.ap, in1=sr_.ap, op=ALU.mult), R=[cs, sr_], W=[w1])
                        kb.do('dve', lambda e: e.tensor_tensor(out=w2.ap, in0=sn.ap, in1=si_.ap, op=ALU.mult), R=[sn, si_], W=[w2])
                        kb.do('dve', lambda e: e.tensor_tensor(out=s2r.ap, in0=w1.ap, in1=w2.ap, op=ALU.subtract), R=[w1, w2], W=[s2r])
                        if tq == NTQ - 1:
                            kb.do('dve', lambda e: e.tensor_tensor(out=fin.ap[:, 0:1], in0=w1.ap[:, TQ - 1:TQ], in1=w2.ap[:, TQ - 1:TQ],
                                                                   op=ALU.subtract), R=[w1, w2], W=[fin])
                        kb.do('pool', lambda e: e.tensor_tensor(out=pw1.ap, in0=cs.ap, in1=si_.ap, op=ALU.mult), R=[cs, si_], W=[pw1])
                        kb.do('pool', lambda e: e.tensor_tensor(out=pw2.ap, in0=sn.ap, in1=sr_.ap, op=ALU.mult), R=[sn, sr_], W=[pw2])
                        kb.do('pool', lambda e: e.tensor_tensor(out=s2i.ap, in0=pw1.ap, in1=pw2.ap, op=ALU.add), R=[pw1, pw2], W=[s2i])
                        if tq == NTQ - 1:
                            fr, fi = v.ap[:, V_FR, q:q + 1], v.ap[:, V_FI, q:q + 1]
                            kb.do('dve', lambda e, w1=pw1, w2=pw2: e.tensor_tensor(out=fin.ap[:, 1:2], in0=w1.ap[:, TQ - 1:TQ], in1=w2.ap[:, TQ - 1:TQ],
                                                                               op=ALU.add), R=[pw1, pw2], W=[fin])
                            kb.do('dve', lambda e, fi=fi: e.tensor_scalar(out=fin.ap[:, 2:3], in0=fin.ap[:, 1:2], scalar1=fi, scalar2=None, op0=ALU.mult), R=[v], W=[fin])
                            kb.do('dve', lambda e, fr=fr, q=q: e.scalar_tensor_tensor(out=sto.ap[:, 0, q, 0:1], in0=fin.ap[:, 0:1], scalar=fr, in1=fin.ap[:, 2:3],
                                                                                   op0=ALU.mult, op1=ALU.subtract), R=[v, fin], W=[sto])
                            kb.do('dve', lambda e, fr=fr: e.tensor_scalar(out=fin.ap[:, 2:3], in0=fin.ap[:, 1:2], scalar1=fr, scalar2=None, op0=ALU.mult), R=[v], W=[fin])
                            kb.do('dve', lambda e, fi=fi, q=q: e.scalar_tensor_tensor(out=sto.ap[:, 1, q, 0:1], in0=fin.ap[:, 0:1], scalar=fi, in1=fin.ap[:, 2:3],
                                                                                   op0=ALU.mult, op1=ALU.add), R=[v, fin], W=[sto])
                        rhs_r, rhs_i = s2r.ap, s2i.ap
                        rd = [s2r, s2i]
                    cw_ = craw if samp else cf_
                    kb.do('pe', lambda e, pr=pr, rhs_r=rhs_r, ypb=ypb: e.matmul(ypb.ap[:, 0:n], lhsT=cw_.ap[:, 0, pr, :], rhs=rhs_r,
                                                                              start=(pr == 0), stop=False), R=[cw_] + rd, W=[ypb])
                    kb.do('pe', lambda e, pr=pr, rhs_i=rhs_i, ypb=ypb: e.matmul(ypb.ap[:, 0:n], lhsT=cw_.ap[:, 1, pr, :], rhs=rhs_i,
                                                                              start=False, stop=(pr == 3)), R=[cw_] + rd, W=[ypb])
                    if not samp:
                        pend.append(kb.end_buffer())
                        if len(pend) == 2:
                            kb.flush_interleaved([x[:-2] for x in pend])
                            run_deferred()
                            deferred.append([x[-2:] for x in pend])
                            pend = []
                if not samp:
                    kb.begin_buffer()
                kb.do('dve', lambda e, ypb=ypb: e.scalar_tensor_tensor(out=EY.ap[:, t0:t0 + n], in0=u.ap[:, t0:t0 + n], scalar=dc.ap[:, kt:kt + 1],
                                                                     in1=ypb.ap[:, 0:n], op0=ALU.mult, op1=ALU.add), R=[u, dc, ypb], W=[EY])
                if not samp:
                    deferred.append([kb.end_buffer()])
            run_deferred()
            pieces = [(c0_, min(512, T - c0_)) for c0_ in range(0, T, 512)]
            for (c0_, n_) in pieces:
                e2, gb = E2[0], GB[0]
                kb.do('act', lambda e: e.activation(out=e2.ap[:, 0:n_], in_=EY.ap[:, c0_:c0_ + n_], func=AF.Square), R=[EY], W=[e2])
                kb.do('dve', lambda e: e.tensor_scalar(out=e2.ap[:, 0:n_], in0=e2.ap[:, 0:n_], scalar1=0.044715, scalar2=1.0, op0=ALU.mult, op1=ALU.add),
                      R=[], W=[e2])
                kb.do('dve', lambda e: e.tensor_tensor(out=e2.ap[:, 0:n_], in0=e2.ap[:, 0:n_], in1=EY.ap[:, c0_:c0_ + n_], op=ALU.mult), R=[EY], W=[e2])
                kb.do('act', lambda e: e.activation(out=e2.ap[:, 0:n_], in_=e2.ap[:, 0:n_], func=AF.Sigmoid, scale=GELU_C), R=[], W=[e2])
                kb.do('dve', lambda e: e.tensor_tensor(out=gb.ap[:, 0:n_], in0=e2.ap[:, 0:n_], in1=EY.ap[:, c0_:c0_ + n_], op=ALU.mult), R=[EY, e2], W=[gb])
                z1, z2 = PS[6], PS[7]
                kb.do('pe', lambda e: e.matmul(z1.ap[:, 0:n_], lhsT=wg.ap[:, kt, 0, :], rhs=gb.ap[:, 0:n_], start=True, stop=True), R=[wg, gb], W=[z1])
                kb.do('pe', lambda e: e.matmul(z2.ap[:, 0:n_], lhsT=wg.ap[:, kt, 1, :], rhs=gb.ap[:, 0:n_], start=True, stop=True), R=[wg, gb], W=[z2])
                kb.do('act', lambda e: e.activation(out=e2.ap[:, 0:n_], in_=z2.ap[:, 0:n_], func=AF.Sigmoid), R=[z2], W=[e2])
                kb.do('dve', lambda e: e.tensor_tensor(out=ys.ap[:, c0_:c0_ + n_], in0=e2.ap[:, 0:n_], in1=z1.ap[:, 0:n_], op=ALU.mult), R=[e2, z1], W=[ys])
            kb.dma('sp', SSd[kt * 128:(kt + 1) * 128, :], ys.ap, R=[ys], W=[ssB])
        outtoks.append(kb.dma('sp', st_out[l], sto.ap, R=[sto]))

    def evac_store(dst_d, row0_of, dt, scale_of=None, also_f32=None):
        stg = [ar.alloc([128, T], dt) for _ in range(2)]
        stf = [ar.alloc([128, T]) for _ in range(2)] if also_f32 is not None else None
        dB = Buf(dst_d)
        nch = len(chunks)

        def ep(mi, ci, t0, n, pb, pb2):
            s = stg[mi % 2]
            sc = 1.0 if scale_of is None else scale_of(mi)
            f32r = also_f32(mi) if also_f32 is not None else None
            if f32r is None:
                kb.do('act', lambda e: e.activation(out=s.ap[:, t0:t0 + n], in_=pb.ap[:, 0:n], func=AF.Copy, scale=sc), R=[pb], W=[s])
            else:
                sf = stf[mi % 2]
                kb.do('act', lambda e: e.activation(out=sf.ap[:, t0:t0 + n], in_=pb.ap[:, 0:n], func=AF.Copy), R=[pb], W=[sf])
                kb.do('dve', lambda e: e.tensor_scalar(out=s.ap[:, t0:t0 + n], in0=sf.ap[:, t0:t0 + n], scalar1=sc, scalar2=None, op0=ALU.mult),
                      R=[sf], W=[s])
            if ci == nch - 1:
                r0 = row0_of(mi)
                kb.dma('sp', dst_d[r0:r0 + 128, :], s.ap, R=[s], W=[dB])
                if f32r is not None:
                    dd, rr, nr = f32r
                    kb.dma('sp', dd[rr:rr + nr, :], stf[mi % 2].ap[0:nr, :], R=[stf[mi % 2]], W=[Buf(dd)])
        return ep

    def _layer(l):
            norm_phase(Xd, KTD, gains[l, 0], out='norm')
            new_phase()
            nmt = INW // 128

            if KW == 64:
                f32sel = lambda mi: (ZFd, 0, 128) if mi * 128 == AW else None
            else:
                f32sel = lambda mi: (ZFd, mi * 128 - AW, 128) if AW <= mi * 128 < AW + 2 * KW else None
            ep = evac_store(ZBd, lambda mi: mi * 128, BF16, scale_of=lambda mi: 0.125 if mi * 128 < AW else 1.0, also_f32=f32sel)
            go, _ = linear_phase(SRC_D, KTD, w_in[l], [m * 128 for m in range(nmt)], ep, tchunks=full_chunks, fresh=False)
            for mi in range(min(nmt, int(_os.environ.get('LIMIT_MT', '999')))):
                go(mi)
            kb.barrier()
            outtoks.append(kb.dma('sp', kvp_out[l], ZFd[:, SEQ - 128:SEQ]))
            import os
            if not os.environ.get('SKIP_ATT'):
                attention_phase(l)
            if not os.environ.get('SKIP_SSM'):
                ssm_phase(l)
            norm_phase(SSd, NKS, gssm_in[l], out='norm', dst_kt0=KTA)
            new_phase()
            ep = evac_store(Od, lambda mi: mi * 128, F32)
            go, _ = linear_phase(SRC_D, KTD, w_out[l], [m * 128 for m in range(KTD)], ep, tchunks=full_chunks, fresh=False)
            for mi in range(KTD):
                go(mi)
            norm_phase(Od, KTD, gains[l, 1], resid_d=Xd, out='norm', g2_ap=gains[l, 2])
            new_phase()
            stg = [ar.alloc([128, T], BF16) for _ in range(2)]
            sil = [ar.alloc([128, NMAX]) for _ in range(3)]
            aB = Buf(ACTd)

            def ep_gu(mi, ci, t0, n, pb, pb2):
                s = stg[mi % 2]
                sl = sil[nxt('sil', 3)]
                kb.do('act', lambda e: e.activation(out=sl.ap[:, 0:n], in_=pb.ap[:, 0:n], func=AF.Silu), R=[pb], W=[sl])
                kb.do('dve', lambda e: e.tensor_tensor(out=s.ap[:, t0:t0 + n], in0=sl.ap[:, 0:n], in1=pb2.ap[:, 0:n], op=ALU.mult), R=[sl, pb2], W=[s])
                if ci == len(chunks) - 1:
                    kb.dma('sp', ACTd[mi * 128:(mi + 1) * 128, :], s.ap, R=[s], W=[aB])
            go, _ = linear_phase(SRC_D, KTD, w_gu[l], [m * 128 for m in range(KTF)], ep_gu, W2cols=[DFF + m * 128 for m in range(KTF)],
                                 tchunks=full_chunks, fresh=False)
            for mi in range(KTF):
                go(mi)
            half = T // 2
            for hf in range(2):
                new_phase()
                SRC_F = src3(KTF, half)
                kb.dma('sp', SRC_F, ACTd[:, hf * half:(hf + 1) * half].rearrange("(k p) t -> p k t", p=128), W=[srcbuf])
                stg2 = [ar.alloc([128, half]) for _ in range(2)]
                oB = Buf(Od)
                hch = [(t0, n, t0 - hf * half) for (t0, n) in chunks[3 * hf:3 * hf + 3]]

                def ep_dn(mi, ci, t0, n, pb, pb2, stg2=stg2, oB=oB, hf=hf):
                    s = stg2[mi % 2]
                    kb.do('act', lambda e: e.activation(out=s.ap[:, t0 - hf * half:t0 - hf * half + n], in_=pb.ap[:, 0:n], func=AF.Copy), R=[pb], W=[s])
                    if ci == 2:
                        kb.dma('sp', Od[mi * 128:(mi + 1) * 128, hf * half:(hf + 1) * half], s.ap, R=[s], W=[oB])
                go, _ = linear_phase(SRC_F, KTF, w_dn[l], [m * 128 for m in range(KTD)], ep_dn, tchunks=hch, fresh=False)
                for mi in range(KTD):
                    go(mi)
            norm_phase(Od, KTD, gains[l, 3], resid_d=Xd, out='cast')
            new_phase()
            peb = ar.alloc([128, 2, T], BF16)
            for k2 in range(2):
                kb.dma('pool', peb.ap[:, k2, :], peT_in[l, k2 * 128:(k2 + 1) * 128, :], W=[peb], max_dma_last_dim=4096)
            wpp = [ar.alloc([128, 2, 128], BF16) for _ in range(2)]
            xrow = [ar.alloc([128, T]) for _ in range(2)]
            sg = [ar.alloc([128, NMAX]) for _ in range(3)]
            wppv = w_pp[l].rearrange("(k p) n -> p k n", p=128)
            xB = Buf(Xd)

            def ep_ple(mi, ci, t0, n, pb, pb2):
                xr_ = xrow[mi % 2]
                wp_ = wpp[mi % 2]
                if ci == 0:
                    kb.dma('pool', wp_.ap, wppv[:, :, mi * 128:(mi + 1) * 128], W=[wp_])
                    kb.dma('sp', xr_.ap, Xd[mi * 128:(mi + 1) * 128, :], R=[xB], W=[xr_])
                pj = PS[6 + nxt('pj', 2)]
                for kt in range(2):
                    kb.do('pe', lambda e, kt=kt: e.matmul(pj.ap[:, 0:n], lhsT=wp_.ap[:, kt, :], rhs=peb.ap[:, kt, t0:t0 + n], start=(kt == 0), stop=(kt == 1)),
                          R=[wp_, peb], W=[pj])
                s_ = sg[nxt('sg', 3)]
                kb.do('act', lambda e: e.activation(out=s_.ap[:, 0:n], in_=pb.ap[:, 0:n], func=AF.Sigmoid), R=[pb], W=[s_])
                kb.do('dve', lambda e: e.tensor_tensor(out=s_.ap[:, 0:n], in0=s_.ap[:, 0:n], in1=pj.ap[:, 0:n], op=ALU.mult), R=[pj], W=[s_])
                kb.do('dve', lambda e: e.tensor_tensor(out=xr_.ap[:, t0:t0 + n], in0=xr_.ap[:, t0:t0 + n], in1=s_.ap[:, 0:n], op=ALU.add), R=[s_], W=[xr_])
                if ci == len(chunks) - 1:
                    outtoks.append(kb.dma('sp', Xd[mi * 128:(mi + 1) * 128, :], xr_.ap, R=[xr_], W=[xB]))
            go, _ = linear_phase(SRC_D, KTD, w_pg[l], [m * 128 for m in range(KTD)], ep_ple, tchunks=full_chunks, fresh=False)
            for mi in range(KTD):
                go(mi)


    try:
        for l in range(DEPTH):
            _layer(l)
    except _Stop:
        print('stopped after phase', _stop_after)

    kb.barrier()
    for r in range(KTD):
        outtoks.append(kb.dma('sp', yT_out[r * 128:(r + 1) * 128, :], Xd[r * 128:(r + 1) * 128, :]))
    kb.barrier()
    if _os.environ.get('PHASE_LOG'):
        import json as _json
        _json.dump(_marks, open(_os.environ['PHASE_LOG'], 'w'))
    kb.emit()
    return nc


def _consts():
    c = np.zeros((128, 1152), np.float32)
    c[:, 0:128] = np.eye(128, dtype=np.float32)
    i = np.arange(128)[:, None]
    j = np.arange(256)[None, :]
    dist = (i - j + 128).astype(np.float32)
    valid = (dist >= 0) & (dist <= 128)
    dm = np.where(valid, dist, np.float32(BIG)).astype(np.float32)
    c[:, 128:384] = dm
    d0 = dm.copy()
    d0[:, 0:128] = BIG
    c[:, 384:640] = d0
    c[:, 640:1152] = np.arange(1, 513, dtype=np.float32)[None, :]
    return c


def prepare_inputs(C, inp):
    f = np.float32
    A = lambda k: np.asarray(inp[k], f)
    L = C.DEPTH
    NP_, NKS = C.NPAIR, C.NKS
    gains = np.stack([A(k) for k in ('g_pre_mix', 'g_post_mix', 'g_pre_ffn', 'g_post_ffn')], axis=1)
    gains = np.ascontiguousarray(gains.reshape(L, 4, C.KTD, 128).transpose(0, 1, 3, 2))
    gssm = np.ascontiguousarray(A('g_ssm_out').reshape(L, NKS, 128).transpose(0, 2, 1))
    gattn = np.ascontiguousarray(np.broadcast_to(A('g_attn_out')[:, None, :], (L, 128, C.AW)))
    gattnT = np.ascontiguousarray(A('g_attn_out').reshape(L, C.NH, 64).transpose(0, 2, 1))
    sinks = np.ascontiguousarray(np.broadcast_to(A('attn_sinks')[:, None, :], (L, 128, C.NH)))
    are = A('ssm_a_re').reshape(L, NP_, 2, 64).transpose(0, 2, 3, 1).reshape(L, 128, NP_)
    aim = A('ssm_a_im').reshape(L, NP_, 2, 64).transpose(0, 2, 3, 1).reshape(L, 128, NP_)
    ldt = np.broadcast_to(A('ssm_log_dt').reshape(L, NP_, 2, 1), (L, NP_, 2, 64)).transpose(0, 2, 3, 1).reshape(L, 128, NP_)
    ssm_ps = np.ascontiguousarray(np.stack([are, aim, ldt], axis=2))
    bpad = np.zeros((L, 128, 2, NP_, 128), f)
    for ri, key in enumerate(('ssm_b_re', 'ssm_b_im')):
        b = A(key)
        for g in range(C.NG):
            q, gh, g8 = g // 2, g % 2, g % 8
            bpad[:, g8 * 16:(g8 + 1) * 16, ri, q, gh * 64:(gh + 1) * 64] = b[:, g].transpose(0, 2, 1)
    cpad = np.zeros((L, 128, NKS, 2, 4, 128), f)
    for ri, key in enumerate(('ssm_c_re', 'ssm_c_im')):
        c = A(key)
        for g in range(C.NG):
            kt, pr, gh, g8 = g // 8, (g % 8) // 2, g % 2, g % 8
            cpad[:, gh * 64:(gh + 1) * 64, kt, ri, pr, g8 * 16:(g8 + 1) * 16] = c[:, g].transpose(0, 2, 1)
    dcol = np.ascontiguousarray(A('ssm_d').reshape(L, NKS, 128).transpose(0, 2, 1))
    wglu = np.zeros((L, 128, NKS, 2, 128), f)
    wg = A('ssm_w_glu')
    for g in range(C.NG):
        kt, g8 = g // 8, g % 8
        for hf in range(2):
            wglu[:, g8 * 16:(g8 + 1) * 16, kt, hf, g8 * 16:(g8 + 1) * 16] = wg[:, g, :, hf * 16:(hf + 1) * 16]
    shared = dict(
        w_in=A('w_in'), w_out=A('w_out'), w_gate_up=A('w_gate_up'), w_down=A('w_down'), w_ple_gate=A('w_ple_gate'),
        w_ple_proj=A('w_ple_proj'), gains=gains, gssm=gssm, gattn=gattn, gattnT=gattnT, sinks=sinks, ssm_ps=ssm_ps,
        bpad=bpad, cpad=cpad, dcol=dcol, wglu=wglu, consts=_consts())
    xp, xs = A('x_prompt'), A('x_sample')
    pp, psm = A('p_prompt'), A('p_sample')
    ck, cv = A('cache_k'), A('cache_v')
    sre, sim = A('state_ssm_re'), A('state_ssm_im')
    in_maps = []
    NS = C.NS
    for c in range(8):
        b = c % C.BATCH
        sl = slice(NS * b, NS * (b + 1))
        m = dict(shared)
        m['xT'] = np.ascontiguousarray(np.concatenate([xp[b].T, xs[sl, 0].T], axis=1))
        m['peT'] = np.ascontiguousarray(np.concatenate([pp[:, b].transpose(0, 2, 1), psm[:, sl, 0].transpose(0, 2, 1)], axis=2))
        ckc = ck[:, sl].reshape(L, NS, 128, C.KW)
        m['ck'] = np.ascontiguousarray(ckc)
        m['cv'] = np.ascontiguousarray(cv[:, sl].reshape(L, NS, 128, C.KW))
        m['ckT'] = np.ascontiguousarray(ckc.reshape(L, NS, 128, C.NKV, 64).transpose(0, 1, 3, 4, 2))
        st = np.stack([sre[:, sl], sim[:, sl]], axis=1)
        st = st.reshape(L, 2, NS, NP_, 2, 64).transpose(0, 4, 5, 1, 3, 2).reshape(L, 128, 2, NP_, NS)
        m['st0'] = np.ascontiguousarray(st)
        in_maps.append(m)
    return in_maps


def assemble(C, results):
    f = np.float32
    L, NS, B = C.DEPTH, C.NS, C.BATCH
    yp = np.zeros((B, C.SEQ, C.D), f)
    ys = np.zeros((C.DEC, 1, C.D), f)
    kp = np.zeros((L, B, 128, C.NKV, 64), f)
    vp = np.zeros_like(kp)
    srp = np.zeros((L, B, C.NG, 64), f)
    sip = np.zeros_like(srp)
    ksm = np.zeros((L, C.DEC, 128, C.NKV, 64), f)
    vsm = np.zeros_like(ksm)
    srs = np.zeros((L, C.DEC, C.NG, 64), f)
    sis = np.zeros_like(srs)
    for b in range(B):
        r = results[b]
        sl = slice(NS * b, NS * (b + 1))
        yT = r['yT']
        yp[b] = yT[:, :C.SEQ].T
        ys[sl, 0] = yT[:, C.SEQ:].T
        kv = r['kvp']
        kp[:, b] = kv[:, :C.KW].transpose(0, 2, 1).reshape(L, 128, C.NKV, 64)
        vp[:, b] = kv[:, C.KW:].transpose(0, 2, 1).reshape(L, 128, C.NKV, 64)
        ksm[:, sl] = r['ks'].reshape(L, NS, 128, C.NKV, 64)
        vsm[:, sl] = r['vs'].reshape(L, NS, 128, C.NKV, 64)
        st = r['st'].reshape(L, 2, 64, 2, C.NPAIR, 1 + NS)
        st = st.transpose(0, 3, 5, 4, 1, 2).reshape(L, 2, 1 + NS, C.NG, 64)
        srp[:, b], sip[:, b] = st[:, 0, 0], st[:, 1, 0]
        srs[:, sl], sis[:, sl] = st[:, 0, 1:], st[:, 1, 1:]
    return (yp, ys, kp, vp, srp, sip, ksm, vsm, srs, sis)


_CACHE = {}


def run(C, inp):
    key = (C.D, C.SEQ, C.DEPTH, C.BATCH, C.DEC)
    if key not in _CACHE:
        _CACHE[key] = build(C)
    in_maps = prepare_inputs(C, inp)
    res = run_bass_kernel_spmd(_CACHE[key], in_maps, core_ids=list(range(8)))
    return assemble(C, res.results)


def kernel(**inputs):
    return run(Cfg(), inputs)
```

```python
import math
import numpy as np
import concourse.bass as bass
import concourse.mybir as mybir
from concourse.bass_utils import run_bass_kernel_spmd

F32 = mybir.dt.float32
BF16 = mybir.dt.bfloat16
I32 = mybir.dt.int32
AF = mybir.ActivationFunctionType
ALU = mybir.AluOpType
AX = mybir.AxisListType

ENGS = ('pe', 'act', 'dve', 'pool', 'sp')
SEM_ROT = 20000
NDMASEM = 56
EPS = 1e-6
BIG = 1.0e9
TWO_PI = 6.283184
GELU_C = 1.5957691216057308
MAGIC = 12582912.0


class Cfg:
    def __init__(self, D=2048, SEQ=2048, DEPTH=4, BATCH=4, DEC=32):
        self.D, self.SEQ, self.DEPTH, self.BATCH, self.DEC = D, SEQ, DEPTH, BATCH, DEC
        self.NS = DEC // BATCH
        self.T = SEQ + self.NS
        self.AW = D // 2
        self.NH = self.AW // 64
        self.NKV = max(1, self.NH // 8)
        self.GRP = self.NH // self.NKV
        self.KW = self.NKV * 64
        self.SW = D - self.AW
        self.NG = self.SW // 16
        self.NKS = self.SW // 128
        self.NPAIR = self.NG // 2
        self.INW = self.AW + 2 * self.KW + self.SW
        self.DFF = -(-8 * D // (3 * 256)) * 256
        self.KTD = D // 128
        self.KTA = self.AW // 128
        self.KTF = self.DFF // 128
        self.NQB = SEQ // 128
        half = self.T // 2
        assert self.T % 2 == 0
        a = -(-half // 3)
        ch = []
        for h0 in (0, half):
            o = h0
            for i in range(3):
                n = min(a, h0 + half - o)
                ch.append((o, n))
                o += n
        self.chunks = ch
        self.NMAX = a
        assert a <= 512
        self.TQ = 256 if SEQ >= 256 else SEQ
        self.slopes = [2.0 ** (-8.0 * (h + 1) / self.NH) for h in range(self.NH)]


class Buf:
    def __init__(self, ap, excl=False):
        self.ap = ap
        self.w = {}
        self.r = {}
        self.excl = excl


def _upd(d, tok):
    k = tok[0].num
    if k not in d or d[k][1] < tok[1]:
        d[k] = tok


class _Rec:
    def __init__(self):
        self.call = None

    def __getattr__(self, name):
        def f(*a, **k):
            assert self.call is None
            self.call = (name, a, k)
            return self
        return f


class KB:
    def __init__(self, nc):
        self.nc = nc
        self.ops = {e: [] for e in ENGS}
        self.cur = {}
        self.nsem = 0
        self.waited = {}
        self.ndma = 0
        self.dpool = []
        for e in ENGS:
            self._new_sem(e)
        for i in range(NDMASEM):
            self.dpool.append([nc.alloc_semaphore("dq%d" % i), 0])

    def _new_sem(self, e):
        self.nsem += 1
        self.cur[e] = [self.nc.alloc_semaphore("p_%s_%d" % (e, self.nsem)), 0]
        if not hasattr(self, 'owner'):
            self.owner = {}
        self.owner[self.cur[e][0].num] = e

    def _waits(self, eng, deps):
        waits = []
        for d in deps:
            sem, val = d
            if eng == 'pe' and self.owner.get(sem.num) == 'pe':
                continue
            key = (eng, sem.num)
            if self.waited.get(key, 0) >= val:
                continue
            self.waited[key] = val
            waits.append((sem, val))
        return waits

    def _deps(self, R, W):
        deps = []
        for b in R:
            deps.extend(b.w.values())
            if b.excl:
                deps.extend(b.r.values())
        for b in W:
            deps.extend(b.w.values())
            deps.extend(b.r.values())
        return deps

    def _record(self, tok, R, W):
        for b in R:
            _upd(b.r, tok)
        for b in W:
            _upd(b.w, tok)

    def do(self, eng, fn, R=(), W=()):
        rec = _Rec()
        fn(rec)
        if getattr(self, 'buf', None) is not None:
            self.buf.append(('do', eng, rec.call, list(R), list(W)))
            return None
        return self._do(eng, rec.call, R, W)

    def begin_buffer(self):
        self.buf = []

    def mark(self, name):
        if getattr(self, 'buf', None) is not None:
            self.buf.append(('mark', name, None, None, None))

    def end_buffer(self):
        b, self.buf = self.buf, None
        return b

    def flush_interleaved(self, lists):
        m = max(len(x) for x in lists)
        for k in range(m):
            for x in lists:
                if k < len(x):
                    kind, eng, call, R, W = x[k]
                    if kind == 'do':
                        self._do(eng, call, R, W)

    def _do(self, eng, call, R=(), W=()):
        waits = self._waits(eng, self._deps(R, W))
        if self.cur[eng][1] >= SEM_ROT:
            self._new_sem(eng)
        c = self.cur[eng]
        c[1] += 1
        name, a, k = call
        self.ops[eng].append((waits, (lambda e, name=name, a=a, k=k: getattr(e, name)(*a, **k)), c[0], 1))
        tok = (c[0], c[1])
        self._record(tok, R, W)
        return tok

    def dma(self, eng, out, in_, R=(), W=(), **kw):
        waits = self._waits(eng, self._deps(R, W))
        ds = self.dpool[self.ndma % NDMASEM]
        self.ndma += 1
        ds[1] += 16
        self.ops[eng].append((waits, lambda e: e.dma_start(out=out, in_=in_, **kw), ds[0], 16))
        tok = (ds[0], ds[1])
        self._record(tok, R, W)
        return tok

    def wait_only(self, eng, deps):
        waits = self._waits(eng, deps)
        if waits:
            self.ops[eng].append((waits, None, None, 0))

    def barrier(self):
        toks = [(c[0], c[1]) for c in self.cur.values() if c[1] > 0]
        toks += [(d[0], d[1]) for d in self.dpool if d[1] > 0]
        for e in ENGS:
            self.wait_only(e, toks)

    def emit(self):
        nc = self.nc
        with nc.Block() as block:
            def run(name):
                def f(e):
                    for waits, fn, sem, inc in self.ops[name]:
                        for (s, v) in waits:
                            e.wait_ge(s, v)
                        if fn is not None:
                            fn(e).then_inc(sem, inc)
                return f
            block.tensor(run('pe'))
            block.scalar(run('act'))
            block.vector(run('dve'))
            block.gpsimd(run('pool'))
            block.sync(run('sp'))


class Arena:
    LO = 16512
    HI = 229376

    def __init__(self, nc):
        self.nc = nc
        self.cur = self.LO
        self.n = 0

    def alloc(self, shape, dt=F32):
        esz = 2 if dt == BF16 else 4
        nb = esz
        for s in shape[1:]:
            nb *= s
        nb = (nb + 63) // 64 * 64
        assert self.cur + nb <= self.HI, ("SBUF overflow", self.cur, nb)
        self.n += 1
        t = self.nc.alloc_sbuf_tensor_at("sb%d" % self.n, list(shape), dt, offset=self.cur)
        self.cur += nb
        return Buf(t.ap())


def build(cfg):
    C = cfg
    nc = bass.Bass("TRN2", target_bir_lowering=False)
    kb = KB(nc)
    ar = Arena(nc)
    D, T, SEQ, NS, DEPTH = C.D, C.T, C.SEQ, C.NS, C.DEPTH
    AW, NH, NKV, KW, SW, NKS, NPAIR, INW, DFF = C.AW, C.NH, C.NKV, C.KW, C.SW, C.NKS, C.NPAIR, C.INW, C.DFF
    KTD, KTA, KTF, NQB = C.KTD, C.KTA, C.KTF, C.NQB
    chunks, NMAX, TQ = C.chunks, C.NMAX, C.TQ
    KTMAX = max(KTF, KTD)

    def din(name, shape):
        return nc.dram_tensor(name, list(shape), F32, kind="ExternalInput").ap()

    def dout(name, shape):
        return nc.dram_tensor(name, list(shape), F32, kind="ExternalOutput").ap()

    def dscr(name, shape, dt=F32):
        return nc.dram_tensor(name, list(shape), dt).ap()

    xT_in = din("xT", [D, T])
    peT_in = din("peT", [DEPTH, 256, T])
    w_in = din("w_in", [DEPTH, D, INW])
    w_out = din("w_out", [DEPTH, D, D])
    w_gu = din("w_gate_up", [DEPTH, D, 2 * DFF])
    w_dn = din("w_down", [DEPTH, DFF, D])
    w_pg = din("w_ple_gate", [DEPTH, D, D])
    w_pp = din("w_ple_proj", [DEPTH, 256, D])
    gains = din("gains", [DEPTH, 4, 128, KTD])
    gssm_in = din("gssm", [DEPTH, 128, NKS])
    gattn_in = din("gattn", [DEPTH, 128, AW])
    gattnT_in = din("gattnT", [DEPTH, 64, NH])
    sinks_in = din("sinks", [DEPTH, 128, NH])
    ckT_in = din("ckT", [DEPTH, NS, NKV, 64, 128])
    ck_in = din("ck", [DEPTH, NS, 128, KW])
    cv_in = din("cv", [DEPTH, NS, 128, KW])
    ssm_ps_in = din("ssm_ps", [DEPTH, 128, 3, NPAIR])
    bpad_in = din("bpad", [DEPTH, 128, 2, NPAIR, 128])
    cpad_in = din("cpad", [DEPTH, 128, NKS, 2, 4, 128])
    dcol_in = din("dcol", [DEPTH, 128, NKS])
    wglu_in = din("wglu", [DEPTH, 128, NKS, 2, 128])
    st0_in = din("st0", [DEPTH, 128, 2, NPAIR, NS])
    consts_in = din("consts", [128, 128 + 256 + 256 + 512])

    yT_out = dout("yT", [D, T])
    kvp_out = dout("kvp", [DEPTH, 2 * KW, 128])
    ks_out = dout("ks", [DEPTH, NS, 128, KW])
    vs_out = dout("vs", [DEPTH, NS, 128, KW])
    st_out = dout("st", [DEPTH, 128, 2, NPAIR, 1 + NS])

    Xd = dscr("X", [D, T])
    ZBd = dscr("ZB", [INW, T], BF16)
    ZFd = dscr("ZF", [2 * KW, T])
    Od = dscr("O", [D, T])
    SSd = dscr("SS", [SW, T])
    ACTd = dscr("ACTs", [DFF, T], BF16)
    MGSd = dscr("MGS", [AW, NS], BF16)

    ident = ar.alloc([128, 128], BF16)
    ones = ar.alloc([128, 128], BF16)
    DIST = ar.alloc([128, 256])
    DIST0 = ar.alloc([128, 256])
    IOT = ar.alloc([128, 512])
    srcbuf = ar.alloc([128, max(KTD * T, KTF * (T // 2))], BF16)
    phase_mark = ar.cur

    PS = [Buf(nc.alloc_psum_tensor("ps%d" % i, [128, 512], F32).ap(), excl=True) for i in range(8)]

    def src3(kt_n, tn):
        return srcbuf.ap[:, 0:kt_n * tn].rearrange("p (k t) -> p k t", t=tn)

    SRC_D = src3(KTD, T)

    outtoks = []

    class _Stop(Exception):
        pass
    import os as _os
    _stop_after = int(_os.environ.get('STOP_AFTER', '100000'))
    _pc = [0]

    _marks = []

    def new_phase(name=None):
        import inspect
        if name is None:
            name = inspect.stack()[1].function + ':' + str(inspect.stack()[1].lineno)
        _marks.append((name, sum(1 for o in kb.ops['pe'] if o[1] is not None)))
        _pc[0] += 1
        if _pc[0] > _stop_after:
            raise _Stop()
        kb.barrier()
        ar.cur = phase_mark

    kb.dma('pool', ident.ap, consts_in[:, 0:128], W=[ident])
    kb.dma('sp', DIST.ap, consts_in[:, 128:384], W=[DIST])
    kb.dma('sp', DIST0.ap, consts_in[:, 384:640], W=[DIST0])
    kb.dma('sp', IOT.ap, consts_in[:, 640:1152], W=[IOT])
    kb.do('pool', lambda e: e.memset(ones.ap, 1.0), W=[ones])
    xw = Buf(Xd)
    for r in range(KTD):
        kb.dma('sp', Xd[r * 128:(r + 1) * 128, :], xT_in[r * 128:(r + 1) * 128, :], W=[xw])

    rot = {}

    def nxt(key, n):
        rot[key] = (rot.get(key, -1) + 1) % n
        return rot[key]

    def norm_phase(src_d, KT, g1_ap, resid_d=None, out=None, g2_ap=None, dst_kt0=0, Fdim=None):
        new_phase()
        Fdim = KT * 128
        g1 = ar.alloc([128, KT])
        kb.dma('sp', g1.ap, g1_ap, W=[g1])
        g2 = None
        if g2_ap is not None:
            g2 = ar.alloc([128, KT])
            kb.dma('sp', g2.ap, g2_ap, W=[g2])
        xin = [ar.alloc([128, KT, NMAX]) for _ in range(2)]
        xr = [ar.alloc([128, KT, NMAX]) for _ in range(2)] if resid_d is not None else None
        sq = [ar.alloc([128, NMAX], BF16) for _ in range(3)]
        rs = [ar.alloc([128, NMAX]) for _ in range(2)]
        tmp = [ar.alloc([128, NMAX]) for _ in range(3)]
        sv = src_d.rearrange("(k p) t -> p k t", p=128)
        xv = resid_d.rearrange("(k p) t -> p k t", p=128) if resid_d is not None else None
        srcB = Buf(src_d)
        psA, psB = PS[6], PS[7]

        def rstd_of(buf_in, n, psb, rsb):
            for kt in range(KT):
                s = sq[nxt('sq', 3)]
                kb.do('act', lambda e, kt=kt, s=s: e.activation(out=s.ap[:, 0:n], in_=buf_in.ap[:, kt, 0:n], func=AF.Square),
                      R=[buf_in], W=[s])
                kb.do('pe', lambda e, kt=kt, s=s: e.matmul(psb.ap[:, 0:n], lhsT=ones.ap, rhs=s.ap[:, 0:n],
                                                           start=(kt == 0), stop=(kt == KT - 1)), R=[s, ones], W=[psb])
            kb.do('dve', lambda e: e.tensor_scalar(out=rsb.ap[:, 0:n], in0=psb.ap[:, 0:n], scalar1=1.0 / Fdim, scalar2=EPS,
                                                   op0=ALU.mult, op1=ALU.add), R=[psb], W=[rsb])
            kb.do('act', lambda e: e.activation(out=rsb.ap[:, 0:n], in_=rsb.ap[:, 0:n], func=AF.Ln), R=[rsb], W=[rsb])
            kb.do('act', lambda e: e.activation(out=rsb.ap[:, 0:n], in_=rsb.ap[:, 0:n], func=AF.Exp, scale=-0.5), R=[rsb], W=[rsb])

        for ci, (t0, n) in enumerate(chunks):
            xi = xin[ci % 2]
            kb.dma('sp', xi.ap[:, :, 0:n], sv[:, :, t0:t0 + n], R=[srcB], W=[xi])
            r1 = rs[0]
            rstd_of(xi, n, psA, r1)
            if resid_d is None:
                for kt in range(KT):
                    kb.do('dve', lambda e, kt=kt: e.scalar_tensor_tensor(
                        out=SRC_D[:, dst_kt0 + kt, t0:t0 + n], in0=xi.ap[:, kt, 0:n], scalar=g1.ap[:, kt:kt + 1],
                        in1=r1.ap[:, 0:n], op0=ALU.mult, op1=ALU.mult), R=[xi, g1, r1], W=[srcbuf])
                continue
            xx = xr[ci % 2]
            xB = Buf(resid_d)
            kb.dma('sp', xx.ap[:, :, 0:n], xv[:, :, t0:t0 + n], R=[xB], W=[xx])
            for kt in range(KT):
                tb = tmp[nxt('tmp', 3)]
                kb.do('dve', lambda e, kt=kt, tb=tb: e.scalar_tensor_tensor(
                    out=tb.ap[:, 0:n], in0=xi.ap[:, kt, 0:n], scalar=g1.ap[:, kt:kt + 1], in1=r1.ap[:, 0:n],
                    op0=ALU.mult, op1=ALU.mult), R=[xi, g1, r1], W=[tb])
                kb.do('dve', lambda e, kt=kt, tb=tb: e.tensor_tensor(out=xx.ap[:, kt, 0:n], in0=xx.ap[:, kt, 0:n],
                                                                      in1=tb.ap[:, 0:n], op=ALU.add), R=[tb], W=[xx])
            outtoks.append(kb.dma('sp', xv[:, :, t0:t0 + n], xx.ap[:, :, 0:n], R=[xx], W=[xB]))
            if out == 'norm':
                r2 = rs[1]
                rstd_of(xx, n, psB, r2)
                for kt in range(KT):
                    kb.do('dve', lambda e, kt=kt: e.scalar_tensor_tensor(
                        out=SRC_D[:, dst_kt0 + kt, t0:t0 + n], in0=xx.ap[:, kt, 0:n], scalar=g2.ap[:, kt:kt + 1],
                        in1=r2.ap[:, 0:n], op0=ALU.mult, op1=ALU.mult), R=[xx, g2, r2], W=[srcbuf])
            elif out == 'cast':
                kb.do('act', lambda e: e.activation(out=SRC_D[:, dst_kt0:dst_kt0 + KT, t0:t0 + n], in_=xx.ap[:, :, 0:n],
                                                    func=AF.Copy), R=[xx], W=[srcbuf])

    def linear_phase(srcv, KT, W_ap, cols, epi, W2cols=None, tchunks=None, fresh=True):
        if fresh:
            new_phase()
        wb = [ar.alloc([128, KT, 128], BF16) for _ in range(3)]
        wb2 = [ar.alloc([128, KT, 128], BF16) for _ in range(2)] if W2cols is not None else None
        wv = W_ap.rearrange("(k p) n -> p k n", p=128)
        tch = tchunks if tchunks is not None else chunks
        st = {}

        def go(mi):
            c0 = cols[mi]
            w = wb[mi % 3]
            kb.dma('pool', w.ap, wv[:, :, c0:c0 + 128], W=[w])
            w2 = None
            if W2cols is not None:
                w2 = wb2[mi % 2]
                kb.dma('pool', w2.ap, wv[:, :, W2cols[mi]:W2cols[mi] + 128], W=[w2])
            for ci, (t0, n, s0) in enumerate(tch):
                if W2cols is None:
                    pb = PS[nxt('lin', 6)]
                    pb2 = None
                else:
                    j = nxt('lin2', 3)
                    pb, pb2 = PS[2 * j], PS[2 * j + 1]
                for kt in range(KT):
                    kb.do('pe', lambda e, kt=kt, pb=pb, w=w: e.matmul(pb.ap[:, 0:n], lhsT=w.ap[:, kt, :], rhs=srcv[:, kt, s0:s0 + n],
                                                                   start=(kt == 0), stop=(kt == KT - 1)), R=[w, srcbuf], W=[pb])
                if pb2 is not None:
                    for kt in range(KT):
                        kb.do('pe', lambda e, kt=kt, pb2=pb2, w2=w2: e.matmul(pb2.ap[:, 0:n], lhsT=w2.ap[:, kt, :], rhs=srcv[:, kt, s0:s0 + n],
                                                                         start=(kt == 0), stop=(kt == KT - 1)), R=[w2, srcbuf], W=[pb2])
                epi(mi, ci, t0, n, pb, pb2)
        return go, st

    full_chunks = [(t0, n, t0) for (t0, n) in chunks]

    def attention_phase(l):
        new_phase()
        kTd = ar.alloc([128, NKV, 128 + T], BF16)
        Vtm = ar.alloc([128, NQB + 1, KW], BF16)
        vTi = [ar.alloc([128, 128], BF16) for _ in range(2)]
        qTb = [ar.alloc([128, KTA, 128], BF16) for _ in range(2)]
        atm = [ar.alloc([128, AW]) for _ in range(2)]
        hn = [ar.alloc([128, AW], BF16) for _ in range(2)]
        junk = ar.alloc([128, AW], BF16)
        sm2 = [ar.alloc([128, 4]) for _ in range(2)]
        gat = ar.alloc([128, AW])
        snk = ar.alloc([128, NH])
        zb = Buf(ZBd)
        kb.dma('sp', gat.ap, gattn_in[l], W=[gat])
        kb.dma('sp', snk.ap, sinks_in[l], W=[snk])
        kb.do('pool', lambda e: e.memset(kTd.ap[:, :, 0:128], 0.0), W=[kTd])
        kb.do('pool', lambda e: e.memset(Vtm.ap[:, 0, :], 0.0), W=[Vtm])
        for g in range(NKV):
            for cp in range(2):
                kb.dma('sp', kTd.ap[cp * 64:(cp + 1) * 64, g, 128:128 + T], ZBd[AW + g * 64:AW + (g + 1) * 64, :], R=[zb], W=[kTd])
        PSb = [Buf(PS[i].ap.bitcast(BF16), excl=True) for i in range(8)]
        for i in range(8):
            PSb[i].w, PSb[i].r = PS[i].w, PS[i].r
        for b in range(NQB):
            vi = vTi[b % 2]
            kb.dma('sp', vi.ap[0:KW, :], ZBd[AW + KW:AW + 2 * KW, b * 128:(b + 1) * 128], R=[zb], W=[vi])
            pt = PSb[7]
            kb.do('pe', lambda e, vi=vi, pt=pt: e.transpose(out=pt.ap[:, 0:KW], in_=vi.ap[0:KW, :], identity=ident.ap[0:KW, 0:KW]),
                  R=[vi, ident], W=[pt])
            kb.do('act', lambda e, b=b, pt=pt: e.activation(out=Vtm.ap[:, b + 1, :], in_=pt.ap[:, 0:KW], func=AF.Copy), R=[pt], W=[Vtm])

        NRR = 8
        Sb = [ar.alloc([128, 256]) for _ in range(NRR)]
        Pb = [ar.alloc([128, 256], BF16) for _ in range(NRR)]
        PTs = [ar.alloc([128, 2, 128], BF16) for _ in range(NRR)]
        sm = [ar.alloc([128, 8]) for _ in range(NRR)]

        def pipeline(units, stages, after=None):
            ns = len(stages)
            for t in range(len(units) + ns - 1):
                for si, st in enumerate(stages):
                    ui = t - si
                    if 0 <= ui < len(units):
                        st(units[ui])
                        if si == ns - 1 and after is not None:
                            after(ui)

        def sA(u):
            S_, m_, npart, nk, h, sps = Sb[u['i']], sm[u['i']], u['np'], u['nk'], u['h'], u['sps']
            kb.do('dve', lambda e: e.scalar_tensor_tensor(out=S_.ap[0:npart, 0:nk], in0=u['dist'], scalar=-C.slopes[h],
                                                          in1=sps.ap[0:npart, 0:nk], op0=ALU.mult, op1=ALU.add),
                  R=[sps, DIST, DIST0], W=[S_])
            kb.do('dve', lambda e: e.reduce_max(out=m_.ap[0:npart, 0:1], in_=S_.ap[0:npart, 0:nk], axis=AX.X), R=[S_], W=[m_])
            kb.do('dve', lambda e: e.tensor_tensor(out=m_.ap[0:npart, 0:1], in0=m_.ap[0:npart, 0:1], in1=snk.ap[0:npart, h:h + 1],
                                                   op=ALU.max), R=[snk], W=[m_])
            kb.do('dve', lambda e: e.tensor_scalar(out=m_.ap[0:npart, 1:2], in0=m_.ap[0:npart, 0:1], scalar1=-1.0, scalar2=None,
                                                   op0=ALU.mult), R=[], W=[m_])

        def sB(u):
            S_, P_, m_, npart, nk, h = Sb[u['i']], Pb[u['i']], sm[u['i']], u['np'], u['nk'], u['h']
            kb.do('act', lambda e: e.activation(out=P_.ap[0:npart, 0:nk], in_=S_.ap[0:npart, 0:nk], func=AF.Exp,
                                                bias=m_.ap[0:npart, 1:2], scale=1.0, accum_out=m_.ap[0:npart, 2:3]),
                  R=[S_, m_], W=[P_, m_])
            kb.do('act', lambda e: e.activation(out=m_.ap[0:npart, 3:4], in_=snk.ap[0:npart, h:h + 1], func=AF.Exp,
                                                bias=m_.ap[0:npart, 1:2], scale=1.0), R=[snk, m_], W=[m_])

        def sC(u):
            m_, npart = sm[u['i']], u['np']
            kb.do('dve', lambda e: e.tensor_tensor(out=m_.ap[0:npart, 4:5], in0=m_.ap[0:npart, 2:3], in1=m_.ap[0:npart, 3:4],
                                                   op=ALU.add), R=[m_], W=[m_])
            kb.do('dve', lambda e: e.reciprocal(out=m_.ap[0:npart, 5:6], in_=m_.ap[0:npart, 4:5]), R=[m_], W=[m_])

        units = []
        for b in range(NQB):
            for h in range(NH):
                units.append(dict(b=b, h=h, g=h // C.GRP, hp=(h % 2) * 64, np=128, nk=256,
                                  dist=(DIST0 if b == 0 else DIST).ap))

        def pA(u):
            b, h, g, hp = u['b'], u['h'], u['g'], u['hp']
            if h == 0:
                qb = qTb[b % 2]
                kb.dma('sp', qb.ap, ZBd[0:AW, b * 128:(b + 1) * 128].rearrange("(k p) t -> p k t", p=128), R=[zb], W=[qb])
            qb = qTb[b % 2]
            u['i'] = nxt('att', NRR)
            u['sps'] = sps = PS[nxt('sps', 3)]
            kb.do('pe', lambda e: e.matmul(sps.ap[:, 0:256], lhsT=qb.ap[hp:hp + 64, h // 2, :],
                                           rhs=kTd.ap[hp:hp + 64, g, b * 128:b * 128 + 256], start=True, stop=True), R=[qb, kTd], W=[sps])
            sA(u)

        def pC(u):
            sC(u)
            i = u['i']
            u['ptp'] = ptp = PSb[3 + nxt('ptp', 2)]
            for j in range(2):
                kb.do('pe', lambda e, j=j: e.transpose(out=ptp.ap[:, j * 128:(j + 1) * 128], in_=Pb[i].ap[:, j * 128:(j + 1) * 128],
                                                       identity=ident.ap), R=[Pb[i], ident], W=[ptp])

        def pD(u):
            i, ptp = u['i'], u['ptp']
            kb.do('act', lambda e: e.activation(out=PTs[i].ap, in_=ptp.ap[:, 0:256].rearrange("p (j q) -> p j q", j=2), func=AF.Copy),
                  R=[ptp], W=[PTs[i]])

        def pE(u):
            i, b, g = u['i'], u['b'], u['g']
            u['ops'] = ops_ = PS[5 + nxt('ops', 2)]
            for j in range(2):
                kb.do('pe', lambda e, j=j: e.matmul(ops_.ap[:, 0:64], lhsT=PTs[i].ap[:, j, :], rhs=Vtm.ap[:, b + j, g * 64:(g + 1) * 64],
                                                    start=(j == 0), stop=(j == 1)), R=[PTs[i], Vtm], W=[ops_])

        def pF(u):
            i, h, ops_ = u['i'], u['h'], u['ops']
            am = atm[u['b'] % 2]
            kb.do('dve', lambda e: e.tensor_scalar(out=am.ap[:, h * 64:(h + 1) * 64], in0=ops_.ap[:, 0:64], scalar1=sm[i].ap[:, 5:6],
                                                   scalar2=None, op0=ALU.mult), R=[ops_, sm[i]], W=[am])

        def block_done(ui):
            u = units[ui]
            if u['h'] != NH - 1:
                return
            b = u['b']
            am, s2, hb = atm[b % 2], sm2[b % 2], hn[b % 2]
            kb.do('act', lambda e: e.activation(out=junk.ap, in_=am.ap, func=AF.Square, accum_out=s2.ap[:, 0:1]), R=[am], W=[junk, s2])
            kb.do('dve', lambda e: e.tensor_scalar(out=s2.ap[:, 1:2], in0=s2.ap[:, 0:1], scalar1=1.0 / AW, scalar2=EPS,
                                                   op0=ALU.mult, op1=ALU.add), R=[s2], W=[s2])
            kb.do('act', lambda e: e.activation(out=s2.ap[:, 1:2], in_=s2.ap[:, 1:2], func=AF.Ln), R=[s2], W=[s2])
            kb.do('act', lambda e: e.activation(out=s2.ap[:, 1:2], in_=s2.ap[:, 1:2], func=AF.Exp, scale=-0.5), R=[s2], W=[s2])
            kb.do('dve', lambda e: e.scalar_tensor_tensor(out=hb.ap, in0=am.ap, scalar=s2.ap[:, 1:2], in1=gat.ap, op0=ALU.mult, op1=ALU.mult),
                  R=[am, s2, gat], W=[hb])
            for kt in range(KTA):
                pt = PSb[7]
                kb.do('pe', lambda e, kt=kt: e.transpose(out=pt.ap[:, 0:128], in_=hb.ap[:, kt * 128:(kt + 1) * 128], identity=ident.ap),
                      R=[hb, ident], W=[pt])
                kb.do('act', lambda e, kt=kt: e.activation(out=SRC_D[:, kt, b * 128:(b + 1) * 128], in_=pt.ap[:, 0:128], func=AF.Copy),
                      R=[pt], W=[srcbuf])

        pipeline(units, [pA, sB, pC, pD, pE, pF], after=block_done)

        kS = [ar.alloc([128, NKV, 132], BF16) for _ in range(2)]
        vS = [ar.alloc([128, KW], BF16) for _ in range(2)]
        vN = [ar.alloc([1, KW], BF16) for _ in range(2)]
        qS = ar.alloc([128, KTA, NS], BF16)
        PnS = [ar.alloc([1, 132], BF16) for _ in range(NRR)]
        PTS = [ar.alloc([128, 2], BF16) for _ in range(NRR)]
        aS = ar.alloc([64, NH, NS])
        zf = Buf(ZFd)
        kb.dma('sp', qS.ap, ZBd[0:AW, SEQ:SEQ + NS].rearrange("(k p) t -> p k t", p=128), R=[zb], W=[qS])
        sunits = []
        for n in range(NS):
            for h in range(NH):
                sunits.append(dict(n=n, h=h, g=h // C.GRP, hp=(h % 2) * 64, np=1, nk=129, dist=DIST.ap[0:1, 0:129]))

        def qA(u):
            n, h, g, hp = u['n'], u['h'], u['g'], u['hp']
            ks_, vs_, vn_ = kS[n % 2], vS[n % 2], vN[n % 2]
            if h == 0:
                for g_ in range(NKV):
                    for cp in range(2):
                        kb.dma('pool', ks_.ap[cp * 64:(cp + 1) * 64, g_, 0:128], ckT_in[l, n, g_], W=[ks_])
                kb.do('pool', lambda e: e.tensor_copy(out=ks_.ap[:, :, 128:129], in_=kTd.ap[:, :, 128 + SEQ + n:128 + SEQ + n + 1]),
                      R=[kTd], W=[ks_])
                kb.dma('pool', vs_.ap, cv_in[l, n], W=[vs_])
                kb.dma('pool', vn_.ap, ZFd[KW:2 * KW, SEQ + n:SEQ + n + 1].rearrange("k o -> o k"), R=[zf], W=[vn_], allow_slow_non_contiguous=True)
                outtoks.append(kb.dma('sp', ks_out[l, n, 0:127, :], ck_in[l, n, 1:128, :]))
                outtoks.append(kb.dma('sp', vs_out[l, n, 0:127, :], cv_in[l, n, 1:128, :]))
                outtoks.append(kb.dma('sp', ks_out[l, n, 127:128, :], ZFd[0:KW, SEQ + n:SEQ + n + 1].rearrange("k o -> o k"), R=[zf], allow_slow_non_contiguous=True))
                outtoks.append(kb.dma('sp', vs_out[l, n, 127:128, :], ZFd[KW:2 * KW, SEQ + n:SEQ + n + 1].rearrange("k o -> o k"), R=[zf], allow_slow_non_contiguous=True))
            u['i'] = nxt('att', NRR)
            u['sps'] = sps = PS[nxt('sps', 3)]
            kb.do('pe', lambda e: e.matmul(sps.ap[0:1, 0:129], lhsT=qS.ap[hp:hp + 64, h // 2, n:n + 1], rhs=ks_.ap[hp:hp + 64, g, 0:129],
                                           start=True, stop=True), R=[qS, ks_], W=[sps])
            sA(u)

        def qC(u):
            sC(u)
            i = u['i']
            pn = PnS[i]
            kb.do('dve', lambda e: e.tensor_scalar(out=pn.ap[0:1, 0:129], in0=Pb[i].ap[0:1, 0:129], scalar1=sm[i].ap[0:1, 5:6], scalar2=None,
                                                   op0=ALU.mult), R=[Pb[i], sm[i]], W=[pn])
            u['ptp'] = ptp = PSb[3 + nxt('ptp', 2)]
            kb.do('pe', lambda e: e.transpose(out=ptp.ap[:, 0:1], in_=pn.ap[0:1, 0:128], identity=ident.ap[0:1, 0:1]), R=[pn, ident], W=[ptp])

        def qD(u):
            i, ptp = u['i'], u['ptp']
            kb.do('act', lambda e: e.activation(out=PTS[i].ap[:, 0:1], in_=ptp.ap[:, 0:1], func=AF.Copy), R=[ptp], W=[PTS[i]])

        def qE(u):
            i, g, n = u['i'], u['g'], u['n']
            vs_, vn_, pn = vS[n % 2], vN[n % 2], PnS[i]
            u['ops'] = ops_ = PS[5 + nxt('ops', 2)]
            kb.do('pe', lambda e: e.matmul(ops_.ap[0:64, 0:1], lhsT=vs_.ap[:, g * 64:(g + 1) * 64], rhs=PTS[i].ap[:, 0:1], start=True, stop=False),
                  R=[PTS[i], vs_], W=[ops_])
            kb.do('pe', lambda e: e.matmul(ops_.ap[0:64, 0:1], lhsT=vn_.ap[0:1, g * 64:(g + 1) * 64], rhs=pn.ap[0:1, 128:129], start=False, stop=True),
                  R=[pn, vn_], W=[ops_])

        def qF(u):
            ops_, h, n = u['ops'], u['h'], u['n']
            kb.do('act', lambda e: e.activation(out=aS.ap[:, h, n:n + 1], in_=ops_.ap[0:64, 0:1], func=AF.Copy), R=[ops_], W=[aS])

        if not _os.environ.get('PIPE_SAMPLE'):
            for u_ in sunits:
                for st_ in (qA, sB, qC, qD, qE, qF):
                    st_(u_)
        else:
            pipeline(sunits, [qA, sB, qC, qD, qE, qF])
        sqS = ar.alloc([64, NH * NS], BF16)
        ssS = ar.alloc([64, NS])
        gT = ar.alloc([64, NH])
        hS = ar.alloc([64, NH, NS])
        hSb = ar.alloc([64, NH, NS], BF16)
        kb.dma('sp', gT.ap, gattnT_in[l], W=[gT])
        kb.do('act', lambda e: e.activation(out=sqS.ap, in_=aS.ap.rearrange("p h n -> p (h n)"), func=AF.Square), R=[aS], W=[sqS])
        kb.do('pe', lambda e: e.matmul(PS[7].ap[0:64, 0:NH * NS], lhsT=ones.ap[0:64, 0:64], rhs=sqS.ap, start=True, stop=True),
              R=[sqS, ones], W=[PS[7]])
        kb.do('dve', lambda e: e.tensor_reduce(out=ssS.ap, in_=PS[7].ap[0:64, 0:NH * NS].rearrange("p (h n) -> p n h", n=NS),
                                               axis=AX.X, op=ALU.add), R=[PS[7]], W=[ssS])
        kb.do('dve', lambda e: e.tensor_scalar(out=ssS.ap, in0=ssS.ap, scalar1=1.0 / AW, scalar2=EPS, op0=ALU.mult, op1=ALU.add),
              R=[ssS], W=[ssS])
        kb.do('act', lambda e: e.activation(out=ssS.ap, in_=ssS.ap, func=AF.Ln), R=[ssS], W=[ssS])
        kb.do('act', lambda e: e.activation(out=ssS.ap, in_=ssS.ap, func=AF.Exp, scale=-0.5), R=[ssS], W=[ssS])
        kb.do('dve', lambda e: e.tensor_tensor(out=hS.ap, in0=aS.ap, in1=gT.ap.unsqueeze(2).broadcast_to([64, NH, NS]), op=ALU.mult),
              R=[aS, gT], W=[hS])
        kb.do('dve', lambda e: e.tensor_tensor(out=hSb.ap, in0=hS.ap, in1=ssS.ap.unsqueeze(1).broadcast_to([64, NH, NS]), op=ALU.mult),
              R=[hS, ssS], W=[hSb])
        mg = Buf(MGSd)
        kb.dma('sp', MGSd.rearrange("(h d) n -> d h n", d=64), hSb.ap, R=[hSb], W=[mg])
        kb.dma('sp', SRC_D[:, 0:KTA, SEQ:SEQ + NS], MGSd.rearrange("(k p) n -> p k n", p=128), R=[mg], W=[srcbuf])

    def ssm_phase(l):
        new_phase()
        ps_ = ar.alloc([128, 3, NPAIR])
        kb.dma('sp', ps_.ap, ssm_ps_in[l], W=[ps_])
        NV = 17
        v = ar.alloc([128, NV, NPAIR])
        vi = ar.alloc([128, NPAIR], I32)
        are, aim, ldt = ps_.ap[:, 0, :], ps_.ap[:, 1, :], ps_.ap[:, 2, :]
        V_DT, V_DRE, V_TH, V_R, V_A, V_SIN, V_COS, V_ABR, V_ABI, V_FR, V_FI, V_IFR, V_IFI, V_T1, V_T2, V_T3, V_NFI = range(17)

        def vv(i):
            return v.ap[:, i, :]

        def tiny(eng, fn):
            kb.do(eng, fn, R=[v, ps_], W=[v])

        def wrap_turns(eng_ap_in, out_i):
            kb.do('dve', lambda e: e.tensor_copy(out=vi.ap, in_=eng_ap_in), R=[v], W=[vi])
            kb.do('dve', lambda e: e.tensor_copy(out=vv(V_T1), in_=vi.ap), R=[vi, v], W=[v])
            tiny('dve', lambda e: e.tensor_tensor(out=vv(out_i), in0=eng_ap_in, in1=vv(V_T1), op=ALU.subtract))
            tiny('dve', lambda e: e.tensor_scalar(out=vv(V_T1), in0=vv(out_i), scalar1=0.5, scalar2=None, op0=ALU.is_gt))
            tiny('dve', lambda e: e.tensor_tensor(out=vv(out_i), in0=vv(out_i), in1=vv(V_T1), op=ALU.subtract))
            tiny('dve', lambda e: e.tensor_scalar(out=vv(V_T1), in0=vv(out_i), scalar1=-0.5, scalar2=None, op0=ALU.is_lt))
            tiny('dve', lambda e: e.tensor_tensor(out=vv(out_i), in0=vv(out_i), in1=vv(V_T1), op=ALU.add))

        tiny('act', lambda e: e.activation(out=vv(V_DT), in_=ldt, func=AF.Exp))
        tiny('dve', lambda e: e.tensor_tensor(out=vv(V_DRE), in0=vv(V_DT), in1=are, op=ALU.mult))
        tiny('dve', lambda e: e.tensor_tensor(out=vv(V_TH), in0=vv(V_DT), in1=aim, op=ALU.mult))
        tiny('act', lambda e: e.activation(out=vv(V_R), in_=vv(V_DRE), func=AF.Exp))
        tiny('dve', lambda e: e.tensor_scalar(out=vv(V_T2), in0=vv(V_TH), scalar1=1.0 / (2 * math.pi), scalar2=None, op0=ALU.mult))
        wrap_turns(vv(V_T2), V_A)
        tiny('act', lambda e: e.activation(out=vv(V_SIN), in_=vv(V_A), func=AF.Sin, scale=TWO_PI))
        tiny('dve', lambda e: e.tensor_scalar(out=vv(V_T2), in0=vv(V_A), scalar1=0.25, scalar2=None, op0=ALU.add))
        wrap_turns(vv(V_T2), V_T3)
        tiny('act', lambda e: e.activation(out=vv(V_COS), in_=vv(V_T3), func=AF.Sin, scale=TWO_PI))
        tiny('dve', lambda e: e.tensor_tensor(out=vv(V_ABR), in0=vv(V_R), in1=vv(V_COS), op=ALU.mult))
        tiny('dve', lambda e: e.tensor_tensor(out=vv(V_ABI), in0=vv(V_R), in1=vv(V_SIN), op=ALU.mult))
        tiny('dve', lambda e: e.tensor_tensor(out=vv(V_T1), in0=are, in1=are, op=ALU.mult))
        tiny('dve', lambda e: e.tensor_tensor(out=vv(V_T2), in0=aim, in1=aim, op=ALU.mult))
        tiny('dve', lambda e: e.tensor_tensor(out=vv(V_T1), in0=vv(V_T1), in1=vv(V_T2), op=ALU.add))
        tiny('dve', lambda e: e.reciprocal(out=vv(V_T1), in_=vv(V_T1)))
        tiny('dve', lambda e: e.tensor_scalar(out=vv(V_T2), in0=vv(V_ABR), scalar1=-1.0, scalar2=None, op0=ALU.add))
        tiny('dve', lambda e: e.tensor_tensor(out=vv(V_FR), in0=vv(V_T2), in1=are, op=ALU.mult))
        tiny('dve', lambda e: e.tensor_tensor(out=vv(V_T3), in0=vv(V_ABI), in1=aim, op=ALU.mult))
        tiny('dve', lambda e: e.tensor_tensor(out=vv(V_FR), in0=vv(V_FR), in1=vv(V_T3), op=ALU.add))
        tiny('dve', lambda e: e.tensor_tensor(out=vv(V_FR), in0=vv(V_FR), in1=vv(V_T1), op=ALU.mult))
        tiny('dve', lambda e: e.tensor_tensor(out=vv(V_FI), in0=vv(V_ABI), in1=are, op=ALU.mult))
        tiny('dve', lambda e: e.tensor_tensor(out=vv(V_T3), in0=vv(V_T2), in1=aim, op=ALU.mult))
        tiny('dve', lambda e: e.tensor_tensor(out=vv(V_FI), in0=vv(V_FI), in1=vv(V_T3), op=ALU.subtract))
        tiny('dve', lambda e: e.tensor_tensor(out=vv(V_FI), in0=vv(V_FI), in1=vv(V_T1), op=ALU.mult))
        tiny('dve', lambda e: e.tensor_tensor(out=vv(V_T1), in0=vv(V_FR), in1=vv(V_FR), op=ALU.mult))
        tiny('dve', lambda e: e.tensor_tensor(out=vv(V_T2), in0=vv(V_FI), in1=vv(V_FI), op=ALU.mult))
        tiny('dve', lambda e: e.tensor_tensor(out=vv(V_T1), in0=vv(V_T1), in1=vv(V_T2), op=ALU.add))
        tiny('dve', lambda e: e.reciprocal(out=vv(V_T1), in_=vv(V_T1)))
        tiny('dve', lambda e: e.tensor_tensor(out=vv(V_IFR), in0=vv(V_FR), in1=vv(V_T1), op=ALU.mult))
        tiny('dve', lambda e: e.tensor_tensor(out=vv(V_IFI), in0=vv(V_FI), in1=vv(V_T1), op=ALU.mult))
        tiny('dve', lambda e: e.tensor_scalar(out=vv(V_IFI), in0=vv(V_IFI), scalar1=-1.0, scalar2=None, op0=ALU.mult))
        tiny('dve', lambda e: e.tensor_scalar(out=vv(V_NFI), in0=vv(V_FI), scalar1=-1.0, scalar2=None, op0=ALU.mult))

        bpk = [ar.alloc([128, 2, 4, 128], BF16) for _ in range(2)]
        wg = ar.alloc([128, NKS, 2, 128], BF16)
        kb.dma('pool', wg.ap, wglu_in[l], W=[wg], max_dma_last_dim=4096)
        dc = ar.alloc([128, NKS])
        kb.dma('sp', dc.ap, dcol_in[l], W=[dc])
        st0 = ar.alloc([128, 2, NPAIR, NS])
        kb.dma('sp', st0.ap, st0_in[l], W=[st0])
        sto = ar.alloc([128, 2, NPAIR, 1 + NS])
        uT = [ar.alloc([128, T], BF16) for _ in range(1)]
        cpf = [ar.alloc([128, 2, 4, 128]) for _ in range(1)]
        cf = [ar.alloc([128, 2, 4, 128], BF16) for _ in range(2)]
        cft = ar.alloc([128, 128])
        RT = [ar.alloc([128, TQ]) for _ in range(4)]
        NW = 4
        AT = [ar.alloc([128, TQ]) for _ in range(NW)]
        A2 = [ar.alloc([128, TQ]) for _ in range(NW)]
        NF = [ar.alloc([128, TQ]) for _ in range(NW)]
        NA = [ar.alloc([128, TQ]) for _ in range(NW)]
        CS = [ar.alloc([128, TQ]) for _ in range(NW)]
        SN = [ar.alloc([128, TQ]) for _ in range(NW)]
        PW1 = [ar.alloc([128, TQ]) for _ in range(2)]
        PW2 = [ar.alloc([128, TQ]) for _ in range(2)]
        THO = ar.alloc([128, SEQ // TQ, NPAIR])
        HPI = ar.alloc([128, 1])
        kb.do('dve', lambda e: e.memset(HPI.ap, math.pi / 2), W=[HPI])
        for tq_ in range(SEQ // TQ):
            kb.do('dve', lambda e, tq_=tq_: e.tensor_scalar(out=THO.ap[:, tq_, :], in0=vv(V_A), scalar1=float(tq_ * TQ), scalar2=None, op0=ALU.mult),
                  R=[v], W=[THO])
        W1 = [ar.alloc([128, TQ]) for _ in range(NW)]
        W2 = [ar.alloc([128, TQ]) for _ in range(NW)]
        XR = [ar.alloc([128, TQ]) for _ in range(NW)]
        XI = [ar.alloc([128, TQ]) for _ in range(NW)]
        SR = [ar.alloc([128, TQ]) for _ in range(NW)]
        SI = [ar.alloc([128, TQ]) for _ in range(NW)]
        SB2R = [ar.alloc([128, TQ], BF16) for _ in range(4)]
        SB2I = [ar.alloc([128, TQ], BF16) for _ in range(4)]
        carry = ar.alloc([128, 2, NPAIR])
        aoff = ar.alloc([128, NPAIR])
        FIN = [ar.alloc([128, 8]) for _ in range(2)]
        ssbb = ar.alloc([128, 2, 4, NS], BF16)
        sw = ar.alloc([128, 3, 4, NS])
        craw = ar.alloc([128, 2, 4, 128], BF16)
        TS = ar.alloc([128, 2, NPAIR, NS])
        tsw = ar.alloc([128, 2, NPAIR, NS])
        abrb = v.ap[:, V_ABR, :].unsqueeze(2).broadcast_to([128, NPAIR, NS])
        abib = v.ap[:, V_ABI, :].unsqueeze(2).broadcast_to([128, NPAIR, NS])
        kb.do('dve', lambda e: e.tensor_tensor(out=TS.ap[:, 0], in0=st0.ap[:, 0], in1=abrb, op=ALU.mult), R=[st0, v], W=[TS])
        kb.do('dve', lambda e: e.tensor_tensor(out=tsw.ap[:, 0], in0=st0.ap[:, 1], in1=abib, op=ALU.mult), R=[st0, v], W=[tsw])
        kb.do('dve', lambda e: e.tensor_tensor(out=TS.ap[:, 0], in0=TS.ap[:, 0], in1=tsw.ap[:, 0], op=ALU.subtract), R=[tsw], W=[TS])
        kb.do('dve', lambda e: e.tensor_tensor(out=TS.ap[:, 1], in0=st0.ap[:, 0], in1=abib, op=ALU.mult), R=[st0, v], W=[TS])
        kb.do('dve', lambda e: e.tensor_tensor(out=tsw.ap[:, 1], in0=st0.ap[:, 1], in1=abrb, op=ALU.mult), R=[st0, v], W=[tsw])
        kb.do('dve', lambda e: e.tensor_tensor(out=TS.ap[:, 1], in0=TS.ap[:, 1], in1=tsw.ap[:, 1], op=ALU.add), R=[tsw], W=[TS])
        yst = [ar.alloc([128, T]) for _ in range(1)]
        EY = ar.alloc([128, T])
        E2 = [ar.alloc([128, 512]) for _ in range(1)]
        GB = [ar.alloc([128, 512], BF16) for _ in range(1)]
        zb = Buf(ZBd)
        ssB = Buf(SSd)
        kb.do('pool', lambda e: e.memset(carry.ap, 0.0), W=[carry])
        NTQ = SEQ // TQ

        deferred = []

        def run_deferred():
            for lst in deferred:
                kb.flush_interleaved(lst)
            del deferred[:]

        for kt in range(NKS):
            u = uT[kt % len(uT)]
            kb.dma('sp', u.ap, ZBd[AW + 2 * KW + kt * 128:AW + 2 * KW + (kt + 1) * 128, :], R=[zb], W=[u])
            cp_, cf_ = cpf[0], cf[kt % 2]
            bp = bpk[kt % 2]
            for ri_ in range(2):
                kb.dma('pool', bp.ap[:, ri_], bpad_in[l, :, ri_, kt * 4:(kt + 1) * 4, :], W=[bp])
            kb.dma('sp', cp_.ap, cpad_in[l, :, kt], W=[cp_])
            kb.do('act', lambda e: e.activation(out=craw.ap[:, 0], in_=cp_.ap[:, 0], func=AF.Copy), R=[cp_], W=[craw])
            kb.do('act', lambda e: e.activation(out=craw.ap[:, 1], in_=cp_.ap[:, 1], func=AF.Copy, scale=-1.0), R=[cp_], W=[craw])
            for pr in range(4):
                q = kt * 4 + pr
                fr, fi = v.ap[:, V_FR, q:q + 1], v.ap[:, V_FI, q:q + 1]
                nfi = v.ap[:, V_NFI, q:q + 1]
                kb.do('dve', lambda e, pr=pr, fi=fi: e.tensor_scalar(out=cft.ap, in0=cp_.ap[:, 1, pr, :], scalar1=fi, scalar2=None, op0=ALU.mult),
                      R=[cp_, v], W=[cft])
                kb.do('dve', lambda e, pr=pr, fr=fr: e.scalar_tensor_tensor(out=cf_.ap[:, 0, pr, :], in0=cp_.ap[:, 0, pr, :], scalar=fr, in1=cft.ap,
                                                                         op0=ALU.mult, op1=ALU.subtract), R=[cp_, v, cft], W=[cf_])
                kb.do('dve', lambda e, pr=pr, fr=fr: e.tensor_scalar(out=cft.ap, in0=cp_.ap[:, 1, pr, :], scalar1=fr, scalar2=-1.0, op0=ALU.mult,
                                                                  op1=ALU.mult), R=[cp_, v, cf_], W=[cft])
                kb.do('dve', lambda e, pr=pr, nfi=nfi: e.scalar_tensor_tensor(out=cf_.ap[:, 1, pr, :], in0=cp_.ap[:, 0, pr, :], scalar=nfi, in1=cft.ap,
                                                                           op0=ALU.mult, op1=ALU.add), R=[cp_, v, cft], W=[cf_])
            ys = yst[kt % len(yst)]
            for tq in range(NTQ + 1):
                samp = (tq == NTQ)
                t0 = tq * TQ
                n = NS if samp else TQ
                ypb = PS[4 + nxt('ypb', 2)]
                pend = []
                if samp:
                    run_deferred()
                    xs_ = PS[0]
                    for pr in range(4):
                        for ri in range(2):
                            c0_ = (ri * 4 + pr) * NS
                            kb.do('pe', lambda e, ri=ri, pr=pr, c0_=c0_: e.matmul(xs_.ap[:, c0_:c0_ + NS], lhsT=bp.ap[:, ri, pr, :], rhs=u.ap[:, t0:t0 + NS],
                                                                               start=True, stop=True), R=[bp, u], W=[xs_])
                    xv_ = xs_.ap[:, 0:8 * NS].rearrange("p (r q n) -> p r q n", r=2, q=4)
                    frb = v.ap[:, V_FR, kt * 4:kt * 4 + 4].unsqueeze(2).broadcast_to([128, 4, NS])
                    fib = v.ap[:, V_FI, kt * 4:kt * 4 + 4].unsqueeze(2).broadcast_to([128, 4, NS])
                    wr, wi, wt_ = sw.ap[:, 0], sw.ap[:, 1], sw.ap[:, 2]
                    kb.do('dve', lambda e: e.tensor_tensor(out=wr, in0=xv_[:, 0], in1=frb, op=ALU.mult), R=[xs_, v], W=[sw])
                    kb.do('dve', lambda e: e.tensor_tensor(out=wt_, in0=xv_[:, 1], in1=fib, op=ALU.mult), R=[xs_, v], W=[sw])
                    kb.do('dve', lambda e: e.tensor_tensor(out=wr, in0=wr, in1=wt_, op=ALU.subtract), R=[], W=[sw])
                    kb.do('dve', lambda e: e.tensor_tensor(out=wi, in0=xv_[:, 0], in1=fib, op=ALU.mult), R=[xs_, v], W=[sw])
                    kb.do('dve', lambda e: e.tensor_tensor(out=wt_, in0=xv_[:, 1], in1=frb, op=ALU.mult), R=[xs_, v], W=[sw])
                    kb.do('dve', lambda e: e.tensor_tensor(out=wi, in0=wi, in1=wt_, op=ALU.add), R=[], W=[sw])
                    kb.do('dve', lambda e: e.tensor_tensor(out=sto.ap[:, 0, kt * 4:kt * 4 + 4, 1:1 + NS], in0=wr, in1=TS.ap[:, 0, kt * 4:kt * 4 + 4, :], op=ALU.add),
                          R=[sw, TS], W=[sto])
                    kb.do('dve', lambda e: e.tensor_tensor(out=sto.ap[:, 1, kt * 4:kt * 4 + 4, 1:1 + NS], in0=wi, in1=TS.ap[:, 1, kt * 4:kt * 4 + 4, :], op=ALU.add),
                          R=[sw, TS], W=[sto])
                    kb.do('act', lambda e: e.activation(out=ssbb.ap, in_=sto.ap[:, :, kt * 4:kt * 4 + 4, 1:1 + NS], func=AF.Copy), R=[sto], W=[ssbb])
                for pr in range(4):
                    q = kt * 4 + pr
                    if not samp:
                        kb.begin_buffer()
                        xps_r, xps_i = PS[2 * nxt('xps', 2)], None
                        xps_i = PS[PS.index(xps_r) + 1]
                        for ri, xp in ((0, xps_r), (1, xps_i)):
                            kb.do('pe', lambda e, ri=ri, xp=xp, q=q: e.matmul(xp.ap[:, 0:n], lhsT=bp.ap[:, ri, pr, :], rhs=u.ap[:, t0:t0 + n],
                                                                           start=True, stop=True), R=[bp, u], W=[xp])
                    _k = nxt('sb2', 4)
                    s2r, s2i = SB2R[_k], SB2I[_k]
                    if samp:
                        rhs_r, rhs_i = ssbb.ap[:, 0, pr, :], ssbb.ap[:, 1, pr, :]
                        rd = [ssbb]
                    else:
                        fin = FIN[pr % 2]
                        w = nxt('ssw', NW)
                        w1, w2, xr_, xi_, sr_, si_ = W1[w], W2[w], XR[w], XI[w], SR[w], SI[w]
                        tw = w
                        pw1, pw2 = PW1[pr % 2], PW2[pr % 2]
                        at, a2, nf, na, cs, sn = AT[tw], A2[tw], NF[tw], NA[tw], CS[tw], SN[tw]
                        rt = RT[pr]
                        if tq == 0:
                            kb.do('act', lambda e, rt=rt, q=q: e.activation(out=rt.ap, in_=IOT.ap[:, 0:TQ], func=AF.Identity, scale=0.0,
                                                                           bias=v.ap[:, V_R, q:q + 1]), R=[IOT, v], W=[rt])
                        kb.do('act', lambda e, at=at, q=q: e.activation(out=at.ap, in_=IOT.ap[:, 0:TQ], func=AF.Identity, scale=v.ap[:, V_A, q:q + 1],
                                                                       bias=THO.ap[:, tq, q:q + 1]), R=[IOT, v, THO], W=[at])
                        kb.do('dve', lambda e, at=at, a2=a2: e.tensor_scalar(out=a2.ap, in0=at.ap, scalar1=MAGIC, scalar2=None, op0=ALU.add), R=[at], W=[a2])
                        kb.do('dve', lambda e, at=at, a2=a2, nf=nf: e.scalar_tensor_tensor(out=nf.ap, in0=a2.ap, scalar=MAGIC, in1=at.ap, op0=ALU.subtract,
                                                                                       op1=ALU.subtract), R=[at, a2], W=[nf])
                        kb.do('act', lambda e, nf=nf, sn=sn: e.activation(out=sn.ap, in_=nf.ap, func=AF.Sin, scale=-TWO_PI), R=[nf], W=[sn])
                        kb.do('act', lambda e, nf=nf, na=na: e.activation(out=na.ap, in_=nf.ap, func=AF.Sin, scale=-TWO_PI / 2), R=[nf], W=[na])
                        kb.do('act', lambda e, na=na: e.activation(out=na.ap, in_=na.ap, func=AF.Square), R=[], W=[na])
                        kb.do('act', lambda e, na=na, cs=cs: e.activation(out=cs.ap, in_=na.ap, func=AF.Identity, scale=-2.0, bias=1.0), R=[na], W=[cs])
                        kb.mark('M')
                        kb.do('dve', lambda e, cs=cs, w1=w1: e.tensor_tensor(out=w1.ap, in0=cs.ap, in1=xps_r.ap[:, 0:n], op=ALU.mult), R=[cs, xps_r], W=[w1])
                        kb.do('dve', lambda e, sn=sn, w2=w2: e.tensor_tensor(out=w2.ap, in0=sn.ap, in1=xps_i.ap[:, 0:n], op=ALU.mult), R=[sn, xps_i], W=[w2])
                        kb.do('dve', lambda e, w1=w1, w2=w2, xr_=xr_: e.tensor_tensor(out=xr_.ap, in0=w1.ap, in1=w2.ap, op=ALU.add), R=[w1, w2], W=[xr_])
                        kb.do('dve', lambda e, cs=cs, w1=w1: e.tensor_tensor(out=w1.ap, in0=cs.ap, in1=xps_i.ap[:, 0:n], op=ALU.mult), R=[cs, xps_i, xr_], W=[w1])
                        kb.do('dve', lambda e, sn=sn, w2=w2: e.tensor_tensor(out=w2.ap, in0=sn.ap, in1=xps_r.ap[:, 0:n], op=ALU.mult), R=[sn, xps_r, xr_], W=[w2])
                        kb.do('dve', lambda e, w1=w1, w2=w2, xi_=xi_: e.tensor_tensor(out=xi_.ap, in0=w1.ap, in1=w2.ap, op=ALU.subtract), R=[w1, w2], W=[xi_])
                        kb.do('dve', lambda e, rt=rt, xr_=xr_, sr_=sr_, q=q: e.tensor_tensor_scan(out=sr_.ap, data0=rt.ap, data1=xr_.ap,
                                                                                             initial=carry.ap[:, 0, q:q + 1], op0=ALU.mult, op1=ALU.add),
                              R=[rt, xr_, carry], W=[sr_])
                        kb.do('dve', lambda e, rt=rt, xi_=xi_, si_=si_, q=q: e.tensor_tensor_scan(out=si_.ap, data0=rt.ap, data1=xi_.ap,
                                                                                             initial=carry.ap[:, 1, q:q + 1], op0=ALU.mult, op1=ALU.add),
                              R=[rt, xi_, carry], W=[si_])
                        kb.do('act', lambda e, sr_=sr_, q=q: e.activation(out=carry.ap[:, 0, q:q + 1], in_=sr_.ap[:, TQ - 1:TQ], func=AF.Copy), R=[sr_], W=[carry])
                        kb.do('act', lambda e, si_=si_, q=q: e.activation(out=carry.ap[:, 1, q:q + 1], in_=si_.ap[:, TQ - 1:TQ], func=AF.Copy), R=[si_], W=[carry])
                        kb.mark('R')
                        kb.do('dve', lambda e: e.tensor_tensor(out=w1.ap, in0=cs.ap, in1=sr_.ap, op=ALU.mult), R=[cs, sr_], W=[w1])
                        kb.do('dve', lambda e: e.tensor_tensor(out=w2.ap, in0=sn.ap, in1=si_.ap, op=ALU.mult), R=[sn, si_], W=[w2])
                        kb.do('dve', lambda e: e.tensor_tensor(out=s2r.ap, in0=w1.ap, in1=w2.ap, op=ALU.subtract), R=[w1, w2], W=[s2r])
                        if tq == NTQ - 1:
                            kb.do('dve', lambda e: e.tensor_tensor(out=fin.ap[:, 0:1], in0=w1.ap[:, TQ - 1:TQ], in1=w2.ap[:, TQ - 1:TQ],
                                                                   op=ALU.subtract), R=[w1, w2], W=[fin])
                        kb.do('pool', lambda e: e.tensor_tensor(out=pw1.ap, in0=cs.ap, in1=si_.ap, op=ALU.mult), R=[cs, si_], W=[pw1])
                        kb.do('pool', lambda e: e.tensor_tensor(out=pw2.ap, in0=sn.ap, in1=sr_.ap, op=ALU.mult), R=[sn, sr_], W=[pw2])
                        kb.do('pool', lambda e: e.tensor_tensor(out=s2i.ap, in0=pw1.ap, in1=pw2.ap, op=ALU.add), R=[pw1, pw2], W=[s2i])
                        if tq == NTQ - 1:
                            fr, fi = v.ap[:, V_FR, q:q + 1], v.ap[:, V_FI, q:q + 1]
                            kb.do('dve', lambda e, w1=pw1, w2=pw2: e.tensor_tensor(out=fin.ap[:, 1:2], in0=w1.ap[:, TQ - 1:TQ], in1=w2.ap[:, TQ - 1:TQ],
                                                                               op=ALU.add), R=[pw1, pw2], W=[fin])
                            kb.do('dve', lambda e, fi=fi: e.tensor_scalar(out=fin.ap[:, 2:3], in0=fin.ap[:, 1:2], scalar1=fi, scalar2=None, op0=ALU.mult), R=[v], W=[fin])
                            kb.do('dve', lambda e, fr=fr, q=q: e.scalar_tensor_tensor(out=sto.ap[:, 0, q, 0:1], in0=fin.ap[:, 0:1], scalar=fr, in1=fin.ap[:, 2:3],
                                                                                   op0=ALU.mult, op1=ALU.subtract), R=[v, fin], W=[sto])
                            kb.do('dve', lambda e, fr=fr: e.tensor_scalar(out=fin.ap[:, 2:3], in0=fin.ap[:, 1:2], scalar1=fr, scalar2=None, op0=ALU.mult), R=[v], W=[fin])
                            kb.do('dve', lambda e, fi=fi, q=q: e.scalar_tensor_tensor(out=sto.ap[:, 1, q, 0:1], in0=fin.ap[:, 0:1], scalar=fi, in1=fin.ap[:, 2:3],
                                                                                   op0=ALU.mult, op1=ALU.add), R=[v, fin], W=[sto])
                        rhs_r, rhs_i = s2r.ap, s2i.ap
                        rd = [s2r, s2i]
                    cw_ = craw if samp else cf_
                    kb.do('pe', lambda e, pr=pr, rhs_r=rhs_r, ypb=ypb: e.matmul(ypb.ap[:, 0:n], lhsT=cw_.ap[:, 0, pr, :], rhs=rhs_r,
                                                                              start=(pr == 0), stop=False), R=[cw_] + rd, W=[ypb])
                    kb.do('pe', lambda e, pr=pr, rhs_i=rhs_i, ypb=ypb: e.matmul(ypb.ap[:, 0:n], lhsT=cw_.ap[:, 1, pr, :], rhs=rhs_i,
                                                                              start=False, stop=(pr == 3)), R=[cw_] + rd, W=[ypb])
                    if not samp:
                        pend.append(kb.end_buffer())
                        if len(pend) == 2:
                            def _split(x):
                                iM = [k_ for k_, o_ in enumerate(x) if o_[0] == 'mark' and o_[1] == 'M'][0]
                                iR = [k_ for k_, o_ in enumerate(x) if o_[0] == 'mark' and o_[1] == 'R'][0]
                                return x[:iM], x[iM:iR], x[iR:]
                            parts = [_split(x) for x in pend]
                            kb.flush_interleaved([p_[0] for p_ in parts])
                            run_deferred()
                            kb.flush_interleaved([p_[1] for p_ in parts])
                            deferred.append([p_[2] for p_ in parts])
                            pend = []
                if not samp:
                    kb.begin_buffer()
                kb.do('dve', lambda e, ypb=ypb: e.scalar_tensor_tensor(out=EY.ap[:, t0:t0 + n], in0=u.ap[:, t0:t0 + n], scalar=dc.ap[:, kt:kt + 1],
                                                                     in1=ypb.ap[:, 0:n], op0=ALU.mult, op1=ALU.add), R=[u, dc, ypb], W=[EY])
                if not samp:
                    deferred.append([kb.end_buffer()])
            run_deferred()
            pieces = [(c0_, min(512, T - c0_)) for c0_ in range(0, T, 512)]
            for (c0_, n_) in pieces:
                e2, gb = E2[0], GB[0]
                kb.do('act', lambda e: e.activation(out=e2.ap[:, 0:n_], in_=EY.ap[:, c0_:c0_ + n_], func=AF.Square), R=[EY], W=[e2])
                kb.do('dve', lambda e: e.tensor_scalar(out=e2.ap[:, 0:n_], in0=e2.ap[:, 0:n_], scalar1=0.044715, scalar2=1.0, op0=ALU.mult, op1=ALU.add),
                      R=[], W=[e2])
                kb.do('dve', lambda e: e.tensor_tensor(out=e2.ap[:, 0:n_], in0=e2.ap[:, 0:n_], in1=EY.ap[:, c0_:c0_ + n_], op=ALU.mult), R=[EY], W=[e2])
                kb.do('act', lambda e: e.activation(out=e2.ap[:, 0:n_], in_=e2.ap[:, 0:n_], func=AF.Sigmoid, scale=GELU_C), R=[], W=[e2])
                kb.do('dve', lambda e: e.tensor_tensor(out=gb.ap[:, 0:n_], in0=e2.ap[:, 0:n_], in1=EY.ap[:, c0_:c0_ + n_], op=ALU.mult), R=[EY, e2], W=[gb])
                z1, z2 = PS[6], PS[7]
                kb.do('pe', lambda e: e.matmul(z1.ap[:, 0:n_], lhsT=wg.ap[:, kt, 0, :], rhs=gb.ap[:, 0:n_], start=True, stop=True), R=[wg, gb], W=[z1])
                kb.do('pe', lambda e: e.matmul(z2.ap[:, 0:n_], lhsT=wg.ap[:, kt, 1, :], rhs=gb.ap[:, 0:n_], start=True, stop=True), R=[wg, gb], W=[z2])
                kb.do('act', lambda e: e.activation(out=e2.ap[:, 0:n_], in_=z2.ap[:, 0:n_], func=AF.Sigmoid), R=[z2], W=[e2])
                kb.do('dve', lambda e: e.tensor_tensor(out=ys.ap[:, c0_:c0_ + n_], in0=e2.ap[:, 0:n_], in1=z1.ap[:, 0:n_], op=ALU.mult), R=[e2, z1], W=[ys])
            kb.dma('sp', SSd[kt * 128:(kt + 1) * 128, :], ys.ap, R=[ys], W=[ssB])
        outtoks.append(kb.dma('sp', st_out[l], sto.ap, R=[sto]))

    def evac_store(dst_d, row0_of, dt, scale_of=None, also_f32=None):
        stg = [ar.alloc([128, T], dt) for _ in range(2)]
        stf = [ar.alloc([128, T]) for _ in range(2)] if also_f32 is not None else None
        dB = Buf(dst_d)
        nch = len(chunks)

        def ep(mi, ci, t0, n, pb, pb2):
            s = stg[mi % 2]
            sc = 1.0 if scale_of is None else scale_of(mi)
            f32r = also_f32(mi) if also_f32 is not None else None
            if f32r is None:
                kb.do('act', lambda e: e.activation(out=s.ap[:, t0:t0 + n], in_=pb.ap[:, 0:n], func=AF.Copy, scale=sc), R=[pb], W=[s])
            else:
                sf = stf[mi % 2]
                kb.do('act', lambda e: e.activation(out=sf.ap[:, t0:t0 + n], in_=pb.ap[:, 0:n], func=AF.Copy), R=[pb], W=[sf])
                kb.do('dve', lambda e: e.tensor_scalar(out=s.ap[:, t0:t0 + n], in0=sf.ap[:, t0:t0 + n], scalar1=sc, scalar2=None, op0=ALU.mult),
                      R=[sf], W=[s])
            if ci == nch - 1:
                r0 = row0_of(mi)
                kb.dma('sp', dst_d[r0:r0 + 128, :], s.ap, R=[s], W=[dB])
                if f32r is not None:
                    dd, rr, nr = f32r
                    kb.dma('sp', dd[rr:rr + nr, :], stf[mi % 2].ap[0:nr, :], R=[stf[mi % 2]], W=[Buf(dd)])
        return ep

    def _layer(l):
            norm_phase(Xd, KTD, gains[l, 0], out='norm')
            new_phase()
            nmt = INW // 128

            if KW == 64:
                f32sel = lambda mi: (ZFd, 0, 128) if mi * 128 == AW else None
            else:
                f32sel = lambda mi: (ZFd, mi * 128 - AW, 128) if AW <= mi * 128 < AW + 2 * KW else None
            ep = evac_store(ZBd, lambda mi: mi * 128, BF16, scale_of=lambda mi: 0.125 if mi * 128 < AW else 1.0, also_f32=f32sel)
            go, _ = linear_phase(SRC_D, KTD, w_in[l], [m * 128 for m in range(nmt)], ep, tchunks=full_chunks, fresh=False)
            for mi in range(min(nmt, int(_os.environ.get('LIMIT_MT', '999')))):
                go(mi)
            kb.barrier()
            outtoks.append(kb.dma('sp', kvp_out[l], ZFd[:, SEQ - 128:SEQ]))
            import os
            if not os.environ.get('SKIP_ATT'):
                attention_phase(l)
            if not os.environ.get('SKIP_SSM'):
                ssm_phase(l)
            norm_phase(SSd, NKS, gssm_in[l], out='norm', dst_kt0=KTA)
            new_phase()
            ep = evac_store(Od, lambda mi: mi * 128, F32)
            go, _ = linear_phase(SRC_D, KTD, w_out[l], [m * 128 for m in range(KTD)], ep, tchunks=full_chunks, fresh=False)
            for mi in range(KTD):
                go(mi)
            norm_phase(Od, KTD, gains[l, 1], resid_d=Xd, out='norm', g2_ap=gains[l, 2])
            new_phase()
            stg = [ar.alloc([128, T], BF16) for _ in range(2)]
            sil = [ar.alloc([128, NMAX]) for _ in range(3)]
            aB = Buf(ACTd)

            def ep_gu(mi, ci, t0, n, pb, pb2):
                s = stg[mi % 2]
                sl = sil[nxt('sil', 3)]
                kb.do('act', lambda e: e.activation(out=sl.ap[:, 0:n], in_=pb.ap[:, 0:n], func=AF.Silu), R=[pb], W=[sl])
                kb.do('dve', lambda e: e.tensor_tensor(out=s.ap[:, t0:t0 + n], in0=sl.ap[:, 0:n], in1=pb2.ap[:, 0:n], op=ALU.mult), R=[sl, pb2], W=[s])
                if ci == len(chunks) - 1:
                    kb.dma('sp', ACTd[mi * 128:(mi + 1) * 128, :], s.ap, R=[s], W=[aB])
            go, _ = linear_phase(SRC_D, KTD, w_gu[l], [m * 128 for m in range(KTF)], ep_gu, W2cols=[DFF + m * 128 for m in range(KTF)],
                                 tchunks=full_chunks, fresh=False)
            for mi in range(KTF):
                go(mi)
            half = T // 2
            for hf in range(2):
                new_phase()
                SRC_F = src3(KTF, half)
                kb.dma('sp', SRC_F, ACTd[:, hf * half:(hf + 1) * half].rearrange("(k p) t -> p k t", p=128), W=[srcbuf])
                stg2 = [ar.alloc([128, half]) for _ in range(2)]
                oB = Buf(Od)
                hch = [(t0, n, t0 - hf * half) for (t0, n) in chunks[3 * hf:3 * hf + 3]]

                def ep_dn(mi, ci, t0, n, pb, pb2, stg2=stg2, oB=oB, hf=hf):
                    s = stg2[mi % 2]
                    kb.do('act', lambda e: e.activation(out=s.ap[:, t0 - hf * half:t0 - hf * half + n], in_=pb.ap[:, 0:n], func=AF.Copy), R=[pb], W=[s])
                    if ci == 2:
                        kb.dma('sp', Od[mi * 128:(mi + 1) * 128, hf * half:(hf + 1) * half], s.ap, R=[s], W=[oB])
                go, _ = linear_phase(SRC_F, KTF, w_dn[l], [m * 128 for m in range(KTD)], ep_dn, tchunks=hch, fresh=False)
                for mi in range(KTD):
                    go(mi)
            norm_phase(Od, KTD, gains[l, 3], resid_d=Xd, out='cast')
            new_phase()
            peb = ar.alloc([128, 2, T], BF16)
            for k2 in range(2):
                kb.dma('pool', peb.ap[:, k2, :], peT_in[l, k2 * 128:(k2 + 1) * 128, :], W=[peb], max_dma_last_dim=4096)
            wpp = [ar.alloc([128, 2, 128], BF16) for _ in range(2)]
            xrow = [ar.alloc([128, T]) for _ in range(2)]
            sg = [ar.alloc([128, NMAX]) for _ in range(3)]
            wppv = w_pp[l].rearrange("(k p) n -> p k n", p=128)
            xB = Buf(Xd)

            def ep_ple(mi, ci, t0, n, pb, pb2):
                xr_ = xrow[mi % 2]
                wp_ = wpp[mi % 2]
                if ci == 0:
                    kb.dma('pool', wp_.ap, wppv[:, :, mi * 128:(mi + 1) * 128], W=[wp_])
                    kb.dma('sp', xr_.ap, Xd[mi * 128:(mi + 1) * 128, :], R=[xB], W=[xr_])
                pj = PS[6 + nxt('pj', 2)]
                for kt in range(2):
                    kb.do('pe', lambda e, kt=kt: e.matmul(pj.ap[:, 0:n], lhsT=wp_.ap[:, kt, :], rhs=peb.ap[:, kt, t0:t0 + n], start=(kt == 0), stop=(kt == 1)),
                          R=[wp_, peb], W=[pj])
                s_ = sg[nxt('sg', 3)]
                kb.do('act', lambda e: e.activation(out=s_.ap[:, 0:n], in_=pb.ap[:, 0:n], func=AF.Sigmoid), R=[pb], W=[s_])
                kb.do('dve', lambda e: e.tensor_tensor(out=s_.ap[:, 0:n], in0=s_.ap[:, 0:n], in1=pj.ap[:, 0:n], op=ALU.mult), R=[pj], W=[s_])
                kb.do('dve', lambda e: e.tensor_tensor(out=xr_.ap[:, t0:t0 + n], in0=xr_.ap[:, t0:t0 + n], in1=s_.ap[:, 0:n], op=ALU.add), R=[s_], W=[xr_])
                if ci == len(chunks) - 1:
                    outtoks.append(kb.dma('sp', Xd[mi * 128:(mi + 1) * 128, :], xr_.ap, R=[xr_], W=[xB]))
            go, _ = linear_phase(SRC_D, KTD, w_pg[l], [m * 128 for m in range(KTD)], ep_ple, tchunks=full_chunks, fresh=False)
            for mi in range(KTD):
                go(mi)


    try:
        for l in range(DEPTH):
            _layer(l)
    except _Stop:
        print('stopped after phase', _stop_after)

    kb.barrier()
    for r in range(KTD):
        outtoks.append(kb.dma('sp', yT_out[r * 128:(r + 1) * 128, :], Xd[r * 128:(r + 1) * 128, :]))
    kb.barrier()
    if _os.environ.get('PHASE_LOG'):
        import json as _json
        _json.dump(_marks, open(_os.environ['PHASE_LOG'], 'w'))
    kb.emit()
    return nc


def _consts():
    c = np.zeros((128, 1152), np.float32)
    c[:, 0:128] = np.eye(128, dtype=np.float32)
    i = np.arange(128)[:, None]
    j = np.arange(256)[None, :]
    dist = (i - j + 128).astype(np.float32)
    valid = (dist >= 0) & (dist <= 128)
    dm = np.where(valid, dist, np.float32(BIG)).astype(np.float32)
    c[:, 128:384] = dm
    d0 = dm.copy()
    d0[:, 0:128] = BIG
    c[:, 384:640] = d0
    c[:, 640:1152] = np.arange(1, 513, dtype=np.float32)[None, :]
    return c


def prepare_inputs(C, inp):
    f = np.float32
    A = lambda k: np.asarray(inp[k], f)
    L = C.DEPTH
    NP_, NKS = C.NPAIR, C.NKS
    gains = np.stack([A(k) for k in ('g_pre_mix', 'g_post_mix', 'g_pre_ffn', 'g_post_ffn')], axis=1)
    gains = np.ascontiguousarray(gains.reshape(L, 4, C.KTD, 128).transpose(0, 1, 3, 2))
    gssm = np.ascontiguousarray(A('g_ssm_out').reshape(L, NKS, 128).transpose(0, 2, 1))
    gattn = np.ascontiguousarray(np.broadcast_to(A('g_attn_out')[:, None, :], (L, 128, C.AW)))
    gattnT = np.ascontiguousarray(A('g_attn_out').reshape(L, C.NH, 64).transpose(0, 2, 1))
    sinks = np.ascontiguousarray(np.broadcast_to(A('attn_sinks')[:, None, :], (L, 128, C.NH)))
    are = A('ssm_a_re').reshape(L, NP_, 2, 64).transpose(0, 2, 3, 1).reshape(L, 128, NP_)
    aim = A('ssm_a_im').reshape(L, NP_, 2, 64).transpose(0, 2, 3, 1).reshape(L, 128, NP_)
    ldt = np.broadcast_to(A('ssm_log_dt').reshape(L, NP_, 2, 1), (L, NP_, 2, 64)).transpose(0, 2, 3, 1).reshape(L, 128, NP_)
    ssm_ps = np.ascontiguousarray(np.stack([are, aim, ldt], axis=2))
    bpad = np.zeros((L, 128, 2, NP_, 128), f)
    for ri, key in enumerate(('ssm_b_re', 'ssm_b_im')):
        b = A(key)
        for g in range(C.NG):
            q, gh, g8 = g // 2, g % 2, g % 8
            bpad[:, g8 * 16:(g8 + 1) * 16, ri, q, gh * 64:(gh + 1) * 64] = b[:, g].transpose(0, 2, 1)
    cpad = np.zeros((L, 128, NKS, 2, 4, 128), f)
    for ri, key in enumerate(('ssm_c_re', 'ssm_c_im')):
        c = A(key)
        for g in range(C.NG):
            kt, pr, gh, g8 = g // 8, (g % 8) // 2, g % 2, g % 8
            cpad[:, gh * 64:(gh + 1) * 64, kt, ri, pr, g8 * 16:(g8 + 1) * 16] = c[:, g].transpose(0, 2, 1)
    dcol = np.ascontiguousarray(A('ssm_d').reshape(L, NKS, 128).transpose(0, 2, 1))
    wglu = np.zeros((L, 128, NKS, 2, 128), f)
    wg = A('ssm_w_glu')
    for g in range(C.NG):
        kt, g8 = g // 8, g % 8
        for hf in range(2):
            wglu[:, g8 * 16:(g8 + 1) * 16, kt, hf, g8 * 16:(g8 + 1) * 16] = wg[:, g, :, hf * 16:(hf + 1) * 16]
    shared = dict(
        w_in=A('w_in'), w_out=A('w_out'), w_gate_up=A('w_gate_up'), w_down=A('w_down'), w_ple_gate=A('w_ple_gate'),
        w_ple_proj=A('w_ple_proj'), gains=gains, gssm=gssm, gattn=gattn, gattnT=gattnT, sinks=sinks, ssm_ps=ssm_ps,
        bpad=bpad, cpad=cpad, dcol=dcol, wglu=wglu, consts=_consts())
    xp, xs = A('x_prompt'), A('x_sample')
    pp, psm = A('p_prompt'), A('p_sample')
    ck, cv = A('cache_k'), A('cache_v')
    sre, sim = A('state_ssm_re'), A('state_ssm_im')
    in_maps = []
    NS = C.NS
    for c in range(8):
        b = c % C.BATCH
        sl = slice(NS * b, NS * (b + 1))
        m = dict(shared)
        m['xT'] = np.ascontiguousarray(np.concatenate([xp[b].T, xs[sl, 0].T], axis=1))
        m['peT'] = np.ascontiguousarray(np.concatenate([pp[:, b].transpose(0, 2, 1), psm[:, sl, 0].transpose(0, 2, 1)], axis=2))
        ckc = ck[:, sl].reshape(L, NS, 128, C.KW)
        m['ck'] = np.ascontiguousarray(ckc)
        m['cv'] = np.ascontiguousarray(cv[:, sl].reshape(L, NS, 128, C.KW))
        m['ckT'] = np.ascontiguousarray(ckc.reshape(L, NS, 128, C.NKV, 64).transpose(0, 1, 3, 4, 2))
        st = np.stack([sre[:, sl], sim[:, sl]], axis=1)
        st = st.reshape(L, 2, NS, NP_, 2, 64).transpose(0, 4, 5, 1, 3, 2).reshape(L, 128, 2, NP_, NS)
        m['st0'] = np.ascontiguousarray(st)
        in_maps.append(m)
    return in_maps


def assemble(C, results):
    f = np.float32
    L, NS, B = C.DEPTH, C.NS, C.BATCH
    yp = np.zeros((B, C.SEQ, C.D), f)
    ys = np.zeros((C.DEC, 1, C.D), f)
    kp = np.zeros((L, B, 128, C.NKV, 64), f)
    vp = np.zeros_like(kp)
    srp = np.zeros((L, B, C.NG, 64), f)
    sip = np.zeros_like(srp)
    ksm = np.zeros((L, C.DEC, 128, C.NKV, 64), f)
    vsm = np.zeros_like(ksm)
    srs = np.zeros((L, C.DEC, C.NG, 64), f)
    sis = np.zeros_like(srs)
    for b in range(B):
        r = results[b]
        sl = slice(NS * b, NS * (b + 1))
        yT = r['yT']
        yp[b] = yT[:, :C.SEQ].T
        ys[sl, 0] = yT[:, C.SEQ:].T
        kv = r['kvp']
        kp[:, b] = kv[:, :C.KW].transpose(0, 2, 1).reshape(L, 128, C.NKV, 64)
        vp[:, b] = kv[:, C.KW:].transpose(0, 2, 1).reshape(L, 128, C.NKV, 64)
        ksm[:, sl] = r['ks'].reshape(L, NS, 128, C.NKV, 64)
        vsm[:, sl] = r['vs'].reshape(L, NS, 128, C.NKV, 64)
        st = r['st'].reshape(L, 2, 64, 2, C.NPAIR, 1 + NS)
        st = st.transpose(0, 3, 5, 4, 1, 2).reshape(L, 2, 1 + NS, C.NG, 64)
        srp[:, b], sip[:, b] = st[:, 0, 0], st[:, 1, 0]
        srs[:, sl], sis[:, sl] = st[:, 0, 1:], st[:, 1, 1:]
    return (yp, ys, kp, vp, srp, sip, ksm, vsm, srs, sis)


_CACHE = {}


def run(C, inp):
    key = (C.D, C.SEQ, C.DEPTH, C.BATCH, C.DEC)
    if key not in _CACHE:
        _CACHE[key] = build(C)
    in_maps = prepare_inputs(C, inp)
    res = run_bass_kernel_spmd(_CACHE[key], in_maps, core_ids=list(range(8)))
    return assemble(C, res.results)


def kernel(**inputs):
    return run(Cfg(), inputs)
```

```python
import math
import numpy as np
import concourse.bass as bass
import concourse.mybir as mybir
from concourse.bass_utils import run_bass_kernel_spmd

F32 = mybir.dt.float32
BF16 = mybir.dt.bfloat16
I32 = mybir.dt.int32
AF = mybir.ActivationFunctionType
ALU = mybir.AluOpType
AX = mybir.AxisListType

ENGS = ('pe', 'act', 'dve', 'pool', 'sp')
SEM_ROT = 20000
NDMASEM = 56
EPS = 1e-6
BIG = 1.0e9
TWO_PI = 6.283184
GELU_C = 1.5957691216057308
MAGIC = 12582912.0


class Cfg:
    def __init__(self, D=2048, SEQ=2048, DEPTH=4, BATCH=4, DEC=32):
        self.D, self.SEQ, self.DEPTH, self.BATCH, self.DEC = D, SEQ, DEPTH, BATCH, DEC
        self.NS = DEC // BATCH
        self.T = SEQ + self.NS
        self.AW = D // 2
        self.NH = self.AW // 64
        self.NKV = max(1, self.NH // 8)
        self.GRP = self.NH // self.NKV
        self.KW = self.NKV * 64
        self.SW = D - self.AW
        self.NG = self.SW // 16
        self.NKS = self.SW // 128
        self.NPAIR = self.NG // 2
        self.INW = self.AW + 2 * self.KW + self.SW
        self.DFF = -(-8 * D // (3 * 256)) * 256
        self.KTD = D // 128
        self.KTA = self.AW // 128
        self.KTF = self.DFF // 128
        self.NQB = SEQ // 128
        half = self.T // 2
        assert self.T % 2 == 0
        a = -(-half // 3)
        ch = []
        for h0 in (0, half):
            o = h0
            for i in range(3):
                n = min(a, h0 + half - o)
                ch.append((o, n))
                o += n
        self.chunks = ch
        self.NMAX = a
        assert a <= 512
        self.TQ = 256 if SEQ >= 256 else SEQ
        self.slopes = [2.0 ** (-8.0 * (h + 1) / self.NH) for h in range(self.NH)]


class Buf:
    def __init__(self, ap, excl=False):
        self.ap = ap
        self.w = {}
        self.r = {}
        self.excl = excl


def _upd(d, tok):
    k = tok[0].num
    if k not in d or d[k][1] < tok[1]:
        d[k] = tok


class _Rec:
    def __init__(self):
        self.call = None

    def __getattr__(self, name):
        def f(*a, **k):
            assert self.call is None
            self.call = (name, a, k)
            return self
        return f


class KB:
    def __init__(self, nc):
        self.nc = nc
        self.ops = {e: [] for e in ENGS}
        self.cur = {}
        self.nsem = 0
        self.waited = {}
        self.ndma = 0
        self.dpool = []
        for e in ENGS:
            self._new_sem(e)
        for i in range(NDMASEM):
            self.dpool.append([nc.alloc_semaphore("dq%d" % i), 0])

    def _new_sem(self, e):
        self.nsem += 1
        self.cur[e] = [self.nc.alloc_semaphore("p_%s_%d" % (e, self.nsem)), 0]
        if not hasattr(self, 'owner'):
            self.owner = {}
        self.owner[self.cur[e][0].num] = e

    def _waits(self, eng, deps):
        waits = []
        for d in deps:
            sem, val = d
            if eng == 'pe' and self.owner.get(sem.num) == 'pe':
                continue
            key = (eng, sem.num)
            if self.waited.get(key, 0) >= val:
                continue
            self.waited[key] = val
            waits.append((sem, val))
        return waits

    def _deps(self, R, W):
        deps = []
        for b in R:
            deps.extend(b.w.values())
            if b.excl:
                deps.extend(b.r.values())
        for b in W:
            deps.extend(b.w.values())
            deps.extend(b.r.values())
        return deps

    def _record(self, tok, R, W):
        for b in R:
            _upd(b.r, tok)
        for b in W:
            _upd(b.w, tok)

    def do(self, eng, fn, R=(), W=()):
        rec = _Rec()
        fn(rec)
        if getattr(self, 'buf', None) is not None:
            self.buf.append(('do', eng, rec.call, list(R), list(W)))
            return None
        return self._do(eng, rec.call, R, W)

    def begin_buffer(self):
        self.buf = []

    def mark(self, name):
        if getattr(self, 'buf', None) is not None:
            self.buf.append(('mark', name, None, None, None))

    def end_buffer(self):
        b, self.buf = self.buf, None
        return b

    def flush_interleaved(self, lists):
        m = max(len(x) for x in lists)
        for k in range(m):
            for x in lists:
                if k < len(x):
                    kind, eng, call, R, W = x[k]
                    if kind == 'do':
                        self._do(eng, call, R, W)

    def _do(self, eng, call, R=(), W=()):
        waits = self._waits(eng, self._deps(R, W))
        if self.cur[eng][1] >= SEM_ROT:
            self._new_sem(eng)
        c = self.cur[eng]
        c[1] += 1
        name, a, k = call
        self.ops[eng].append((waits, (lambda e, name=name, a=a, k=k: getattr(e, name)(*a, **k)), c[0], 1))
        tok = (c[0], c[1])
        self._record(tok, R, W)
        return tok

    def dma(self, eng, out, in_, R=(), W=(), **kw):
        waits = self._waits(eng, self._deps(R, W))
        ds = self.dpool[self.ndma % NDMASEM]
        self.ndma += 1
        ds[1] += 16
        self.ops[eng].append((waits, lambda e: e.dma_start(out=out, in_=in_, **kw), ds[0], 16))
        tok = (ds[0], ds[1])
        self._record(tok, R, W)
        return tok

    def wait_only(self, eng, deps):
        waits = self._waits(eng, deps)
        if waits:
            self.ops[eng].append((waits, None, None, 0))

    def barrier(self):
        toks = [(c[0], c[1]) for c in self.cur.values() if c[1] > 0]
        toks += [(d[0], d[1]) for d in self.dpool if d[1] > 0]
        for e in ENGS:
            self.wait_only(e, toks)

    def emit(self):
        nc = self.nc
        with nc.Block() as block:
            def run(name):
                def f(e):
                    for waits, fn, sem, inc in self.ops[name]:
                        for (s, v) in waits:
                            e.wait_ge(s, v)
                        if fn is not None:
                            fn(e).then_inc(sem, inc)
                return f
            block.tensor(run('pe'))
            block.scalar(run('act'))
            block.vector(run('dve'))
            block.gpsimd(run('pool'))
            block.sync(run('sp'))


class Arena:
    LO = 16512
    HI = 229376

    def __init__(self, nc):
        self.nc = nc
        self.cur = self.LO
        self.n = 0

    def alloc(self, shape, dt=F32):
        esz = 2 if dt == BF16 else 4
        nb = esz
        for s in shape[1:]:
            nb *= s
        nb = (nb + 63) // 64 * 64
        assert self.cur + nb <= self.HI, ("SBUF overflow", self.cur, nb)
        self.n += 1
        t = self.nc.alloc_sbuf_tensor_at("sb%d" % self.n, list(shape), dt, offset=self.cur)
        self.cur += nb
        return Buf(t.ap())


def build(cfg):
    C = cfg
    nc = bass.Bass("TRN2", target_bir_lowering=False)
    kb = KB(nc)
    ar = Arena(nc)
    D, T, SEQ, NS, DEPTH = C.D, C.T, C.SEQ, C.NS, C.DEPTH
    AW, NH, NKV, KW, SW, NKS, NPAIR, INW, DFF = C.AW, C.NH, C.NKV, C.KW, C.SW, C.NKS, C.NPAIR, C.INW, C.DFF
    KTD, KTA, KTF, NQB = C.KTD, C.KTA, C.KTF, C.NQB
    chunks, NMAX, TQ = C.chunks, C.NMAX, C.TQ
    KTMAX = max(KTF, KTD)

    def din(name, shape):
        return nc.dram_tensor(name, list(shape), F32, kind="ExternalInput").ap()

    def dout(name, shape):
        return nc.dram_tensor(name, list(shape), F32, kind="ExternalOutput").ap()

    def dscr(name, shape, dt=F32):
        return nc.dram_tensor(name, list(shape), dt).ap()

    xT_in = din("xT", [D, T])
    peT_in = din("peT", [DEPTH, 256, T])
    w_in = din("w_in", [DEPTH, D, INW])
    w_out = din("w_out", [DEPTH, D, D])
    w_gu = din("w_gate_up", [DEPTH, D, 2 * DFF])
    w_dn = din("w_down", [DEPTH, DFF, D])
    w_pg = din("w_ple_gate", [DEPTH, D, D])
    w_pp = din("w_ple_proj", [DEPTH, 256, D])
    gains = din("gains", [DEPTH, 4, 128, KTD])
    gssm_in = din("gssm", [DEPTH, 128, NKS])
    gattn_in = din("gattn", [DEPTH, 128, AW])
    gattnT_in = din("gattnT", [DEPTH, 64, NH])
    sinks_in = din("sinks", [DEPTH, 128, NH])
    ckT_in = din("ckT", [DEPTH, NS, NKV, 64, 128])
    ck_in = din("ck", [DEPTH, NS, 128, KW])
    cv_in = din("cv", [DEPTH, NS, 128, KW])
    ssm_ps_in = din("ssm_ps", [DEPTH, 128, 3, NPAIR])
    bpad_in = din("bpad", [DEPTH, 128, 2, NPAIR, 128])
    cpad_in = din("cpad", [DEPTH, 128, NKS, 2, 4, 128])
    dcol_in = din("dcol", [DEPTH, 128, NKS])
    wglu_in = din("wglu", [DEPTH, 128, NKS, 2, 128])
    st0_in = din("st0", [DEPTH, 128, 2, NPAIR, NS])
    consts_in = din("consts", [128, 128 + 256 + 256 + 512])

    yT_out = dout("yT", [D, T])
    kvp_out = dout("kvp", [DEPTH, 2 * KW, 128])
    ks_out = dout("ks", [DEPTH, NS, 128, KW])
    vs_out = dout("vs", [DEPTH, NS, 128, KW])
    st_out = dout("st", [DEPTH, 128, 2, NPAIR, 1 + NS])

    Xd = dscr("X", [D, T])
    ZBd = dscr("ZB", [INW, T], BF16)
    ZFd = dscr("ZF", [2 * KW, T])
    Od = dscr("O", [D, T])
    SSd = dscr("SS", [SW, T])
    ACTd = dscr("ACTs", [DFF, T], BF16)
    MGSd = dscr("MGS", [AW, NS], BF16)

    ident = ar.alloc([128, 128], BF16)
    ones = ar.alloc([128, 128], BF16)
    DIST = ar.alloc([128, 256])
    DIST0 = ar.alloc([128, 256])
    IOT = ar.alloc([128, 512])
    srcbuf = ar.alloc([128, max(KTD * T, KTF * (T // 2))], BF16)
    phase_mark = ar.cur

    PS = [Buf(nc.alloc_psum_tensor("ps%d" % i, [128, 512], F32).ap(), excl=True) for i in range(8)]

    def src3(kt_n, tn):
        return srcbuf.ap[:, 0:kt_n * tn].rearrange("p (k t) -> p k t", t=tn)

    SRC_D = src3(KTD, T)

    outtoks = []

    class _Stop(Exception):
        pass
    import os as _os
    _stop_after = int(_os.environ.get('STOP_AFTER', '100000'))
    _pc = [0]

    _marks = []

    def new_phase(name=None):
        import inspect
        if name is None:
            name = inspect.stack()[1].function + ':' + str(inspect.stack()[1].lineno)
        _marks.append((name, sum(1 for o in kb.ops['pe'] if o[1] is not None)))
        _pc[0] += 1
        if _pc[0] > _stop_after:
            raise _Stop()
        kb.barrier()
        ar.cur = phase_mark

    kb.dma('pool', ident.ap, consts_in[:, 0:128], W=[ident])
    kb.dma('sp', DIST.ap, consts_in[:, 128:384], W=[DIST])
    kb.dma('sp', DIST0.ap, consts_in[:, 384:640], W=[DIST0])
    kb.dma('sp', IOT.ap, consts_in[:, 640:1152], W=[IOT])
    kb.do('pool', lambda e: e.memset(ones.ap, 1.0), W=[ones])
    xw = Buf(Xd)
    for r in range(KTD):
        kb.dma('sp', Xd[r * 128:(r + 1) * 128, :], xT_in[r * 128:(r + 1) * 128, :], W=[xw])

    rot = {}

    def nxt(key, n):
        rot[key] = (rot.get(key, -1) + 1) % n
        return rot[key]

    def norm_phase(src_d, KT, g1_ap, resid_d=None, out=None, g2_ap=None, dst_kt0=0, Fdim=None):
        new_phase()
        Fdim = KT * 128
        g1 = ar.alloc([128, KT])
        kb.dma('sp', g1.ap, g1_ap, W=[g1])
        g2 = None
        if g2_ap is not None:
            g2 = ar.alloc([128, KT])
            kb.dma('sp', g2.ap, g2_ap, W=[g2])
        xin = [ar.alloc([128, KT, NMAX]) for _ in range(2)]
        xr = [ar.alloc([128, KT, NMAX]) for _ in range(2)] if resid_d is not None else None
        sq = [ar.alloc([128, NMAX], BF16) for _ in range(3)]
        rs = [ar.alloc([128, NMAX]) for _ in range(2)]
        tmp = [ar.alloc([128, NMAX]) for _ in range(3)]
        sv = src_d.rearrange("(k p) t -> p k t", p=128)
        xv = resid_d.rearrange("(k p) t -> p k t", p=128) if resid_d is not None else None
        srcB = Buf(src_d)
        psA, psB = PS[6], PS[7]

        def rstd_of(buf_in, n, psb, rsb):
            for kt in range(KT):
                s = sq[nxt('sq', 3)]
                kb.do('act', lambda e, kt=kt, s=s: e.activation(out=s.ap[:, 0:n], in_=buf_in.ap[:, kt, 0:n], func=AF.Square),
                      R=[buf_in], W=[s])
                kb.do('pe', lambda e, kt=kt, s=s: e.matmul(psb.ap[:, 0:n], lhsT=ones.ap, rhs=s.ap[:, 0:n],
                                                           start=(kt == 0), stop=(kt == KT - 1)), R=[s, ones], W=[psb])
            kb.do('dve', lambda e: e.tensor_scalar(out=rsb.ap[:, 0:n], in0=psb.ap[:, 0:n], scalar1=1.0 / Fdim, scalar2=EPS,
                                                   op0=ALU.mult, op1=ALU.add), R=[psb], W=[rsb])
            kb.do('act', lambda e: e.activation(out=rsb.ap[:, 0:n], in_=rsb.ap[:, 0:n], func=AF.Ln), R=[rsb], W=[rsb])
            kb.do('act', lambda e: e.activation(out=rsb.ap[:, 0:n], in_=rsb.ap[:, 0:n], func=AF.Exp, scale=-0.5), R=[rsb], W=[rsb])

        for ci, (t0, n) in enumerate(chunks):
            xi = xin[ci % 2]
            kb.dma('sp', xi.ap[:, :, 0:n], sv[:, :, t0:t0 + n], R=[srcB], W=[xi])
            r1 = rs[0]
            rstd_of(xi, n, psA, r1)
            if resid_d is None:
                for kt in range(KT):
                    kb.do('dve', lambda e, kt=kt: e.scalar_tensor_tensor(
                        out=SRC_D[:, dst_kt0 + kt, t0:t0 + n], in0=xi.ap[:, kt, 0:n], scalar=g1.ap[:, kt:kt + 1],
                        in1=r1.ap[:, 0:n], op0=ALU.mult, op1=ALU.mult), R=[xi, g1, r1], W=[srcbuf])
                continue
            xx = xr[ci % 2]
            xB = Buf(resid_d)
            kb.dma('sp', xx.ap[:, :, 0:n], xv[:, :, t0:t0 + n], R=[xB], W=[xx])
            for kt in range(KT):
                tb = tmp[nxt('tmp', 3)]
                kb.do('dve', lambda e, kt=kt, tb=tb: e.scalar_tensor_tensor(
                    out=tb.ap[:, 0:n], in0=xi.ap[:, kt, 0:n], scalar=g1.ap[:, kt:kt + 1], in1=r1.ap[:, 0:n],
                    op0=ALU.mult, op1=ALU.mult), R=[xi, g1, r1], W=[tb])
                kb.do('dve', lambda e, kt=kt, tb=tb: e.tensor_tensor(out=xx.ap[:, kt, 0:n], in0=xx.ap[:, kt, 0:n],
                                                                      in1=tb.ap[:, 0:n], op=ALU.add), R=[tb], W=[xx])
            outtoks.append(kb.dma('sp', xv[:, :, t0:t0 + n], xx.ap[:, :, 0:n], R=[xx], W=[xB]))
            if out == 'norm':
                r2 = rs[1]
                rstd_of(xx, n, psB, r2)
                for kt in range(KT):
                    kb.do('dve', lambda e, kt=kt: e.scalar_tensor_tensor(
                        out=SRC_D[:, dst_kt0 + kt, t0:t0 + n], in0=xx.ap[:, kt, 0:n], scalar=g2.ap[:, kt:kt + 1],
                        in1=r2.ap[:, 0:n], op0=ALU.mult, op1=ALU.mult), R=[xx, g2, r2], W=[srcbuf])
            elif out == 'cast':
                kb.do('act', lambda e: e.activation(out=SRC_D[:, dst_kt0:dst_kt0 + KT, t0:t0 + n], in_=xx.ap[:, :, 0:n],
                                                    func=AF.Copy), R=[xx], W=[srcbuf])

    def linear_phase(srcv, KT, W_ap, cols, epi, W2cols=None, tchunks=None, fresh=True):
        if fresh:
            new_phase()
        wb = [ar.alloc([128, KT, 128], BF16) for _ in range(3)]
        wb2 = [ar.alloc([128, KT, 128], BF16) for _ in range(2)] if W2cols is not None else None
        wv = W_ap.rearrange("(k p) n -> p k n", p=128)
        tch = tchunks if tchunks is not None else chunks
        st = {}

        def go(mi):
            c0 = cols[mi]
            w = wb[mi % 3]
            kb.dma('pool', w.ap, wv[:, :, c0:c0 + 128], W=[w])
            w2 = None
            if W2cols is not None:
                w2 = wb2[mi % 2]
                kb.dma('pool', w2.ap, wv[:, :, W2cols[mi]:W2cols[mi] + 128], W=[w2])
            for ci, (t0, n, s0) in enumerate(tch):
                if W2cols is None:
                    pb = PS[nxt('lin', 6)]
                    pb2 = None
                else:
                    j = nxt('lin2', 3)
                    pb, pb2 = PS[2 * j], PS[2 * j + 1]
                for kt in range(KT):
                    kb.do('pe', lambda e, kt=kt, pb=pb, w=w: e.matmul(pb.ap[:, 0:n], lhsT=w.ap[:, kt, :], rhs=srcv[:, kt, s0:s0 + n],
                                                                   start=(kt == 0), stop=(kt == KT - 1)), R=[w, srcbuf], W=[pb])
                if pb2 is not None:
                    for kt in range(KT):
                        kb.do('pe', lambda e, kt=kt, pb2=pb2, w2=w2: e.matmul(pb2.ap[:, 0:n], lhsT=w2.ap[:, kt, :], rhs=srcv[:, kt, s0:s0 + n],
                                                                         start=(kt == 0), stop=(kt == KT - 1)), R=[w2, srcbuf], W=[pb2])
                epi(mi, ci, t0, n, pb, pb2)
        return go, st

    full_chunks = [(t0, n, t0) for (t0, n) in chunks]

    def attention_phase(l):
        new_phase()
        kTd = ar.alloc([128, NKV, 128 + T], BF16)
        Vtm = ar.alloc([128, NQB + 1, KW], BF16)
        vTi = [ar.alloc([128, 128], BF16) for _ in range(2)]
        qTb = [ar.alloc([128, KTA, 128], BF16) for _ in range(2)]
        atm = [ar.alloc([128, AW]) for _ in range(2)]
        hn = [ar.alloc([128, AW], BF16) for _ in range(2)]
        junk = ar.alloc([128, AW], BF16)
        sm2 = [ar.alloc([128, 4]) for _ in range(2)]
        gat = ar.alloc([128, AW])
        snk = ar.alloc([128, NH])
        zb = Buf(ZBd)
        kb.dma('sp', gat.ap, gattn_in[l], W=[gat])
        kb.dma('sp', snk.ap, sinks_in[l], W=[snk])
        kb.do('pool', lambda e: e.memset(kTd.ap[:, :, 0:128], 0.0), W=[kTd])
        kb.do('pool', lambda e: e.memset(Vtm.ap[:, 0, :], 0.0), W=[Vtm])
        for g in range(NKV):
            for cp in range(2):
                kb.dma('sp', kTd.ap[cp * 64:(cp + 1) * 64, g, 128:128 + T], ZBd[AW + g * 64:AW + (g + 1) * 64, :], R=[zb], W=[kTd])
        PSb = [Buf(PS[i].ap.bitcast(BF16), excl=True) for i in range(8)]
        for i in range(8):
            PSb[i].w, PSb[i].r = PS[i].w, PS[i].r
        for b in range(NQB):
            vi = vTi[b % 2]
            kb.dma('sp', vi.ap[0:KW, :], ZBd[AW + KW:AW + 2 * KW, b * 128:(b + 1) * 128], R=[zb], W=[vi])
            pt = PSb[7]
            kb.do('pe', lambda e, vi=vi, pt=pt: e.transpose(out=pt.ap[:, 0:KW], in_=vi.ap[0:KW, :], identity=ident.ap[0:KW, 0:KW]),
                  R=[vi, ident], W=[pt])
            kb.do('act', lambda e, b=b, pt=pt: e.activation(out=Vtm.ap[:, b + 1, :], in_=pt.ap[:, 0:KW], func=AF.Copy), R=[pt], W=[Vtm])

        NRR = 8
        Sb = [ar.alloc([128, 256]) for _ in range(NRR)]
        Pb = [ar.alloc([128, 256], BF16) for _ in range(NRR)]
        PTs = [ar.alloc([128, 2, 128], BF16) for _ in range(NRR)]
        sm = [ar.alloc([128, 8]) for _ in range(NRR)]

        def pipeline(units, stages, after=None):
            ns = len(stages)
            for t in range(len(units) + ns - 1):
                for si, st in enumerate(stages):
                    ui = t - si
                    if 0 <= ui < len(units):
                        st(units[ui])
                        if si == ns - 1 and after is not None:
                            after(ui)

        def sA(u):
            S_, m_, npart, nk, h, sps = Sb[u['i']], sm[u['i']], u['np'], u['nk'], u['h'], u['sps']
            kb.do('dve', lambda e: e.scalar_tensor_tensor(out=S_.ap[0:npart, 0:nk], in0=u['dist'], scalar=-C.slopes[h],
                                                          in1=sps.ap[0:npart, 0:nk], op0=ALU.mult, op1=ALU.add),
                  R=[sps, DIST, DIST0], W=[S_])
            kb.do('dve', lambda e: e.reduce_max(out=m_.ap[0:npart, 0:1], in_=S_.ap[0:npart, 0:nk], axis=AX.X), R=[S_], W=[m_])
            kb.do('dve', lambda e: e.tensor_tensor(out=m_.ap[0:npart, 0:1], in0=m_.ap[0:npart, 0:1], in1=snk.ap[0:npart, h:h + 1],
                                                   op=ALU.max), R=[snk], W=[m_])
            kb.do('dve', lambda e: e.tensor_scalar(out=m_.ap[0:npart, 1:2], in0=m_.ap[0:npart, 0:1], scalar1=-1.0, scalar2=None,
                                                   op0=ALU.mult), R=[], W=[m_])

        def sB(u):
            S_, P_, m_, npart, nk, h = Sb[u['i']], Pb[u['i']], sm[u['i']], u['np'], u['nk'], u['h']
            kb.do('act', lambda e: e.activation(out=P_.ap[0:npart, 0:nk], in_=S_.ap[0:npart, 0:nk], func=AF.Exp,
                                                bias=m_.ap[0:npart, 1:2], scale=1.0, accum_out=m_.ap[0:npart, 2:3]),
                  R=[S_, m_], W=[P_, m_])
            kb.do('act', lambda e: e.activation(out=m_.ap[0:npart, 3:4], in_=snk.ap[0:npart, h:h + 1], func=AF.Exp,
                                                bias=m_.ap[0:npart, 1:2], scale=1.0), R=[snk, m_], W=[m_])

        def sC(u):
            m_, npart = sm[u['i']], u['np']
            kb.do('dve', lambda e: e.tensor_tensor(out=m_.ap[0:npart, 4:5], in0=m_.ap[0:npart, 2:3], in1=m_.ap[0:npart, 3:4],
                                                   op=ALU.add), R=[m_], W=[m_])
            kb.do('dve', lambda e: e.reciprocal(out=m_.ap[0:npart, 5:6], in_=m_.ap[0:npart, 4:5]), R=[m_], W=[m_])

        units = []
        for b in range(NQB):
            for h in range(NH):
                units.append(dict(b=b, h=h, g=h // C.GRP, hp=(h % 2) * 64, np=128, nk=256,
                                  dist=(DIST0 if b == 0 else DIST).ap))

        def pA(u):
            b, h, g, hp = u['b'], u['h'], u['g'], u['hp']
            if h == 0:
                qb = qTb[b % 2]
                kb.dma('sp', qb.ap, ZBd[0:AW, b * 128:(b + 1) * 128].rearrange("(k p) t -> p k t", p=128), R=[zb], W=[qb])
            qb = qTb[b % 2]
            u['i'] = nxt('att', NRR)
            u['sps'] = sps = PS[nxt('sps', 3)]
            kb.do('pe', lambda e: e.matmul(sps.ap[:, 0:256], lhsT=qb.ap[hp:hp + 64, h // 2, :],
                                           rhs=kTd.ap[hp:hp + 64, g, b * 128:b * 128 + 256], start=True, stop=True), R=[qb, kTd], W=[sps])
            sA(u)

        def pC(u):
            sC(u)
            i = u['i']
            u['ptp'] = ptp = PSb[3 + nxt('ptp', 2)]
            for j in range(2):
                kb.do('pe', lambda e, j=j: e.transpose(out=ptp.ap[:, j * 128:(j + 1) * 128], in_=Pb[i].ap[:, j * 128:(j + 1) * 128],
                                                       identity=ident.ap), R=[Pb[i], ident], W=[ptp])

        def pD(u):
            i, ptp = u['i'], u['ptp']
            kb.do('act', lambda e: e.activation(out=PTs[i].ap, in_=ptp.ap[:, 0:256].rearrange("p (j q) -> p j q", j=2), func=AF.Copy),
                  R=[ptp], W=[PTs[i]])

        def pE(u):
            i, b, g = u['i'], u['b'], u['g']
            u['ops'] = ops_ = PS[5 + nxt('ops', 2)]
            for j in range(2):
                kb.do('pe', lambda e, j=j: e.matmul(ops_.ap[:, 0:64], lhsT=PTs[i].ap[:, j, :], rhs=Vtm.ap[:, b + j, g * 64:(g + 1) * 64],
                                                    start=(j == 0), stop=(j == 1)), R=[PTs[i], Vtm], W=[ops_])

        def pF(u):
            i, h, ops_ = u['i'], u['h'], u['ops']
            am = atm[u['b'] % 2]
            kb.do('dve', lambda e: e.tensor_scalar(out=am.ap[:, h * 64:(h + 1) * 64], in0=ops_.ap[:, 0:64], scalar1=sm[i].ap[:, 5:6],
                                                   scalar2=None, op0=ALU.mult), R=[ops_, sm[i]], W=[am])

        def block_done(ui):
            u = units[ui]
            if u['h'] != NH - 1:
                return
            b = u['b']
            am, s2, hb = atm[b % 2], sm2[b % 2], hn[b % 2]
            kb.do('act', lambda e: e.activation(out=junk.ap, in_=am.ap, func=AF.Square, accum_out=s2.ap[:, 0:1]), R=[am], W=[junk, s2])
            kb.do('dve', lambda e: e.tensor_scalar(out=s2.ap[:, 1:2], in0=s2.ap[:, 0:1], scalar1=1.0 / AW, scalar2=EPS,
                                                   op0=ALU.mult, op1=ALU.add), R=[s2], W=[s2])
            kb.do('act', lambda e: e.activation(out=s2.ap[:, 1:2], in_=s2.ap[:, 1:2], func=AF.Ln), R=[s2], W=[s2])
            kb.do('act', lambda e: e.activation(out=s2.ap[:, 1:2], in_=s2.ap[:, 1:2], func=AF.Exp, scale=-0.5), R=[s2], W=[s2])
            kb.do('dve', lambda e: e.scalar_tensor_tensor(out=hb.ap, in0=am.ap, scalar=s2.ap[:, 1:2], in1=gat.ap, op0=ALU.mult, op1=ALU.mult),
                  R=[am, s2, gat], W=[hb])
            for kt in range(KTA):
                pt = PSb[7]
                kb.do('pe', lambda e, kt=kt: e.transpose(out=pt.ap[:, 0:128], in_=hb.ap[:, kt * 128:(kt + 1) * 128], identity=ident.ap),
                      R=[hb, ident], W=[pt])
                kb.do('act', lambda e, kt=kt: e.activation(out=SRC_D[:, kt, b * 128:(b + 1) * 128], in_=pt.ap[:, 0:128], func=AF.Copy),
                      R=[pt], W=[srcbuf])

        pipeline(units, [pA, sB, pC, pD, pE, pF], after=block_done)

        kS = [ar.alloc([128, NKV, 132], BF16) for _ in range(2)]
        vS = [ar.alloc([128, KW], BF16) for _ in range(2)]
        vN = [ar.alloc([1, KW], BF16) for _ in range(2)]
        qS = ar.alloc([128, KTA, NS], BF16)
        PnS = [ar.alloc([1, 132], BF16) for _ in range(NRR)]
        PTS = [ar.alloc([128, 2], BF16) for _ in range(NRR)]
        aS = ar.alloc([64, NH, NS])
        zf = Buf(ZFd)
        kb.dma('sp', qS.ap, ZBd[0:AW, SEQ:SEQ + NS].rearrange("(k p) t -> p k t", p=128), R=[zb], W=[qS])
        sunits = []
        for n in range(NS):
            for h in range(NH):
                sunits.append(dict(n=n, h=h, g=h // C.GRP, hp=(h % 2) * 64, np=1, nk=129, dist=DIST.ap[0:1, 0:129]))

        def qA(u):
            n, h, g, hp = u['n'], u['h'], u['g'], u['hp']
            ks_, vs_, vn_ = kS[n % 2], vS[n % 2], vN[n % 2]
            if h == 0:
                for g_ in range(NKV):
                    for cp in range(2):
                        kb.dma('pool', ks_.ap[cp * 64:(cp + 1) * 64, g_, 0:128], ckT_in[l, n, g_], W=[ks_])
                kb.do('pool', lambda e: e.tensor_copy(out=ks_.ap[:, :, 128:129], in_=kTd.ap[:, :, 128 + SEQ + n:128 + SEQ + n + 1]),
                      R=[kTd], W=[ks_])
                kb.dma('pool', vs_.ap, cv_in[l, n], W=[vs_])
                kb.dma('pool', vn_.ap, ZFd[KW:2 * KW, SEQ + n:SEQ + n + 1].rearrange("k o -> o k"), R=[zf], W=[vn_], allow_slow_non_contiguous=True)
                outtoks.append(kb.dma('sp', ks_out[l, n, 0:127, :], ck_in[l, n, 1:128, :]))
                outtoks.append(kb.dma('sp', vs_out[l, n, 0:127, :], cv_in[l, n, 1:128, :]))
                outtoks.append(kb.dma('sp', ks_out[l, n, 127:128, :], ZFd[0:KW, SEQ + n:SEQ + n + 1].rearrange("k o -> o k"), R=[zf], allow_slow_non_contiguous=True))
                outtoks.append(kb.dma('sp', vs_out[l, n, 127:128, :], ZFd[KW:2 * KW, SEQ + n:SEQ + n + 1].rearrange("k o -> o k"), R=[zf], allow_slow_non_contiguous=True))
            u['i'] = nxt('att', NRR)
            u['sps'] = sps = PS[nxt('sps', 3)]
            kb.do('pe', lambda e: e.matmul(sps.ap[0:1, 0:129], lhsT=qS.ap[hp:hp + 64, h // 2, n:n + 1], rhs=ks_.ap[hp:hp + 64, g, 0:129],
                                           start=True, stop=True), R=[qS, ks_], W=[sps])
            sA(u)

        def qC(u):
            sC(u)
            i = u['i']
            pn = PnS[i]
            kb.do('dve', lambda e: e.tensor_scalar(out=pn.ap[0:1, 0:129], in0=Pb[i].ap[0:1, 0:129], scalar1=sm[i].ap[0:1, 5:6], scalar2=None,
                                                   op0=ALU.mult), R=[Pb[i], sm[i]], W=[pn])
            u['ptp'] = ptp = PSb[3 + nxt('ptp', 2)]
            kb.do('pe', lambda e: e.transpose(out=ptp.ap[:, 0:1], in_=pn.ap[0:1, 0:128], identity=ident.ap[0:1, 0:1]), R=[pn, ident], W=[ptp])

        def qD(u):
            i, ptp = u['i'], u['ptp']
            kb.do('act', lambda e: e.activation(out=PTS[i].ap[:, 0:1], in_=ptp.ap[:, 0:1], func=AF.Copy), R=[ptp], W=[PTS[i]])

        def qE(u):
            i, g, n = u['i'], u['g'], u['n']
            vs_, vn_, pn = vS[n % 2], vN[n % 2], PnS[i]
            u['ops'] = ops_ = PS[5 + nxt('ops', 2)]
            kb.do('pe', lambda e: e.matmul(ops_.ap[0:64, 0:1], lhsT=vs_.ap[:, g * 64:(g + 1) * 64], rhs=PTS[i].ap[:, 0:1], start=True, stop=False),
                  R=[PTS[i], vs_], W=[ops_])
            kb.do('pe', lambda e: e.matmul(ops_.ap[0:64, 0:1], lhsT=vn_.ap[0:1, g * 64:(g + 1) * 64], rhs=pn.ap[0:1, 128:129], start=False, stop=True),
                  R=[pn, vn_], W=[ops_])

        def qF(u):
            ops_, h, n = u['ops'], u['h'], u['n']
            kb.do('act', lambda e: e.activation(out=aS.ap[:, h, n:n + 1], in_=ops_.ap[0:64, 0:1], func=AF.Copy), R=[ops_], W=[aS])

        if not _os.environ.get('PIPE_SAMPLE'):
            for u_ in sunits:
                for st_ in (qA, sB, qC, qD, qE, qF):
                    st_(u_)
        else:
            pipeline(sunits, [qA, sB, qC, qD, qE, qF])
        sqS = ar.alloc([64, NH * NS], BF16)
        ssS = ar.alloc([64, NS])
        gT = ar.alloc([64, NH])
        hS = ar.alloc([64, NH, NS])
        hSb = ar.alloc([64, NH, NS], BF16)
        kb.dma('sp', gT.ap, gattnT_in[l], W=[gT])
        kb.do('act', lambda e: e.activation(out=sqS.ap, in_=aS.ap.rearrange("p h n -> p (h n)"), func=AF.Square), R=[aS], W=[sqS])
        kb.do('pe', lambda e: e.matmul(PS[7].ap[0:64, 0:NH * NS], lhsT=ones.ap[0:64, 0:64], rhs=sqS.ap, start=True, stop=True),
              R=[sqS, ones], W=[PS[7]])
        kb.do('dve', lambda e: e.tensor_reduce(out=ssS.ap, in_=PS[7].ap[0:64, 0:NH * NS].rearrange("p (h n) -> p n h", n=NS),
                                               axis=AX.X, op=ALU.add), R=[PS[7]], W=[ssS])
        kb.do('dve', lambda e: e.tensor_scalar(out=ssS.ap, in0=ssS.ap, scalar1=1.0 / AW, scalar2=EPS, op0=ALU.mult, op1=ALU.add),
              R=[ssS], W=[ssS])
        kb.do('act', lambda e: e.activation(out=ssS.ap, in_=ssS.ap, func=AF.Ln), R=[ssS], W=[ssS])
        kb.do('act', lambda e: e.activation(out=ssS.ap, in_=ssS.ap, func=AF.Exp, scale=-0.5), R=[ssS], W=[ssS])
        kb.do('dve', lambda e: e.tensor_tensor(out=hS.ap, in0=aS.ap, in1=gT.ap.unsqueeze(2).broadcast_to([64, NH, NS]), op=ALU.mult),
              R=[aS, gT], W=[hS])
        kb.do('dve', lambda e: e.tensor_tensor(out=hSb.ap, in0=hS.ap, in1=ssS.ap.unsqueeze(1).broadcast_to([64, NH, NS]), op=ALU.mult),
              R=[hS, ssS], W=[hSb])
        mg = Buf(MGSd)
        kb.dma('sp', MGSd.rearrange("(h d) n -> d h n", d=64), hSb.ap, R=[hSb], W=[mg])
        kb.dma('sp', SRC_D[:, 0:KTA, SEQ:SEQ + NS], MGSd.rearrange("(k p) n -> p k n", p=128), R=[mg], W=[srcbuf])

    def ssm_phase(l):
        new_phase()
        ps_ = ar.alloc([128, 3, NPAIR])
        kb.dma('sp', ps_.ap, ssm_ps_in[l], W=[ps_])
        NV = 17
        v = ar.alloc([128, NV, NPAIR])
        vi = ar.alloc([128, NPAIR], I32)
        are, aim, ldt = ps_.ap[:, 0, :], ps_.ap[:, 1, :], ps_.ap[:, 2, :]
        V_DT, V_DRE, V_TH, V_R, V_A, V_SIN, V_COS, V_ABR, V_ABI, V_FR, V_FI, V_IFR, V_IFI, V_T1, V_T2, V_T3, V_NFI = range(17)

        def vv(i):
            return v.ap[:, i, :]

        def tiny(eng, fn):
            kb.do(eng, fn, R=[v, ps_], W=[v])

        def wrap_turns(eng_ap_in, out_i):
            kb.do('dve', lambda e: e.tensor_copy(out=vi.ap, in_=eng_ap_in), R=[v], W=[vi])
            kb.do('dve', lambda e: e.tensor_copy(out=vv(V_T1), in_=vi.ap), R=[vi, v], W=[v])
            tiny('dve', lambda e: e.tensor_tensor(out=vv(out_i), in0=eng_ap_in, in1=vv(V_T1), op=ALU.subtract))
            tiny('dve', lambda e: e.tensor_scalar(out=vv(V_T1), in0=vv(out_i), scalar1=0.5, scalar2=None, op0=ALU.is_gt))
            tiny('dve', lambda e: e.tensor_tensor(out=vv(out_i), in0=vv(out_i), in1=vv(V_T1), op=ALU.subtract))
            tiny('dve', lambda e: e.tensor_scalar(out=vv(V_T1), in0=vv(out_i), scalar1=-0.5, scalar2=None, op0=ALU.is_lt))
            tiny('dve', lambda e: e.tensor_tensor(out=vv(out_i), in0=vv(out_i), in1=vv(V_T1), op=ALU.add))

        tiny('act', lambda e: e.activation(out=vv(V_DT), in_=ldt, func=AF.Exp))
        tiny('dve', lambda e: e.tensor_tensor(out=vv(V_DRE), in0=vv(V_DT), in1=are, op=ALU.mult))
        tiny('dve', lambda e: e.tensor_tensor(out=vv(V_TH), in0=vv(V_DT), in1=aim, op=ALU.mult))
        tiny('act', lambda e: e.activation(out=vv(V_R), in_=vv(V_DRE), func=AF.Exp))
        tiny('dve', lambda e: e.tensor_scalar(out=vv(V_T2), in0=vv(V_TH), scalar1=1.0 / (2 * math.pi), scalar2=None, op0=ALU.mult))
        wrap_turns(vv(V_T2), V_A)
        tiny('act', lambda e: e.activation(out=vv(V_SIN), in_=vv(V_A), func=AF.Sin, scale=TWO_PI))
        tiny('dve', lambda e: e.tensor_scalar(out=vv(V_T2), in0=vv(V_A), scalar1=0.25, scalar2=None, op0=ALU.add))
        wrap_turns(vv(V_T2), V_T3)
        tiny('act', lambda e: e.activation(out=vv(V_COS), in_=vv(V_T3), func=AF.Sin, scale=TWO_PI))
        tiny('dve', lambda e: e.tensor_tensor(out=vv(V_ABR), in0=vv(V_R), in1=vv(V_COS), op=ALU.mult))
        tiny('dve', lambda e: e.tensor_tensor(out=vv(V_ABI), in0=vv(V_R), in1=vv(V_SIN), op=ALU.mult))
        tiny('dve', lambda e: e.tensor_tensor(out=vv(V_T1), in0=are, in1=are, op=ALU.mult))
        tiny('dve', lambda e: e.tensor_tensor(out=vv(V_T2), in0=aim, in1=aim, op=ALU.mult))
        tiny('dve', lambda e: e.tensor_tensor(out=vv(V_T1), in0=vv(V_T1), in1=vv(V_T2), op=ALU.add))
        tiny('dve', lambda e: e.reciprocal(out=vv(V_T1), in_=vv(V_T1)))
        tiny('dve', lambda e: e.tensor_scalar(out=vv(V_T2), in0=vv(V_ABR), scalar1=-1.0, scalar2=None, op0=ALU.add))
        tiny('dve', lambda e: e.tensor_tensor(out=vv(V_FR), in0=vv(V_T2), in1=are, op=ALU.mult))
        tiny('dve', lambda e: e.tensor_tensor(out=vv(V_T3), in0=vv(V_ABI), in1=aim, op=ALU.mult))
        tiny('dve', lambda e: e.tensor_tensor(out=vv(V_FR), in0=vv(V_FR), in1=vv(V_T3), op=ALU.add))
        tiny('dve', lambda e: e.tensor_tensor(out=vv(V_FR), in0=vv(V_FR), in1=vv(V_T1), op=ALU.mult))
        tiny('dve', lambda e: e.tensor_tensor(out=vv(V_FI), in0=vv(V_ABI), in1=are, op=ALU.mult))
        tiny('dve', lambda e: e.tensor_tensor(out=vv(V_T3), in0=vv(V_T2), in1=aim, op=ALU.mult))
        tiny('dve', lambda e: e.tensor_tensor(out=vv(V_FI), in0=vv(V_FI), in1=vv(V_T3), op=ALU.subtract))
        tiny('dve', lambda e: e.tensor_tensor(out=vv(V_FI), in0=vv(V_FI), in1=vv(V_T1), op=ALU.mult))
        tiny('dve', lambda e: e.tensor_tensor(out=vv(V_T1), in0=vv(V_FR), in1=vv(V_FR), op=ALU.mult))
        tiny('dve', lambda e: e.tensor_tensor(out=vv(V_T2), in0=vv(V_FI), in1=vv(V_FI), op=ALU.mult))
        tiny('dve', lambda e: e.tensor_tensor(out=vv(V_T1), in0=vv(V_T1), in1=vv(V_T2), op=ALU.add))
        tiny('dve', lambda e: e.reciprocal(out=vv(V_T1), in_=vv(V_T1)))
        tiny('dve', lambda e: e.tensor_tensor(out=vv(V_IFR), in0=vv(V_FR), in1=vv(V_T1), op=ALU.mult))
        tiny('dve', lambda e: e.tensor_tensor(out=vv(V_IFI), in0=vv(V_FI), in1=vv(V_T1), op=ALU.mult))
        tiny('dve', lambda e: e.tensor_scalar(out=vv(V_IFI), in0=vv(V_IFI), scalar1=-1.0, scalar2=None, op0=ALU.mult))
        tiny('dve', lambda e: e.tensor_scalar(out=vv(V_NFI), in0=vv(V_FI), scalar1=-1.0, scalar2=None, op0=ALU.mult))

        bpk = [ar.alloc([128, 2, 4, 128], BF16) for _ in range(2)]
        wg = ar.alloc([128, NKS, 2, 128], BF16)
        kb.dma('pool', wg.ap, wglu_in[l], W=[wg], max_dma_last_dim=4096)
        dc = ar.alloc([128, NKS])
        kb.dma('sp', dc.ap, dcol_in[l], W=[dc])
        st0 = ar.alloc([128, 2, NPAIR, NS])
        kb.dma('sp', st0.ap, st0_in[l], W=[st0])
        sto = ar.alloc([128, 2, NPAIR, 1 + NS])
        uT = [ar.alloc([128, T], BF16) for _ in range(1)]
        cpf = [ar.alloc([128, 2, 4, 128]) for _ in range(1)]
        cf = [ar.alloc([128, 2, 4, 128], BF16) for _ in range(2)]
        cft = ar.alloc([128, 128])
        RT = [ar.alloc([128, TQ]) for _ in range(4)]
        NW = 4
        AT = [ar.alloc([128, TQ]) for _ in range(NW)]
        A2 = [ar.alloc([128, TQ]) for _ in range(NW)]
        NF = [ar.alloc([128, TQ]) for _ in range(NW)]
        NA = [ar.alloc([128, TQ]) for _ in range(NW)]
        CS = [ar.alloc([128, TQ]) for _ in range(NW)]
        SN = [ar.alloc([128, TQ]) for _ in range(NW)]
        PW1 = [ar.alloc([128, TQ]) for _ in range(2)]
        PW2 = [ar.alloc([128, TQ]) for _ in range(2)]
        THO = ar.alloc([128, SEQ // TQ, NPAIR])
        HPI = ar.alloc([128, 1])
        kb.do('dve', lambda e: e.memset(HPI.ap, math.pi / 2), W=[HPI])
        for tq_ in range(SEQ // TQ):
            kb.do('dve', lambda e, tq_=tq_: e.tensor_scalar(out=THO.ap[:, tq_, :], in0=vv(V_A), scalar1=float(tq_ * TQ), scalar2=None, op0=ALU.mult),
                  R=[v], W=[THO])
        W1 = [ar.alloc([128, TQ]) for _ in range(NW)]
        W2 = [ar.alloc([128, TQ]) for _ in range(NW)]
        XR = [ar.alloc([128, TQ]) for _ in range(NW)]
        XI = [ar.alloc([128, TQ]) for _ in range(NW)]
        SR = [ar.alloc([128, TQ]) for _ in range(NW)]
        SI = [ar.alloc([128, TQ]) for _ in range(NW)]
        SB2R = [ar.alloc([128, TQ], BF16) for _ in range(4)]
        SB2I = [ar.alloc([128, TQ], BF16) for _ in range(4)]
        carry = ar.alloc([128, 2, NPAIR])
        aoff = ar.alloc([128, NPAIR])
        FIN = [ar.alloc([128, 8]) for _ in range(2)]
        ssbb = ar.alloc([128, 2, 4, NS], BF16)
        sw = ar.alloc([128, 3, 4, NS])
        craw = ar.alloc([128, 2, 4, 128], BF16)
        TS = ar.alloc([128, 2, NPAIR, NS])
        tsw = ar.alloc([128, 2, NPAIR, NS])
        abrb = v.ap[:, V_ABR, :].unsqueeze(2).broadcast_to([128, NPAIR, NS])
        abib = v.ap[:, V_ABI, :].unsqueeze(2).broadcast_to([128, NPAIR, NS])
        kb.do('dve', lambda e: e.tensor_tensor(out=TS.ap[:, 0], in0=st0.ap[:, 0], in1=abrb, op=ALU.mult), R=[st0, v], W=[TS])
        kb.do('dve', lambda e: e.tensor_tensor(out=tsw.ap[:, 0], in0=st0.ap[:, 1], in1=abib, op=ALU.mult), R=[st0, v], W=[tsw])
        kb.do('dve', lambda e: e.tensor_tensor(out=TS.ap[:, 0], in0=TS.ap[:, 0], in1=tsw.ap[:, 0], op=ALU.subtract), R=[tsw], W=[TS])
        kb.do('dve', lambda e: e.tensor_tensor(out=TS.ap[:, 1], in0=st0.ap[:, 0], in1=abib, op=ALU.mult), R=[st0, v], W=[TS])
        kb.do('dve', lambda e: e.tensor_tensor(out=tsw.ap[:, 1], in0=st0.ap[:, 1], in1=abrb, op=ALU.mult), R=[st0, v], W=[tsw])
        kb.do('dve', lambda e: e.tensor_tensor(out=TS.ap[:, 1], in0=TS.ap[:, 1], in1=tsw.ap[:, 1], op=ALU.add), R=[tsw], W=[TS])
        yst = [ar.alloc([128, T]) for _ in range(1)]
        EY = ar.alloc([128, T])
        E2 = [ar.alloc([128, 512]) for _ in range(1)]
        GB = [ar.alloc([128, 512], BF16) for _ in range(1)]
        zb = Buf(ZBd)
        ssB = Buf(SSd)
        kb.do('pool', lambda e: e.memset(carry.ap, 0.0), W=[carry])
        NTQ = SEQ // TQ

        deferred = []

        def run_deferred():
            for lst in deferred:
                kb.flush_interleaved(lst)
            del deferred[:]

        for kt in range(NKS):
            u = uT[kt % len(uT)]
            kb.dma('sp', u.ap, ZBd[AW + 2 * KW + kt * 128:AW + 2 * KW + (kt + 1) * 128, :], R=[zb], W=[u])
            cp_, cf_ = cpf[0], cf[kt % 2]
            bp = bpk[kt % 2]
            for ri_ in range(2):
                kb.dma('pool', bp.ap[:, ri_], bpad_in[l, :, ri_, kt * 4:(kt + 1) * 4, :], W=[bp])
            kb.dma('sp', cp_.ap, cpad_in[l, :, kt], W=[cp_])
            kb.do('act', lambda e: e.activation(out=craw.ap[:, 0], in_=cp_.ap[:, 0], func=AF.Copy), R=[cp_], W=[craw])
            kb.do('act', lambda e: e.activation(out=craw.ap[:, 1], in_=cp_.ap[:, 1], func=AF.Copy, scale=-1.0), R=[cp_], W=[craw])
            for pr in range(4):
                q = kt * 4 + pr
                fr, fi = v.ap[:, V_FR, q:q + 1], v.ap[:, V_FI, q:q + 1]
                nfi = v.ap[:, V_NFI, q:q + 1]
                kb.do('dve', lambda e, pr=pr, fi=fi: e.tensor_scalar(out=cft.ap, in0=cp_.ap[:, 1, pr, :], scalar1=fi, scalar2=None, op0=ALU.mult),
                      R=[cp_, v], W=[cft])
                kb.do('dve', lambda e, pr=pr, fr=fr: e.scalar_tensor_tensor(out=cf_.ap[:, 0, pr, :], in0=cp_.ap[:, 0, pr, :], scalar=fr, in1=cft.ap,
                                                                         op0=ALU.mult, op1=ALU.subtract), R=[cp_, v, cft], W=[cf_])
                kb.do('dve', lambda e, pr=pr, fr=fr: e.tensor_scalar(out=cft.ap, in0=cp_.ap[:, 1, pr, :], scalar1=fr, scalar2=-1.0, op0=ALU.mult,
                                                                  op1=ALU.mult), R=[cp_, v, cf_], W=[cft])
                kb.do('dve', lambda e, pr=pr, nfi=nfi: e.scalar_tensor_tensor(out=cf_.ap[:, 1, pr, :], in0=cp_.ap[:, 0, pr, :], scalar=nfi, in1=cft.ap,
                                                                           op0=ALU.mult, op1=ALU.add), R=[cp_, v, cft], W=[cf_])
            ys = yst[kt % len(yst)]
            for tq in range(NTQ + 1):
                samp = (tq == NTQ)
                t0 = tq * TQ
                n = NS if samp else TQ
                ypb = PS[4 + nxt('ypb', 2)]
                pend = []
                if samp:
                    run_deferred()
                    xs_ = PS[0]
                    for pr in range(4):
                        for ri in range(2):
                            c0_ = (ri * 4 + pr) * NS
                            kb.do('pe', lambda e, ri=ri, pr=pr, c0_=c0_: e.matmul(xs_.ap[:, c0_:c0_ + NS], lhsT=bp.ap[:, ri, pr, :], rhs=u.ap[:, t0:t0 + NS],
                                                                               start=True, stop=True), R=[bp, u], W=[xs_])
                    xv_ = xs_.ap[:, 0:8 * NS].rearrange("p (r q n) -> p r q n", r=2, q=4)
                    frb = v.ap[:, V_FR, kt * 4:kt * 4 + 4].unsqueeze(2).broadcast_to([128, 4, NS])
                    fib = v.ap[:, V_FI, kt * 4:kt * 4 + 4].unsqueeze(2).broadcast_to([128, 4, NS])
                    wr, wi, wt_ = sw.ap[:, 0], sw.ap[:, 1], sw.ap[:, 2]
                    kb.do('dve', lambda e: e.tensor_tensor(out=wr, in0=xv_[:, 0], in1=frb, op=ALU.mult), R=[xs_, v], W=[sw])
                    kb.do('dve', lambda e: e.tensor_tensor(out=wt_, in0=xv_[:, 1], in1=fib, op=ALU.mult), R=[xs_, v], W=[sw])
                    kb.do('dve', lambda e: e.tensor_tensor(out=wr, in0=wr, in1=wt_, op=ALU.subtract), R=[], W=[sw])
                    kb.do('dve', lambda e: e.tensor_tensor(out=wi, in0=xv_[:, 0], in1=fib, op=ALU.mult), R=[xs_, v], W=[sw])
                    kb.do('dve', lambda e: e.tensor_tensor(out=wt_, in0=xv_[:, 1], in1=frb, op=ALU.mult), R=[xs_, v], W=[sw])
                    kb.do('dve', lambda e: e.tensor_tensor(out=wi, in0=wi, in1=wt_, op=ALU.add), R=[], W=[sw])
                    kb.do('dve', lambda e: e.tensor_tensor(out=sto.ap[:, 0, kt * 4:kt * 4 + 4, 1:1 + NS], in0=wr, in1=TS.ap[:, 0, kt * 4:kt * 4 + 4, :], op=ALU.add),
                          R=[sw, TS], W=[sto])
                    kb.do('dve', lambda e: e.tensor_tensor(out=sto.ap[:, 1, kt * 4:kt * 4 + 4, 1:1 + NS], in0=wi, in1=TS.ap[:, 1, kt * 4:kt * 4 + 4, :], op=ALU.add),
                          R=[sw, TS], W=[sto])
                    kb.do('act', lambda e: e.activation(out=ssbb.ap, in_=sto.ap[:, :, kt * 4:kt * 4 + 4, 1:1 + NS], func=AF.Copy), R=[sto], W=[ssbb])
                for pr in range(4):
                    q = kt * 4 + pr
                    if not samp:
                        kb.begin_buffer()
                        xps_r, xps_i = PS[2 * nxt('xps', 2)], None
                        xps_i = PS[PS.index(xps_r) + 1]
                        for ri, xp in ((0, xps_r), (1, xps_i)):
                            kb.do('pe', lambda e, ri=ri, xp=xp, q=q: e.matmul(xp.ap[:, 0:n], lhsT=bp.ap[:, ri, pr, :], rhs=u.ap[:, t0:t0 + n],
                                                                           start=True, stop=True), R=[bp, u], W=[xp])
                    _k = nxt('sb2', 4)
                    s2r, s2i = SB2R[_k], SB2I[_k]
                    if samp:
                        rhs_r, rhs_i = ssbb.ap[:, 0, pr, :], ssbb.ap[:, 1, pr, :]
                        rd = [ssbb]
                    else:
                        fin = FIN[pr % 2]
                        w = nxt('ssw', NW)
                        w1, w2, xr_, xi_, sr_, si_ = W1[w], W2[w], XR[w], XI[w], SR[w], SI[w]
                        tw = w
                        pw1, pw2 = PW1[pr % 2], PW2[pr % 2]
                        at, a2, nf, na, cs, sn = AT[tw], A2[tw], NF[tw], NA[tw], CS[tw], SN[tw]
                        rt = RT[pr]
                        if tq == 0:
                            kb.do('act', lambda e, rt=rt, q=q: e.activation(out=rt.ap, in_=IOT.ap[:, 0:TQ], func=AF.Identity, scale=0.0,
                                                                           bias=v.ap[:, V_R, q:q + 1]), R=[IOT, v], W=[rt])
                        kb.do('act', lambda e, at=at, q=q: e.activation(out=at.ap, in_=IOT.ap[:, 0:TQ], func=AF.Identity, scale=v.ap[:, V_A, q:q + 1],
                                                                       bias=THO.ap[:, tq, q:q + 1]), R=[IOT, v, THO], W=[at])
                        kb.do('dve', lambda e, at=at, a2=a2: e.tensor_scalar(out=a2.ap, in0=at.ap, scalar1=MAGIC, scalar2=None, op0=ALU.add), R=[at], W=[a2])
                        kb.do('dve', lambda e, at=at, a2=a2, nf=nf: e.scalar_tensor_tensor(out=nf.ap, in0=a2.ap, scalar=MAGIC, in1=at.ap, op0=ALU.subtract,
                                                                                       op1=ALU.subtract), R=[at, a2], W=[nf])
                        kb.do('act', lambda e, nf=nf, sn=sn: e.activation(out=sn.ap, in_=nf.ap, func=AF.Sin, scale=-TWO_PI), R=[nf], W=[sn])
                        kb.do('act', lambda e, nf=nf, na=na: e.activation(out=na.ap, in_=nf.ap, func=AF.Sin, scale=-TWO_PI / 2), R=[nf], W=[na])
                        kb.do('act', lambda e, na=na: e.activation(out=na.ap, in_=na.ap, func=AF.Square), R=[], W=[na])
                        kb.do('act', lambda e, na=na, cs=cs: e.activation(out=cs.ap, in_=na.ap, func=AF.Identity, scale=-2.0, bias=1.0), R=[na], W=[cs])
                        kb.mark('M')
                        kb.do('dve', lambda e, cs=cs, w1=w1: e.tensor_tensor(out=w1.ap, in0=cs.ap, in1=xps_r.ap[:, 0:n], op=ALU.mult), R=[cs, xps_r], W=[w1])
                        kb.do('dve', lambda e, sn=sn, w2=w2: e.tensor_tensor(out=w2.ap, in0=sn.ap, in1=xps_i.ap[:, 0:n], op=ALU.mult), R=[sn, xps_i], W=[w2])
                        kb.do('dve', lambda e, w1=w1, w2=w2, xr_=xr_: e.tensor_tensor(out=xr_.ap, in0=w1.ap, in1=w2.ap, op=ALU.add), R=[w1, w2], W=[xr_])
                        kb.do('dve', lambda e, cs=cs, w1=w1: e.tensor_tensor(out=w1.ap, in0=cs.ap, in1=xps_i.ap[:, 0:n], op=ALU.mult), R=[cs, xps_i, xr_], W=[w1])
                        kb.do('dve', lambda e, sn=sn, w2=w2: e.tensor_tensor(out=w2.ap, in0=sn.ap, in1=xps_r.ap[:, 0:n], op=ALU.mult), R=[sn, xps_r, xr_], W=[w2])
                        kb.do('dve', lambda e, w1=w1, w2=w2, xi_=xi_: e.tensor_tensor(out=xi_.ap, in0=w1.ap, in1=w2.ap, op=ALU.subtract), R=[w1, w2], W=[xi_])
                        kb.do('dve', lambda e, rt=rt, xr_=xr_, sr_=sr_, q=q: e.tensor_tensor_scan(out=sr_.ap, data0=rt.ap, data1=xr_.ap,
                                                                                             initial=carry.ap[:, 0, q:q + 1], op0=ALU.mult, op1=ALU.add),
                              R=[rt, xr_, carry], W=[sr_])
                        kb.do('dve', lambda e, rt=rt, xi_=xi_, si_=si_, q=q: e.tensor_tensor_scan(out=si_.ap, data0=rt.ap, data1=xi_.ap,
                                                                                             initial=carry.ap[:, 1, q:q + 1], op0=ALU.mult, op1=ALU.add),
                              R=[rt, xi_, carry], W=[si_])
                        kb.do('dve', lambda e, sr_=sr_, q=q: e.tensor_copy(out=carry.ap[:, 0, q:q + 1], in_=sr_.ap[:, TQ - 1:TQ]), R=[sr_], W=[carry])
                        kb.do('dve', lambda e, si_=si_, q=q: e.tensor_copy(out=carry.ap[:, 1, q:q + 1], in_=si_.ap[:, TQ - 1:TQ]), R=[si_], W=[carry])
                        kb.mark('R')
                        kb.do('pool', lambda e: e.tensor_tensor(out=pw1.ap, in0=cs.ap, in1=sr_.ap, op=ALU.mult), R=[cs, sr_], W=[pw1])
                        kb.do('pool', lambda e: e.tensor_tensor(out=pw2.ap, in0=sn.ap, in1=si_.ap, op=ALU.mult), R=[sn, si_], W=[pw2])
                        kb.do('pool', lambda e: e.tensor_tensor(out=s2r.ap, in0=pw1.ap, in1=pw2.ap, op=ALU.subtract), R=[pw1, pw2], W=[s2r])
                        if tq == NTQ - 1:
                            kb.do('dve', lambda e: e.tensor_tensor(out=fin.ap[:, 0:1], in0=pw1.ap[:, TQ - 1:TQ], in1=pw2.ap[:, TQ - 1:TQ],
                                                                   op=ALU.subtract), R=[pw1, pw2], W=[fin])
                        kb.do('pool', lambda e: e.tensor_tensor(out=pw1.ap, in0=cs.ap, in1=si_.ap, op=ALU.mult), R=[cs, si_], W=[pw1])
                        kb.do('pool', lambda e: e.tensor_tensor(out=pw2.ap, in0=sn.ap, in1=sr_.ap, op=ALU.mult), R=[sn, sr_], W=[pw2])
                        kb.do('pool', lambda e: e.tensor_tensor(out=s2i.ap, in0=pw1.ap, in1=pw2.ap, op=ALU.add), R=[pw1, pw2], W=[s2i])
                        if tq == NTQ - 1:
                            fr, fi = v.ap[:, V_FR, q:q + 1], v.ap[:, V_FI, q:q + 1]
                            kb.do('dve', lambda e, w1=pw1, w2=pw2: e.tensor_tensor(out=fin.ap[:, 1:2], in0=w1.ap[:, TQ - 1:TQ], in1=w2.ap[:, TQ - 1:TQ],
                                                                               op=ALU.add), R=[pw1, pw2], W=[fin])
                            kb.do('dve', lambda e, fi=fi: e.tensor_scalar(out=fin.ap[:, 2:3], in0=fin.ap[:, 1:2], scalar1=fi, scalar2=None, op0=ALU.mult), R=[v], W=[fin])
                            kb.do('dve', lambda e, fr=fr, q=q: e.scalar_tensor_tensor(out=sto.ap[:, 0, q, 0:1], in0=fin.ap[:, 0:1], scalar=fr, in1=fin.ap[:, 2:3],
                                                                                   op0=ALU.mult, op1=ALU.subtract), R=[v, fin], W=[sto])
                            kb.do('dve', lambda e, fr=fr: e.tensor_scalar(out=fin.ap[:, 2:3], in0=fin.ap[:, 1:2], scalar1=fr, scalar2=None, op0=ALU.mult), R=[v], W=[fin])
                            kb.do('dve', lambda e, fi=fi, q=q: e.scalar_tensor_tensor(out=sto.ap[:, 1, q, 0:1], in0=fin.ap[:, 0:1], scalar=fi, in1=fin.ap[:, 2:3],
                                                                                   op0=ALU.mult, op1=ALU.add), R=[v, fin], W=[sto])
                        rhs_r, rhs_i = s2r.ap, s2i.ap
                        rd = [s2r, s2i]
                    cw_ = craw if samp else cf_
                    kb.do('pe', lambda e, pr=pr, rhs_r=rhs_r, ypb=ypb: e.matmul(ypb.ap[:, 0:n], lhsT=cw_.ap[:, 0, pr, :], rhs=rhs_r,
                                                                              start=(pr == 0), stop=False), R=[cw_] + rd, W=[ypb])
                    kb.do('pe', lambda e, pr=pr, rhs_i=rhs_i, ypb=ypb: e.matmul(ypb.ap[:, 0:n], lhsT=cw_.ap[:, 1, pr, :], rhs=rhs_i,
                                                                              start=False, stop=(pr == 3)), R=[cw_] + rd, W=[ypb])
                    if not samp:
                        pend.append(kb.end_buffer())
                        if len(pend) == 2:
                            def _split(x):
                                iM = [k_ for k_, o_ in enumerate(x) if o_[0] == 'mark' and o_[1] == 'M'][0]
                                iR = [k_ for k_, o_ in enumerate(x) if o_[0] == 'mark' and o_[1] == 'R'][0]
                                return x[:iM], x[iM:iR], x[iR:]
                            parts = [_split(x) for x in pend]
                            kb.flush_interleaved([p_[0] for p_ in parts])
                            run_deferred()
                            kb.flush_interleaved([p_[1] for p_ in parts])
                            deferred.append([p_[2] for p_ in parts])
                            pend = []
                if not samp:
                    kb.begin_buffer()
                kb.do('dve', lambda e, ypb=ypb: e.scalar_tensor_tensor(out=EY.ap[:, t0:t0 + n], in0=u.ap[:, t0:t0 + n], scalar=dc.ap[:, kt:kt + 1],
                                                                     in1=ypb.ap[:, 0:n], op0=ALU.mult, op1=ALU.add), R=[u, dc, ypb], W=[EY])
                if not samp:
                    deferred.append([kb.end_buffer()])
            run_deferred()
            pieces = [(c0_, min(512, T - c0_)) for c0_ in range(0, T, 512)]
            for (c0_, n_) in pieces:
                e2, gb = E2[0], GB[0]
                kb.do('act', lambda e: e.activation(out=e2.ap[:, 0:n_], in_=EY.ap[:, c0_:c0_ + n_], func=AF.Square), R=[EY], W=[e2])
                kb.do('dve', lambda e: e.tensor_scalar(out=e2.ap[:, 0:n_], in0=e2.ap[:, 0:n_], scalar1=0.044715, scalar2=1.0, op0=ALU.mult, op1=ALU.add),
                      R=[], W=[e2])
                kb.do('dve', lambda e: e.tensor_tensor(out=e2.ap[:, 0:n_], in0=e2.ap[:, 0:n_], in1=EY.ap[:, c0_:c0_ + n_], op=ALU.mult), R=[EY], W=[e2])
                kb.do('act', lambda e: e.activation(out=e2.ap[:, 0:n_], in_=e2.ap[:, 0:n_], func=AF.Sigmoid, scale=GELU_C), R=[], W=[e2])
                kb.do('dve', lambda e: e.tensor_tensor(out=gb.ap[:, 0:n_], in0=e2.ap[:, 0:n_], in1=EY.ap[:, c0_:c0_ + n_], op=ALU.mult), R=[EY, e2], W=[gb])
                z1, z2 = PS[6], PS[7]
                kb.do('pe', lambda e: e.matmul(z1.ap[:, 0:n_], lhsT=wg.ap[:, kt, 0, :], rhs=gb.ap[:, 0:n_], start=True, stop=True), R=[wg, gb], W=[z1])
                kb.do('pe', lambda e: e.matmul(z2.ap[:, 0:n_], lhsT=wg.ap[:, kt, 1, :], rhs=gb.ap[:, 0:n_], start=True, stop=True), R=[wg, gb], W=[z2])
                kb.do('act', lambda e: e.activation(out=e2.ap[:, 0:n_], in_=z2.ap[:, 0:n_], func=AF.Sigmoid), R=[z2], W=[e2])
                kb.do('dve', lambda e: e.tensor_tensor(out=ys.ap[:, c0_:c0_ + n_], in0=e2.ap[:, 0:n_], in1=z1.ap[:, 0:n_], op=ALU.mult), R=[e2, z1], W=[ys])
            kb.dma('sp', SSd[kt * 128:(kt + 1) * 128, :], ys.ap, R=[ys], W=[ssB])
        outtoks.append(kb.dma('sp', st_out[l], sto.ap, R=[sto]))

    def evac_store(dst_d, row0_of, dt, scale_of=None, also_f32=None):
        stg = [ar.alloc([128, T], dt) for _ in range(2)]
        stf = [ar.alloc([128, T]) for _ in range(2)] if also_f32 is not None else None
        dB = Buf(dst_d)
        nch = len(chunks)

        def ep(mi, ci, t0, n, pb, pb2):
            s = stg[mi % 2]
            sc = 1.0 if scale_of is None else scale_of(mi)
            f32r = also_f32(mi) if also_f32 is not None else None
            if f32r is None:
                kb.do('act', lambda e: e.activation(out=s.ap[:, t0:t0 + n], in_=pb.ap[:, 0:n], func=AF.Copy, scale=sc), R=[pb], W=[s])
            else:
                sf = stf[mi % 2]
                kb.do('act', lambda e: e.activation(out=sf.ap[:, t0:t0 + n], in_=pb.ap[:, 0:n], func=AF.Copy), R=[pb], W=[sf])
                kb.do('dve', lambda e: e.tensor_scalar(out=s.ap[:, t0:t0 + n], in0=sf.ap[:, t0:t0 + n], scalar1=sc, scalar2=None, op0=ALU.mult),
                      R=[sf], W=[s])
            if ci == nch - 1:
                r0 = row0_of(mi)
                kb.dma('sp', dst_d[r0:r0 + 128, :], s.ap, R=[s], W=[dB])
                if f32r is not None:
                    dd, rr, nr = f32r
                    kb.dma('sp', dd[rr:rr + nr, :], stf[mi % 2].ap[0:nr, :], R=[stf[mi % 2]], W=[Buf(dd)])
        return ep

    def _layer(l):
            norm_phase(Xd, KTD, gains[l, 0], out='norm')
            new_phase()
            nmt = INW // 128

            if KW == 64:
                f32sel = lambda mi: (ZFd, 0, 128) if mi * 128 == AW else None
            else:
                f32sel = lambda mi: (ZFd, mi * 128 - AW, 128) if AW <= mi * 128 < AW + 2 * KW else None
            ep = evac_store(ZBd, lambda mi: mi * 128, BF16, scale_of=lambda mi: 0.125 if mi * 128 < AW else 1.0, also_f32=f32sel)
            go, _ = linear_phase(SRC_D, KTD, w_in[l], [m * 128 for m in range(nmt)], ep, tchunks=full_chunks, fresh=False)
            for mi in range(min(nmt, int(_os.environ.get('LIMIT_MT', '999')))):
                go(mi)
            kb.barrier()
            outtoks.append(kb.dma('sp', kvp_out[l], ZFd[:, SEQ - 128:SEQ]))
            import os
            if not os.environ.get('SKIP_ATT'):
                attention_phase(l)
            if not os.environ.get('SKIP_SSM'):
                ssm_phase(l)
            norm_phase(SSd, NKS, gssm_in[l], out='norm', dst_kt0=KTA)
            new_phase()
            ep = evac_store(Od, lambda mi: mi * 128, F32)
            go, _ = linear_phase(SRC_D, KTD, w_out[l], [m * 128 for m in range(KTD)], ep, tchunks=full_chunks, fresh=False)
            for mi in range(KTD):
                go(mi)
            norm_phase(Od, KTD, gains[l, 1], resid_d=Xd, out='norm', g2_ap=gains[l, 2])
            new_phase()
            stg = [ar.alloc([128, T], BF16) for _ in range(2)]
            sil = [ar.alloc([128, NMAX]) for _ in range(3)]
            aB = Buf(ACTd)

            def ep_gu(mi, ci, t0, n, pb, pb2):
                s = stg[mi % 2]
                sl = sil[nxt('sil', 3)]
                kb.do('act', lambda e: e.activation(out=sl.ap[:, 0:n], in_=pb.ap[:, 0:n], func=AF.Silu), R=[pb], W=[sl])
                kb.do('dve', lambda e: e.tensor_tensor(out=s.ap[:, t0:t0 + n], in0=sl.ap[:, 0:n], in1=pb2.ap[:, 0:n], op=ALU.mult), R=[sl, pb2], W=[s])
                if ci == len(chunks) - 1:
                    kb.dma('sp', ACTd[mi * 128:(mi + 1) * 128, :], s.ap, R=[s], W=[aB])
            go, _ = linear_phase(SRC_D, KTD, w_gu[l], [m * 128 for m in range(KTF)], ep_gu, W2cols=[DFF + m * 128 for m in range(KTF)],
                                 tchunks=full_chunks, fresh=False)
            for mi in range(KTF):
                go(mi)
            half = T // 2
            for hf in range(2):
                new_phase()
                SRC_F = src3(KTF, half)
                kb.dma('sp', SRC_F, ACTd[:, hf * half:(hf + 1) * half].rearrange("(k p) t -> p k t", p=128), W=[srcbuf])
                stg2 = [ar.alloc([128, half]) for _ in range(2)]
                oB = Buf(Od)
                hch = [(t0, n, t0 - hf * half) for (t0, n) in chunks[3 * hf:3 * hf + 3]]

                def ep_dn(mi, ci, t0, n, pb, pb2, stg2=stg2, oB=oB, hf=hf):
                    s = stg2[mi % 2]
                    kb.do('act', lambda e: e.activation(out=s.ap[:, t0 - hf * half:t0 - hf * half + n], in_=pb.ap[:, 0:n], func=AF.Copy), R=[pb], W=[s])
                    if ci == 2:
                        kb.dma('sp', Od[mi * 128:(mi + 1) * 128, hf * half:(hf + 1) * half], s.ap, R=[s], W=[oB])
                go, _ = linear_phase(SRC_F, KTF, w_dn[l], [m * 128 for m in range(KTD)], ep_dn, tchunks=hch, fresh=False)
                for mi in range(KTD):
                    go(mi)
            norm_phase(Od, KTD, gains[l, 3], resid_d=Xd, out='cast')
            new_phase()
            peb = ar.alloc([128, 2, T], BF16)
            for k2 in range(2):
                kb.dma('pool', peb.ap[:, k2, :], peT_in[l, k2 * 128:(k2 + 1) * 128, :], W=[peb], max_dma_last_dim=4096)
            wpp = [ar.alloc([128, 2, 128], BF16) for _ in range(2)]
            xrow = [ar.alloc([128, T]) for _ in range(2)]
            sg = [ar.alloc([128, NMAX]) for _ in range(3)]
            wppv = w_pp[l].rearrange("(k p) n -> p k n", p=128)
            xB = Buf(Xd)

            def ep_ple(mi, ci, t0, n, pb, pb2):
                xr_ = xrow[mi % 2]
                wp_ = wpp[mi % 2]
                if ci == 0:
                    kb.dma('pool', wp_.ap, wppv[:, :, mi * 128:(mi + 1) * 128], W=[wp_])
                    kb.dma('sp', xr_.ap, Xd[mi * 128:(mi + 1) * 128, :], R=[xB], W=[xr_])
                pj = PS[6 + nxt('pj', 2)]
                for kt in range(2):
                    kb.do('pe', lambda e, kt=kt: e.matmul(pj.ap[:, 0:n], lhsT=wp_.ap[:, kt, :], rhs=peb.ap[:, kt, t0:t0 + n], start=(kt == 0), stop=(kt == 1)),
                          R=[wp_, peb], W=[pj])
                s_ = sg[nxt('sg', 3)]
                kb.do('act', lambda e: e.activation(out=s_.ap[:, 0:n], in_=pb.ap[:, 0:n], func=AF.Sigmoid), R=[pb], W=[s_])
                kb.do('dve', lambda e: e.tensor_tensor(out=s_.ap[:, 0:n], in0=s_.ap[:, 0:n], in1=pj.ap[:, 0:n], op=ALU.mult), R=[pj], W=[s_])
                kb.do('dve', lambda e: e.tensor_tensor(out=xr_.ap[:, t0:t0 + n], in0=xr_.ap[:, t0:t0 + n], in1=s_.ap[:, 0:n], op=ALU.add), R=[s_], W=[xr_])
                if ci == len(chunks) - 1:
                    outtoks.append(kb.dma('sp', Xd[mi * 128:(mi + 1) * 128, :], xr_.ap, R=[xr_], W=[xB]))
            go, _ = linear_phase(SRC_D, KTD, w_pg[l], [m * 128 for m in range(KTD)], ep_ple, tchunks=full_chunks, fresh=False)
            for mi in range(KTD):
                go(mi)


    try:
        for l in range(DEPTH):
            _layer(l)
    except _Stop:
        print('stopped after phase', _stop_after)

    kb.barrier()
    for r in range(KTD):
        outtoks.append(kb.dma('sp', yT_out[r * 128:(r + 1) * 128, :], Xd[r * 128:(r + 1) * 128, :]))
    kb.barrier()
    if _os.environ.get('PHASE_LOG'):
        import json as _json
        _json.dump(_marks, open(_os.environ['PHASE_LOG'], 'w'))
    kb.emit()
    return nc


def _consts():
    c = np.zeros((128, 1152), np.float32)
    c[:, 0:128] = np.eye(128, dtype=np.float32)
    i = np.arange(128)[:, None]
    j = np.arange(256)[None, :]
    dist = (i - j + 128).astype(np.float32)
    valid = (dist >= 0) & (dist <= 128)
    dm = np.where(valid, dist, np.float32(BIG)).astype(np.float32)
    c[:, 128:384] = dm
    d0 = dm.copy()
    d0[:, 0:128] = BIG
    c[:, 384:640] = d0
    c[:, 640:1152] = np.arange(1, 513, dtype=np.float32)[None, :]
    return c


def prepare_inputs(C, inp):
    f = np.float32
    A = lambda k: np.asarray(inp[k], f)
    L = C.DEPTH
    NP_, NKS = C.NPAIR, C.NKS
    gains = np.stack([A(k) for k in ('g_pre_mix', 'g_post_mix', 'g_pre_ffn', 'g_post_ffn')], axis=1)
    gains = np.ascontiguousarray(gains.reshape(L, 4, C.KTD, 128).transpose(0, 1, 3, 2))
    gssm = np.ascontiguousarray(A('g_ssm_out').reshape(L, NKS, 128).transpose(0, 2, 1))
    gattn = np.ascontiguousarray(np.broadcast_to(A('g_attn_out')[:, None, :], (L, 128, C.AW)))
    gattnT = np.ascontiguousarray(A('g_attn_out').reshape(L, C.NH, 64).transpose(0, 2, 1))
    sinks = np.ascontiguousarray(np.broadcast_to(A('attn_sinks')[:, None, :], (L, 128, C.NH)))
    are = A('ssm_a_re').reshape(L, NP_, 2, 64).transpose(0, 2, 3, 1).reshape(L, 128, NP_)
    aim = A('ssm_a_im').reshape(L, NP_, 2, 64).transpose(0, 2, 3, 1).reshape(L, 128, NP_)
    ldt = np.broadcast_to(A('ssm_log_dt').reshape(L, NP_, 2, 1), (L, NP_, 2, 64)).transpose(0, 2, 3, 1).reshape(L, 128, NP_)
    ssm_ps = np.ascontiguousarray(np.stack([are, aim, ldt], axis=2))
    bpad = np.zeros((L, 128, 2, NP_, 128), f)
    for ri, key in enumerate(('ssm_b_re', 'ssm_b_im')):
        b = A(key)
        for g in range(C.NG):
            q, gh, g8 = g // 2, g % 2, g % 8
            bpad[:, g8 * 16:(g8 + 1) * 16, ri, q, gh * 64:(gh + 1) * 64] = b[:, g].transpose(0, 2, 1)
    cpad = np.zeros((L, 128, NKS, 2, 4, 128), f)
    for ri, key in enumerate(('ssm_c_re', 'ssm_c_im')):
        c = A(key)
        for g in range(C.NG):
            kt, pr, gh, g8 = g // 8, (g % 8) // 2, g % 2, g % 8
            cpad[:, gh * 64:(gh + 1) * 64, kt, ri, pr, g8 * 16:(g8 + 1) * 16] = c[:, g].transpose(0, 2, 1)
    dcol = np.ascontiguousarray(A('ssm_d').reshape(L, NKS, 128).transpose(0, 2, 1))
    wglu = np.zeros((L, 128, NKS, 2, 128), f)
    wg = A('ssm_w_glu')
    for g in range(C.NG):
        kt, g8 = g // 8, g % 8
        for hf in range(2):
            wglu[:, g8 * 16:(g8 + 1) * 16, kt, hf, g8 * 16:(g8 + 1) * 16] = wg[:, g, :, hf * 16:(hf + 1) * 16]
    shared = dict(
        w_in=A('w_in'), w_out=A('w_out'), w_gate_up=A('w_gate_up'), w_down=A('w_down'), w_ple_gate=A('w_ple_gate'),
        w_ple_proj=A('w_ple_proj'), gains=gains, gssm=gssm, gattn=gattn, gattnT=gattnT, sinks=sinks, ssm_ps=ssm_ps,
        bpad=bpad, cpad=cpad, dcol=dcol, wglu=wglu, consts=_consts())
    xp, xs = A('x_prompt'), A('x_sample')
    pp, psm = A('p_prompt'), A('p_sample')
    ck, cv = A('cache_k'), A('cache_v')
    sre, sim = A('state_ssm_re'), A('state_ssm_im')
    in_maps = []
    NS = C.NS
    for c in range(8):
        b = c % C.BATCH
        sl = slice(NS * b, NS * (b + 1))
        m = dict(shared)
        m['xT'] = np.ascontiguousarray(np.concatenate([xp[b].T, xs[sl, 0].T], axis=1))
        m['peT'] = np.ascontiguousarray(np.concatenate([pp[:, b].transpose(0, 2, 1), psm[:, sl, 0].transpose(0, 2, 1)], axis=2))
        ckc = ck[:, sl].reshape(L, NS, 128, C.KW)
        m['ck'] = np.ascontiguousarray(ckc)
        m['cv'] = np.ascontiguousarray(cv[:, sl].reshape(L, NS, 128, C.KW))
        m['ckT'] = np.ascontiguousarray(ckc.reshape(L, NS, 128, C.NKV, 64).transpose(0, 1, 3, 4, 2))
        st = np.stack([sre[:, sl], sim[:, sl]], axis=1)
        st = st.reshape(L, 2, NS, NP_, 2, 64).transpose(0, 4, 5, 1, 3, 2).reshape(L, 128, 2, NP_, NS)
        m['st0'] = np.ascontiguousarray(st)
        in_maps.append(m)
    return in_maps


def assemble(C, results):
    f = np.float32
    L, NS, B = C.DEPTH, C.NS, C.BATCH
    yp = np.zeros((B, C.SEQ, C.D), f)
    ys = np.zeros((C.DEC, 1, C.D), f)
    kp = np.zeros((L, B, 128, C.NKV, 64), f)
    vp = np.zeros_like(kp)
    srp = np.zeros((L, B, C.NG, 64), f)
    sip = np.zeros_like(srp)
    ksm = np.zeros((L, C.DEC, 128, C.NKV, 64), f)
    vsm = np.zeros_like(ksm)
    srs = np.zeros((L, C.DEC, C.NG, 64), f)
    sis = np.zeros_like(srs)
    for b in range(B):
        r = results[b]
        sl = slice(NS * b, NS * (b + 1))
        yT = r['yT']
        yp[b] = yT[:, :C.SEQ].T
        ys[sl, 0] = yT[:, C.SEQ:].T
        kv = r['kvp']
        kp[:, b] = kv[:, :C.KW].transpose(0, 2, 1).reshape(L, 128, C.NKV, 64)
        vp[:, b] = kv[:, C.KW:].transpose(0, 2, 1).reshape(L, 128, C.NKV, 64)
        ksm[:, sl] = r['ks'].reshape(L, NS, 128, C.NKV, 64)
        vsm[:, sl] = r['vs'].reshape(L, NS, 128, C.NKV, 64)
        st = r['st'].reshape(L, 2, 64, 2, C.NPAIR, 1 + NS)
        st = st.transpose(0, 3, 5, 4, 1, 2).reshape(L, 2, 1 + NS, C.NG, 64)
        srp[:, b], sip[:, b] = st[:, 0, 0], st[:, 1, 0]
        srs[:, sl], sis[:, sl] = st[:, 0, 1:], st[:, 1, 1:]
    return (yp, ys, kp, vp, srp, sip, ksm, vsm, srs, sis)


_CACHE = {}


def run(C, inp):
    key = (C.D, C.SEQ, C.DEPTH, C.BATCH, C.DEC)
    if key not in _CACHE:
        _CACHE[key] = build(C)
    in_maps = prepare_inputs(C, inp)
    res = run_bass_kernel_spmd(_CACHE[key], in_maps, core_ids=list(range(8)))
    return assemble(C, res.results)


def kernel(**inputs):
    return run(Cfg(), inputs)
```

```python
import math
import numpy as np
import concourse.bass as bass
import concourse.mybir as mybir
from concourse.bass_utils import run_bass_kernel_spmd

F32 = mybir.dt.float32
BF16 = mybir.dt.bfloat16
I32 = mybir.dt.int32
AF = mybir.ActivationFunctionType
ALU = mybir.AluOpType
AX = mybir.AxisListType

ENGS = ('pe', 'act', 'dve', 'pool', 'sp')
SEM_ROT = 20000
NDMASEM = 56
EPS = 1e-6
BIG = 1.0e9
TWO_PI = 6.283184
GELU_C = 1.5957691216057308
MAGIC = 12582912.0


class Cfg:
    def __init__(self, D=2048, SEQ=2048, DEPTH=4, BATCH=4, DEC=32):
        self.D, self.SEQ, self.DEPTH, self.BATCH, self.DEC = D, SEQ, DEPTH, BATCH, DEC
        self.NS = DEC // BATCH
        self.T = SEQ + self.NS
        self.AW = D // 2
        self.NH = self.AW // 64
        self.NKV = max(1, self.NH // 8)
        self.GRP = self.NH // self.NKV
        self.KW = self.NKV * 64
        self.SW = D - self.AW
        self.NG = self.SW // 16
        self.NKS = self.SW // 128
        self.NPAIR = self.NG // 2
        self.INW = self.AW + 2 * self.KW + self.SW
        self.DFF = -(-8 * D // (3 * 256)) * 256
        self.KTD = D // 128
        self.KTA = self.AW // 128
        self.KTF = self.DFF // 128
        self.NQB = SEQ // 128
        half = self.T // 2
        assert self.T % 2 == 0
        a = -(-half // 3)
        ch = []
        for h0 in (0, half):
            o = h0
            for i in range(3):
                n = min(a, h0 + half - o)
                ch.append((o, n))
                o += n
        self.chunks = ch
        self.NMAX = a
        assert a <= 512
        self.TQ = 256 if SEQ >= 256 else SEQ
        self.slopes = [2.0 ** (-8.0 * (h + 1) / self.NH) for h in range(self.NH)]


class Buf:
    def __init__(self, ap, excl=False):
        self.ap = ap
        self.w = {}
        self.r = {}
        self.excl = excl


def _upd(d, tok):
    k = tok[0].num
    if k not in d or d[k][1] < tok[1]:
        d[k] = tok


class _Rec:
    def __init__(self):
        self.call = None

    def __getattr__(self, name):
        def f(*a, **k):
            assert self.call is None
            self.call = (name, a, k)
            return self
        return f


class KB:
    def __init__(self, nc):
        self.nc = nc
        self.ops = {e: [] for e in ENGS}
        self.cur = {}
        self.nsem = 0
        self.waited = {}
        self.ndma = 0
        self.dpool = []
        for e in ENGS:
            self._new_sem(e)
        for i in range(NDMASEM):
            self.dpool.append([nc.alloc_semaphore("dq%d" % i), 0])

    def _new_sem(self, e):
        self.nsem += 1
        self.cur[e] = [self.nc.alloc_semaphore("p_%s_%d" % (e, self.nsem)), 0]
        if not hasattr(self, 'owner'):
            self.owner = {}
        self.owner[self.cur[e][0].num] = e

    def _waits(self, eng, deps):
        waits = []
        for d in deps:
            sem, val = d
            if eng == 'pe' and self.owner.get(sem.num) == 'pe':
                continue
            key = (eng, sem.num)
            if self.waited.get(key, 0) >= val:
                continue
            self.waited[key] = val
            waits.append((sem, val))
        return waits

    def _deps(self, R, W):
        deps = []
        for b in R:
            deps.extend(b.w.values())
            if b.excl:
                deps.extend(b.r.values())
        for b in W:
            deps.extend(b.w.values())
            deps.extend(b.r.values())
        return deps

    def _record(self, tok, R, W):
        for b in R:
            _upd(b.r, tok)
        for b in W:
            _upd(b.w, tok)

    def do(self, eng, fn, R=(), W=()):
        rec = _Rec()
        fn(rec)
        if getattr(self, 'buf', None) is not None:
            self.buf.append(('do', eng, rec.call, list(R), list(W)))
            return None
        return self._do(eng, rec.call, R, W)

    def begin_buffer(self):
        self.buf = []

    def mark(self, name):
        if getattr(self, 'buf', None) is not None:
            self.buf.append(('mark', name, None, None, None))

    def end_buffer(self):
        b, self.buf = self.buf, None
        return b

    def flush_interleaved(self, lists):
        m = max(len(x) for x in lists)
        for k in range(m):
            for x in lists:
                if k < len(x):
                    kind, eng, call, R, W = x[k]
                    if kind == 'do':
                        self._do(eng, call, R, W)

    def _do(self, eng, call, R=(), W=()):
        waits = self._waits(eng, self._deps(R, W))
        if self.cur[eng][1] >= SEM_ROT:
            self._new_sem(eng)
        c = self.cur[eng]
        c[1] += 1
        name, a, k = call
        self.ops[eng].append((waits, (lambda e, name=name, a=a, k=k: getattr(e, name)(*a, **k)), c[0], 1))
        tok = (c[0], c[1])
        self._record(tok, R, W)
        return tok

    def dma(self, eng, out, in_, R=(), W=(), **kw):
        waits = self._waits(eng, self._deps(R, W))
        ds = self.dpool[self.ndma % NDMASEM]
        self.ndma += 1
        ds[1] += 16
        self.ops[eng].append((waits, lambda e: e.dma_start(out=out, in_=in_, **kw), ds[0], 16))
        tok = (ds[0], ds[1])
        self._record(tok, R, W)
        return tok

    def wait_only(self, eng, deps):
        waits = self._waits(eng, deps)
        if waits:
            self.ops[eng].append((waits, None, None, 0))

    def barrier(self):
        toks = [(c[0], c[1]) for c in self.cur.values() if c[1] > 0]
        toks += [(d[0], d[1]) for d in self.dpool if d[1] > 0]
        for e in ENGS:
            self.wait_only(e, toks)

    def emit(self):
        nc = self.nc
        with nc.Block() as block:
            def run(name):
                def f(e):
                    for waits, fn, sem, inc in self.ops[name]:
                        for (s, v) in waits:
                            e.wait_ge(s, v)
                        if fn is not None:
                            fn(e).then_inc(sem, inc)
                return f
            block.tensor(run('pe'))
            block.scalar(run('act'))
            block.vector(run('dve'))
            block.gpsimd(run('pool'))
            block.sync(run('sp'))


class Arena:
    LO = 16512
    HI = 229376

    def __init__(self, nc):
        self.nc = nc
        self.cur = self.LO
        self.n = 0

    def alloc(self, shape, dt=F32):
        esz = 2 if dt == BF16 else 4
        nb = esz
        for s in shape[1:]:
            nb *= s
        nb = (nb + 63) // 64 * 64
        assert self.cur + nb <= self.HI, ("SBUF overflow", self.cur, nb)
        self.n += 1
        t = self.nc.alloc_sbuf_tensor_at("sb%d" % self.n, list(shape), dt, offset=self.cur)
        self.cur += nb
        return Buf(t.ap())


def build(cfg):
    C = cfg
    nc = bass.Bass("TRN2", target_bir_lowering=False)
    kb = KB(nc)
    ar = Arena(nc)
    D, T, SEQ, NS, DEPTH = C.D, C.T, C.SEQ, C.NS, C.DEPTH
    AW, NH, NKV, KW, SW, NKS, NPAIR, INW, DFF = C.AW, C.NH, C.NKV, C.KW, C.SW, C.NKS, C.NPAIR, C.INW, C.DFF
    KTD, KTA, KTF, NQB = C.KTD, C.KTA, C.KTF, C.NQB
    chunks, NMAX, TQ = C.chunks, C.NMAX, C.TQ
    KTMAX = max(KTF, KTD)

    def din(name, shape):
        return nc.dram_tensor(name, list(shape), F32, kind="ExternalInput").ap()

    def dout(name, shape):
        return nc.dram_tensor(name, list(shape), F32, kind="ExternalOutput").ap()

    def dscr(name, shape, dt=F32):
        return nc.dram_tensor(name, list(shape), dt).ap()

    xT_in = din("xT", [D, T])
    peT_in = din("peT", [DEPTH, 256, T])
    w_in = din("w_in", [DEPTH, D, INW])
    w_out = din("w_out", [DEPTH, D, D])
    w_gu = din("w_gate_up", [DEPTH, D, 2 * DFF])
    w_dn = din("w_down", [DEPTH, DFF, D])
    w_pg = din("w_ple_gate", [DEPTH, D, D])
    w_pp = din("w_ple_proj", [DEPTH, 256, D])
    gains = din("gains", [DEPTH, 4, 128, KTD])
    gssm_in = din("gssm", [DEPTH, 128, NKS])
    gattn_in = din("gattn", [DEPTH, 128, AW])
    gattnT_in = din("gattnT", [DEPTH, 64, NH])
    sinks_in = din("sinks", [DEPTH, 128, NH])
    ckT_in = din("ckT", [DEPTH, NS, NKV, 64, 128])
    ck_in = din("ck", [DEPTH, NS, 128, KW])
    cv_in = din("cv", [DEPTH, NS, 128, KW])
    ssm_ps_in = din("ssm_ps", [DEPTH, 128, 3, NPAIR])
    bpad_in = din("bpad", [DEPTH, 128, 2, NPAIR, 128])
    cpad_in = din("cpad", [DEPTH, 128, NKS, 2, 4, 128])
    dcol_in = din("dcol", [DEPTH, 128, NKS])
    wglu_in = din("wglu", [DEPTH, 128, NKS, 2, 128])
    st0_in = din("st0", [DEPTH, 128, 2, NPAIR, NS])
    consts_in = din("consts", [128, 128 + 256 + 256 + 512])

    yT_out = dout("yT", [D, T])
    kvp_out = dout("kvp", [DEPTH, 2 * KW, 128])
    ks_out = dout("ks", [DEPTH, NS, 128, KW])
    vs_out = dout("vs", [DEPTH, NS, 128, KW])
    st_out = dout("st", [DEPTH, 128, 2, NPAIR, 1 + NS])

    Xd = dscr("X", [D, T])
    ZBd = dscr("ZB", [INW, T], BF16)
    ZFd = dscr("ZF", [2 * KW, T])
    Od = dscr("O", [D, T])
    SSd = dscr("SS", [SW, T])
    ACTd = dscr("ACTs", [DFF, T], BF16)
    MGSd = dscr("MGS", [AW, NS], BF16)

    ident = ar.alloc([128, 128], BF16)
    ones = ar.alloc([128, 128], BF16)
    DIST = ar.alloc([128, 256])
    DIST0 = ar.alloc([128, 256])
    IOT = ar.alloc([128, 512])
    srcbuf = ar.alloc([128, max(KTD * T, KTF * (T // 2))], BF16)
    phase_mark = ar.cur

    PS = [Buf(nc.alloc_psum_tensor("ps%d" % i, [128, 512], F32).ap(), excl=True) for i in range(8)]

    def src3(kt_n, tn):
        return srcbuf.ap[:, 0:kt_n * tn].rearrange("p (k t) -> p k t", t=tn)

    SRC_D = src3(KTD, T)

    outtoks = []

    class _Stop(Exception):
        pass
    import os as _os
    _stop_after = int(_os.environ.get('STOP_AFTER', '100000'))
    _pc = [0]

    _marks = []

    def new_phase(name=None):
        import inspect
        if name is None:
            name = inspect.stack()[1].function + ':' + str(inspect.stack()[1].lineno)
        _marks.append((name, sum(1 for o in kb.ops['pe'] if o[1] is not None)))
        _pc[0] += 1
        if _pc[0] > _stop_after:
            raise _Stop()
        kb.barrier()
        ar.cur = phase_mark

    kb.dma('pool', ident.ap, consts_in[:, 0:128], W=[ident])
    kb.dma('sp', DIST.ap, consts_in[:, 128:384], W=[DIST])
    kb.dma('sp', DIST0.ap, consts_in[:, 384:640], W=[DIST0])
    kb.dma('sp', IOT.ap, consts_in[:, 640:1152], W=[IOT])
    kb.do('pool', lambda e: e.memset(ones.ap, 1.0), W=[ones])
    xw = Buf(Xd)
    for r in range(KTD):
        kb.dma('sp', Xd[r * 128:(r + 1) * 128, :], xT_in[r * 128:(r + 1) * 128, :], W=[xw])

    rot = {}

    def nxt(key, n):
        rot[key] = (rot.get(key, -1) + 1) % n
        return rot[key]

    def norm_phase(src_d, KT, g1_ap, resid_d=None, out=None, g2_ap=None, dst_kt0=0, Fdim=None):
        new_phase()
        Fdim = KT * 128
        g1 = ar.alloc([128, KT])
        kb.dma('sp', g1.ap, g1_ap, W=[g1])
        g2 = None
        if g2_ap is not None:
            g2 = ar.alloc([128, KT])
            kb.dma('sp', g2.ap, g2_ap, W=[g2])
        xin = [ar.alloc([128, KT, NMAX]) for _ in range(2)]
        xr = [ar.alloc([128, KT, NMAX]) for _ in range(2)] if resid_d is not None else None
        sq = [ar.alloc([128, NMAX], BF16) for _ in range(3)]
        rs = [ar.alloc([128, NMAX]) for _ in range(2)]
        tmp = [ar.alloc([128, NMAX]) for _ in range(3)]
        sv = src_d.rearrange("(k p) t -> p k t", p=128)
        xv = resid_d.rearrange("(k p) t -> p k t", p=128) if resid_d is not None else None
        srcB = Buf(src_d)
        psA, psB = PS[6], PS[7]

        def rstd_of(buf_in, n, psb, rsb):
            for kt in range(KT):
                s = sq[nxt('sq', 3)]
                kb.do('act', lambda e, kt=kt, s=s: e.activation(out=s.ap[:, 0:n], in_=buf_in.ap[:, kt, 0:n], func=AF.Square),
                      R=[buf_in], W=[s])
                kb.do('pe', lambda e, kt=kt, s=s: e.matmul(psb.ap[:, 0:n], lhsT=ones.ap, rhs=s.ap[:, 0:n],
                                                           start=(kt == 0), stop=(kt == KT - 1)), R=[s, ones], W=[psb])
            kb.do('dve', lambda e: e.tensor_scalar(out=rsb.ap[:, 0:n], in0=psb.ap[:, 0:n], scalar1=1.0 / Fdim, scalar2=EPS,
                                                   op0=ALU.mult, op1=ALU.add), R=[psb], W=[rsb])
            kb.do('act', lambda e: e.activation(out=rsb.ap[:, 0:n], in_=rsb.ap[:, 0:n], func=AF.Ln), R=[rsb], W=[rsb])
            kb.do('act', lambda e: e.activation(out=rsb.ap[:, 0:n], in_=rsb.ap[:, 0:n], func=AF.Exp, scale=-0.5), R=[rsb], W=[rsb])

        for ci, (t0, n) in enumerate(chunks):
            xi = xin[ci % 2]
            kb.dma('sp', xi.ap[:, :, 0:n], sv[:, :, t0:t0 + n], R=[srcB], W=[xi])
            r1 = rs[0]
            rstd_of(xi, n, psA, r1)
            if resid_d is None:
                for kt in range(KT):
                    kb.do('dve', lambda e, kt=kt: e.scalar_tensor_tensor(
                        out=SRC_D[:, dst_kt0 + kt, t0:t0 + n], in0=xi.ap[:, kt, 0:n], scalar=g1.ap[:, kt:kt + 1],
                        in1=r1.ap[:, 0:n], op0=ALU.mult, op1=ALU.mult), R=[xi, g1, r1], W=[srcbuf])
                continue
            xx = xr[ci % 2]
            xB = Buf(resid_d)
            kb.dma('sp', xx.ap[:, :, 0:n], xv[:, :, t0:t0 + n], R=[xB], W=[xx])
            for kt in range(KT):
                tb = tmp[nxt('tmp', 3)]
                kb.do('dve', lambda e, kt=kt, tb=tb: e.scalar_tensor_tensor(
                    out=tb.ap[:, 0:n], in0=xi.ap[:, kt, 0:n], scalar=g1.ap[:, kt:kt + 1], in1=r1.ap[:, 0:n],
                    op0=ALU.mult, op1=ALU.mult), R=[xi, g1, r1], W=[tb])
                kb.do('dve', lambda e, kt=kt, tb=tb: e.tensor_tensor(out=xx.ap[:, kt, 0:n], in0=xx.ap[:, kt, 0:n],
                                                                      in1=tb.ap[:, 0:n], op=ALU.add), R=[tb], W=[xx])
            outtoks.append(kb.dma('sp', xv[:, :, t0:t0 + n], xx.ap[:, :, 0:n], R=[xx], W=[xB]))
            if out == 'norm':
                r2 = rs[1]
                rstd_of(xx, n, psB, r2)
                for kt in range(KT):
                    kb.do('dve', lambda e, kt=kt: e.scalar_tensor_tensor(
                        out=SRC_D[:, dst_kt0 + kt, t0:t0 + n], in0=xx.ap[:, kt, 0:n], scalar=g2.ap[:, kt:kt + 1],
                        in1=r2.ap[:, 0:n], op0=ALU.mult, op1=ALU.mult), R=[xx, g2, r2], W=[srcbuf])
            elif out == 'cast':
                kb.do('act', lambda e: e.activation(out=SRC_D[:, dst_kt0:dst_kt0 + KT, t0:t0 + n], in_=xx.ap[:, :, 0:n],
                                                    func=AF.Copy), R=[xx], W=[srcbuf])

    def linear_phase(srcv, KT, W_ap, cols, epi, W2cols=None, tchunks=None, fresh=True):
        if fresh:
            new_phase()
        wb = [ar.alloc([128, KT, 128], BF16) for _ in range(3)]
        wb2 = [ar.alloc([128, KT, 128], BF16) for _ in range(2)] if W2cols is not None else None
        wv = W_ap.rearrange("(k p) n -> p k n", p=128)
        tch = tchunks if tchunks is not None else chunks
        st = {}

        def go(mi):
            c0 = cols[mi]
            w = wb[mi % 3]
            kb.dma('pool', w.ap, wv[:, :, c0:c0 + 128], W=[w])
            w2 = None
            if W2cols is not None:
                w2 = wb2[mi % 2]
                kb.dma('pool', w2.ap, wv[:, :, W2cols[mi]:W2cols[mi] + 128], W=[w2])
            for ci, (t0, n, s0) in enumerate(tch):
                if W2cols is None:
                    pb = PS[nxt('lin', 6)]
                    pb2 = None
                else:
                    j = nxt('lin2', 3)
                    pb, pb2 = PS[2 * j], PS[2 * j + 1]
                for kt in range(KT):
                    kb.do('pe', lambda e, kt=kt, pb=pb, w=w: e.matmul(pb.ap[:, 0:n], lhsT=w.ap[:, kt, :], rhs=srcv[:, kt, s0:s0 + n],
                                                                   start=(kt == 0), stop=(kt == KT - 1)), R=[w, srcbuf], W=[pb])
                if pb2 is not None:
                    for kt in range(KT):
                        kb.do('pe', lambda e, kt=kt, pb2=pb2, w2=w2: e.matmul(pb2.ap[:, 0:n], lhsT=w2.ap[:, kt, :], rhs=srcv[:, kt, s0:s0 + n],
                                                                         start=(kt == 0), stop=(kt == KT - 1)), R=[w2, srcbuf], W=[pb2])
                epi(mi, ci, t0, n, pb, pb2)
        return go, st

    full_chunks = [(t0, n, t0) for (t0, n) in chunks]

    def attention_phase(l):
        new_phase()
        kTd = ar.alloc([128, NKV, 128 + T], BF16)
        Vtm = ar.alloc([128, NQB + 1, KW], BF16)
        vTi = [ar.alloc([128, 128], BF16) for _ in range(2)]
        qTb = [ar.alloc([128, KTA, 128], BF16) for _ in range(2)]
        atm = [ar.alloc([128, AW]) for _ in range(2)]
        hn = [ar.alloc([128, AW], BF16) for _ in range(2)]
        junk = ar.alloc([128, AW], BF16)
        sm2 = [ar.alloc([128, 4]) for _ in range(2)]
        gat = ar.alloc([128, AW])
        snk = ar.alloc([128, NH])
        zb = Buf(ZBd)
        kb.dma('sp', gat.ap, gattn_in[l], W=[gat])
        kb.dma('sp', snk.ap, sinks_in[l], W=[snk])
        kb.do('pool', lambda e: e.memset(kTd.ap[:, :, 0:128], 0.0), W=[kTd])
        kb.do('pool', lambda e: e.memset(Vtm.ap[:, 0, :], 0.0), W=[Vtm])
        for g in range(NKV):
            for cp in range(2):
                kb.dma('sp', kTd.ap[cp * 64:(cp + 1) * 64, g, 128:128 + T], ZBd[AW + g * 64:AW + (g + 1) * 64, :], R=[zb], W=[kTd])
        PSb = [Buf(PS[i].ap.bitcast(BF16), excl=True) for i in range(8)]
        for i in range(8):
            PSb[i].w, PSb[i].r = PS[i].w, PS[i].r
        for b in range(NQB):
            vi = vTi[b % 2]
            kb.dma('sp', vi.ap[0:KW, :], ZBd[AW + KW:AW + 2 * KW, b * 128:(b + 1) * 128], R=[zb], W=[vi])
            pt = PSb[7]
            kb.do('pe', lambda e, vi=vi, pt=pt: e.transpose(out=pt.ap[:, 0:KW], in_=vi.ap[0:KW, :], identity=ident.ap[0:KW, 0:KW]),
                  R=[vi, ident], W=[pt])
            kb.do('act', lambda e, b=b, pt=pt: e.activation(out=Vtm.ap[:, b + 1, :], in_=pt.ap[:, 0:KW], func=AF.Copy), R=[pt], W=[Vtm])

        NRR = 8
        Sb = [ar.alloc([128, 256]) for _ in range(NRR)]
        Pb = [ar.alloc([128, 256], BF16) for _ in range(NRR)]
        PTs = [ar.alloc([128, 2, 128], BF16) for _ in range(NRR)]
        sm = [ar.alloc([128, 8]) for _ in range(NRR)]

        def pipeline(units, stages, after=None):
            ns = len(stages)
            for t in range(len(units) + ns - 1):
                for si, st in enumerate(stages):
                    ui = t - si
                    if 0 <= ui < len(units):
                        st(units[ui])
                        if si == ns - 1 and after is not None:
                            after(ui)

        def sA(u):
            S_, m_, npart, nk, h, sps = Sb[u['i']], sm[u['i']], u['np'], u['nk'], u['h'], u['sps']
            kb.do('dve', lambda e: e.scalar_tensor_tensor(out=S_.ap[0:npart, 0:nk], in0=u['dist'], scalar=-C.slopes[h],
                                                          in1=sps.ap[0:npart, 0:nk], op0=ALU.mult, op1=ALU.add),
                  R=[sps, DIST, DIST0], W=[S_])
            kb.do('dve', lambda e: e.reduce_max(out=m_.ap[0:npart, 0:1], in_=S_.ap[0:npart, 0:nk], axis=AX.X), R=[S_], W=[m_])
            kb.do('dve', lambda e: e.tensor_tensor(out=m_.ap[0:npart, 0:1], in0=m_.ap[0:npart, 0:1], in1=snk.ap[0:npart, h:h + 1],
                                                   op=ALU.max), R=[snk], W=[m_])
            kb.do('dve', lambda e: e.tensor_scalar(out=m_.ap[0:npart, 1:2], in0=m_.ap[0:npart, 0:1], scalar1=-1.0, scalar2=None,
                                                   op0=ALU.mult), R=[], W=[m_])

        def sB(u):
            S_, P_, m_, npart, nk, h = Sb[u['i']], Pb[u['i']], sm[u['i']], u['np'], u['nk'], u['h']
            kb.do('act', lambda e: e.activation(out=P_.ap[0:npart, 0:nk], in_=S_.ap[0:npart, 0:nk], func=AF.Exp,
                                                bias=m_.ap[0:npart, 1:2], scale=1.0, accum_out=m_.ap[0:npart, 2:3]),
                  R=[S_, m_], W=[P_, m_])
            kb.do('act', lambda e: e.activation(out=m_.ap[0:npart, 3:4], in_=snk.ap[0:npart, h:h + 1], func=AF.Exp,
                                                bias=m_.ap[0:npart, 1:2], scale=1.0), R=[snk, m_], W=[m_])

        def sC(u):
            m_, npart = sm[u['i']], u['np']
            kb.do('dve', lambda e: e.tensor_tensor(out=m_.ap[0:npart, 4:5], in0=m_.ap[0:npart, 2:3], in1=m_.ap[0:npart, 3:4],
                                                   op=ALU.add), R=[m_], W=[m_])
            kb.do('dve', lambda e: e.reciprocal(out=m_.ap[0:npart, 5:6], in_=m_.ap[0:npart, 4:5]), R=[m_], W=[m_])

        units = []
        for b in range(NQB):
            for h in range(NH):
                units.append(dict(b=b, h=h, g=h // C.GRP, hp=(h % 2) * 64, np=128, nk=256,
                                  dist=(DIST0 if b == 0 else DIST).ap))

        def pA(u):
            b, h, g, hp = u['b'], u['h'], u['g'], u['hp']
            if h == 0:
                qb = qTb[b % 2]
                kb.dma('sp', qb.ap, ZBd[0:AW, b * 128:(b + 1) * 128].rearrange("(k p) t -> p k t", p=128), R=[zb], W=[qb])
            qb = qTb[b % 2]
            u['i'] = nxt('att', NRR)
            u['sps'] = sps = PS[nxt('sps', 3)]
            kb.do('pe', lambda e: e.matmul(sps.ap[:, 0:256], lhsT=qb.ap[hp:hp + 64, h // 2, :],
                                           rhs=kTd.ap[hp:hp + 64, g, b * 128:b * 128 + 256], start=True, stop=True), R=[qb, kTd], W=[sps])
            sA(u)

        def pC(u):
            sC(u)
            i = u['i']
            u['ptp'] = ptp = PSb[3 + nxt('ptp', 2)]
            for j in range(2):
                kb.do('pe', lambda e, j=j: e.transpose(out=ptp.ap[:, j * 128:(j + 1) * 128], in_=Pb[i].ap[:, j * 128:(j + 1) * 128],
                                                       identity=ident.ap), R=[Pb[i], ident], W=[ptp])

        def pD(u):
            i, ptp = u['i'], u['ptp']
            kb.do('act', lambda e: e.activation(out=PTs[i].ap, in_=ptp.ap[:, 0:256].rearrange("p (j q) -> p j q", j=2), func=AF.Copy),
                  R=[ptp], W=[PTs[i]])

        def pE(u):
            i, b, g = u['i'], u['b'], u['g']
            u['ops'] = ops_ = PS[5 + nxt('ops', 2)]
            for j in range(2):
                kb.do('pe', lambda e, j=j: e.matmul(ops_.ap[:, 0:64], lhsT=PTs[i].ap[:, j, :], rhs=Vtm.ap[:, b + j, g * 64:(g + 1) * 64],
                                                    start=(j == 0), stop=(j == 1)), R=[PTs[i], Vtm], W=[ops_])

        def pF(u):
            i, h, ops_ = u['i'], u['h'], u['ops']
            am = atm[u['b'] % 2]
            kb.do('dve', lambda e: e.tensor_scalar(out=am.ap[:, h * 64:(h + 1) * 64], in0=ops_.ap[:, 0:64], scalar1=sm[i].ap[:, 5:6],
                                                   scalar2=None, op0=ALU.mult), R=[ops_, sm[i]], W=[am])

        def block_done(ui):
            u = units[ui]
            if u['h'] != NH - 1:
                return
            b = u['b']
            am, s2, hb = atm[b % 2], sm2[b % 2], hn[b % 2]
            kb.do('act', lambda e: e.activation(out=junk.ap, in_=am.ap, func=AF.Square, accum_out=s2.ap[:, 0:1]), R=[am], W=[junk, s2])
            kb.do('dve', lambda e: e.tensor_scalar(out=s2.ap[:, 1:2], in0=s2.ap[:, 0:1], scalar1=1.0 / AW, scalar2=EPS,
                                                   op0=ALU.mult, op1=ALU.add), R=[s2], W=[s2])
            kb.do('act', lambda e: e.activation(out=s2.ap[:, 1:2], in_=s2.ap[:, 1:2], func=AF.Ln), R=[s2], W=[s2])
            kb.do('act', lambda e: e.activation(out=s2.ap[:, 1:2], in_=s2.ap[:, 1:2], func=AF.Exp, scale=-0.5), R=[s2], W=[s2])
            kb.do('dve', lambda e: e.scalar_tensor_tensor(out=hb.ap, in0=am.ap, scalar=s2.ap[:, 1:2], in1=gat.ap, op0=ALU.mult, op1=ALU.mult),
                  R=[am, s2, gat], W=[hb])
            for kt in range(KTA):
                pt = PSb[7]
                kb.do('pe', lambda e, kt=kt: e.transpose(out=pt.ap[:, 0:128], in_=hb.ap[:, kt * 128:(kt + 1) * 128], identity=ident.ap),
                      R=[hb, ident], W=[pt])
                kb.do('act', lambda e, kt=kt: e.activation(out=SRC_D[:, kt, b * 128:(b + 1) * 128], in_=pt.ap[:, 0:128], func=AF.Copy),
                      R=[pt], W=[srcbuf])

        pipeline(units, [pA, sB, pC, pD, pE, pF], after=block_done)

        kS = [ar.alloc([128, NKV, 132], BF16) for _ in range(2)]
        vS = [ar.alloc([128, KW], BF16) for _ in range(2)]
        vN = [ar.alloc([1, KW], BF16) for _ in range(2)]
        qS = ar.alloc([128, KTA, NS], BF16)
        PnS = [ar.alloc([1, 132], BF16) for _ in range(NRR)]
        PTS = [ar.alloc([128, 2], BF16) for _ in range(NRR)]
        aS = ar.alloc([64, NH, NS])
        zf = Buf(ZFd)
        kb.dma('sp', qS.ap, ZBd[0:AW, SEQ:SEQ + NS].rearrange("(k p) t -> p k t", p=128), R=[zb], W=[qS])
        sunits = []
        for n in range(NS):
            for h in range(NH):
                sunits.append(dict(n=n, h=h, g=h // C.GRP, hp=(h % 2) * 64, np=1, nk=129, dist=DIST.ap[0:1, 0:129]))

        def qA(u):
            n, h, g, hp = u['n'], u['h'], u['g'], u['hp']
            ks_, vs_, vn_ = kS[n % 2], vS[n % 2], vN[n % 2]
            if h == 0:
                for g_ in range(NKV):
                    for cp in range(2):
                        kb.dma('pool', ks_.ap[cp * 64:(cp + 1) * 64, g_, 0:128], ckT_in[l, n, g_], W=[ks_])
                kb.do('pool', lambda e: e.tensor_copy(out=ks_.ap[:, :, 128:129], in_=kTd.ap[:, :, 128 + SEQ + n:128 + SEQ + n + 1]),
                      R=[kTd], W=[ks_])
                kb.dma('pool', vs_.ap, cv_in[l, n], W=[vs_])
                kb.dma('pool', vn_.ap, ZFd[KW:2 * KW, SEQ + n:SEQ + n + 1].rearrange("k o -> o k"), R=[zf], W=[vn_], allow_slow_non_contiguous=True)
                outtoks.append(kb.dma('sp', ks_out[l, n, 0:127, :], ck_in[l, n, 1:128, :]))
                outtoks.append(kb.dma('sp', vs_out[l, n, 0:127, :], cv_in[l, n, 1:128, :]))
                outtoks.append(kb.dma('sp', ks_out[l, n, 127:128, :], ZFd[0:KW, SEQ + n:SEQ + n + 1].rearrange("k o -> o k"), R=[zf], allow_slow_non_contiguous=True))
                outtoks.append(kb.dma('sp', vs_out[l, n, 127:128, :], ZFd[KW:2 * KW, SEQ + n:SEQ + n + 1].rearrange("k o -> o k"), R=[zf], allow_slow_non_contiguous=True))
            u['i'] = nxt('att', NRR)
            u['sps'] = sps = PS[nxt('sps', 3)]
            kb.do('pe', lambda e: e.matmul(sps.ap[0:1, 0:129], lhsT=qS.ap[hp:hp + 64, h // 2, n:n + 1], rhs=ks_.ap[hp:hp + 64, g, 0:129],
                                           start=True, stop=True), R=[qS, ks_], W=[sps])
            sA(u)

        def qC(u):
            sC(u)
            i = u['i']
            pn = PnS[i]
            kb.do('dve', lambda e: e.tensor_scalar(out=pn.ap[0:1, 0:129], in0=Pb[i].ap[0:1, 0:129], scalar1=sm[i].ap[0:1, 5:6], scalar2=None,
                                                   op0=ALU.mult), R=[Pb[i], sm[i]], W=[pn])
            u['ptp'] = ptp = PSb[3 + nxt('ptp', 2)]
            kb.do('pe', lambda e: e.transpose(out=ptp.ap[:, 0:1], in_=pn.ap[0:1, 0:128], identity=ident.ap[0:1, 0:1]), R=[pn, ident], W=[ptp])

        def qD(u):
            i, ptp = u['i'], u['ptp']
            kb.do('act', lambda e: e.activation(out=PTS[i].ap[:, 0:1], in_=ptp.ap[:, 0:1], func=AF.Copy), R=[ptp], W=[PTS[i]])

        def qE(u):
            i, g, n = u['i'], u['g'], u['n']
            vs_, vn_, pn = vS[n % 2], vN[n % 2], PnS[i]
            u['ops'] = ops_ = PS[5 + nxt('ops', 2)]
            kb.do('pe', lambda e: e.matmul(ops_.ap[0:64, 0:1], lhsT=vs_.ap[:, g * 64:(g + 1) * 64], rhs=PTS[i].ap[:, 0:1], start=True, stop=False),
                  R=[PTS[i], vs_], W=[ops_])
            kb.do('pe', lambda e: e.matmul(ops_.ap[0:64, 0:1], lhsT=vn_.ap[0:1, g * 64:(g + 1) * 64], rhs=pn.ap[0:1, 128:129], start=False, stop=True),
                  R=[pn, vn_], W=[ops_])

        def qF(u):
            ops_, h, n = u['ops'], u['h'], u['n']
            kb.do('act', lambda e: e.activation(out=aS.ap[:, h, n:n + 1], in_=ops_.ap[0:64, 0:1], func=AF.Copy), R=[ops_], W=[aS])

        if not _os.environ.get('PIPE_SAMPLE'):
            for u_ in sunits:
                for st_ in (qA, sB, qC, qD, qE, qF):
                    st_(u_)
        else:
            pipeline(sunits, [qA, sB, qC, qD, qE, qF])
        sqS = ar.alloc([64, NH * NS], BF16)
        ssS = ar.alloc([64, NS])
        gT = ar.alloc([64, NH])
        hS = ar.alloc([64, NH, NS])
        hSb = ar.alloc([64, NH, NS], BF16)
        kb.dma('sp', gT.ap, gattnT_in[l], W=[gT])
        kb.do('act', lambda e: e.activation(out=sqS.ap, in_=aS.ap.rearrange("p h n -> p (h n)"), func=AF.Square), R=[aS], W=[sqS])
        kb.do('pe', lambda e: e.matmul(PS[7].ap[0:64, 0:NH * NS], lhsT=ones.ap[0:64, 0:64], rhs=sqS.ap, start=True, stop=True),
              R=[sqS, ones], W=[PS[7]])
        kb.do('dve', lambda e: e.tensor_reduce(out=ssS.ap, in_=PS[7].ap[0:64, 0:NH * NS].rearrange("p (h n) -> p n h", n=NS),
                                               axis=AX.X, op=ALU.add), R=[PS[7]], W=[ssS])
        kb.do('dve', lambda e: e.tensor_scalar(out=ssS.ap, in0=ssS.ap, scalar1=1.0 / AW, scalar2=EPS, op0=ALU.mult, op1=ALU.add),
              R=[ssS], W=[ssS])
        kb.do('act', lambda e: e.activation(out=ssS.ap, in_=ssS.ap, func=AF.Ln), R=[ssS], W=[ssS])
        kb.do('act', lambda e: e.activation(out=ssS.ap, in_=ssS.ap, func=AF.Exp, scale=-0.5), R=[ssS], W=[ssS])
        kb.do('dve', lambda e: e.tensor_tensor(out=hS.ap, in0=aS.ap, in1=gT.ap.unsqueeze(2).broadcast_to([64, NH, NS]), op=ALU.mult),
              R=[aS, gT], W=[hS])
        kb.do('dve', lambda e: e.tensor_tensor(out=hSb.ap, in0=hS.ap, in1=ssS.ap.unsqueeze(1).broadcast_to([64, NH, NS]), op=ALU.mult),
              R=[hS, ssS], W=[hSb])
        mg = Buf(MGSd)
        kb.dma('sp', MGSd.rearrange("(h d) n -> d h n", d=64), hSb.ap, R=[hSb], W=[mg])
        kb.dma('sp', SRC_D[:, 0:KTA, SEQ:SEQ + NS], MGSd.rearrange("(k p) n -> p k n", p=128), R=[mg], W=[srcbuf])

    def ssm_phase(l):
        new_phase()
        ps_ = ar.alloc([128, 3, NPAIR])
        kb.dma('sp', ps_.ap, ssm_ps_in[l], W=[ps_])
        NV = 17
        v = ar.alloc([128, NV, NPAIR])
        vi = ar.alloc([128, NPAIR], I32)
        are, aim, ldt = ps_.ap[:, 0, :], ps_.ap[:, 1, :], ps_.ap[:, 2, :]
        V_DT, V_DRE, V_TH, V_R, V_A, V_SIN, V_COS, V_ABR, V_ABI, V_FR, V_FI, V_IFR, V_IFI, V_T1, V_T2, V_T3, V_NFI = range(17)

        def vv(i):
            return v.ap[:, i, :]

        def tiny(eng, fn):
            kb.do(eng, fn, R=[v, ps_], W=[v])

        def wrap_turns(eng_ap_in, out_i):
            kb.do('dve', lambda e: e.tensor_copy(out=vi.ap, in_=eng_ap_in), R=[v], W=[vi])
            kb.do('dve', lambda e: e.tensor_copy(out=vv(V_T1), in_=vi.ap), R=[vi, v], W=[v])
            tiny('dve', lambda e: e.tensor_tensor(out=vv(out_i), in0=eng_ap_in, in1=vv(V_T1), op=ALU.subtract))
            tiny('dve', lambda e: e.tensor_scalar(out=vv(V_T1), in0=vv(out_i), scalar1=0.5, scalar2=None, op0=ALU.is_gt))
            tiny('dve', lambda e: e.tensor_tensor(out=vv(out_i), in0=vv(out_i), in1=vv(V_T1), op=ALU.subtract))
            tiny('dve', lambda e: e.tensor_scalar(out=vv(V_T1), in0=vv(out_i), scalar1=-0.5, scalar2=None, op0=ALU.is_lt))
            tiny('dve', lambda e: e.tensor_tensor(out=vv(out_i), in0=vv(out_i), in1=vv(V_T1), op=ALU.add))

        tiny('act', lambda e: e.activation(out=vv(V_DT), in_=ldt, func=AF.Exp))
        tiny('dve', lambda e: e.tensor_tensor(out=vv(V_DRE), in0=vv(V_DT), in1=are, op=ALU.mult))
        tiny('dve', lambda e: e.tensor_tensor(out=vv(V_TH), in0=vv(V_DT), in1=aim, op=ALU.mult))
        tiny('act', lambda e: e.activation(out=vv(V_R), in_=vv(V_DRE), func=AF.Exp))
        tiny('dve', lambda e: e.tensor_scalar(out=vv(V_T2), in0=vv(V_TH), scalar1=1.0 / (2 * math.pi), scalar2=None, op0=ALU.mult))
        wrap_turns(vv(V_T2), V_A)
        tiny('act', lambda e: e.activation(out=vv(V_SIN), in_=vv(V_A), func=AF.Sin, scale=TWO_PI))
        tiny('dve', lambda e: e.tensor_scalar(out=vv(V_T2), in0=vv(V_A), scalar1=0.25, scalar2=None, op0=ALU.add))
        wrap_turns(vv(V_T2), V_T3)
        tiny('act', lambda e: e.activation(out=vv(V_COS), in_=vv(V_T3), func=AF.Sin, scale=TWO_PI))
        tiny('dve', lambda e: e.tensor_tensor(out=vv(V_ABR), in0=vv(V_R), in1=vv(V_COS), op=ALU.mult))
        tiny('dve', lambda e: e.tensor_tensor(out=vv(V_ABI), in0=vv(V_R), in1=vv(V_SIN), op=ALU.mult))
        tiny('dve', lambda e: e.tensor_tensor(out=vv(V_T1), in0=are, in1=are, op=ALU.mult))
        tiny('dve', lambda e: e.tensor_tensor(out=vv(V_T2), in0=aim, in1=aim, op=ALU.mult))
        tiny('dve', lambda e: e.tensor_tensor(out=vv(V_T1), in0=vv(V_T1), in1=vv(V_T2), op=ALU.add))
        tiny('dve', lambda e: e.reciprocal(out=vv(V_T1), in_=vv(V_T1)))
        tiny('dve', lambda e: e.tensor_scalar(out=vv(V_T2), in0=vv(V_ABR), scalar1=-1.0, scalar2=None, op0=ALU.add))
        tiny('dve', lambda e: e.tensor_tensor(out=vv(V_FR), in0=vv(V_T2), in1=are, op=ALU.mult))
        tiny('dve', lambda e: e.tensor_tensor(out=vv(V_T3), in0=vv(V_ABI), in1=aim, op=ALU.mult))
        tiny('dve', lambda e: e.tensor_tensor(out=vv(V_FR), in0=vv(V_FR), in1=vv(V_T3), op=ALU.add))
        tiny('dve', lambda e: e.tensor_tensor(out=vv(V_FR), in0=vv(V_FR), in1=vv(V_T1), op=ALU.mult))
        tiny('dve', lambda e: e.tensor_tensor(out=vv(V_FI), in0=vv(V_ABI), in1=are, op=ALU.mult))
        tiny('dve', lambda e: e.tensor_tensor(out=vv(V_T3), in0=vv(V_T2), in1=aim, op=ALU.mult))
        tiny('dve', lambda e: e.tensor_tensor(out=vv(V_FI), in0=vv(V_FI), in1=vv(V_T3), op=ALU.subtract))
        tiny('dve', lambda e: e.tensor_tensor(out=vv(V_FI), in0=vv(V_FI), in1=vv(V_T1), op=ALU.mult))
        tiny('dve', lambda e: e.tensor_tensor(out=vv(V_T1), in0=vv(V_FR), in1=vv(V_FR), op=ALU.mult))
        tiny('dve', lambda e: e.tensor_tensor(out=vv(V_T2), in0=vv(V_FI), in1=vv(V_FI), op=ALU.mult))
        tiny('dve', lambda e: e.tensor_tensor(out=vv(V_T1), in0=vv(V_T1), in1=vv(V_T2), op=ALU.add))
        tiny('dve', lambda e: e.reciprocal(out=vv(V_T1), in_=vv(V_T1)))
        tiny('dve', lambda e: e.tensor_tensor(out=vv(V_IFR), in0=vv(V_FR), in1=vv(V_T1), op=ALU.mult))
        tiny('dve', lambda e: e.tensor_tensor(out=vv(V_IFI), in0=vv(V_FI), in1=vv(V_T1), op=ALU.mult))
        tiny('dve', lambda e: e.tensor_scalar(out=vv(V_IFI), in0=vv(V_IFI), scalar1=-1.0, scalar2=None, op0=ALU.mult))
        tiny('dve', lambda e: e.tensor_scalar(out=vv(V_NFI), in0=vv(V_FI), scalar1=-1.0, scalar2=None, op0=ALU.mult))

        bpk = [ar.alloc([128, 2, 4, 128], BF16) for _ in range(2)]
        wg = ar.alloc([128, NKS, 2, 128], BF16)
        kb.dma('pool', wg.ap, wglu_in[l], W=[wg], max_dma_last_dim=4096)
        dc = ar.alloc([128, NKS])
        kb.dma('sp', dc.ap, dcol_in[l], W=[dc])
        st0 = ar.alloc([128, 2, NPAIR, NS])
        kb.dma('sp', st0.ap, st0_in[l], W=[st0])
        sto = ar.alloc([128, 2, NPAIR, 1 + NS])
        uT = [ar.alloc([128, T], BF16) for _ in range(1)]
        cpf = [ar.alloc([128, 2, 4, 128]) for _ in range(1)]
        cf = [ar.alloc([128, 2, 4, 128], BF16) for _ in range(2)]
        cft = ar.alloc([128, 128])
        RT = [ar.alloc([128, TQ]) for _ in range(4)]
        NW = 4
        AT = [ar.alloc([128, TQ]) for _ in range(NW)]
        A2 = [ar.alloc([128, TQ]) for _ in range(NW)]
        NF = [ar.alloc([128, TQ]) for _ in range(NW)]
        NA = [ar.alloc([128, TQ]) for _ in range(NW)]
        CS = [ar.alloc([128, TQ]) for _ in range(NW)]
        SN = [ar.alloc([128, TQ]) for _ in range(NW)]
        PW1 = [ar.alloc([128, TQ]) for _ in range(2)]
        PW2 = [ar.alloc([128, TQ]) for _ in range(2)]
        THO = ar.alloc([128, SEQ // TQ, NPAIR])
        HPI = ar.alloc([128, 1])
        kb.do('dve', lambda e: e.memset(HPI.ap, math.pi / 2), W=[HPI])
        for tq_ in range(SEQ // TQ):
            kb.do('dve', lambda e, tq_=tq_: e.tensor_scalar(out=THO.ap[:, tq_, :], in0=vv(V_A), scalar1=float(tq_ * TQ), scalar2=None, op0=ALU.mult),
                  R=[v], W=[THO])
        W1 = [ar.alloc([128, TQ]) for _ in range(NW)]
        W2 = [ar.alloc([128, TQ]) for _ in range(NW)]
        XR = [ar.alloc([128, TQ]) for _ in range(NW)]
        XI = [ar.alloc([128, TQ]) for _ in range(NW)]
        SR = [ar.alloc([128, TQ]) for _ in range(NW)]
        SI = [ar.alloc([128, TQ]) for _ in range(NW)]
        SB2R = [ar.alloc([128, TQ], BF16) for _ in range(4)]
        SB2I = [ar.alloc([128, TQ], BF16) for _ in range(4)]
        carry = ar.alloc([128, 2, NPAIR])
        aoff = ar.alloc([128, NPAIR])
        FIN = [ar.alloc([128, 8]) for _ in range(2)]
        ssbb = ar.alloc([128, 2, 4, NS], BF16)
        sw = ar.alloc([128, 3, 4, NS])
        craw = ar.alloc([128, 2, 4, 128], BF16)
        TS = ar.alloc([128, 2, NPAIR, NS])
        tsw = ar.alloc([128, 2, NPAIR, NS])
        abrb = v.ap[:, V_ABR, :].unsqueeze(2).broadcast_to([128, NPAIR, NS])
        abib = v.ap[:, V_ABI, :].unsqueeze(2).broadcast_to([128, NPAIR, NS])
        kb.do('dve', lambda e: e.tensor_tensor(out=TS.ap[:, 0], in0=st0.ap[:, 0], in1=abrb, op=ALU.mult), R=[st0, v], W=[TS])
        kb.do('dve', lambda e: e.tensor_tensor(out=tsw.ap[:, 0], in0=st0.ap[:, 1], in1=abib, op=ALU.mult), R=[st0, v], W=[tsw])
        kb.do('dve', lambda e: e.tensor_tensor(out=TS.ap[:, 0], in0=TS.ap[:, 0], in1=tsw.ap[:, 0], op=ALU.subtract), R=[tsw], W=[TS])
        kb.do('dve', lambda e: e.tensor_tensor(out=TS.ap[:, 1], in0=st0.ap[:, 0], in1=abib, op=ALU.mult), R=[st0, v], W=[TS])
        kb.do('dve', lambda e: e.tensor_tensor(out=tsw.ap[:, 1], in0=st0.ap[:, 1], in1=abrb, op=ALU.mult), R=[st0, v], W=[tsw])
        kb.do('dve', lambda e: e.tensor_tensor(out=TS.ap[:, 1], in0=TS.ap[:, 1], in1=tsw.ap[:, 1], op=ALU.add), R=[tsw], W=[TS])
        yst = [ar.alloc([128, T]) for _ in range(1)]
        EY = ar.alloc([128, T])
        E2 = [ar.alloc([128, 512]) for _ in range(1)]
        GB = [ar.alloc([128, 512], BF16) for _ in range(1)]
        zb = Buf(ZBd)
        ssB = Buf(SSd)
        kb.do('pool', lambda e: e.memset(carry.ap, 0.0), W=[carry])
        NTQ = SEQ // TQ

        deferred = []

        def run_deferred():
            for lst in deferred:
                kb.flush_interleaved(lst)
            del deferred[:]

        for kt in range(NKS):
            u = uT[kt % len(uT)]
            kb.dma('sp', u.ap, ZBd[AW + 2 * KW + kt * 128:AW + 2 * KW + (kt + 1) * 128, :], R=[zb], W=[u])
            cp_, cf_ = cpf[0], cf[kt % 2]
            bp = bpk[kt % 2]
            for ri_ in range(2):
                kb.dma('pool', bp.ap[:, ri_], bpad_in[l, :, ri_, kt * 4:(kt + 1) * 4, :], W=[bp])
            kb.dma('sp', cp_.ap, cpad_in[l, :, kt], W=[cp_])
            kb.do('act', lambda e: e.activation(out=craw.ap[:, 0], in_=cp_.ap[:, 0], func=AF.Copy), R=[cp_], W=[craw])
            kb.do('act', lambda e: e.activation(out=craw.ap[:, 1], in_=cp_.ap[:, 1], func=AF.Copy, scale=-1.0), R=[cp_], W=[craw])
            for pr in range(4):
                q = kt * 4 + pr
                fr, fi = v.ap[:, V_FR, q:q + 1], v.ap[:, V_FI, q:q + 1]
                nfi = v.ap[:, V_NFI, q:q + 1]
                kb.do('dve', lambda e, pr=pr, fi=fi: e.tensor_scalar(out=cft.ap, in0=cp_.ap[:, 1, pr, :], scalar1=fi, scalar2=None, op0=ALU.mult),
                      R=[cp_, v], W=[cft])
                kb.do('dve', lambda e, pr=pr, fr=fr: e.scalar_tensor_tensor(out=cf_.ap[:, 0, pr, :], in0=cp_.ap[:, 0, pr, :], scalar=fr, in1=cft.ap,
                                                                         op0=ALU.mult, op1=ALU.subtract), R=[cp_, v, cft], W=[cf_])
                kb.do('dve', lambda e, pr=pr, fr=fr: e.tensor_scalar(out=cft.ap, in0=cp_.ap[:, 1, pr, :], scalar1=fr, scalar2=-1.0, op0=ALU.mult,
                                                                  op1=ALU.mult), R=[cp_, v, cf_], W=[cft])
                kb.do('dve', lambda e, pr=pr, nfi=nfi: e.scalar_tensor_tensor(out=cf_.ap[:, 1, pr, :], in0=cp_.ap[:, 0, pr, :], scalar=nfi, in1=cft.ap,
                                                                           op0=ALU.mult, op1=ALU.add), R=[cp_, v, cft], W=[cf_])
            ys = yst[kt % len(yst)]
            for tq in range(NTQ + 1):
                samp = (tq == NTQ)
                t0 = tq * TQ
                n = NS if samp else TQ
                ypb = PS[4 + nxt('ypb', 2)]
                pend = []
                if samp:
                    run_deferred()
                    xs_ = PS[0]
                    for pr in range(4):
                        for ri in range(2):
                            c0_ = (ri * 4 + pr) * NS
                            kb.do('pe', lambda e, ri=ri, pr=pr, c0_=c0_: e.matmul(xs_.ap[:, c0_:c0_ + NS], lhsT=bp.ap[:, ri, pr, :], rhs=u.ap[:, t0:t0 + NS],
                                                                               start=True, stop=True), R=[bp, u], W=[xs_])
                    xv_ = xs_.ap[:, 0:8 * NS].rearrange("p (r q n) -> p r q n", r=2, q=4)
                    frb = v.ap[:, V_FR, kt * 4:kt * 4 + 4].unsqueeze(2).broadcast_to([128, 4, NS])
                    fib = v.ap[:, V_FI, kt * 4:kt * 4 + 4].unsqueeze(2).broadcast_to([128, 4, NS])
                    wr, wi, wt_ = sw.ap[:, 0], sw.ap[:, 1], sw.ap[:, 2]
                    kb.do('dve', lambda e: e.tensor_tensor(out=wr, in0=xv_[:, 0], in1=frb, op=ALU.mult), R=[xs_, v], W=[sw])
                    kb.do('dve', lambda e: e.tensor_tensor(out=wt_, in0=xv_[:, 1], in1=fib, op=ALU.mult), R=[xs_, v], W=[sw])
                    kb.do('dve', lambda e: e.tensor_tensor(out=wr, in0=wr, in1=wt_, op=ALU.subtract), R=[], W=[sw])
                    kb.do('dve', lambda e: e.tensor_tensor(out=wi, in0=xv_[:, 0], in1=fib, op=ALU.mult), R=[xs_, v], W=[sw])
                    kb.do('dve', lambda e: e.tensor_tensor(out=wt_, in0=xv_[:, 1], in1=frb, op=ALU.mult), R=[xs_, v], W=[sw])
                    kb.do('dve', lambda e: e.tensor_tensor(out=wi, in0=wi, in1=wt_, op=ALU.add), R=[], W=[sw])
                    kb.do('dve', lambda e: e.tensor_tensor(out=sto.ap[:, 0, kt * 4:kt * 4 + 4, 1:1 + NS], in0=wr, in1=TS.ap[:, 0, kt * 4:kt * 4 + 4, :], op=ALU.add),
                          R=[sw, TS], W=[sto])
                    kb.do('dve', lambda e: e.tensor_tensor(out=sto.ap[:, 1, kt * 4:kt * 4 + 4, 1:1 + NS], in0=wi, in1=TS.ap[:, 1, kt * 4:kt * 4 + 4, :], op=ALU.add),
                          R=[sw, TS], W=[sto])
                    kb.do('act', lambda e: e.activation(out=ssbb.ap, in_=sto.ap[:, :, kt * 4:kt * 4 + 4, 1:1 + NS], func=AF.Copy), R=[sto], W=[ssbb])
                for pr in range(4):
                    q = kt * 4 + pr
                    if not samp:
                        kb.begin_buffer()
                        xps_r, xps_i = PS[2 * nxt('xps', 2)], None
                        xps_i = PS[PS.index(xps_r) + 1]
                        for ri, xp in ((0, xps_r), (1, xps_i)):
                            kb.do('pe', lambda e, ri=ri, xp=xp, q=q: e.matmul(xp.ap[:, 0:n], lhsT=bp.ap[:, ri, pr, :], rhs=u.ap[:, t0:t0 + n],
                                                                           start=True, stop=True), R=[bp, u], W=[xp])
                    _k = nxt('sb2', 4)
                    s2r, s2i = SB2R[_k], SB2I[_k]
                    if samp:
                        rhs_r, rhs_i = ssbb.ap[:, 0, pr, :], ssbb.ap[:, 1, pr, :]
                        rd = [ssbb]
                    else:
                        fin = FIN[pr % 2]
                        w = nxt('ssw', NW)
                        w1, w2, xr_, xi_, sr_, si_ = W1[w], W2[w], XR[w], XI[w], SR[w], SI[w]
                        tw = w
                        pw1, pw2 = PW1[pr % 2], PW2[pr % 2]
                        at, a2, nf, na, cs, sn = AT[tw], A2[tw], NF[tw], NA[tw], CS[tw], SN[tw]
                        rt = RT[pr]
                        if tq == 0:
                            kb.do('act', lambda e, rt=rt, q=q: e.activation(out=rt.ap, in_=IOT.ap[:, 0:TQ], func=AF.Identity, scale=0.0,
                                                                           bias=v.ap[:, V_R, q:q + 1]), R=[IOT, v], W=[rt])
                        kb.do('act', lambda e, at=at, q=q: e.activation(out=at.ap, in_=IOT.ap[:, 0:TQ], func=AF.Identity, scale=v.ap[:, V_A, q:q + 1],
                                                                       bias=THO.ap[:, tq, q:q + 1]), R=[IOT, v, THO], W=[at])
                        kb.do('dve', lambda e, at=at, a2=a2: e.tensor_scalar(out=a2.ap, in0=at.ap, scalar1=MAGIC, scalar2=None, op0=ALU.add), R=[at], W=[a2])
                        kb.do('dve', lambda e, at=at, a2=a2, nf=nf: e.scalar_tensor_tensor(out=nf.ap, in0=a2.ap, scalar=MAGIC, in1=at.ap, op0=ALU.subtract,
                                                                                       op1=ALU.subtract), R=[at, a2], W=[nf])
                        kb.do('act', lambda e, nf=nf, sn=sn: e.activation(out=sn.ap, in_=nf.ap, func=AF.Sin, scale=-TWO_PI), R=[nf], W=[sn])
                        kb.do('act', lambda e, nf=nf, na=na: e.activation(out=na.ap, in_=nf.ap, func=AF.Sin, scale=-TWO_PI / 2), R=[nf], W=[na])
                        kb.do('act', lambda e, na=na: e.activation(out=na.ap, in_=na.ap, func=AF.Square), R=[], W=[na])
                        kb.do('act', lambda e, na=na, cs=cs: e.activation(out=cs.ap, in_=na.ap, func=AF.Identity, scale=-2.0, bias=1.0), R=[na], W=[cs])
                        kb.mark('M')
                        kb.do('dve', lambda e, cs=cs, w1=w1: e.tensor_tensor(out=w1.ap, in0=cs.ap, in1=xps_r.ap[:, 0:n], op=ALU.mult), R=[cs, xps_r], W=[w1])
                        kb.do('dve', lambda e, sn=sn, w2=w2: e.tensor_tensor(out=w2.ap, in0=sn.ap, in1=xps_i.ap[:, 0:n], op=ALU.mult), R=[sn, xps_i], W=[w2])
                        kb.do('dve', lambda e, w1=w1, w2=w2, xr_=xr_: e.tensor_tensor(out=xr_.ap, in0=w1.ap, in1=w2.ap, op=ALU.add), R=[w1, w2], W=[xr_])
                        kb.do('dve', lambda e, cs=cs, w1=w1: e.tensor_tensor(out=w1.ap, in0=cs.ap, in1=xps_i.ap[:, 0:n], op=ALU.mult), R=[cs, xps_i, xr_], W=[w1])
                        kb.do('dve', lambda e, sn=sn, w2=w2: e.tensor_tensor(out=w2.ap, in0=sn.ap, in1=xps_r.ap[:, 0:n], op=ALU.mult), R=[sn, xps_r, xr_], W=[w2])
                        kb.do('dve', lambda e, w1=w1, w2=w2, xi_=xi_: e.tensor_tensor(out=xi_.ap, in0=w1.ap, in1=w2.ap, op=ALU.subtract), R=[w1, w2], W=[xi_])
                        kb.do('dve', lambda e, rt=rt, xr_=xr_, sr_=sr_, q=q: e.tensor_tensor_scan(out=sr_.ap, data0=rt.ap, data1=xr_.ap,
                                                                                             initial=carry.ap[:, 0, q:q + 1], op0=ALU.mult, op1=ALU.add),
                              R=[rt, xr_, carry], W=[sr_])
                        kb.do('dve', lambda e, rt=rt, xi_=xi_, si_=si_, q=q: e.tensor_tensor_scan(out=si_.ap, data0=rt.ap, data1=xi_.ap,
                                                                                             initial=carry.ap[:, 1, q:q + 1], op0=ALU.mult, op1=ALU.add),
                              R=[rt, xi_, carry], W=[si_])
                        kb.do('dve', lambda e, sr_=sr_, q=q: e.tensor_copy(out=carry.ap[:, 0, q:q + 1], in_=sr_.ap[:, TQ - 1:TQ]), R=[sr_], W=[carry])
                        kb.do('dve', lambda e, si_=si_, q=q: e.tensor_copy(out=carry.ap[:, 1, q:q + 1], in_=si_.ap[:, TQ - 1:TQ]), R=[si_], W=[carry])
                        kb.mark('R')
                        kb.do('dve', lambda e: e.tensor_tensor(out=w1.ap, in0=cs.ap, in1=sr_.ap, op=ALU.mult), R=[cs, sr_], W=[w1])
                        kb.do('dve', lambda e: e.tensor_tensor(out=w2.ap, in0=sn.ap, in1=si_.ap, op=ALU.mult), R=[sn, si_], W=[w2])
                        kb.do('dve', lambda e: e.tensor_tensor(out=s2r.ap, in0=w1.ap, in1=w2.ap, op=ALU.subtract), R=[w1, w2], W=[s2r])
                        if tq == NTQ - 1:
                            kb.do('dve', lambda e: e.tensor_tensor(out=fin.ap[:, 0:1], in0=w1.ap[:, TQ - 1:TQ], in1=w2.ap[:, TQ - 1:TQ],
                                                                   op=ALU.subtract), R=[w1, w2], W=[fin])
                        kb.do('pool', lambda e: e.tensor_tensor(out=pw1.ap, in0=cs.ap, in1=si_.ap, op=ALU.mult), R=[cs, si_], W=[pw1])
                        kb.do('pool', lambda e: e.tensor_tensor(out=pw2.ap, in0=sn.ap, in1=sr_.ap, op=ALU.mult), R=[sn, sr_], W=[pw2])
                        kb.do('pool', lambda e: e.tensor_tensor(out=s2i.ap, in0=pw1.ap, in1=pw2.ap, op=ALU.add), R=[pw1, pw2], W=[s2i])
                        if tq == NTQ - 1:
                            fr, fi = v.ap[:, V_FR, q:q + 1], v.ap[:, V_FI, q:q + 1]
                            kb.do('dve', lambda e, w1=pw1, w2=pw2: e.tensor_tensor(out=fin.ap[:, 1:2], in0=w1.ap[:, TQ - 1:TQ], in1=w2.ap[:, TQ - 1:TQ],
                                                                               op=ALU.add), R=[pw1, pw2], W=[fin])
                            kb.do('dve', lambda e, fi=fi: e.tensor_scalar(out=fin.ap[:, 2:3], in0=fin.ap[:, 1:2], scalar1=fi, scalar2=None, op0=ALU.mult), R=[v], W=[fin])
                            kb.do('dve', lambda e, fr=fr, q=q: e.scalar_tensor_tensor(out=sto.ap[:, 0, q, 0:1], in0=fin.ap[:, 0:1], scalar=fr, in1=fin.ap[:, 2:3],
                                                                                   op0=ALU.mult, op1=ALU.subtract), R=[v, fin], W=[sto])
                            kb.do('dve', lambda e, fr=fr: e.tensor_scalar(out=fin.ap[:, 2:3], in0=fin.ap[:, 1:2], scalar1=fr, scalar2=None, op0=ALU.mult), R=[v], W=[fin])
                            kb.do('dve', lambda e, fi=fi, q=q: e.scalar_tensor_tensor(out=sto.ap[:, 1, q, 0:1], in0=fin.ap[:, 0:1], scalar=fi, in1=fin.ap[:, 2:3],
                                                                                   op0=ALU.mult, op1=ALU.add), R=[v, fin], W=[sto])
                        rhs_r, rhs_i = s2r.ap, s2i.ap
                        rd = [s2r, s2i]
                    cw_ = craw if samp else cf_
                    kb.do('pe', lambda e, pr=pr, rhs_r=rhs_r, ypb=ypb: e.matmul(ypb.ap[:, 0:n], lhsT=cw_.ap[:, 0, pr, :], rhs=rhs_r,
                                                                              start=(pr == 0), stop=False), R=[cw_] + rd, W=[ypb])
                    kb.do('pe', lambda e, pr=pr, rhs_i=rhs_i, ypb=ypb: e.matmul(ypb.ap[:, 0:n], lhsT=cw_.ap[:, 1, pr, :], rhs=rhs_i,
                                                                              start=False, stop=(pr == 3)), R=[cw_] + rd, W=[ypb])
                    if not samp:
                        pend.append(kb.end_buffer())
                        if len(pend) == 2:
                            def _split(x):
                                iM = [k_ for k_, o_ in enumerate(x) if o_[0] == 'mark' and o_[1] == 'M'][0]
                                iR = [k_ for k_, o_ in enumerate(x) if o_[0] == 'mark' and o_[1] == 'R'][0]
                                return x[:iM], x[iM:iR], x[iR:]
                            parts = [_split(x) for x in pend]
                            kb.flush_interleaved([p_[0] for p_ in parts])
                            run_deferred()
                            kb.flush_interleaved([p_[1] for p_ in parts])
                            deferred.append([p_[2] for p_ in parts])
                            pend = []
                if not samp:
                    kb.begin_buffer()
                kb.do('dve', lambda e, ypb=ypb: e.scalar_tensor_tensor(out=EY.ap[:, t0:t0 + n], in0=u.ap[:, t0:t0 + n], scalar=dc.ap[:, kt:kt + 1],
                                                                     in1=ypb.ap[:, 0:n], op0=ALU.mult, op1=ALU.add), R=[u, dc, ypb], W=[EY])
                if not samp:
                    deferred.append([kb.end_buffer()])
            run_deferred()
            pieces = [(c0_, min(512, T - c0_)) for c0_ in range(0, T, 512)]
            for (c0_, n_) in pieces:
                e2, gb = E2[0], GB[0]
                kb.do('act', lambda e: e.activation(out=e2.ap[:, 0:n_], in_=EY.ap[:, c0_:c0_ + n_], func=AF.Square), R=[EY], W=[e2])
                kb.do('dve', lambda e: e.tensor_scalar(out=e2.ap[:, 0:n_], in0=e2.ap[:, 0:n_], scalar1=0.044715, scalar2=1.0, op0=ALU.mult, op1=ALU.add),
                      R=[], W=[e2])
                kb.do('dve', lambda e: e.tensor_tensor(out=e2.ap[:, 0:n_], in0=e2.ap[:, 0:n_], in1=EY.ap[:, c0_:c0_ + n_], op=ALU.mult), R=[EY], W=[e2])
                kb.do('act', lambda e: e.activation(out=e2.ap[:, 0:n_], in_=e2.ap[:, 0:n_], func=AF.Sigmoid, scale=GELU_C), R=[], W=[e2])
                kb.do('dve', lambda e: e.tensor_tensor(out=gb.ap[:, 0:n_], in0=e2.ap[:, 0:n_], in1=EY.ap[:, c0_:c0_ + n_], op=ALU.mult), R=[EY, e2], W=[gb])
                z1, z2 = PS[6], PS[7]
                kb.do('pe', lambda e: e.matmul(z1.ap[:, 0:n_], lhsT=wg.ap[:, kt, 0, :], rhs=gb.ap[:, 0:n_], start=True, stop=True), R=[wg, gb], W=[z1])
                kb.do('pe', lambda e: e.matmul(z2.ap[:, 0:n_], lhsT=wg.ap[:, kt, 1, :], rhs=gb.ap[:, 0:n_], start=True, stop=True), R=[wg, gb], W=[z2])
                kb.do('act', lambda e: e.activation(out=e2.ap[:, 0:n_], in_=z2.ap[:, 0:n_], func=AF.Sigmoid), R=[z2], W=[e2])
                kb.do('dve', lambda e: e.tensor_tensor(out=ys.ap[:, c0_:c0_ + n_], in0=e2.ap[:, 0:n_], in1=z1.ap[:, 0:n_], op=ALU.mult), R=[e2, z1], W=[ys])
            kb.dma('sp', SSd[kt * 128:(kt + 1) * 128, :], ys.ap, R=[ys], W=[ssB])
        outtoks.append(kb.dma('sp', st_out[l], sto.ap, R=[sto]))

    def evac_store(dst_d, row0_of, dt, scale_of=None, also_f32=None):
        stg = [ar.alloc([128, T], dt) for _ in range(2)]
        stf = [ar.alloc([128, T]) for _ in range(2)] if also_f32 is not None else None
        dB = Buf(dst_d)
        nch = len(chunks)

        def ep(mi, ci, t0, n, pb, pb2):
            s = stg[mi % 2]
            sc = 1.0 if scale_of is None else scale_of(mi)
            f32r = also_f32(mi) if also_f32 is not None else None
            if f32r is None:
                kb.do('act', lambda e: e.activation(out=s.ap[:, t0:t0 + n], in_=pb.ap[:, 0:n], func=AF.Copy, scale=sc), R=[pb], W=[s])
            else:
                sf = stf[mi % 2]
                kb.do('act', lambda e: e.activation(out=sf.ap[:, t0:t0 + n], in_=pb.ap[:, 0:n], func=AF.Copy), R=[pb], W=[sf])
                kb.do('dve', lambda e: e.tensor_scalar(out=s.ap[:, t0:t0 + n], in0=sf.ap[:, t0:t0 + n], scalar1=sc, scalar2=None, op0=ALU.mult),
                      R=[sf], W=[s])
            if ci == nch - 1:
                r0 = row0_of(mi)
                kb.dma('sp', dst_d[r0:r0 + 128, :], s.ap, R=[s], W=[dB])
                if f32r is not None:
                    dd, rr, nr = f32r
                    kb.dma('sp', dd[rr:rr + nr, :], stf[mi % 2].ap[0:nr, :], R=[stf[mi % 2]], W=[Buf(dd)])
        return ep

    def _layer(l):
            norm_phase(Xd, KTD, gains[l, 0], out='norm')
            new_phase()
            nmt = INW // 128

            if KW == 64:
                f32sel = lambda mi: (ZFd, 0, 128) if mi * 128 == AW else None
            else:
                f32sel = lambda mi: (ZFd, mi * 128 - AW, 128) if AW <= mi * 128 < AW + 2 * KW else None
            ep = evac_store(ZBd, lambda mi: mi * 128, BF16, scale_of=lambda mi: 0.125 if mi * 128 < AW else 1.0, also_f32=f32sel)
            go, _ = linear_phase(SRC_D, KTD, w_in[l], [m * 128 for m in range(nmt)], ep, tchunks=full_chunks, fresh=False)
            for mi in range(min(nmt, int(_os.environ.get('LIMIT_MT', '999')))):
                go(mi)
            kb.barrier()
            outtoks.append(kb.dma('sp', kvp_out[l], ZFd[:, SEQ - 128:SEQ]))
            import os
            if not os.environ.get('SKIP_ATT'):
                attention_phase(l)
            if not os.environ.get('SKIP_SSM'):
                ssm_phase(l)
            norm_phase(SSd, NKS, gssm_in[l], out='norm', dst_kt0=KTA)
            new_phase()
            ep = evac_store(Od, lambda mi: mi * 128, F32)
            go, _ = linear_phase(SRC_D, KTD, w_out[l], [m * 128 for m in range(KTD)], ep, tchunks=full_chunks, fresh=False)
            for mi in range(KTD):
                go(mi)
            norm_phase(Od, KTD, gains[l, 1], resid_d=Xd, out='norm', g2_ap=gains[l, 2])
            new_phase()
            stg = [ar.alloc([128, T], BF16) for _ in range(2)]
            sil = [ar.alloc([128, NMAX]) for _ in range(3)]
            aB = Buf(ACTd)

            def ep_gu(mi, ci, t0, n, pb, pb2):
                s = stg[mi % 2]
                sl = sil[nxt('sil', 3)]
                kb.do('act', lambda e: e.activation(out=sl.ap[:, 0:n], in_=pb.ap[:, 0:n], func=AF.Silu), R=[pb], W=[sl])
                kb.do('dve', lambda e: e.tensor_tensor(out=s.ap[:, t0:t0 + n], in0=sl.ap[:, 0:n], in1=pb2.ap[:, 0:n], op=ALU.mult), R=[sl, pb2], W=[s])
                if ci == len(chunks) - 1:
                    kb.dma('sp', ACTd[mi * 128:(mi + 1) * 128, :], s.ap, R=[s], W=[aB])
            go, _ = linear_phase(SRC_D, KTD, w_gu[l], [m * 128 for m in range(KTF)], ep_gu, W2cols=[DFF + m * 128 for m in range(KTF)],
                                 tchunks=full_chunks, fresh=False)
            for mi in range(KTF):
                go(mi)
            half = T // 2
            for hf in range(2):
                new_phase()
                SRC_F = src3(KTF, half)
                kb.dma('sp', SRC_F, ACTd[:, hf * half:(hf + 1) * half].rearrange("(k p) t -> p k t", p=128), W=[srcbuf])
                stg2 = [ar.alloc([128, half]) for _ in range(2)]
                oB = Buf(Od)
                hch = [(t0, n, t0 - hf * half) for (t0, n) in chunks[3 * hf:3 * hf + 3]]

                def ep_dn(mi, ci, t0, n, pb, pb2, stg2=stg2, oB=oB, hf=hf):
                    s = stg2[mi % 2]
                    kb.do('act', lambda e: e.activation(out=s.ap[:, t0 - hf * half:t0 - hf * half + n], in_=pb.ap[:, 0:n], func=AF.Copy), R=[pb], W=[s])
                    if ci == 2:
                        kb.dma('sp', Od[mi * 128:(mi + 1) * 128, hf * half:(hf + 1) * half], s.ap, R=[s], W=[oB])
                go, _ = linear_phase(SRC_F, KTF, w_dn[l], [m * 128 for m in range(KTD)], ep_dn, tchunks=hch, fresh=False)
                for mi in range(KTD):
                    go(mi)
            norm_phase(Od, KTD, gains[l, 3], resid_d=Xd, out='cast')
            new_phase()
            peb = ar.alloc([128, 2, T], BF16)
            for k2 in range(2):
                kb.dma('pool', peb.ap[:, k2, :], peT_in[l, k2 * 128:(k2 + 1) * 128, :], W=[peb], max_dma_last_dim=4096)
            wpp = [ar.alloc([128, 2, 128], BF16) for _ in range(2)]
            xrow = [ar.alloc([128, T]) for _ in range(2)]
            sg = [ar.alloc([128, NMAX]) for _ in range(3)]
            wppv = w_pp[l].rearrange("(k p) n -> p k n", p=128)
            xB = Buf(Xd)

            def ep_ple(mi, ci, t0, n, pb, pb2):
                xr_ = xrow[mi % 2]
                wp_ = wpp[mi % 2]
                if ci == 0:
                    kb.dma('pool', wp_.ap, wppv[:, :, mi * 128:(mi + 1) * 128], W=[wp_])
                    kb.dma('sp', xr_.ap, Xd[mi * 128:(mi + 1) * 128, :], R=[xB], W=[xr_])
                pj = PS[6 + nxt('pj', 2)]
                for kt in range(2):
                    kb.do('pe', lambda e, kt=kt: e.matmul(pj.ap[:, 0:n], lhsT=wp_.ap[:, kt, :], rhs=peb.ap[:, kt, t0:t0 + n], start=(kt == 0), stop=(kt == 1)),
                          R=[wp_, peb], W=[pj])
                s_ = sg[nxt('sg', 3)]
                kb.do('act', lambda e: e.activation(out=s_.ap[:, 0:n], in_=pb.ap[:, 0:n], func=AF.Sigmoid), R=[pb], W=[s_])
                kb.do('dve', lambda e: e.tensor_tensor(out=s_.ap[:, 0:n], in0=s_.ap[:, 0:n], in1=pj.ap[:, 0:n], op=ALU.mult), R=[pj], W=[s_])
                kb.do('dve', lambda e: e.tensor_tensor(out=xr_.ap[:, t0:t0 + n], in0=xr_.ap[:, t0:t0 + n], in1=s_.ap[:, 0:n], op=ALU.add), R=[s_], W=[xr_])
                if ci == len(chunks) - 1:
                    outtoks.append(kb.dma('sp', Xd[mi * 128:(mi + 1) * 128, :], xr_.ap, R=[xr_], W=[xB]))
            go, _ = linear_phase(SRC_D, KTD, w_pg[l], [m * 128 for m in range(KTD)], ep_ple, tchunks=full_chunks, fresh=False)
            for mi in range(KTD):
                go(mi)


    try:
        for l in range(DEPTH):
            _layer(l)
    except _Stop:
        print('stopped after phase', _stop_after)

    kb.barrier()
    for r in range(KTD):
        outtoks.append(kb.dma('sp', yT_out[r * 128:(r + 1) * 128, :], Xd[r * 128:(r + 1) * 128, :]))
    kb.barrier()
    if _os.environ.get('PHASE_LOG'):
        import json as _json
        _json.dump(_marks, open(_os.environ['PHASE_LOG'], 'w'))
    kb.emit()
    return nc


def _consts():
    c = np.zeros((128, 1152), np.float32)
    c[:, 0:128] = np.eye(128, dtype=np.float32)
    i = np.arange(128)[:, None]
    j = np.arange(256)[None, :]
    dist = (i - j + 128).astype(np.float32)
    valid = (dist >= 0) & (dist <= 128)
    dm = np.where(valid, dist, np.float32(BIG)).astype(np.float32)
    c[:, 128:384] = dm
    d0 = dm.copy()
    d0[:, 0:128] = BIG
    c[:, 384:640] = d0
    c[:, 640:1152] = np.arange(1, 513, dtype=np.float32)[None, :]
    return c


def prepare_inputs(C, inp):
    f = np.float32
    A = lambda k: np.asarray(inp[k], f)
    L = C.DEPTH
    NP_, NKS = C.NPAIR, C.NKS
    gains = np.stack([A(k) for k in ('g_pre_mix', 'g_post_mix', 'g_pre_ffn', 'g_post_ffn')], axis=1)
    gains = np.ascontiguousarray(gains.reshape(L, 4, C.KTD, 128).transpose(0, 1, 3, 2))
    gssm = np.ascontiguousarray(A('g_ssm_out').reshape(L, NKS, 128).transpose(0, 2, 1))
    gattn = np.ascontiguousarray(np.broadcast_to(A('g_attn_out')[:, None, :], (L, 128, C.AW)))
    gattnT = np.ascontiguousarray(A('g_attn_out').reshape(L, C.NH, 64).transpose(0, 2, 1))
    sinks = np.ascontiguousarray(np.broadcast_to(A('attn_sinks')[:, None, :], (L, 128, C.NH)))
    are = A('ssm_a_re').reshape(L, NP_, 2, 64).transpose(0, 2, 3, 1).reshape(L, 128, NP_)
    aim = A('ssm_a_im').reshape(L, NP_, 2, 64).transpose(0, 2, 3, 1).reshape(L, 128, NP_)
    ldt = np.broadcast_to(A('ssm_log_dt').reshape(L, NP_, 2, 1), (L, NP_, 2, 64)).transpose(0, 2, 3, 1).reshape(L, 128, NP_)
    ssm_ps = np.ascontiguousarray(np.stack([are, aim, ldt], axis=2))
    bpad = np.zeros((L, 128, 2, NP_, 128), f)
    for ri, key in enumerate(('ssm_b_re', 'ssm_b_im')):
        b = A(key)
        for g in range(C.NG):
            q, gh, g8 = g // 2, g % 2, g % 8
            bpad[:, g8 * 16:(g8 + 1) * 16, ri, q, gh * 64:(gh + 1) * 64] = b[:, g].transpose(0, 2, 1)
    cpad = np.zeros((L, 128, NKS, 2, 4, 128), f)
    for ri, key in enumerate(('ssm_c_re', 'ssm_c_im')):
        c = A(key)
        for g in range(C.NG):
            kt, pr, gh, g8 = g // 8, (g % 8) // 2, g % 2, g % 8
            cpad[:, gh * 64:(gh + 1) * 64, kt, ri, pr, g8 * 16:(g8 + 1) * 16] = c[:, g].transpose(0, 2, 1)
    dcol = np.ascontiguousarray(A('ssm_d').reshape(L, NKS, 128).transpose(0, 2, 1))
    wglu = np.zeros((L, 128, NKS, 2, 128), f)
    wg = A('ssm_w_glu')
    for g in range(C.NG):
        kt, g8 = g // 8, g % 8
        for hf in range(2):
            wglu[:, g8 * 16:(g8 + 1) * 16, kt, hf, g8 * 16:(g8 + 1) * 16] = wg[:, g, :, hf * 16:(hf + 1) * 16]
    shared = dict(
        w_in=A('w_in'), w_out=A('w_out'), w_gate_up=A('w_gate_up'), w_down=A('w_down'), w_ple_gate=A('w_ple_gate'),
        w_ple_proj=A('w_ple_proj'), gains=gains, gssm=gssm, gattn=gattn, gattnT=gattnT, sinks=sinks, ssm_ps=ssm_ps,
        bpad=bpad, cpad=cpad, dcol=dcol, wglu=wglu, consts=_consts())
    xp, xs = A('x_prompt'), A('x_sample')
    pp, psm = A('p_prompt'), A('p_sample')
    ck, cv = A('cache_k'), A('cache_v')
    sre, sim = A('state_ssm_re'), A('state_ssm_im')
    in_maps = []
    NS = C.NS
    for c in range(8):
        b = c % C.BATCH
        sl = slice(NS * b, NS * (b + 1))
        m = dict(shared)
        m['xT'] = np.ascontiguousarray(np.concatenate([xp[b].T, xs[sl, 0].T], axis=1))
        m['peT'] = np.ascontiguousarray(np.concatenate([pp[:, b].transpose(0, 2, 1), psm[:, sl, 0].transpose(0, 2, 1)], axis=2))
        ckc = ck[:, sl].reshape(L, NS, 128, C.KW)
        m['ck'] = np.ascontiguousarray(ckc)
        m['cv'] = np.ascontiguousarray(cv[:, sl].reshape(L, NS, 128, C.KW))
        m['ckT'] = np.ascontiguousarray(ckc.reshape(L, NS, 128, C.NKV, 64).transpose(0, 1, 3, 4, 2))
        st = np.stack([sre[:, sl], sim[:, sl]], axis=1)
        st = st.reshape(L, 2, NS, NP_, 2, 64).transpose(0, 4, 5, 1, 3, 2).reshape(L, 128, 2, NP_, NS)
        m['st0'] = np.ascontiguousarray(st)
        in_maps.append(m)
    return in_maps


def assemble(C, results):
    f = np.float32
    L, NS, B = C.DEPTH, C.NS, C.BATCH
    yp = np.zeros((B, C.SEQ, C.D), f)
    ys = np.zeros((C.DEC, 1, C.D), f)
    kp = np.zeros((L, B, 128, C.NKV, 64), f)
    vp = np.zeros_like(kp)
    srp = np.zeros((L, B, C.NG, 64), f)
    sip = np.zeros_like(srp)
    ksm = np.zeros((L, C.DEC, 128, C.NKV, 64), f)
    vsm = np.zeros_like(ksm)
    srs = np.zeros((L, C.DEC, C.NG, 64), f)
    sis = np.zeros_like(srs)
    for b in range(B):
        r = results[b]
        sl = slice(NS * b, NS * (b + 1))
        yT = r['yT']
        yp[b] = yT[:, :C.SEQ].T
        ys[sl, 0] = yT[:, C.SEQ:].T
        kv = r['kvp']
        kp[:, b] = kv[:, :C.KW].transpose(0, 2, 1).reshape(L, 128, C.NKV, 64)
        vp[:, b] = kv[:, C.KW:].transpose(0, 2, 1).reshape(L, 128, C.NKV, 64)
        ksm[:, sl] = r['ks'].reshape(L, NS, 128, C.NKV, 64)
        vsm[:, sl] = r['vs'].reshape(L, NS, 128, C.NKV, 64)
        st = r['st'].reshape(L, 2, 64, 2, C.NPAIR, 1 + NS)
        st = st.transpose(0, 3, 5, 4, 1, 2).reshape(L, 2, 1 + NS, C.NG, 64)
        srp[:, b], sip[:, b] = st[:, 0, 0], st[:, 1, 0]
        srs[:, sl], sis[:, sl] = st[:, 0, 1:], st[:, 1, 1:]
    return (yp, ys, kp, vp, srp, sip, ksm, vsm, srs, sis)


_CACHE = {}


def run(C, inp):
    key = (C.D, C.SEQ, C.DEPTH, C.BATCH, C.DEC)
    if key not in _CACHE:
        _CACHE[key] = build(C)
    in_maps = prepare_inputs(C, inp)
    res = run_bass_kernel_spmd(_CACHE[key], in_maps, core_ids=list(range(8)))
    return assemble(C, res.results)


def kernel(**inputs):
    return run(Cfg(), inputs)
```

```python
import math
import numpy as np
import concourse.bass as bass
import concourse.mybir as mybir
from concourse.bass_utils import run_bass_kernel_spmd

F32 = mybir.dt.float32
BF16 = mybir.dt.bfloat16
I32 = mybir.dt.int32
AF = mybir.ActivationFunctionType
ALU = mybir.AluOpType
AX = mybir.AxisListType

ENGS = ('pe', 'act', 'dve', 'pool', 'sp')
SEM_ROT = 20000
NDMASEM = 56
EPS = 1e-6
BIG = 1.0e9
TWO_PI = 6.283184
GELU_C = 1.5957691216057308
MAGIC = 12582912.0


class Cfg:
    def __init__(self, D=2048, SEQ=2048, DEPTH=4, BATCH=4, DEC=32):
        self.D, self.SEQ, self.DEPTH, self.BATCH, self.DEC = D, SEQ, DEPTH, BATCH, DEC
        self.NS = DEC // BATCH
        self.T = SEQ + self.NS
        self.AW = D // 2
        self.NH = self.AW // 64
        self.NKV = max(1, self.NH // 8)
        self.GRP = self.NH // self.NKV
        self.KW = self.NKV * 64
        self.SW = D - self.AW
        self.NG = self.SW // 16
        self.NKS = self.SW // 128
        self.NPAIR = self.NG // 2
        self.INW = self.AW + 2 * self.KW + self.SW
        self.DFF = -(-8 * D // (3 * 256)) * 256
        self.KTD = D // 128
        self.KTA = self.AW // 128
        self.KTF = self.DFF // 128
        self.NQB = SEQ // 128
        half = self.T // 2
        assert self.T % 2 == 0
        a = -(-half // 3)
        ch = []
        for h0 in (0, half):
            o = h0
            for i in range(3):
                n = min(a, h0 + half - o)
                ch.append((o, n))
                o += n
        self.chunks = ch
        self.NMAX = a
        assert a <= 512
        self.TQ = 256 if SEQ >= 256 else SEQ
        self.slopes = [2.0 ** (-8.0 * (h + 1) / self.NH) for h in range(self.NH)]


class Buf:
    def __init__(self, ap, excl=False):
        self.ap = ap
        self.w = {}
        self.r = {}
        self.excl = excl


def _upd(d, tok):
    k = tok[0].num
    if k not in d or d[k][1] < tok[1]:
        d[k] = tok


class _Rec:
    def __init__(self):
        self.call = None

    def __getattr__(self, name):
        def f(*a, **k):
            assert self.call is None
            self.call = (name, a, k)
            return self
        return f


class KB:
    def __init__(self, nc):
        self.nc = nc
        self.ops = {e: [] for e in ENGS}
        self.cur = {}
        self.nsem = 0
        self.waited = {}
        self.ndma = 0
        self.dpool = []
        for e in ENGS:
            self._new_sem(e)
        for i in range(NDMASEM):
            self.dpool.append([nc.alloc_semaphore("dq%d" % i), 0])

    def _new_sem(self, e):
        self.nsem += 1
        self.cur[e] = [self.nc.alloc_semaphore("p_%s_%d" % (e, self.nsem)), 0]
        if not hasattr(self, 'owner'):
            self.owner = {}
        self.owner[self.cur[e][0].num] = e

    def _waits(self, eng, deps):
        waits = []
        for d in deps:
            sem, val = d
            if eng == 'pe' and self.owner.get(sem.num) == 'pe':
                continue
            key = (eng, sem.num)
            if self.waited.get(key, 0) >= val:
                continue
            self.waited[key] = val
            waits.append((sem, val))
        return waits

    def _deps(self, R, W):
        deps = []
        for b in R:
            deps.extend(b.w.values())
            if b.excl:
                deps.extend(b.r.values())
        for b in W:
            deps.extend(b.w.values())
            deps.extend(b.r.values())
        return deps

    def _record(self, tok, R, W):
        for b in R:
            _upd(b.r, tok)
        for b in W:
            _upd(b.w, tok)

    def do(self, eng, fn, R=(), W=()):
        rec = _Rec()
        fn(rec)
        if getattr(self, 'buf', None) is not None:
            self.buf.append(('do', eng, rec.call, list(R), list(W)))
            return None
        return self._do(eng, rec.call, R, W)

    def begin_buffer(self):
        self.buf = []

    def mark(self, name):
        if getattr(self, 'buf', None) is not None:
            self.buf.append(('mark', name, None, None, None))

    def end_buffer(self):
        b, self.buf = self.buf, None
        return b

    def flush_interleaved(self, lists):
        m = max(len(x) for x in lists)
        for k in range(m):
            for x in lists:
                if k < len(x):
                    kind, eng, call, R, W = x[k]
                    if kind == 'do':
                        self._do(eng, call, R, W)

    def _do(self, eng, call, R=(), W=()):
        waits = self._waits(eng, self._deps(R, W))
        if self.cur[eng][1] >= SEM_ROT:
            self._new_sem(eng)
        c = self.cur[eng]
        c[1] += 1
        name, a, k = call
        self.ops[eng].append((waits, (lambda e, name=name, a=a, k=k: getattr(e, name)(*a, **k)), c[0], 1))
        tok = (c[0], c[1])
        self._record(tok, R, W)
        return tok

    def dma(self, eng, out, in_, R=(), W=(), **kw):
        waits = self._waits(eng, self._deps(R, W))
        ds = self.dpool[self.ndma % NDMASEM]
        self.ndma += 1
        ds[1] += 16
        self.ops[eng].append((waits, lambda e: e.dma_start(out=out, in_=in_, **kw), ds[0], 16))
        tok = (ds[0], ds[1])
        self._record(tok, R, W)
        return tok

    def wait_only(self, eng, deps):
        waits = self._waits(eng, deps)
        if waits:
            self.ops[eng].append((waits, None, None, 0))

    def barrier(self):
        toks = [(c[0], c[1]) for c in self.cur.values() if c[1] > 0]
        toks += [(d[0], d[1]) for d in self.dpool if d[1] > 0]
        for e in ENGS:
            self.wait_only(e, toks)

    def emit(self):
        nc = self.nc
        with nc.Block() as block:
            def run(name):
                def f(e):
                    for waits, fn, sem, inc in self.ops[name]:
                        for (s, v) in waits:
                            e.wait_ge(s, v)
                        if fn is not None:
                            fn(e).then_inc(sem, inc)
                return f
            block.tensor(run('pe'))
            block.scalar(run('act'))
            block.vector(run('dve'))
            block.gpsimd(run('pool'))
            block.sync(run('sp'))


class Arena:
    LO = 16512
    HI = 229376

    def __init__(self, nc):
        self.nc = nc
        self.cur = self.LO
        self.n = 0

    def alloc(self, shape, dt=F32):
        esz = 2 if dt == BF16 else 4
        nb = esz
        for s in shape[1:]:
            nb *= s
        nb = (nb + 63) // 64 * 64
        assert self.cur + nb <= self.HI, ("SBUF overflow", self.cur, nb)
        self.n += 1
        t = self.nc.alloc_sbuf_tensor_at("sb%d" % self.n, list(shape), dt, offset=self.cur)
        self.cur += nb
        return Buf(t.ap())


def build(cfg):
    C = cfg
    nc = bass.Bass("TRN2", target_bir_lowering=False)
    kb = KB(nc)
    ar = Arena(nc)
    D, T, SEQ, NS, DEPTH = C.D, C.T, C.SEQ, C.NS, C.DEPTH
    AW, NH, NKV, KW, SW, NKS, NPAIR, INW, DFF = C.AW, C.NH, C.NKV, C.KW, C.SW, C.NKS, C.NPAIR, C.INW, C.DFF
    KTD, KTA, KTF, NQB = C.KTD, C.KTA, C.KTF, C.NQB
    chunks, NMAX, TQ = C.chunks, C.NMAX, C.TQ
    KTMAX = max(KTF, KTD)

    def din(name, shape):
        return nc.dram_tensor(name, list(shape), F32, kind="ExternalInput").ap()

    def dout(name, shape):
        return nc.dram_tensor(name, list(shape), F32, kind="ExternalOutput").ap()

    def dscr(name, shape, dt=F32):
        return nc.dram_tensor(name, list(shape), dt).ap()

    xT_in = din("xT", [D, T])
    peT_in = din("peT", [DEPTH, 256, T])
    w_in = din("w_in", [DEPTH, D, INW])
    w_out = din("w_out", [DEPTH, D, D])
    w_gu = din("w_gate_up", [DEPTH, D, 2 * DFF])
    w_dn = din("w_down", [DEPTH, DFF, D])
    w_pg = din("w_ple_gate", [DEPTH, D, D])
    w_pp = din("w_ple_proj", [DEPTH, 256, D])
    gains = din("gains", [DEPTH, 4, 128, KTD])
    gssm_in = din("gssm", [DEPTH, 128, NKS])
    gattn_in = din("gattn", [DEPTH, 128, AW])
    gattnT_in = din("gattnT", [DEPTH, 64, NH])
    sinks_in = din("sinks", [DEPTH, 128, NH])
    ckT_in = din("ckT", [DEPTH, NS, NKV, 64, 128])
    ck_in = din("ck", [DEPTH, NS, 128, KW])
    cv_in = din("cv", [DEPTH, NS, 128, KW])
    ssm_ps_in = din("ssm_ps", [DEPTH, 128, 3, NPAIR])
    bpad_in = din("bpad", [DEPTH, 128, 2, NPAIR, 128])
    cpad_in = din("cpad", [DEPTH, 128, NKS, 2, 4, 128])
    dcol_in = din("dcol", [DEPTH, 128, NKS])
    wglu_in = din("wglu", [DEPTH, 128, NKS, 2, 128])
    st0_in = din("st0", [DEPTH, 128, 2, NPAIR, NS])
    consts_in = din("consts", [128, 128 + 256 + 256 + 512])

    yT_out = dout("yT", [D, T])
    kvp_out = dout("kvp", [DEPTH, 2 * KW, 128])
    ks_out = dout("ks", [DEPTH, NS, 128, KW])
    vs_out = dout("vs", [DEPTH, NS, 128, KW])
    st_out = dout("st", [DEPTH, 128, 2, NPAIR, 1 + NS])

    Xd = dscr("X", [D, T])
    ZBd = dscr("ZB", [INW, T], BF16)
    ZFd = dscr("ZF", [2 * KW, T])
    Od = dscr("O", [D, T])
    SSd = dscr("SS", [SW, T])
    ACTd = dscr("ACTs", [DFF, T], BF16)
    MGSd = dscr("MGS", [AW, NS], BF16)

    ident = ar.alloc([128, 128], BF16)
    ones = ar.alloc([128, 128], BF16)
    DIST = ar.alloc([128, 256])
    DIST0 = ar.alloc([128, 256])
    IOT = ar.alloc([128, 512])
    srcbuf = ar.alloc([128, max(KTD * T, KTF * (T // 2))], BF16)
    phase_mark = ar.cur

    PS = [Buf(nc.alloc_psum_tensor("ps%d" % i, [128, 512], F32).ap(), excl=True) for i in range(8)]

    def src3(kt_n, tn):
        return srcbuf.ap[:, 0:kt_n * tn].rearrange("p (k t) -> p k t", t=tn)

    SRC_D = src3(KTD, T)

    outtoks = []

    class _Stop(Exception):
        pass
    import os as _os
    _stop_after = int(_os.environ.get('STOP_AFTER', '100000'))
    _pc = [0]

    _marks = []

    def new_phase(name=None):
        import inspect
        if name is None:
            name = inspect.stack()[1].function + ':' + str(inspect.stack()[1].lineno)
        _marks.append((name, sum(1 for o in kb.ops['pe'] if o[1] is not None)))
        _pc[0] += 1
        if _pc[0] > _stop_after:
            raise _Stop()
        kb.barrier()
        ar.cur = phase_mark

    kb.dma('pool', ident.ap, consts_in[:, 0:128], W=[ident])
    kb.dma('sp', DIST.ap, consts_in[:, 128:384], W=[DIST])
    kb.dma('sp', DIST0.ap, consts_in[:, 384:640], W=[DIST0])
    kb.dma('sp', IOT.ap, consts_in[:, 640:1152], W=[IOT])
    kb.do('pool', lambda e: e.memset(ones.ap, 1.0), W=[ones])
    xw = Buf(Xd)
    for r in range(KTD):
        kb.dma('sp', Xd[r * 128:(r + 1) * 128, :], xT_in[r * 128:(r + 1) * 128, :], W=[xw])

    rot = {}

    def nxt(key, n):
        rot[key] = (rot.get(key, -1) + 1) % n
        return rot[key]

    def norm_phase(src_d, KT, g1_ap, resid_d=None, out=None, g2_ap=None, dst_kt0=0, Fdim=None):
        new_phase()
        Fdim = KT * 128
        g1 = ar.alloc([128, KT])
        kb.dma('sp', g1.ap, g1_ap, W=[g1])
        g2 = None
        if g2_ap is not None:
            g2 = ar.alloc([128, KT])
            kb.dma('sp', g2.ap, g2_ap, W=[g2])
        xin = [ar.alloc([128, KT, NMAX]) for _ in range(2)]
        xr = [ar.alloc([128, KT, NMAX]) for _ in range(2)] if resid_d is not None else None
        sq = [ar.alloc([128, NMAX], BF16) for _ in range(3)]
        rs = [ar.alloc([128, NMAX]) for _ in range(2)]
        tmp = [ar.alloc([128, NMAX]) for _ in range(3)]
        sv = src_d.rearrange("(k p) t -> p k t", p=128)
        xv = resid_d.rearrange("(k p) t -> p k t", p=128) if resid_d is not None else None
        srcB = Buf(src_d)
        psA, psB = PS[6], PS[7]

        def rstd_of(buf_in, n, psb, rsb):
            for kt in range(KT):
                s = sq[nxt('sq', 3)]
                kb.do('act', lambda e, kt=kt, s=s: e.activation(out=s.ap[:, 0:n], in_=buf_in.ap[:, kt, 0:n], func=AF.Square),
                      R=[buf_in], W=[s])
                kb.do('pe', lambda e, kt=kt, s=s: e.matmul(psb.ap[:, 0:n], lhsT=ones.ap, rhs=s.ap[:, 0:n],
                                                           start=(kt == 0), stop=(kt == KT - 1)), R=[s, ones], W=[psb])
            kb.do('dve', lambda e: e.tensor_scalar(out=rsb.ap[:, 0:n], in0=psb.ap[:, 0:n], scalar1=1.0 / Fdim, scalar2=EPS,
                                                   op0=ALU.mult, op1=ALU.add), R=[psb], W=[rsb])
            kb.do('act', lambda e: e.activation(out=rsb.ap[:, 0:n], in_=rsb.ap[:, 0:n], func=AF.Ln), R=[rsb], W=[rsb])
            kb.do('act', lambda e: e.activation(out=rsb.ap[:, 0:n], in_=rsb.ap[:, 0:n], func=AF.Exp, scale=-0.5), R=[rsb], W=[rsb])

        for ci, (t0, n) in enumerate(chunks):
            xi = xin[ci % 2]
            kb.dma('sp', xi.ap[:, :, 0:n], sv[:, :, t0:t0 + n], R=[srcB], W=[xi])
            r1 = rs[0]
            rstd_of(xi, n, psA, r1)
            if resid_d is None:
                for kt in range(KT):
                    kb.do('dve', lambda e, kt=kt: e.scalar_tensor_tensor(
                        out=SRC_D[:, dst_kt0 + kt, t0:t0 + n], in0=xi.ap[:, kt, 0:n], scalar=g1.ap[:, kt:kt + 1],
                        in1=r1.ap[:, 0:n], op0=ALU.mult, op1=ALU.mult), R=[xi, g1, r1], W=[srcbuf])
                continue
            xx = xr[ci % 2]
            xB = Buf(resid_d)
            kb.dma('sp', xx.ap[:, :, 0:n], xv[:, :, t0:t0 + n], R=[xB], W=[xx])
            for kt in range(KT):
                tb = tmp[nxt('tmp', 3)]
                kb.do('dve', lambda e, kt=kt, tb=tb: e.scalar_tensor_tensor(
                    out=tb.ap[:, 0:n], in0=xi.ap[:, kt, 0:n], scalar=g1.ap[:, kt:kt + 1], in1=r1.ap[:, 0:n],
                    op0=ALU.mult, op1=ALU.mult), R=[xi, g1, r1], W=[tb])
                kb.do('dve', lambda e, kt=kt, tb=tb: e.tensor_tensor(out=xx.ap[:, kt, 0:n], in0=xx.ap[:, kt, 0:n],
                                                                      in1=tb.ap[:, 0:n], op=ALU.add), R=[tb], W=[xx])
            outtoks.append(kb.dma('sp', xv[:, :, t0:t0 + n], xx.ap[:, :, 0:n], R=[xx], W=[xB]))
            if out == 'norm':
                r2 = rs[1]
                rstd_of(xx, n, psB, r2)
                for kt in range(KT):
                    kb.do('dve', lambda e, kt=kt: e.scalar_tensor_tensor(
                        out=SRC_D[:, dst_kt0 + kt, t0:t0 + n], in0=xx.ap[:, kt, 0:n], scalar=g2.ap[:, kt:kt + 1],
                        in1=r2.ap[:, 0:n], op0=ALU.mult, op1=ALU.mult), R=[xx, g2, r2], W=[srcbuf])
            elif out == 'cast':
                kb.do('act', lambda e: e.activation(out=SRC_D[:, dst_kt0:dst_kt0 + KT, t0:t0 + n], in_=xx.ap[:, :, 0:n],
                                                    func=AF.Copy), R=[xx], W=[srcbuf])

    def linear_phase(srcv, KT, W_ap, cols, epi, W2cols=None, tchunks=None, fresh=True):
        if fresh:
            new_phase()
        wb = [ar.alloc([128, KT, 128], BF16) for _ in range(3)]
        wb2 = [ar.alloc([128, KT, 128], BF16) for _ in range(2)] if W2cols is not None else None
        wv = W_ap.rearrange("(k p) n -> p k n", p=128)
        tch = tchunks if tchunks is not None else chunks
        st = {}

        def go(mi):
            c0 = cols[mi]
            w = wb[mi % 3]
            kb.dma('pool', w.ap, wv[:, :, c0:c0 + 128], W=[w])
            w2 = None
            if W2cols is not None:
                w2 = wb2[mi % 2]
                kb.dma('pool', w2.ap, wv[:, :, W2cols[mi]:W2cols[mi] + 128], W=[w2])
            for ci, (t0, n, s0) in enumerate(tch):
                if W2cols is None:
                    pb = PS[nxt('lin', 6)]
                    pb2 = None
                else:
                    j = nxt('lin2', 3)
                    pb, pb2 = PS[2 * j], PS[2 * j + 1]
                for kt in range(KT):
                    kb.do('pe', lambda e, kt=kt, pb=pb, w=w: e.matmul(pb.ap[:, 0:n], lhsT=w.ap[:, kt, :], rhs=srcv[:, kt, s0:s0 + n],
                                                                   start=(kt == 0), stop=(kt == KT - 1)), R=[w, srcbuf], W=[pb])
                if pb2 is not None:
                    for kt in range(KT):
                        kb.do('pe', lambda e, kt=kt, pb2=pb2, w2=w2: e.matmul(pb2.ap[:, 0:n], lhsT=w2.ap[:, kt, :], rhs=srcv[:, kt, s0:s0 + n],
                                                                         start=(kt == 0), stop=(kt == KT - 1)), R=[w2, srcbuf], W=[pb2])
                epi(mi, ci, t0, n, pb, pb2)
        return go, st

    full_chunks = [(t0, n, t0) for (t0, n) in chunks]

    def attention_phase(l):
        new_phase()
        kTd = ar.alloc([128, NKV, 128 + T], BF16)
        Vtm = ar.alloc([128, NQB + 1, KW], BF16)
        vTi = [ar.alloc([128, 128], BF16) for _ in range(2)]
        qTb = [ar.alloc([128, KTA, 128], BF16) for _ in range(2)]
        atm = [ar.alloc([128, AW]) for _ in range(2)]
        hn = [ar.alloc([128, AW], BF16) for _ in range(2)]
        junk = ar.alloc([128, AW], BF16)
        sm2 = [ar.alloc([128, 4]) for _ in range(2)]
        gat = ar.alloc([128, AW])
        snk = ar.alloc([128, NH])
        zb = Buf(ZBd)
        kb.dma('sp', gat.ap, gattn_in[l], W=[gat])
        kb.dma('sp', snk.ap, sinks_in[l], W=[snk])
        kb.do('pool', lambda e: e.memset(kTd.ap[:, :, 0:128], 0.0), W=[kTd])
        kb.do('pool', lambda e: e.memset(Vtm.ap[:, 0, :], 0.0), W=[Vtm])
        for g in range(NKV):
            for cp in range(2):
                kb.dma('sp', kTd.ap[cp * 64:(cp + 1) * 64, g, 128:128 + T], ZBd[AW + g * 64:AW + (g + 1) * 64, :], R=[zb], W=[kTd])
        PSb = [Buf(PS[i].ap.bitcast(BF16), excl=True) for i in range(8)]
        for i in range(8):
            PSb[i].w, PSb[i].r = PS[i].w, PS[i].r
        for b in range(NQB):
            vi = vTi[b % 2]
            kb.dma('sp', vi.ap[0:KW, :], ZBd[AW + KW:AW + 2 * KW, b * 128:(b + 1) * 128], R=[zb], W=[vi])
            pt = PSb[7]
            kb.do('pe', lambda e, vi=vi, pt=pt: e.transpose(out=pt.ap[:, 0:KW], in_=vi.ap[0:KW, :], identity=ident.ap[0:KW, 0:KW]),
                  R=[vi, ident], W=[pt])
            kb.do('act', lambda e, b=b, pt=pt: e.activation(out=Vtm.ap[:, b + 1, :], in_=pt.ap[:, 0:KW], func=AF.Copy), R=[pt], W=[Vtm])

        NRR = 8
        Sb = [ar.alloc([128, 256]) for _ in range(NRR)]
        Pb = [ar.alloc([128, 256], BF16) for _ in range(NRR)]
        PTs = [ar.alloc([128, 2, 128], BF16) for _ in range(NRR)]
        sm = [ar.alloc([128, 8]) for _ in range(NRR)]

        def pipeline(units, stages, after=None):
            ns = len(stages)
            for t in range(len(units) + ns - 1):
                for si, st in enumerate(stages):
                    ui = t - si
                    if 0 <= ui < len(units):
                        st(units[ui])
                        if si == ns - 1 and after is not None:
                            after(ui)

        def sA(u):
            S_, m_, npart, nk, h, sps = Sb[u['i']], sm[u['i']], u['np'], u['nk'], u['h'], u['sps']
            kb.do('dve', lambda e: e.scalar_tensor_tensor(out=S_.ap[0:npart, 0:nk], in0=u['dist'], scalar=-C.slopes[h],
                                                          in1=sps.ap[0:npart, 0:nk], op0=ALU.mult, op1=ALU.add),
                  R=[sps, DIST, DIST0], W=[S_])
            kb.do('dve', lambda e: e.reduce_max(out=m_.ap[0:npart, 0:1], in_=S_.ap[0:npart, 0:nk], axis=AX.X), R=[S_], W=[m_])
            kb.do('dve', lambda e: e.tensor_tensor(out=m_.ap[0:npart, 0:1], in0=m_.ap[0:npart, 0:1], in1=snk.ap[0:npart, h:h + 1],
                                                   op=ALU.max), R=[snk], W=[m_])
            kb.do('dve', lambda e: e.tensor_scalar(out=m_.ap[0:npart, 1:2], in0=m_.ap[0:npart, 0:1], scalar1=-1.0, scalar2=None,
                                                   op0=ALU.mult), R=[], W=[m_])

        def sB(u):
            S_, P_, m_, npart, nk, h = Sb[u['i']], Pb[u['i']], sm[u['i']], u['np'], u['nk'], u['h']
            kb.do('act', lambda e: e.activation(out=P_.ap[0:npart, 0:nk], in_=S_.ap[0:npart, 0:nk], func=AF.Exp,
                                                bias=m_.ap[0:npart, 1:2], scale=1.0, accum_out=m_.ap[0:npart, 2:3]),
                  R=[S_, m_], W=[P_, m_])
            kb.do('act', lambda e: e.activation(out=m_.ap[0:npart, 3:4], in_=snk.ap[0:npart, h:h + 1], func=AF.Exp,
                                                bias=m_.ap[0:npart, 1:2], scale=1.0), R=[snk, m_], W=[m_])

        def sC(u):
            m_, npart = sm[u['i']], u['np']
            kb.do('dve', lambda e: e.tensor_tensor(out=m_.ap[0:npart, 4:5], in0=m_.ap[0:npart, 2:3], in1=m_.ap[0:npart, 3:4],
                                                   op=ALU.add), R=[m_], W=[m_])
            kb.do('dve', lambda e: e.reciprocal(out=m_.ap[0:npart, 5:6], in_=m_.ap[0:npart, 4:5]), R=[m_], W=[m_])

        units = []
        for b in range(NQB):
            for h in range(NH):
                units.append(dict(b=b, h=h, g=h // C.GRP, hp=(h % 2) * 64, np=128, nk=256,
                                  dist=(DIST0 if b == 0 else DIST).ap))

        def pA(u):
            b, h, g, hp = u['b'], u['h'], u['g'], u['hp']
            if h == 0:
                qb = qTb[b % 2]
                kb.dma('sp', qb.ap, ZBd[0:AW, b * 128:(b + 1) * 128].rearrange("(k p) t -> p k t", p=128), R=[zb], W=[qb])
            qb = qTb[b % 2]
            u['i'] = nxt('att', NRR)
            u['sps'] = sps = PS[nxt('sps', 3)]
            kb.do('pe', lambda e: e.matmul(sps.ap[:, 0:256], lhsT=qb.ap[hp:hp + 64, h // 2, :],
                                           rhs=kTd.ap[hp:hp + 64, g, b * 128:b * 128 + 256], start=True, stop=True), R=[qb, kTd], W=[sps])
            sA(u)

        def pC(u):
            sC(u)
            i = u['i']
            u['ptp'] = ptp = PSb[3 + nxt('ptp', 2)]
            for j in range(2):
                kb.do('pe', lambda e, j=j: e.transpose(out=ptp.ap[:, j * 128:(j + 1) * 128], in_=Pb[i].ap[:, j * 128:(j + 1) * 128],
                                                       identity=ident.ap), R=[Pb[i], ident], W=[ptp])

        def pD(u):
            i, ptp = u['i'], u['ptp']
            kb.do('act', lambda e: e.activation(out=PTs[i].ap, in_=ptp.ap[:, 0:256].rearrange("p (j q) -> p j q", j=2), func=AF.Copy),
                  R=[ptp], W=[PTs[i]])

        def pE(u):
            i, b, g = u['i'], u['b'], u['g']
            u['ops'] = ops_ = PS[5 + nxt('ops', 2)]
            for j in range(2):
                kb.do('pe', lambda e, j=j: e.matmul(ops_.ap[:, 0:64], lhsT=PTs[i].ap[:, j, :], rhs=Vtm.ap[:, b + j, g * 64:(g + 1) * 64],
                                                    start=(j == 0), stop=(j == 1)), R=[PTs[i], Vtm], W=[ops_])

        def pF(u):
            i, h, ops_ = u['i'], u['h'], u['ops']
            am = atm[u['b'] % 2]
            kb.do('dve', lambda e: e.tensor_scalar(out=am.ap[:, h * 64:(h + 1) * 64], in0=ops_.ap[:, 0:64], scalar1=sm[i].ap[:, 5:6],
                                                   scalar2=None, op0=ALU.mult), R=[ops_, sm[i]], W=[am])

        def block_done(ui):
            u = units[ui]
            if u['h'] != NH - 1:
                return
            b = u['b']
            am, s2, hb = atm[b % 2], sm2[b % 2], hn[b % 2]
            kb.do('act', lambda e: e.activation(out=junk.ap, in_=am.ap, func=AF.Square, accum_out=s2.ap[:, 0:1]), R=[am], W=[junk, s2])
            kb.do('dve', lambda e: e.tensor_scalar(out=s2.ap[:, 1:2], in0=s2.ap[:, 0:1], scalar1=1.0 / AW, scalar2=EPS,
                                                   op0=ALU.mult, op1=ALU.add), R=[s2], W=[s2])
            kb.do('act', lambda e: e.activation(out=s2.ap[:, 1:2], in_=s2.ap[:, 1:2], func=AF.Ln), R=[s2], W=[s2])
            kb.do('act', lambda e: e.activation(out=s2.ap[:, 1:2], in_=s2.ap[:, 1:2], func=AF.Exp, scale=-0.5), R=[s2], W=[s2])
            kb.do('dve', lambda e: e.scalar_tensor_tensor(out=hb.ap, in0=am.ap, scalar=s2.ap[:, 1:2], in1=gat.ap, op0=ALU.mult, op1=ALU.mult),
                  R=[am, s2, gat], W=[hb])
            for kt in range(KTA):
                pt = PSb[7]
                kb.do('pe', lambda e, kt=kt: e.transpose(out=pt.ap[:, 0:128], in_=hb.ap[:, kt * 128:(kt + 1) * 128], identity=ident.ap),
                      R=[hb, ident], W=[pt])
                kb.do('act', lambda e, kt=kt: e.activation(out=SRC_D[:, kt, b * 128:(b + 1) * 128], in_=pt.ap[:, 0:128], func=AF.Copy),
                      R=[pt], W=[srcbuf])

        pipeline(units, [pA, sB, pC, pD, pE, pF], after=block_done)

        kS = [ar.alloc([128, NKV, 132], BF16) for _ in range(2)]
        vS = [ar.alloc([128, KW], BF16) for _ in range(2)]
        vN = [ar.alloc([1, KW], BF16) for _ in range(2)]
        qS = ar.alloc([128, KTA, NS], BF16)
        PnS = [ar.alloc([1, 132], BF16) for _ in range(NRR)]
        PTS = [ar.alloc([128, 2], BF16) for _ in range(NRR)]
        aS = ar.alloc([64, NH, NS])
        zf = Buf(ZFd)
        kb.dma('sp', qS.ap, ZBd[0:AW, SEQ:SEQ + NS].rearrange("(k p) t -> p k t", p=128), R=[zb], W=[qS])
        sunits = []
        for n in range(NS):
            for h in range(NH):
                sunits.append(dict(n=n, h=h, g=h // C.GRP, hp=(h % 2) * 64, np=1, nk=129, dist=DIST.ap[0:1, 0:129]))

        def qA(u):
            n, h, g, hp = u['n'], u['h'], u['g'], u['hp']
            ks_, vs_, vn_ = kS[n % 2], vS[n % 2], vN[n % 2]
            if h == 0:
                for g_ in range(NKV):
                    for cp in range(2):
                        kb.dma('pool', ks_.ap[cp * 64:(cp + 1) * 64, g_, 0:128], ckT_in[l, n, g_], W=[ks_])
                kb.do('pool', lambda e: e.tensor_copy(out=ks_.ap[:, :, 128:129], in_=kTd.ap[:, :, 128 + SEQ + n:128 + SEQ + n + 1]),
                      R=[kTd], W=[ks_])
                kb.dma('pool', vs_.ap, cv_in[l, n], W=[vs_])
                kb.dma('pool', vn_.ap, ZFd[KW:2 * KW, SEQ + n:SEQ + n + 1].rearrange("k o -> o k"), R=[zf], W=[vn_], allow_slow_non_contiguous=True)
                outtoks.append(kb.dma('sp', ks_out[l, n, 0:127, :], ck_in[l, n, 1:128, :]))
                outtoks.append(kb.dma('sp', vs_out[l, n, 0:127, :], cv_in[l, n, 1:128, :]))
                outtoks.append(kb.dma('sp', ks_out[l, n, 127:128, :], ZFd[0:KW, SEQ + n:SEQ + n + 1].rearrange("k o -> o k"), R=[zf], allow_slow_non_contiguous=True))
                outtoks.append(kb.dma('sp', vs_out[l, n, 127:128, :], ZFd[KW:2 * KW, SEQ + n:SEQ + n + 1].rearrange("k o -> o k"), R=[zf], allow_slow_non_contiguous=True))
            u['i'] = nxt('att', NRR)
            u['sps'] = sps = PS[nxt('sps', 3)]
            kb.do('pe', lambda e: e.matmul(sps.ap[0:1, 0:129], lhsT=qS.ap[hp:hp + 64, h // 2, n:n + 1], rhs=ks_.ap[hp:hp + 64, g, 0:129],
                                           start=True, stop=True), R=[qS, ks_], W=[sps])
            sA(u)

        def qC(u):
            sC(u)
            i = u['i']
            pn = PnS[i]
            kb.do('dve', lambda e: e.tensor_scalar(out=pn.ap[0:1, 0:129], in0=Pb[i].ap[0:1, 0:129], scalar1=sm[i].ap[0:1, 5:6], scalar2=None,
                                                   op0=ALU.mult), R=[Pb[i], sm[i]], W=[pn])
            u['ptp'] = ptp = PSb[3 + nxt('ptp', 2)]
            kb.do('pe', lambda e: e.transpose(out=ptp.ap[:, 0:1], in_=pn.ap[0:1, 0:128], identity=ident.ap[0:1, 0:1]), R=[pn, ident], W=[ptp])

        def qD(u):
            i, ptp = u['i'], u['ptp']
            kb.do('act', lambda e: e.activation(out=PTS[i].ap[:, 0:1], in_=ptp.ap[:, 0:1], func=AF.Copy), R=[ptp], W=[PTS[i]])

        def qE(u):
            i, g, n = u['i'], u['g'], u['n']
            vs_, vn_, pn = vS[n % 2], vN[n % 2], PnS[i]
            u['ops'] = ops_ = PS[5 + nxt('ops', 2)]
            kb.do('pe', lambda e: e.matmul(ops_.ap[0:64, 0:1], lhsT=vs_.ap[:, g * 64:(g + 1) * 64], rhs=PTS[i].ap[:, 0:1], start=True, stop=False),
                  R=[PTS[i], vs_], W=[ops_])
            kb.do('pe', lambda e: e.matmul(ops_.ap[0:64, 0:1], lhsT=vn_.ap[0:1, g * 64:(g + 1) * 64], rhs=pn.ap[0:1, 128:129], start=False, stop=True),
                  R=[pn, vn_], W=[ops_])

        def qF(u):
            ops_, h, n = u['ops'], u['h'], u['n']
            kb.do('act', lambda e: e.activation(out=aS.ap[:, h, n:n + 1], in_=ops_.ap[0:64, 0:1], func=AF.Copy), R=[ops_], W=[aS])

        if _os.environ.get('SEQ_SAMPLE'):
            for u_ in sunits:
                for st_ in (qA, sB, qC, qD, qE, qF):
                    st_(u_)
        else:
            _grp = _os.environ.get('SGRP', '0,2,4')
            _st = [qA, sB, qC, qD, qE, qF]
            _cuts = [int(x) for x in _grp.split(',')]
            _stages = []
            for _i, _c in enumerate(_cuts):
                _hi = _cuts[_i + 1] if _i + 1 < len(_cuts) else 6
                _fs = _st[_c:_hi]
                _stages.append(lambda u, _fs=_fs: [f_(u) for f_ in _fs])
            pipeline(sunits, _stages)
        sqS = ar.alloc([64, NH * NS], BF16)
        ssS = ar.alloc([64, NS])
        gT = ar.alloc([64, NH])
        hS = ar.alloc([64, NH, NS])
        hSb = ar.alloc([64, NH, NS], BF16)
        kb.dma('sp', gT.ap, gattnT_in[l], W=[gT])
        kb.do('act', lambda e: e.activation(out=sqS.ap, in_=aS.ap.rearrange("p h n -> p (h n)"), func=AF.Square), R=[aS], W=[sqS])
        kb.do('pe', lambda e: e.matmul(PS[7].ap[0:64, 0:NH * NS], lhsT=ones.ap[0:64, 0:64], rhs=sqS.ap, start=True, stop=True),
              R=[sqS, ones], W=[PS[7]])
        kb.do('dve', lambda e: e.tensor_reduce(out=ssS.ap, in_=PS[7].ap[0:64, 0:NH * NS].rearrange("p (h n) -> p n h", n=NS),
                                               axis=AX.X, op=ALU.add), R=[PS[7]], W=[ssS])
        kb.do('dve', lambda e: e.tensor_scalar(out=ssS.ap, in0=ssS.ap, scalar1=1.0 / AW, scalar2=EPS, op0=ALU.mult, op1=ALU.add),
              R=[ssS], W=[ssS])
        kb.do('act', lambda e: e.activation(out=ssS.ap, in_=ssS.ap, func=AF.Ln), R=[ssS], W=[ssS])
        kb.do('act', lambda e: e.activation(out=ssS.ap, in_=ssS.ap, func=AF.Exp, scale=-0.5), R=[ssS], W=[ssS])
        kb.do('dve', lambda e: e.tensor_tensor(out=hS.ap, in0=aS.ap, in1=gT.ap.unsqueeze(2).broadcast_to([64, NH, NS]), op=ALU.mult),
              R=[aS, gT], W=[hS])
        kb.do('dve', lambda e: e.tensor_tensor(out=hSb.ap, in0=hS.ap, in1=ssS.ap.unsqueeze(1).broadcast_to([64, NH, NS]), op=ALU.mult),
              R=[hS, ssS], W=[hSb])
        mg = Buf(MGSd)
        kb.dma('sp', MGSd.rearrange("(h d) n -> d h n", d=64), hSb.ap, R=[hSb], W=[mg])
        kb.dma('sp', SRC_D[:, 0:KTA, SEQ:SEQ + NS], MGSd.rearrange("(k p) n -> p k n", p=128), R=[mg], W=[srcbuf])

    def ssm_phase(l):
        new_phase()
        ps_ = ar.alloc([128, 3, NPAIR])
        kb.dma('sp', ps_.ap, ssm_ps_in[l], W=[ps_])
        NV = 17
        v = ar.alloc([128, NV, NPAIR])
        vi = ar.alloc([128, NPAIR], I32)
        are, aim, ldt = ps_.ap[:, 0, :], ps_.ap[:, 1, :], ps_.ap[:, 2, :]
        V_DT, V_DRE, V_TH, V_R, V_A, V_SIN, V_COS, V_ABR, V_ABI, V_FR, V_FI, V_IFR, V_IFI, V_T1, V_T2, V_T3, V_NFI = range(17)

        def vv(i):
            return v.ap[:, i, :]

        def tiny(eng, fn):
            kb.do(eng, fn, R=[v, ps_], W=[v])

        def wrap_turns(eng_ap_in, out_i):
            kb.do('dve', lambda e: e.tensor_copy(out=vi.ap, in_=eng_ap_in), R=[v], W=[vi])
            kb.do('dve', lambda e: e.tensor_copy(out=vv(V_T1), in_=vi.ap), R=[vi, v], W=[v])
            tiny('dve', lambda e: e.tensor_tensor(out=vv(out_i), in0=eng_ap_in, in1=vv(V_T1), op=ALU.subtract))
            tiny('dve', lambda e: e.tensor_scalar(out=vv(V_T1), in0=vv(out_i), scalar1=0.5, scalar2=None, op0=ALU.is_gt))
            tiny('dve', lambda e: e.tensor_tensor(out=vv(out_i), in0=vv(out_i), in1=vv(V_T1), op=ALU.subtract))
            tiny('dve', lambda e: e.tensor_scalar(out=vv(V_T1), in0=vv(out_i), scalar1=-0.5, scalar2=None, op0=ALU.is_lt))
            tiny('dve', lambda e: e.tensor_tensor(out=vv(out_i), in0=vv(out_i), in1=vv(V_T1), op=ALU.add))

        tiny('act', lambda e: e.activation(out=vv(V_DT), in_=ldt, func=AF.Exp))
        tiny('dve', lambda e: e.tensor_tensor(out=vv(V_DRE), in0=vv(V_DT), in1=are, op=ALU.mult))
        tiny('dve', lambda e: e.tensor_tensor(out=vv(V_TH), in0=vv(V_DT), in1=aim, op=ALU.mult))
        tiny('act', lambda e: e.activation(out=vv(V_R), in_=vv(V_DRE), func=AF.Exp))
        tiny('dve', lambda e: e.tensor_scalar(out=vv(V_T2), in0=vv(V_TH), scalar1=1.0 / (2 * math.pi), scalar2=None, op0=ALU.mult))
        wrap_turns(vv(V_T2), V_A)
        tiny('act', lambda e: e.activation(out=vv(V_SIN), in_=vv(V_A), func=AF.Sin, scale=TWO_PI))
        tiny('dve', lambda e: e.tensor_scalar(out=vv(V_T2), in0=vv(V_A), scalar1=0.25, scalar2=None, op0=ALU.add))
        wrap_turns(vv(V_T2), V_T3)
        tiny('act', lambda e: e.activation(out=vv(V_COS), in_=vv(V_T3), func=AF.Sin, scale=TWO_PI))
        tiny('dve', lambda e: e.tensor_tensor(out=vv(V_ABR), in0=vv(V_R), in1=vv(V_COS), op=ALU.mult))
        tiny('dve', lambda e: e.tensor_tensor(out=vv(V_ABI), in0=vv(V_R), in1=vv(V_SIN), op=ALU.mult))
        tiny('dve', lambda e: e.tensor_tensor(out=vv(V_T1), in0=are, in1=are, op=ALU.mult))
        tiny('dve', lambda e: e.tensor_tensor(out=vv(V_T2), in0=aim, in1=aim, op=ALU.mult))
        tiny('dve', lambda e: e.tensor_tensor(out=vv(V_T1), in0=vv(V_T1), in1=vv(V_T2), op=ALU.add))
        tiny('dve', lambda e: e.reciprocal(out=vv(V_T1), in_=vv(V_T1)))
        tiny('dve', lambda e: e.tensor_scalar(out=vv(V_T2), in0=vv(V_ABR), scalar1=-1.0, scalar2=None, op0=ALU.add))
        tiny('dve', lambda e: e.tensor_tensor(out=vv(V_FR), in0=vv(V_T2), in1=are, op=ALU.mult))
        tiny('dve', lambda e: e.tensor_tensor(out=vv(V_T3), in0=vv(V_ABI), in1=aim, op=ALU.mult))
        tiny('dve', lambda e: e.tensor_tensor(out=vv(V_FR), in0=vv(V_FR), in1=vv(V_T3), op=ALU.add))
        tiny('dve', lambda e: e.tensor_tensor(out=vv(V_FR), in0=vv(V_FR), in1=vv(V_T1), op=ALU.mult))
        tiny('dve', lambda e: e.tensor_tensor(out=vv(V_FI), in0=vv(V_ABI), in1=are, op=ALU.mult))
        tiny('dve', lambda e: e.tensor_tensor(out=vv(V_T3), in0=vv(V_T2), in1=aim, op=ALU.mult))
        tiny('dve', lambda e: e.tensor_tensor(out=vv(V_FI), in0=vv(V_FI), in1=vv(V_T3), op=ALU.subtract))
        tiny('dve', lambda e: e.tensor_tensor(out=vv(V_FI), in0=vv(V_FI), in1=vv(V_T1), op=ALU.mult))
        tiny('dve', lambda e: e.tensor_tensor(out=vv(V_T1), in0=vv(V_FR), in1=vv(V_FR), op=ALU.mult))
        tiny('dve', lambda e: e.tensor_tensor(out=vv(V_T2), in0=vv(V_FI), in1=vv(V_FI), op=ALU.mult))
        tiny('dve', lambda e: e.tensor_tensor(out=vv(V_T1), in0=vv(V_T1), in1=vv(V_T2), op=ALU.add))
        tiny('dve', lambda e: e.reciprocal(out=vv(V_T1), in_=vv(V_T1)))
        tiny('dve', lambda e: e.tensor_tensor(out=vv(V_IFR), in0=vv(V_FR), in1=vv(V_T1), op=ALU.mult))
        tiny('dve', lambda e: e.tensor_tensor(out=vv(V_IFI), in0=vv(V_FI), in1=vv(V_T1), op=ALU.mult))
        tiny('dve', lambda e: e.tensor_scalar(out=vv(V_IFI), in0=vv(V_IFI), scalar1=-1.0, scalar2=None, op0=ALU.mult))
        tiny('dve', lambda e: e.tensor_scalar(out=vv(V_NFI), in0=vv(V_FI), scalar1=-1.0, scalar2=None, op0=ALU.mult))

        bpk = [ar.alloc([128, 2, 4, 128], BF16) for _ in range(2)]
        wg = ar.alloc([128, NKS, 2, 128], BF16)
        kb.dma('pool', wg.ap, wglu_in[l], W=[wg], max_dma_last_dim=4096)
        dc = ar.alloc([128, NKS])
        kb.dma('sp', dc.ap, dcol_in[l], W=[dc])
        st0 = ar.alloc([128, 2, NPAIR, NS])
        kb.dma('sp', st0.ap, st0_in[l], W=[st0])
        sto = ar.alloc([128, 2, NPAIR, 1 + NS])
        uT = [ar.alloc([128, T], BF16) for _ in range(1)]
        cpf = [ar.alloc([128, 2, 4, 128]) for _ in range(1)]
        cf = [ar.alloc([128, 2, 4, 128], BF16) for _ in range(2)]
        cft = ar.alloc([128, 128])
        RT = [ar.alloc([128, TQ]) for _ in range(4)]
        NW = 4
        AT = [ar.alloc([128, TQ]) for _ in range(NW)]
        A2 = [ar.alloc([128, TQ]) for _ in range(NW)]
        NF = [ar.alloc([128, TQ]) for _ in range(NW)]
        NA = [ar.alloc([128, TQ]) for _ in range(NW)]
        CS = [ar.alloc([128, TQ]) for _ in range(NW)]
        SN = [ar.alloc([128, TQ]) for _ in range(NW)]
        PW1 = [ar.alloc([128, TQ]) for _ in range(2)]
        PW2 = [ar.alloc([128, TQ]) for _ in range(2)]
        THO = ar.alloc([128, SEQ // TQ, NPAIR])
        HPI = ar.alloc([128, 1])
        kb.do('dve', lambda e: e.memset(HPI.ap, math.pi / 2), W=[HPI])
        for tq_ in range(SEQ // TQ):
            kb.do('dve', lambda e, tq_=tq_: e.tensor_scalar(out=THO.ap[:, tq_, :], in0=vv(V_A), scalar1=float(tq_ * TQ), scalar2=None, op0=ALU.mult),
                  R=[v], W=[THO])
        W1 = [ar.alloc([128, TQ]) for _ in range(NW)]
        W2 = [ar.alloc([128, TQ]) for _ in range(NW)]
        XR = [ar.alloc([128, TQ]) for _ in range(NW)]
        XI = [ar.alloc([128, TQ]) for _ in range(NW)]
        SR = [ar.alloc([128, TQ]) for _ in range(NW)]
        SI = [ar.alloc([128, TQ]) for _ in range(NW)]
        SB2R = [ar.alloc([128, TQ], BF16) for _ in range(4)]
        SB2I = [ar.alloc([128, TQ], BF16) for _ in range(4)]
        carry = ar.alloc([128, 2, NPAIR])
        aoff = ar.alloc([128, NPAIR])
        FIN = [ar.alloc([128, 8]) for _ in range(2)]
        ssbb = ar.alloc([128, 2, 4, NS], BF16)
        sw = ar.alloc([128, 3, 4, NS])
        craw = ar.alloc([128, 2, 4, 128], BF16)
        TS = ar.alloc([128, 2, NPAIR, NS])
        tsw = ar.alloc([128, 2, NPAIR, NS])
        abrb = v.ap[:, V_ABR, :].unsqueeze(2).broadcast_to([128, NPAIR, NS])
        abib = v.ap[:, V_ABI, :].unsqueeze(2).broadcast_to([128, NPAIR, NS])
        kb.do('dve', lambda e: e.tensor_tensor(out=TS.ap[:, 0], in0=st0.ap[:, 0], in1=abrb, op=ALU.mult), R=[st0, v], W=[TS])
        kb.do('dve', lambda e: e.tensor_tensor(out=tsw.ap[:, 0], in0=st0.ap[:, 1], in1=abib, op=ALU.mult), R=[st0, v], W=[tsw])
        kb.do('dve', lambda e: e.tensor_tensor(out=TS.ap[:, 0], in0=TS.ap[:, 0], in1=tsw.ap[:, 0], op=ALU.subtract), R=[tsw], W=[TS])
        kb.do('dve', lambda e: e.tensor_tensor(out=TS.ap[:, 1], in0=st0.ap[:, 0], in1=abib, op=ALU.mult), R=[st0, v], W=[TS])
        kb.do('dve', lambda e: e.tensor_tensor(out=tsw.ap[:, 1], in0=st0.ap[:, 1], in1=abrb, op=ALU.mult), R=[st0, v], W=[tsw])
        kb.do('dve', lambda e: e.tensor_tensor(out=TS.ap[:, 1], in0=TS.ap[:, 1], in1=tsw.ap[:, 1], op=ALU.add), R=[tsw], W=[TS])
        yst = [ar.alloc([128, T]) for _ in range(1)]
        EY = ar.alloc([128, T])
        E2 = [ar.alloc([128, 512]) for _ in range(1)]
        GB = [ar.alloc([128, 512], BF16) for _ in range(1)]
        zb = Buf(ZBd)
        ssB = Buf(SSd)
        kb.do('pool', lambda e: e.memset(carry.ap, 0.0), W=[carry])
        NTQ = SEQ // TQ

        deferred = []

        def run_deferred():
            for lst in deferred:
                kb.flush_interleaved(lst)
            del deferred[:]

        for kt in range(NKS):
            u = uT[kt % len(uT)]
            kb.dma('sp', u.ap, ZBd[AW + 2 * KW + kt * 128:AW + 2 * KW + (kt + 1) * 128, :], R=[zb], W=[u])
            cp_, cf_ = cpf[0], cf[kt % 2]
            bp = bpk[kt % 2]
            for ri_ in range(2):
                kb.dma('pool', bp.ap[:, ri_], bpad_in[l, :, ri_, kt * 4:(kt + 1) * 4, :], W=[bp])
            kb.dma('sp', cp_.ap, cpad_in[l, :, kt], W=[cp_])
            kb.do('act', lambda e: e.activation(out=craw.ap[:, 0], in_=cp_.ap[:, 0], func=AF.Copy), R=[cp_], W=[craw])
            kb.do('act', lambda e: e.activation(out=craw.ap[:, 1], in_=cp_.ap[:, 1], func=AF.Copy, scale=-1.0), R=[cp_], W=[craw])
            for pr in range(4):
                q = kt * 4 + pr
                fr, fi = v.ap[:, V_FR, q:q + 1], v.ap[:, V_FI, q:q + 1]
                nfi = v.ap[:, V_NFI, q:q + 1]
                kb.do('dve', lambda e, pr=pr, fi=fi: e.tensor_scalar(out=cft.ap, in0=cp_.ap[:, 1, pr, :], scalar1=fi, scalar2=None, op0=ALU.mult),
                      R=[cp_, v], W=[cft])
                kb.do('dve', lambda e, pr=pr, fr=fr: e.scalar_tensor_tensor(out=cf_.ap[:, 0, pr, :], in0=cp_.ap[:, 0, pr, :], scalar=fr, in1=cft.ap,
                                                                         op0=ALU.mult, op1=ALU.subtract), R=[cp_, v, cft], W=[cf_])
                kb.do('dve', lambda e, pr=pr, fr=fr: e.tensor_scalar(out=cft.ap, in0=cp_.ap[:, 1, pr, :], scalar1=fr, scalar2=-1.0, op0=ALU.mult,
                                                                  op1=ALU.mult), R=[cp_, v, cf_], W=[cft])
                kb.do('dve', lambda e, pr=pr, nfi=nfi: e.scalar_tensor_tensor(out=cf_.ap[:, 1, pr, :], in0=cp_.ap[:, 0, pr, :], scalar=nfi, in1=cft.ap,
                                                                           op0=ALU.mult, op1=ALU.add), R=[cp_, v, cft], W=[cf_])
            ys = yst[kt % len(yst)]
            for tq in range(NTQ + 1):
                samp = (tq == NTQ)
                t0 = tq * TQ
                n = NS if samp else TQ
                ypb = PS[4 + nxt('ypb', 2)]
                pend = []
                if samp:
                    run_deferred()
                    xs_ = PS[0]
                    for pr in range(4):
                        for ri in range(2):
                            c0_ = (ri * 4 + pr) * NS
                            kb.do('pe', lambda e, ri=ri, pr=pr, c0_=c0_: e.matmul(xs_.ap[:, c0_:c0_ + NS], lhsT=bp.ap[:, ri, pr, :], rhs=u.ap[:, t0:t0 + NS],
                                                                               start=True, stop=True), R=[bp, u], W=[xs_])
                    xv_ = xs_.ap[:, 0:8 * NS].rearrange("p (r q n) -> p r q n", r=2, q=4)
                    frb = v.ap[:, V_FR, kt * 4:kt * 4 + 4].unsqueeze(2).broadcast_to([128, 4, NS])
                    fib = v.ap[:, V_FI, kt * 4:kt * 4 + 4].unsqueeze(2).broadcast_to([128, 4, NS])
                    wr, wi, wt_ = sw.ap[:, 0], sw.ap[:, 1], sw.ap[:, 2]
                    kb.do('dve', lambda e: e.tensor_tensor(out=wr, in0=xv_[:, 0], in1=frb, op=ALU.mult), R=[xs_, v], W=[sw])
                    kb.do('dve', lambda e: e.tensor_tensor(out=wt_, in0=xv_[:, 1], in1=fib, op=ALU.mult), R=[xs_, v], W=[sw])
                    kb.do('dve', lambda e: e.tensor_tensor(out=wr, in0=wr, in1=wt_, op=ALU.subtract), R=[], W=[sw])
                    kb.do('dve', lambda e: e.tensor_tensor(out=wi, in0=xv_[:, 0], in1=fib, op=ALU.mult), R=[xs_, v], W=[sw])
                    kb.do('dve', lambda e: e.tensor_tensor(out=wt_, in0=xv_[:, 1], in1=frb, op=ALU.mult), R=[xs_, v], W=[sw])
                    kb.do('dve', lambda e: e.tensor_tensor(out=wi, in0=wi, in1=wt_, op=ALU.add), R=[], W=[sw])
                    kb.do('dve', lambda e: e.tensor_tensor(out=sto.ap[:, 0, kt * 4:kt * 4 + 4, 1:1 + NS], in0=wr, in1=TS.ap[:, 0, kt * 4:kt * 4 + 4, :], op=ALU.add),
                          R=[sw, TS], W=[sto])
                    kb.do('dve', lambda e: e.tensor_tensor(out=sto.ap[:, 1, kt * 4:kt * 4 + 4, 1:1 + NS], in0=wi, in1=TS.ap[:, 1, kt * 4:kt * 4 + 4, :], op=ALU.add),
                          R=[sw, TS], W=[sto])
                    kb.do('act', lambda e: e.activation(out=ssbb.ap, in_=sto.ap[:, :, kt * 4:kt * 4 + 4, 1:1 + NS], func=AF.Copy), R=[sto], W=[ssbb])
                for pr in range(4):
                    q = kt * 4 + pr
                    if not samp:
                        kb.begin_buffer()
                        xps_r, xps_i = PS[2 * nxt('xps', 2)], None
                        xps_i = PS[PS.index(xps_r) + 1]
                        for ri, xp in ((0, xps_r), (1, xps_i)):
                            kb.do('pe', lambda e, ri=ri, xp=xp, q=q: e.matmul(xp.ap[:, 0:n], lhsT=bp.ap[:, ri, pr, :], rhs=u.ap[:, t0:t0 + n],
                                                                           start=True, stop=True), R=[bp, u], W=[xp])
                    _k = nxt('sb2', 4)
                    s2r, s2i = SB2R[_k], SB2I[_k]
                    if samp:
                        rhs_r, rhs_i = ssbb.ap[:, 0, pr, :], ssbb.ap[:, 1, pr, :]
                        rd = [ssbb]
                    else:
                        fin = FIN[pr % 2]
                        w = nxt('ssw', NW)
                        w1, w2, xr_, xi_, sr_, si_ = W1[w], W2[w], XR[w], XI[w], SR[w], SI[w]
                        tw = w
                        pw1, pw2 = PW1[pr % 2], PW2[pr % 2]
                        at, a2, nf, na, cs, sn = AT[tw], A2[tw], NF[tw], NA[tw], CS[tw], SN[tw]
                        rt = RT[pr]
                        if tq == 0:
                            kb.do('act', lambda e, rt=rt, q=q: e.activation(out=rt.ap, in_=IOT.ap[:, 0:TQ], func=AF.Identity, scale=0.0,
                                                                           bias=v.ap[:, V_R, q:q + 1]), R=[IOT, v], W=[rt])
                        kb.do('act', lambda e, at=at, q=q: e.activation(out=at.ap, in_=IOT.ap[:, 0:TQ], func=AF.Identity, scale=v.ap[:, V_A, q:q + 1],
                                                                       bias=THO.ap[:, tq, q:q + 1]), R=[IOT, v, THO], W=[at])
                        kb.do('dve', lambda e, at=at, a2=a2: e.tensor_scalar(out=a2.ap, in0=at.ap, scalar1=MAGIC, scalar2=None, op0=ALU.add), R=[at], W=[a2])
                        kb.do('dve', lambda e, at=at, a2=a2, nf=nf: e.scalar_tensor_tensor(out=nf.ap, in0=a2.ap, scalar=MAGIC, in1=at.ap, op0=ALU.subtract,
                                                                                       op1=ALU.subtract), R=[at, a2], W=[nf])
                        kb.do('act', lambda e, nf=nf, sn=sn: e.activation(out=sn.ap, in_=nf.ap, func=AF.Sin, scale=-TWO_PI), R=[nf], W=[sn])
                        kb.do('act', lambda e, nf=nf, na=na: e.activation(out=na.ap, in_=nf.ap, func=AF.Sin, scale=-TWO_PI / 2), R=[nf], W=[na])
                        kb.do('act', lambda e, na=na: e.activation(out=na.ap, in_=na.ap, func=AF.Square), R=[], W=[na])
                        kb.do('act', lambda e, na=na, cs=cs: e.activation(out=cs.ap, in_=na.ap, func=AF.Identity, scale=-2.0, bias=1.0), R=[na], W=[cs])
                        kb.mark('M')
                        kb.do('dve', lambda e, cs=cs, w1=w1: e.tensor_tensor(out=w1.ap, in0=cs.ap, in1=xps_r.ap[:, 0:n], op=ALU.mult), R=[cs, xps_r], W=[w1])
                        kb.do('dve', lambda e, sn=sn, w2=w2: e.tensor_tensor(out=w2.ap, in0=sn.ap, in1=xps_i.ap[:, 0:n], op=ALU.mult), R=[sn, xps_i], W=[w2])
                        kb.do('dve', lambda e, w1=w1, w2=w2, xr_=xr_: e.tensor_tensor(out=xr_.ap, in0=w1.ap, in1=w2.ap, op=ALU.add), R=[w1, w2], W=[xr_])
                        kb.do('dve', lambda e, cs=cs, w1=w1: e.tensor_tensor(out=w1.ap, in0=cs.ap, in1=xps_i.ap[:, 0:n], op=ALU.mult), R=[cs, xps_i, xr_], W=[w1])
                        kb.do('dve', lambda e, sn=sn, w2=w2: e.tensor_tensor(out=w2.ap, in0=sn.ap, in1=xps_r.ap[:, 0:n], op=ALU.mult), R=[sn, xps_r, xr_], W=[w2])
                        kb.do('dve', lambda e, w1=w1, w2=w2, xi_=xi_: e.tensor_tensor(out=xi_.ap, in0=w1.ap, in1=w2.ap, op=ALU.subtract), R=[w1, w2], W=[xi_])
                        kb.do('dve', lambda e, rt=rt, xr_=xr_, sr_=sr_, q=q: e.tensor_tensor_scan(out=sr_.ap, data0=rt.ap, data1=xr_.ap,
                                                                                             initial=carry.ap[:, 0, q:q + 1], op0=ALU.mult, op1=ALU.add),
                              R=[rt, xr_, carry], W=[sr_])
                        kb.do('dve', lambda e, rt=rt, xi_=xi_, si_=si_, q=q: e.tensor_tensor_scan(out=si_.ap, data0=rt.ap, data1=xi_.ap,
                                                                                             initial=carry.ap[:, 1, q:q + 1], op0=ALU.mult, op1=ALU.add),
                              R=[rt, xi_, carry], W=[si_])
                        kb.do('act', lambda e, sr_=sr_, q=q: e.activation(out=carry.ap[:, 0, q:q + 1], in_=sr_.ap[:, TQ - 1:TQ], func=AF.Copy), R=[sr_], W=[carry])
                        kb.do('act', lambda e, si_=si_, q=q: e.activation(out=carry.ap[:, 1, q:q + 1], in_=si_.ap[:, TQ - 1:TQ], func=AF.Copy), R=[si_], W=[carry])
                        kb.mark('R')
                        kb.do('dve', lambda e: e.tensor_tensor(out=w1.ap, in0=cs.ap, in1=sr_.ap, op=ALU.mult), R=[cs, sr_], W=[w1])
                        kb.do('dve', lambda e: e.tensor_tensor(out=w2.ap, in0=sn.ap, in1=si_.ap, op=ALU.mult), R=[sn, si_], W=[w2])
                        kb.do('dve', lambda e: e.tensor_tensor(out=s2r.ap, in0=w1.ap, in1=w2.ap, op=ALU.subtract), R=[w1, w2], W=[s2r])
                        if tq == NTQ - 1:
                            kb.do('dve', lambda e: e.tensor_tensor(out=fin.ap[:, 0:1], in0=w1.ap[:, TQ - 1:TQ], in1=w2.ap[:, TQ - 1:TQ],
                                                                   op=ALU.subtract), R=[w1, w2], W=[fin])
                        kb.do('pool', lambda e: e.tensor_tensor(out=pw1.ap, in0=cs.ap, in1=si_.ap, op=ALU.mult), R=[cs, si_], W=[pw1])
                        kb.do('pool', lambda e: e.tensor_tensor(out=pw2.ap, in0=sn.ap, in1=sr_.ap, op=ALU.mult), R=[sn, sr_], W=[pw2])
                        kb.do('pool', lambda e: e.tensor_tensor(out=s2i.ap, in0=pw1.ap, in1=pw2.ap, op=ALU.add), R=[pw1, pw2], W=[s2i])
                        if tq == NTQ - 1:
                            fr, fi = v.ap[:, V_FR, q:q + 1], v.ap[:, V_FI, q:q + 1]
                            kb.do('dve', lambda e, w1=pw1, w2=pw2: e.tensor_tensor(out=fin.ap[:, 1:2], in0=w1.ap[:, TQ - 1:TQ], in1=w2.ap[:, TQ - 1:TQ],
                                                                               op=ALU.add), R=[pw1, pw2], W=[fin])
                            kb.do('dve', lambda e, fi=fi: e.tensor_scalar(out=fin.ap[:, 2:3], in0=fin.ap[:, 1:2], scalar1=fi, scalar2=None, op0=ALU.mult), R=[v], W=[fin])
                            kb.do('dve', lambda e, fr=fr, q=q: e.scalar_tensor_tensor(out=sto.ap[:, 0, q, 0:1], in0=fin.ap[:, 0:1], scalar=fr, in1=fin.ap[:, 2:3],
                                                                                   op0=ALU.mult, op1=ALU.subtract), R=[v, fin], W=[sto])
                            kb.do('dve', lambda e, fr=fr: e.tensor_scalar(out=fin.ap[:, 2:3], in0=fin.ap[:, 1:2], scalar1=fr, scalar2=None, op0=ALU.mult), R=[v], W=[fin])
                            kb.do('dve', lambda e, fi=fi, q=q: e.scalar_tensor_tensor(out=sto.ap[:, 1, q, 0:1], in0=fin.ap[:, 0:1], scalar=fi, in1=fin.ap[:, 2:3],
                                                                                   op0=ALU.mult, op1=ALU.add), R=[v, fin], W=[sto])
                        rhs_r, rhs_i = s2r.ap, s2i.ap
                        rd = [s2r, s2i]
                    cw_ = craw if samp else cf_
                    kb.do('pe', lambda e, pr=pr, rhs_r=rhs_r, ypb=ypb: e.matmul(ypb.ap[:, 0:n], lhsT=cw_.ap[:, 0, pr, :], rhs=rhs_r,
                                                                              start=(pr == 0), stop=False), R=[cw_] + rd, W=[ypb])
                    kb.do('pe', lambda e, pr=pr, rhs_i=rhs_i, ypb=ypb: e.matmul(ypb.ap[:, 0:n], lhsT=cw_.ap[:, 1, pr, :], rhs=rhs_i,
                                                                              start=False, stop=(pr == 3)), R=[cw_] + rd, W=[ypb])
                    if not samp:
                        pend.append(kb.end_buffer())
                        if len(pend) == 2:
                            def _split(x):
                                iM = [k_ for k_, o_ in enumerate(x) if o_[0] == 'mark' and o_[1] == 'M'][0]
                                iR = [k_ for k_, o_ in enumerate(x) if o_[0] == 'mark' and o_[1] == 'R'][0]
                                return x[:iM], x[iM:iR], x[iR:]
                            parts = [_split(x) for x in pend]
                            kb.flush_interleaved([p_[0] for p_ in parts])
                            run_deferred()
                            kb.flush_interleaved([p_[1] for p_ in parts])
                            deferred.append([p_[2] for p_ in parts])
                            pend = []
                if not samp:
                    kb.begin_buffer()
                kb.do('dve', lambda e, ypb=ypb: e.scalar_tensor_tensor(out=EY.ap[:, t0:t0 + n], in0=u.ap[:, t0:t0 + n], scalar=dc.ap[:, kt:kt + 1],
                                                                     in1=ypb.ap[:, 0:n], op0=ALU.mult, op1=ALU.add), R=[u, dc, ypb], W=[EY])
                if not samp:
                    deferred.append([kb.end_buffer()])
            run_deferred()
            pieces = [(c0_, min(512, T - c0_)) for c0_ in range(0, T, 512)]
            for (c0_, n_) in pieces:
                e2, gb = E2[0], GB[0]
                kb.do('act', lambda e: e.activation(out=e2.ap[:, 0:n_], in_=EY.ap[:, c0_:c0_ + n_], func=AF.Square), R=[EY], W=[e2])
                kb.do('dve', lambda e: e.tensor_scalar(out=e2.ap[:, 0:n_], in0=e2.ap[:, 0:n_], scalar1=0.044715, scalar2=1.0, op0=ALU.mult, op1=ALU.add),
                      R=[], W=[e2])
                kb.do('dve', lambda e: e.tensor_tensor(out=e2.ap[:, 0:n_], in0=e2.ap[:, 0:n_], in1=EY.ap[:, c0_:c0_ + n_], op=ALU.mult), R=[EY], W=[e2])
                kb.do('act', lambda e: e.activation(out=e2.ap[:, 0:n_], in_=e2.ap[:, 0:n_], func=AF.Sigmoid, scale=GELU_C), R=[], W=[e2])
                kb.do('dve', lambda e: e.tensor_tensor(out=gb.ap[:, 0:n_], in0=e2.ap[:, 0:n_], in1=EY.ap[:, c0_:c0_ + n_], op=ALU.mult), R=[EY, e2], W=[gb])
                z1, z2 = PS[6], PS[7]
                kb.do('pe', lambda e: e.matmul(z1.ap[:, 0:n_], lhsT=wg.ap[:, kt, 0, :], rhs=gb.ap[:, 0:n_], start=True, stop=True), R=[wg, gb], W=[z1])
                kb.do('pe', lambda e: e.matmul(z2.ap[:, 0:n_], lhsT=wg.ap[:, kt, 1, :], rhs=gb.ap[:, 0:n_], start=True, stop=True), R=[wg, gb], W=[z2])
                kb.do('act', lambda e: e.activation(out=e2.ap[:, 0:n_], in_=z2.ap[:, 0:n_], func=AF.Sigmoid), R=[z2], W=[e2])
                kb.do('dve', lambda e: e.tensor_tensor(out=ys.ap[:, c0_:c0_ + n_], in0=e2.ap[:, 0:n_], in1=z1.ap[:, 0:n_], op=ALU.mult), R=[e2, z1], W=[ys])
            kb.dma('sp', SSd[kt * 128:(kt + 1) * 128, :], ys.ap, R=[ys], W=[ssB])
        outtoks.append(kb.dma('sp', st_out[l], sto.ap, R=[sto]))

    def evac_store(dst_d, row0_of, dt, scale_of=None, also_f32=None):
        stg = [ar.alloc([128, T], dt) for _ in range(2)]
        stf = [ar.alloc([128, T]) for _ in range(2)] if also_f32 is not None else None
        dB = Buf(dst_d)
        nch = len(chunks)

        def ep(mi, ci, t0, n, pb, pb2):
            s = stg[mi % 2]
            sc = 1.0 if scale_of is None else scale_of(mi)
            f32r = also_f32(mi) if also_f32 is not None else None
            if f32r is None:
                kb.do('act', lambda e: e.activation(out=s.ap[:, t0:t0 + n], in_=pb.ap[:, 0:n], func=AF.Copy, scale=sc), R=[pb], W=[s])
            else:
                sf = stf[mi % 2]
                kb.do('act', lambda e: e.activation(out=sf.ap[:, t0:t0 + n], in_=pb.ap[:, 0:n], func=AF.Copy), R=[pb], W=[sf])
                kb.do('dve', lambda e: e.tensor_scalar(out=s.ap[:, t0:t0 + n], in0=sf.ap[:, t0:t0 + n], scalar1=sc, scalar2=None, op0=ALU.mult),
                      R=[sf], W=[s])
            if ci == nch - 1:
                r0 = row0_of(mi)
                kb.dma('sp', dst_d[r0:r0 + 128, :], s.ap, R=[s], W=[dB])
                if f32r is not None:
                    dd, rr, nr = f32r
                    kb.dma('sp', dd[rr:rr + nr, :], stf[mi % 2].ap[0:nr, :], R=[stf[mi % 2]], W=[Buf(dd)])
        return ep

    def _layer(l):
            norm_phase(Xd, KTD, gains[l, 0], out='norm')
            new_phase()
            nmt = INW // 128

            if KW == 64:
                f32sel = lambda mi: (ZFd, 0, 128) if mi * 128 == AW else None
            else:
                f32sel = lambda mi: (ZFd, mi * 128 - AW, 128) if AW <= mi * 128 < AW + 2 * KW else None
            ep = evac_store(ZBd, lambda mi: mi * 128, BF16, scale_of=lambda mi: 0.125 if mi * 128 < AW else 1.0, also_f32=f32sel)
            go, _ = linear_phase(SRC_D, KTD, w_in[l], [m * 128 for m in range(nmt)], ep, tchunks=full_chunks, fresh=False)
            for mi in range(min(nmt, int(_os.environ.get('LIMIT_MT', '999')))):
                go(mi)
            kb.barrier()
            outtoks.append(kb.dma('sp', kvp_out[l], ZFd[:, SEQ - 128:SEQ]))
            import os
            if not os.environ.get('SKIP_ATT'):
                attention_phase(l)
            if not os.environ.get('SKIP_SSM'):
                ssm_phase(l)
            norm_phase(SSd, NKS, gssm_in[l], out='norm', dst_kt0=KTA)
            new_phase()
            ep = evac_store(Od, lambda mi: mi * 128, F32)
            go, _ = linear_phase(SRC_D, KTD, w_out[l], [m * 128 for m in range(KTD)], ep, tchunks=full_chunks, fresh=False)
            for mi in range(KTD):
                go(mi)
            norm_phase(Od, KTD, gains[l, 1], resid_d=Xd, out='norm', g2_ap=gains[l, 2])
            new_phase()
            stg = [ar.alloc([128, T], BF16) for _ in range(2)]
            sil = [ar.alloc([128, NMAX]) for _ in range(3)]
            aB = Buf(ACTd)

            def ep_gu(mi, ci, t0, n, pb, pb2):
                s = stg[mi % 2]
                sl = sil[nxt('sil', 3)]
                kb.do('act', lambda e: e.activation(out=sl.ap[:, 0:n], in_=pb.ap[:, 0:n], func=AF.Silu), R=[pb], W=[sl])
                kb.do('dve', lambda e: e.tensor_tensor(out=s.ap[:, t0:t0 + n], in0=sl.ap[:, 0:n], in1=pb2.ap[:, 0:n], op=ALU.mult), R=[sl, pb2], W=[s])
                if ci == len(chunks) - 1:
                    kb.dma('sp', ACTd[mi * 128:(mi + 1) * 128, :], s.ap, R=[s], W=[aB])
            go, _ = linear_phase(SRC_D, KTD, w_gu[l], [m * 128 for m in range(KTF)], ep_gu, W2cols=[DFF + m * 128 for m in range(KTF)],
                                 tchunks=full_chunks, fresh=False)
            for mi in range(KTF):
                go(mi)
            half = T // 2
            for hf in range(2):
                new_phase()
                SRC_F = src3(KTF, half)
                kb.dma('sp', SRC_F, ACTd[:, hf * half:(hf + 1) * half].rearrange("(k p) t -> p k t", p=128), W=[srcbuf])
                stg2 = [ar.alloc([128, half]) for _ in range(2)]
                oB = Buf(Od)
                hch = [(t0, n, t0 - hf * half) for (t0, n) in chunks[3 * hf:3 * hf + 3]]

                def ep_dn(mi, ci, t0, n, pb, pb2, stg2=stg2, oB=oB, hf=hf):
                    s = stg2[mi % 2]
                    kb.do('act', lambda e: e.activation(out=s.ap[:, t0 - hf * half:t0 - hf * half + n], in_=pb.ap[:, 0:n], func=AF.Copy), R=[pb], W=[s])
                    if ci == 2:
                        kb.dma('sp', Od[mi * 128:(mi + 1) * 128, hf * half:(hf + 1) * half], s.ap, R=[s], W=[oB])
                go, _ = linear_phase(SRC_F, KTF, w_dn[l], [m * 128 for m in range(KTD)], ep_dn, tchunks=hch, fresh=False)
                for mi in range(KTD):
                    go(mi)
            norm_phase(Od, KTD, gains[l, 3], resid_d=Xd, out='cast')
            new_phase()
            peb = ar.alloc([128, 2, T], BF16)
            for k2 in range(2):
                kb.dma('pool', peb.ap[:, k2, :], peT_in[l, k2 * 128:(k2 + 1) * 128, :], W=[peb], max_dma_last_dim=4096)
            wpp = [ar.alloc([128, 2, 128], BF16) for _ in range(2)]
            xrow = [ar.alloc([128, T]) for _ in range(2)]
            sg = [ar.alloc([128, NMAX]) for _ in range(3)]
            wppv = w_pp[l].rearrange("(k p) n -> p k n", p=128)
            xB = Buf(Xd)

            def ep_ple(mi, ci, t0, n, pb, pb2):
                xr_ = xrow[mi % 2]
                wp_ = wpp[mi % 2]
                if ci == 0:
                    kb.dma('pool', wp_.ap, wppv[:, :, mi * 128:(mi + 1) * 128], W=[wp_])
                    kb.dma('sp', xr_.ap, Xd[mi * 128:(mi + 1) * 128, :], R=[xB], W=[xr_])
                pj = PS[6 + nxt('pj', 2)]
                for kt in range(2):
                    kb.do('pe', lambda e, kt=kt: e.matmul(pj.ap[:, 0:n], lhsT=wp_.ap[:, kt, :], rhs=peb.ap[:, kt, t0:t0 + n], start=(kt == 0), stop=(kt == 1)),
                          R=[wp_, peb], W=[pj])
                s_ = sg[nxt('sg', 3)]
                kb.do('act', lambda e: e.activation(out=s_.ap[:, 0:n], in_=pb.ap[:, 0:n], func=AF.Sigmoid), R=[pb], W=[s_])
                kb.do('dve', lambda e: e.tensor_tensor(out=s_.ap[:, 0:n], in0=s_.ap[:, 0:n], in1=pj.ap[:, 0:n], op=ALU.mult), R=[pj], W=[s_])
                kb.do('dve', lambda e: e.tensor_tensor(out=xr_.ap[:, t0:t0 + n], in0=xr_.ap[:, t0:t0 + n], in1=s_.ap[:, 0:n], op=ALU.add), R=[s_], W=[xr_])
                if ci == len(chunks) - 1:
                    outtoks.append(kb.dma('sp', Xd[mi * 128:(mi + 1) * 128, :], xr_.ap, R=[xr_], W=[xB]))
            go, _ = linear_phase(SRC_D, KTD, w_pg[l], [m * 128 for m in range(KTD)], ep_ple, tchunks=full_chunks, fresh=False)
            for mi in range(KTD):
                go(mi)


    try:
        for l in range(DEPTH):
            _layer(l)
    except _Stop:
        print('stopped after phase', _stop_after)

    kb.barrier()
    for r in range(KTD):
        outtoks.append(kb.dma('sp', yT_out[r * 128:(r + 1) * 128, :], Xd[r * 128:(r + 1) * 128, :]))
    kb.barrier()
    if _os.environ.get('PHASE_LOG'):
        import json as _json
        _json.dump(_marks, open(_os.environ['PHASE_LOG'], 'w'))
    kb.emit()
    return nc


def _consts():
    c = np.zeros((128, 1152), np.float32)
    c[:, 0:128] = np.eye(128, dtype=np.float32)
    i = np.arange(128)[:, None]
    j = np.arange(256)[None, :]
    dist = (i - j + 128).astype(np.float32)
    valid = (dist >= 0) & (dist <= 128)
    dm = np.where(valid, dist, np.float32(BIG)).astype(np.float32)
    c[:, 128:384] = dm
    d0 = dm.copy()
    d0[:, 0:128] = BIG
    c[:, 384:640] = d0
    c[:, 640:1152] = np.arange(1, 513, dtype=np.float32)[None, :]
    return c


def prepare_inputs(C, inp):
    f = np.float32
    A = lambda k: np.asarray(inp[k], f)
    L = C.DEPTH
    NP_, NKS = C.NPAIR, C.NKS
    gains = np.stack([A(k) for k in ('g_pre_mix', 'g_post_mix', 'g_pre_ffn', 'g_post_ffn')], axis=1)
    gains = np.ascontiguousarray(gains.reshape(L, 4, C.KTD, 128).transpose(0, 1, 3, 2))
    gssm = np.ascontiguousarray(A('g_ssm_out').reshape(L, NKS, 128).transpose(0, 2, 1))
    gattn = np.ascontiguousarray(np.broadcast_to(A('g_attn_out')[:, None, :], (L, 128, C.AW)))
    gattnT = np.ascontiguousarray(A('g_attn_out').reshape(L, C.NH, 64).transpose(0, 2, 1))
    sinks = np.ascontiguousarray(np.broadcast_to(A('attn_sinks')[:, None, :], (L, 128, C.NH)))
    are = A('ssm_a_re').reshape(L, NP_, 2, 64).transpose(0, 2, 3, 1).reshape(L, 128, NP_)
    aim = A('ssm_a_im').reshape(L, NP_, 2, 64).transpose(0, 2, 3, 1).reshape(L, 128, NP_)
    ldt = np.broadcast_to(A('ssm_log_dt').reshape(L, NP_, 2, 1), (L, NP_, 2, 64)).transpose(0, 2, 3, 1).reshape(L, 128, NP_)
    ssm_ps = np.ascontiguousarray(np.stack([are, aim, ldt], axis=2))
    bpad = np.zeros((L, 128, 2, NP_, 128), f)
    for ri, key in enumerate(('ssm_b_re', 'ssm_b_im')):
        b = A(key)
        for g in range(C.NG):
            q, gh, g8 = g // 2, g % 2, g % 8
            bpad[:, g8 * 16:(g8 + 1) * 16, ri, q, gh * 64:(gh + 1) * 64] = b[:, g].transpose(0, 2, 1)
    cpad = np.zeros((L, 128, NKS, 2, 4, 128), f)
    for ri, key in enumerate(('ssm_c_re', 'ssm_c_im')):
        c = A(key)
        for g in range(C.NG):
            kt, pr, gh, g8 = g // 8, (g % 8) // 2, g % 2, g % 8
            cpad[:, gh * 64:(gh + 1) * 64, kt, ri, pr, g8 * 16:(g8 + 1) * 16] = c[:, g].transpose(0, 2, 1)
    dcol = np.ascontiguousarray(A('ssm_d').reshape(L, NKS, 128).transpose(0, 2, 1))
    wglu = np.zeros((L, 128, NKS, 2, 128), f)
    wg = A('ssm_w_glu')
    for g in range(C.NG):
        kt, g8 = g // 8, g % 8
        for hf in range(2):
            wglu[:, g8 * 16:(g8 + 1) * 16, kt, hf, g8 * 16:(g8 + 1) * 16] = wg[:, g, :, hf * 16:(hf + 1) * 16]
    shared = dict(
        w_in=A('w_in'), w_out=A('w_out'), w_gate_up=A('w_gate_up'), w_down=A('w_down'), w_ple_gate=A('w_ple_gate'),
        w_ple_proj=A('w_ple_proj'), gains=gains, gssm=gssm, gattn=gattn, gattnT=gattnT, sinks=sinks, ssm_ps=ssm_ps,
        bpad=bpad, cpad=cpad, dcol=dcol, wglu=wglu, consts=_consts())
    xp, xs = A('x_prompt'), A('x_sample')
    pp, psm = A('p_prompt'), A('p_sample')
    ck, cv = A('cache_k'), A('cache_v')
    sre, sim = A('state_ssm_re'), A('state_ssm_im')
    in_maps = []
    NS = C.NS
    for c in range(8):
        b = c % C.BATCH
        sl = slice(NS * b, NS * (b + 1))
        m = dict(shared)
        m['xT'] = np.ascontiguousarray(np.concatenate([xp[b].T, xs[sl, 0].T], axis=1))
        m['peT'] = np.ascontiguousarray(np.concatenate([pp[:, b].transpose(0, 2, 1), psm[:, sl, 0].transpose(0, 2, 1)], axis=2))
        ckc = ck[:, sl].reshape(L, NS, 128, C.KW)
        m['ck'] = np.ascontiguousarray(ckc)
        m['cv'] = np.ascontiguousarray(cv[:, sl].reshape(L, NS, 128, C.KW))
        m['ckT'] = np.ascontiguousarray(ckc.reshape(L, NS, 128, C.NKV, 64).transpose(0, 1, 3, 4, 2))
        st = np.stack([sre[:, sl], sim[:, sl]], axis=1)
        st = st.reshape(L, 2, NS, NP_, 2, 64).transpose(0, 4, 5, 1, 3, 2).reshape(L, 128, 2, NP_, NS)
        m['st0'] = np.ascontiguousarray(st)
        in_maps.append(m)
    return in_maps


def assemble(C, results):
    f = np.float32
    L, NS, B = C.DEPTH, C.NS, C.BATCH
    yp = np.zeros((B, C.SEQ, C.D), f)
    ys = np.zeros((C.DEC, 1, C.D), f)
    kp = np.zeros((L, B, 128, C.NKV, 64), f)
    vp = np.zeros_like(kp)
    srp = np.zeros((L, B, C.NG, 64), f)
    sip = np.zeros_like(srp)
    ksm = np.zeros((L, C.DEC, 128, C.NKV, 64), f)
    vsm = np.zeros_like(ksm)
    srs = np.zeros((L, C.DEC, C.NG, 64), f)
    sis = np.zeros_like(srs)
    for b in range(B):
        r = results[b]
        sl = slice(NS * b, NS * (b + 1))
        yT = r['yT']
        yp[b] = yT[:, :C.SEQ].T
        ys[sl, 0] = yT[:, C.SEQ:].T
        kv = r['kvp']
        kp[:, b] = kv[:, :C.KW].transpose(0, 2, 1).reshape(L, 128, C.NKV, 64)
        vp[:, b] = kv[:, C.KW:].transpose(0, 2, 1).reshape(L, 128, C.NKV, 64)
        ksm[:, sl] = r['ks'].reshape(L, NS, 128, C.NKV, 64)
        vsm[:, sl] = r['vs'].reshape(L, NS, 128, C.NKV, 64)
        st = r['st'].reshape(L, 2, 64, 2, C.NPAIR, 1 + NS)
        st = st.transpose(0, 3, 5, 4, 1, 2).reshape(L, 2, 1 + NS, C.NG, 64)
        srp[:, b], sip[:, b] = st[:, 0, 0], st[:, 1, 0]
        srs[:, sl], sis[:, sl] = st[:, 0, 1:], st[:, 1, 1:]
    return (yp, ys, kp, vp, srp, sip, ksm, vsm, srs, sis)


_CACHE = {}


def run(C, inp):
    key = (C.D, C.SEQ, C.DEPTH, C.BATCH, C.DEC)
    if key not in _CACHE:
        _CACHE[key] = build(C)
    in_maps = prepare_inputs(C, inp)
    res = run_bass_kernel_spmd(_CACHE[key], in_maps, core_ids=list(range(8)))
    return assemble(C, res.results)


def kernel(**inputs):
    return run(Cfg(), inputs)
```
